# Optimizing a Trainium2 kernel written in Bass

```python
import jax, jax.numpy as jnp
from jax import lax
import numpy as np

D_MODEL = 1024
BATCH = 8
SEQ = 8192
DEPTH = 4

CTX_LEN = 256
GRID_W = 64
N_MIXERS = 3
N_LAYERS_MLA = (DEPTH + 2) // 3
N_LAYERS_SSD = (DEPTH + 1) // 3
N_LAYERS_FNET = DEPTH // 3
N_MOD = 6
NORM_EPS = 1e-6

MLA_HEADS = 8
MLA_Q_LORA = 256
MLA_KV_LORA = 128
MLA_NOPE = 128
MLA_ROPE = 64
MLA_QK_DIM = MLA_NOPE + MLA_ROPE
MLA_V = 128
MLA_IN_DIM = MLA_Q_LORA + MLA_KV_LORA + MLA_ROPE
ROPE_THETA = 10000.0
ROPE_HALF = MLA_ROPE // 2
ROPE_AXIS_FREQS = ROPE_HALF // 2
Q_BLOCK = 128

SSD_INNER = 2 * D_MODEL
SSD_HEADDIM = 64
SSD_HEADS = SSD_INNER // SSD_HEADDIM
SSD_GROUPS = 4
SSD_HEADS_PER_GROUP = SSD_HEADS // SSD_GROUPS
SSD_STATE = 128
SSD_CONV = 3
SSD_CHUNK = 128
SSD_GN = SSD_GROUPS * SSD_STATE
SSD_CONV_DIM = SSD_INNER + 2 * SSD_GN
SSD_IN_DIM = SSD_INNER + SSD_CONV_DIM + 2 * SSD_HEADS

FNET_GROUPS = 4
FNET_GROUP_DIM = D_MODEL // FNET_GROUPS

N_EXPERTS = 16
EXPERT_FF = 1024
CAPACITY_FACTOR = 2

kernel_name = "hybrid_mla_ssd_fnet_ecmoe_diffusion"


def rmsnorm(x, g):
    xf = x.astype(jnp.float32)
    y = xf * lax.rsqrt(jnp.mean(xf * xf, axis=-1, keepdims=True) + NORM_EPS)
    return (y * g.astype(jnp.float32)).astype(x.dtype)


def modulate(h, shift, scale):
    return h * (1 + scale) + shift


def rope_2d_tables(n_tokens):
    rows_n = n_tokens // GRID_W
    row = jnp.repeat(jnp.arange(rows_n, dtype=jnp.float32), GRID_W)
    col = jnp.tile(jnp.arange(GRID_W, dtype=jnp.float32), rows_n)
    inv = ROPE_THETA ** (-jnp.arange(ROPE_AXIS_FREQS, dtype=jnp.float32) / ROPE_AXIS_FREQS)
    ang = jnp.concatenate([row[:, None] * inv, col[:, None] * inv], axis=-1)
    return jnp.cos(ang), jnp.sin(ang)


def apply_rope_tail(t, cos, sin):
    cs = cos[None, :, None, :].astype(t.dtype)
    sn = sin[None, :, None, :].astype(t.dtype)
    nope, r = t[..., :MLA_NOPE], t[..., MLA_NOPE:]
    r1, r2 = r[..., :ROPE_HALF], r[..., ROPE_HALF:]
    return jnp.concatenate([nope, r1 * cs - r2 * sn, r2 * cs + r1 * sn], axis=-1)


def mla_project(h, w_in, g_q, w_uq, g_kv, w_ukv, g_qn, g_kn):
    b, t, _ = h.shape
    a = h @ w_in
    q_a, kv_a, k_r = a[..., :MLA_Q_LORA], a[..., MLA_Q_LORA:MLA_Q_LORA + MLA_KV_LORA], a[..., MLA_Q_LORA + MLA_KV_LORA:]
    q = (rmsnorm(q_a, g_q) @ w_uq).reshape(b, t, MLA_HEADS, MLA_QK_DIM)
    kv = (rmsnorm(kv_a, g_kv) @ w_ukv).reshape(b, t, MLA_HEADS, MLA_NOPE + MLA_V)
    k_nope, v = kv[..., :MLA_NOPE], kv[..., MLA_NOPE:]
    k = jnp.concatenate([k_nope, jnp.broadcast_to(k_r[:, :, None, :], (b, t, MLA_HEADS, MLA_ROPE))], axis=-1)
    return rmsnorm(q, g_qn), rmsnorm(k, g_kn), v


def attend(q, k, v):
    s = jnp.einsum('bqhd,bkhd->bhqk', q, k).astype(jnp.float32) * (MLA_QK_DIM ** -0.5)
    p = jax.nn.softmax(s, axis=-1).astype(v.dtype)
    return jnp.einsum('bhqk,bkhd->bqhd', p, v)


def mla_mixer(hl, hc, w_in, g_q, w_uq, g_kv, w_ukv, g_qn, g_kn, w_o, need_ctx):
    b, t, _ = hl.shape
    cos, sin = rope_2d_tables(t)
    ql, kl, vl = mla_project(hl, w_in, g_q, w_uq, g_kv, w_ukv, g_qn, g_kn)
    ql, kl = apply_rope_tail(ql, cos, sin), apply_rope_tail(kl, cos, sin)
    qc, kc, vc = mla_project(hc, w_in, g_q, w_uq, g_kv, w_ukv, g_qn, g_kn)
    k_all = jnp.concatenate([kc, kl], axis=1)
    v_all = jnp.concatenate([vc, vl], axis=1)
    nb = t // Q_BLOCK
    qb = ql.reshape(b, nb, Q_BLOCK, MLA_HEADS, MLA_QK_DIM).swapaxes(0, 1)
    ob = lax.map(lambda blk: attend(blk, k_all, v_all), qb)
    out_l = ob.swapaxes(0, 1).reshape(b, t, MLA_HEADS * MLA_V) @ w_o
    out_c = None
    if need_ctx:
        out_c = attend(qc, kc, vc).reshape(b, hc.shape[1], MLA_HEADS * MLA_V) @ w_o
    return out_l, out_c


def depthwise_conv_centred(x, w, bias):
    ch = x.shape[-1]
    pad = (SSD_CONV - 1) // 2
    y = lax.conv_general_dilated(x, w[:, None, :].astype(x.dtype), window_strides=(1,),
                                 padding=[(pad, pad)], dimension_numbers=('NWC', 'WIO', 'NWC'),
                                 feature_group_count=ch)
    return y + bias.astype(x.dtype)


def ssd_in(h, w_in, conv_w, conv_b, dt_bias):
    b, t, _ = h.shape
    zxbcdt = h @ w_in
    z = zxbcdt[..., :SSD_INNER]
    xbc = jax.nn.silu(depthwise_conv_centred(zxbcdt[..., SSD_INNER:SSD_INNER + SSD_CONV_DIM], conv_w, conv_b))
    dt_raw = zxbcdt[..., SSD_INNER + SSD_CONV_DIM:]
    xs = xbc[..., :SSD_INNER].reshape(b, t, SSD_HEADS, SSD_HEADDIM)
    bm = xbc[..., SSD_INNER:SSD_INNER + SSD_GN].reshape(b, t, SSD_GROUPS, SSD_STATE)
    cm = xbc[..., SSD_INNER + SSD_GN:].reshape(b, t, SSD_GROUPS, SSD_STATE)
    dt = jax.nn.softplus(dt_raw.astype(jnp.float32) + dt_bias.reshape(-1).astype(jnp.float32)).reshape(b, t, 2, SSD_HEADS)
    return z, xs, bm, cm, dt[:, :, 0], dt[:, :, 1]


def ssd_scan(xs, dt, a_dir, bm, cm, init_state):
    b, t = xs.shape[:2]
    nc = t // SSD_CHUNK

    def to_chunks(u):
        return u.reshape((b, nc, SSD_CHUNK) + u.shape[2:]).swapaxes(0, 1)

    mask = jnp.tril(jnp.ones((SSD_CHUNK, SSD_CHUNK), dtype=bool))[None, :, :, None]

    def body(state, inp):
        xc, dtc, bc, cc = inp
        acum = jnp.cumsum(dtc * a_dir, axis=1)
        diff = acum[:, :, None, :] - acum[:, None, :, :]
        lmat = jnp.exp(jnp.where(mask, diff, -jnp.inf))
        cb = jnp.repeat(jnp.einsum('bign,bjgn->bijg', cc, bc), SSD_HEADS_PER_GROUP, axis=-1)
        xdt = xc * dtc[..., None]
        y_diag = jnp.einsum('bijh,bjhp->bihp', cb * lmat, xdt)
        ch = jnp.repeat(cc, SSD_HEADS_PER_GROUP, axis=2)
        bh = jnp.repeat(bc, SSD_HEADS_PER_GROUP, axis=2)
        y_off = jnp.einsum('bihn,bhpn->bihp', ch, state) * jnp.exp(acum)[..., None]
        decay = jnp.exp(acum[:, -1:, :] - acum)
        new_state = (state * jnp.exp(acum[:, -1, :])[:, :, None, None]
                     + jnp.einsum('bjhn,bjhp->bhpn', bh * decay[..., None], xdt))
        return new_state, y_diag + y_off

    f32 = jnp.float32
    inputs = (to_chunks(xs.astype(f32)), to_chunks(dt), to_chunks(bm.astype(f32)), to_chunks(cm.astype(f32)))
    final, ys = lax.scan(body, init_state, inputs)
    return ys.swapaxes(0, 1).reshape(b, t, SSD_HEADS, SSD_HEADDIM), final


def ssd_bidir(xs, bm, cm, dt_f, dt_b, a, init_f, init_b):
    flip = lambda u: jnp.flip(u, axis=1)
    y_f, s_f = ssd_scan(xs, dt_f, a[0], bm, cm, init_f)
    y_b, s_b = ssd_scan(flip(xs), flip(dt_b), a[1], flip(bm), flip(cm), init_b)
    return y_f + flip(y_b), s_f, s_b


def ssd_out(y, xs, z, d_skip, g_norm, w_out):
    b, t = xs.shape[:2]
    y = (y + xs.astype(jnp.float32) * d_skip.astype(jnp.float32)[:, None]).astype(xs.dtype)
    gated = (y.reshape(b, t, SSD_INNER) * jax.nn.silu(z)).reshape(b, t, SSD_GROUPS, SSD_INNER // SSD_GROUPS)
    normed = rmsnorm(gated, g_norm.reshape(SSD_GROUPS, SSD_INNER // SSD_GROUPS)).reshape(b, t, SSD_INNER)
    return normed @ w_out


def ssd_mixer(hl, hc, w_in, conv_w, conv_b, dt_bias, a_log, d_skip, g_norm, w_out, need_ctx):
    b = hl.shape[0]
    a = -jnp.exp(a_log.astype(jnp.float32))
    zero = jnp.zeros((b, SSD_HEADS, SSD_HEADDIM, SSD_STATE), jnp.float32)
    zc, xc, bc, cc, dcf, dcb = ssd_in(hc, w_in, conv_w, conv_b, dt_bias)
    yc, s_f, s_b = ssd_bidir(xc, bc, cc, dcf, dcb, a, zero, zero)
    zl, xl, bl, cl, dlf, dlb = ssd_in(hl, w_in, conv_w, conv_b, dt_bias)
    yl, _, _ = ssd_bidir(xl, bl, cl, dlf, dlb, a, s_f, s_b)
    out_l = ssd_out(yl, xl, zl, d_skip, g_norm, w_out)
    out_c = ssd_out(yc, xc, zc, d_skip, g_norm, w_out) if need_ctx else None
    return out_l, out_c


def fnet_mixer(h, w_o):
    b, t, d = h.shape
    hg = h.astype(jnp.float32).reshape(b, t, FNET_GROUPS, FNET_GROUP_DIM)
    mixed = jnp.fft.fft2(hg, axes=(1, 3), norm='ortho').real.astype(h.dtype)
    return mixed.reshape(b, t, d) @ w_o


def ec_moe(h, w_router, w_gate, w_up, w_down):
    b, t, d = h.shape
    cap = CAPACITY_FACTOR * t // N_EXPERTS
    aff = jax.nn.softmax(jnp.einsum('btd,de->bte', h, w_router).astype(jnp.float32), axis=-1)
    gate, idx = lax.top_k(jnp.swapaxes(aff, 1, 2), cap)
    xe = jax.vmap(lambda hb, ib: hb[ib])(h, idx)
    hid = jax.nn.silu(jnp.einsum('becd,edf->becf', xe, w_gate)) * jnp.einsum('becd,edf->becf', xe, w_up)
    ye = jnp.einsum('becf,efd->becd', hid, w_down) * gate[..., None].astype(h.dtype)
    return jax.vmap(lambda ib, yb: jnp.zeros((t, d), yb.dtype).at[ib.reshape(-1)].add(yb.reshape(-1, d)))(idx, ye)


def setup_inputs(seed: int = 0) -> dict:
    key = jax.random.key(seed)
    ks = iter(jax.random.split(key, 40))
    f32 = jnp.float32
    nrm = lambda shape, fan_in: jax.random.normal(next(ks), shape, f32) * (fan_in ** -0.5)
    gain = lambda shape: 1.0 + 0.02 * jax.random.normal(next(ks), shape, f32)
    small = lambda shape: 0.02 * jax.random.normal(next(ks), shape, f32)
    D = D_MODEL
    x = jax.random.normal(next(ks), (BATCH, SEQ, D), f32)
    c = jax.random.normal(next(ks), (BATCH, D), f32)
    ctx = jax.random.normal(next(ks), (BATCH, CTX_LEN, D), f32)
    c_ctx = jax.random.normal(next(ks), (D,), f32)
    w_mod = 0.5 * nrm((DEPTH, D, N_MOD * D), D)
    b_mod = small((DEPTH, N_MOD * D))
    g_mix = gain((DEPTH, D))
    g_ffn = gain((DEPTH, D))
    nA, nB, nC = N_LAYERS_MLA, N_LAYERS_SSD, N_LAYERS_FNET
    mla_w_in = nrm((nA, D, MLA_IN_DIM), D)
    mla_g_q = gain((nA, MLA_Q_LORA))
    mla_w_uq = nrm((nA, MLA_Q_LORA, MLA_HEADS * MLA_QK_DIM), MLA_Q_LORA)
    mla_g_kv = gain((nA, MLA_KV_LORA))
    mla_w_ukv = nrm((nA, MLA_KV_LORA, MLA_HEADS * (MLA_NOPE + MLA_V)), MLA_KV_LORA)
    mla_g_qn = gain((nA, MLA_QK_DIM))
    mla_g_kn = gain((nA, MLA_QK_DIM))
    mla_w_o = nrm((nA, MLA_HEADS * MLA_V, D), MLA_HEADS * MLA_V)
    ssd_w_in = nrm((nB, D, SSD_IN_DIM), D)
    ssd_conv_w = nrm((nB, SSD_CONV, SSD_CONV_DIM), SSD_CONV)
    ssd_conv_b = small((nB, SSD_CONV_DIM))
    dt0 = jnp.exp(jax.random.uniform(next(ks), (nB, 2, SSD_HEADS), f32, np.log(1e-3), np.log(1e-1)))
    ssd_dt_bias = dt0 + jnp.log(-jnp.expm1(-dt0))
    ssd_a_log = jnp.log(jax.random.uniform(next(ks), (nB, 2, SSD_HEADS), f32, 1.0, 16.0))
    ssd_d = gain((nB, SSD_HEADS))
    ssd_g_norm = gain((nB, SSD_INNER))
    ssd_w_out = nrm((nB, SSD_INNER, D), SSD_INNER)
    fnet_w_o = nrm((nC, D, D), D)
    moe_w_router = nrm((DEPTH, D, N_EXPERTS), D)
    moe_w_gate = nrm((DEPTH, N_EXPERTS, D, EXPERT_FF), D)
    moe_w_up = nrm((DEPTH, N_EXPERTS, D, EXPERT_FF), D)
    moe_w_down = nrm((DEPTH, N_EXPERTS, EXPERT_FF, D), EXPERT_FF)
    return {"x": x, "c": c, "ctx": ctx, "c_ctx": c_ctx, "w_mod": w_mod, "b_mod": b_mod,
            "g_mix": g_mix, "g_ffn": g_ffn,
            "mla_w_in": mla_w_in, "mla_g_q": mla_g_q, "mla_w_uq": mla_w_uq, "mla_g_kv": mla_g_kv,
            "mla_w_ukv": mla_w_ukv, "mla_g_qn": mla_g_qn, "mla_g_kn": mla_g_kn, "mla_w_o": mla_w_o,
            "ssd_w_in": ssd_w_in, "ssd_conv_w": ssd_conv_w, "ssd_conv_b": ssd_conv_b,
            "ssd_dt_bias": ssd_dt_bias, "ssd_a_log": ssd_a_log, "ssd_d": ssd_d,
            "ssd_g_norm": ssd_g_norm, "ssd_w_out": ssd_w_out, "fnet_w_o": fnet_w_o,
            "moe_w_router": moe_w_router, "moe_w_gate": moe_w_gate, "moe_w_up": moe_w_up,
            "moe_w_down": moe_w_down}


def reference(x, c, ctx, c_ctx, w_mod, b_mod, g_mix, g_ffn,
              mla_w_in, mla_g_q, mla_w_uq, mla_g_kv, mla_w_ukv, mla_g_qn, mla_g_kn, mla_w_o,
              ssd_w_in, ssd_conv_w, ssd_conv_b, ssd_dt_bias, ssd_a_log, ssd_d, ssd_g_norm, ssd_w_out,
              fnet_w_o, moe_w_router, moe_w_gate, moe_w_up, moe_w_down):
    cx = ctx
    silu_c = jax.nn.silu(c)
    silu_cc = jax.nn.silu(c_ctx)
    for i in range(DEPTH):
        kind, j = i % N_MIXERS, i // N_MIXERS
        last = i == DEPTH - 1
        mod_l = (silu_c @ w_mod[i] + b_mod[i])[:, None, :]
        mod_c = (silu_cc @ w_mod[i] + b_mod[i])[None, None, :]
        sh_a, sc_a, gt_a, sh_f, sc_f, gt_f = jnp.split(mod_l, N_MOD, axis=-1)
        csh_a, csc_a, cgt_a, csh_f, csc_f, cgt_f = jnp.split(mod_c, N_MOD, axis=-1)
        hl = modulate(rmsnorm(x, g_mix[i]), sh_a, sc_a)
        if kind == 0:
            hc = modulate(rmsnorm(cx, g_mix[i]), csh_a, csc_a)
            ol, oc = mla_mixer(hl, hc, mla_w_in[j], mla_g_q[j], mla_w_uq[j], mla_g_kv[j], mla_w_ukv[j],
                               mla_g_qn[j], mla_g_kn[j], mla_w_o[j], not last)
        elif kind == 1:
            hc = modulate(rmsnorm(cx, g_mix[i]), csh_a, csc_a)
            ol, oc = ssd_mixer(hl, hc, ssd_w_in[j], ssd_conv_w[j], ssd_conv_b[j], ssd_dt_bias[j],
                               ssd_a_log[j], ssd_d[j], ssd_g_norm[j], ssd_w_out[j], not last)
        else:
            ol = fnet_mixer(hl, fnet_w_o[j])
            oc = fnet_mixer(modulate(rmsnorm(cx, g_mix[i]), csh_a, csc_a), fnet_w_o[j]) if not last else None
        x = x + gt_a * ol
        x = x + gt_f * ec_moe(modulate(rmsnorm(x, g_ffn[i]), sh_f, sc_f),
                              moe_w_router[i], moe_w_gate[i], moe_w_up[i], moe_w_down[i])
        if not last:
            cx = cx + cgt_a * oc
            cx = cx + cgt_f * ec_moe(modulate(rmsnorm(cx, g_ffn[i]), csh_f, csc_f),
                                     moe_w_router[i], moe_w_gate[i], moe_w_up[i], moe_w_down[i])
    return x
```

```python
from contextlib import ExitStack
import numpy as np
import concourse.bass as bass
import concourse.mybir as mybir

F32 = mybir.dt.float32
BF16 = mybir.dt.bfloat16
I32 = mybir.dt.int32
U32 = mybir.dt.uint32
ALU = mybir.AluOpType
AF = mybir.ActivationFunctionType
AX = mybir.AxisListType

COMPUTE = ("pe", "act", "dve", "pool")
EPOCH = 30000
DMA_EPOCH = 2000


class Res:
    __slots__ = ("name", "last_w", "readers")

    def __init__(self, name):
        self.name = name
        self.last_w = None
        self.readers = {}


class Tile:
    def __init__(self, name, t):
        self.name = name
        self.t = t
        self._res = {}

    def r(self, key=None):
        x = self._res.get(key)
        if x is None:
            x = self._res[key] = Res(f"{self.name}:{key}")
        return x

    def __getitem__(self, k):
        return self.t[k]


class Prog:
    def __init__(self, nc, n_dma_sems=16):
        self.nc = nc
        self.es = ExitStack()
        self.streams = {e: [] for e in ("pe", "act", "dve", "pool", "sp")}
        self.cnt = {e: 0 for e in COMPUTE}
        self.waited = {e: {} for e in self.streams}
        self.sems = {}
        self.n_dma_sems = n_dma_sems
        self.dma_rr = {q: 0 for q in ("sp", "pool", "act")}
        self.dma_cnt = {}
        self.n_inst = 0
        self.scopes = []
        self._uid = 0

    def sbuf(self, name, shape, dt):
        t = self.es.enter_context(self.nc.sbuf_tensor(name, list(shape), dt))
        return Tile(name, t)

    def psum(self, name, shape, dt=F32):
        t = self.es.enter_context(self.nc.psum_tensor(name, list(shape), dt))
        return Tile(name, t)

    def dram(self, name, shape, dt, kind="Internal"):
        t = self.nc.dram_tensor(name, list(shape), dt, kind=kind)
        return Tile(name, t.ap())

    def _sem(self, key):
        s = self.sems.get(key)
        if s is None:
            s = self.sems[key] = self.es.enter_context(
                self.nc.semaphore("s_" + "_".join(str(k) for k in key)))
        return s

    def _deps(self, eng, R, W):
        deps = {}

        def add(tok, kind):
            if tok is None:
                return
            semkey, val, teng = tok
            if teng == eng and eng in COMPUTE:
                if eng == "pe" or kind != "raw":
                    return
            if deps.get(semkey, 0) < val:
                deps[semkey] = val

        for r in R:
            add(r.last_w, "raw")
        for w in W:
            add(w.last_w, "waw")
            for sk, (v, e) in w.readers.items():
                add((sk, v, e), "war")
        return deps

    def _emit_waits(self, eng, deps):
        wd = self.waited[eng]
        st = self.streams[eng]
        for semkey, val in deps.items():
            if wd.get(semkey, 0) >= val:
                continue
            wd[semkey] = val
            st.append(("wait", self._sem(semkey), val))

    def _commit(self, tok, R, W):
        semkey, val, eng = tok
        for r in R:
            cur = r.readers.get(semkey)
            if cur is None or cur[0] < val:
                r.readers[semkey] = (val, eng)
        for w in W:
            w.last_w = tok
            w.readers = {}

    @staticmethod
    def _resl(xs):
        out = []
        for x in xs:
            out.append(x.r() if isinstance(x, Tile) else x)
        return out

    def op(self, eng, fn, R=(), W=()):
        R = self._resl(R)
        W = self._resl(W)
        self._emit_waits(eng, self._deps(eng, R, W))
        i = self.cnt[eng]
        self.cnt[eng] = i + 1
        semkey = ("e", eng, i // EPOCH)
        val = i % EPOCH + 1
        tok = (semkey, val, eng)
        self.streams[eng].append(("op", fn, self._sem(semkey), 1))
        self._commit(tok, R, W)
        self.n_inst += 1
        return tok

    def dma(self, q, fn, R=(), W=()):
        R = self._resl(R)
        W = self._resl(W)
        s = self.dma_rr[q]
        self.dma_rr[q] = (s + 1) % self.n_dma_sems
        base = (q, s)
        m = self.dma_cnt.get(base, 0) + 1
        self.dma_cnt[base] = m

        def key(mi):
            ep = (mi - 1) // DMA_EPOCH
            return ("d", q, s, ep), 16 * ((mi - 1) % DMA_EPOCH + 1)

        deps = self._deps("dma:" + q, R, W)
        if m > 1:
            pk, pv = key(m - 1)
            if deps.get(pk, 0) < pv:
                deps[pk] = pv
        self._emit_waits(q, deps)
        semkey, val = key(m)
        tok = (semkey, val, "dma:" + q)
        self.streams[q].append(("op", fn, self._sem(semkey), 16))
        self._commit(tok, R, W)
        self.n_inst += 1
        return tok

    def final_wait(self, eng, toks):
        deps = {}
        for semkey, val, _ in toks:
            if deps.get(semkey, 0) < val:
                deps[semkey] = val
        self._emit_waits(eng, deps)

    def emit(self):
        nc = self.nc
        streams = self.streams

        def run(e, items):
            for it in items:
                if it[0] == "wait":
                    e.wait_ge(it[1], it[2])
                else:
                    it[1](e).then_inc(it[2], it[3])

        with nc.Block() as block:
            @block.tensor
            def _(e):
                run(e, streams["pe"])

            @block.scalar
            def _(e):
                run(e, streams["act"])

            @block.vector
            def _(e):
                run(e, streams["dve"])

            @block.gpsimd
            def _(e):
                run(e, streams["pool"])

            @block.sync
            def _(e):
                run(e, streams["sp"])

    def close(self):
        self.es.close()


def _push(self):
    self.scopes.append(ExitStack())


def _pop(self):
    self.barrier()
    self.scopes.pop().close()


def _sbuf(self, name, shape, dt):
    st = self.scopes[-1] if self.scopes else self.es
    self._uid += 1
    t = st.enter_context(self.nc.sbuf_tensor(f"{name}_{self._uid}", list(shape), dt))
    return Tile(name, t)


def _psum(self, name, shape, dt=F32):
    st = self.scopes[-1] if self.scopes else self.es
    self._uid += 1
    t = st.enter_context(self.nc.psum_tensor(f"{name}_{self._uid}", list(shape), dt))
    return Tile(name, t)


def _barrier(self):
    toks = {}
    for eng in COMPUTE:
        i = self.cnt[eng]
        if i > 0:
            toks[("e", eng, (i - 1) // EPOCH)] = (i - 1) % EPOCH + 1
    for (q, s), m in self.dma_cnt.items():
        toks[("d", q, s, (m - 1) // DMA_EPOCH)] = 16 * ((m - 1) % DMA_EPOCH + 1)
    for eng in self.streams:
        self._emit_waits(eng, dict(toks))


def _mm(self, out, lhsT, rhs, start=True, stop=True, R=(), W=()):
    return self.op("pe", lambda e: e.matmul(out, lhsT, rhs, start=start, stop=stop), R, W)


def _tr(self, out, in_, ident, R=(), W=()):
    return self.op("pe", lambda e: e.transpose(out, in_, ident), R, W)


def _act(self, out, in_, func, R=(), W=(), **kw):
    return self.op("act", lambda e: e.activation(out, in_, func, **kw), R, W)


def _tsc(self, eng, out, in0, s1, s2, op0, op1=None, R=(), W=(), **kw):
    if op1 is None:
        return self.op(eng, lambda e: e.tensor_scalar(out, in0, s1, None, op0, **kw), R, W)
    return self.op(eng, lambda e: e.tensor_scalar(out, in0, s1, s2, op0, op1, **kw), R, W)


def _tt(self, eng, out, in0, in1, op, R=(), W=()):
    return self.op(eng, lambda e: e.tensor_tensor(out, in0, in1, op), R, W)


def _stt(self, eng, out, in0, scalar, in1, op0, op1, R=(), W=()):
    return self.op(eng, lambda e: e.scalar_tensor_tensor(out, in0, scalar, in1, op0, op1), R, W)


def _cp(self, eng, out, in_, R=(), W=()):
    if eng == "act":
        return self.op("act", lambda e: e.copy(out, in_), R, W)
    return self.op(eng, lambda e: e.tensor_copy(out, in_), R, W)


def _red(self, eng, out, in_, op, R=(), W=(), axis=None):
    ax = AX.X if axis is None else axis
    return self.op(eng, lambda e: e.tensor_reduce(out, in_, ax, op), R, W)


def _memset(self, eng, out, val, W=()):
    return self.op(eng, lambda e: e.memset(out, val), (), W)


def _ld(self, q, out, in_, R=(), W=(), **kw):
    kw.setdefault("allow_slow_non_contiguous", True)
    return self.dma(q, lambda e: e.dma_start(out=out, in_=in_, **kw), R, W)


def _gather(self, out, in_, idx_ap, R=(), W=()):
    return self.dma("pool", lambda e: e.indirect_dma_start(
        out=out, out_offset=None, in_=in_,
        in_offset=bass.IndirectOffsetOnAxis(ap=idx_ap, axis=0)), R, W)


def _scatter(self, out, in_, idx_ap, R=(), W=(), add=False, bound=None):
    if add:
        def f(e):
            try:
                return e.indirect_dma_start(
                    out=out, out_offset=bass.IndirectOffsetOnAxis(ap=idx_ap, axis=0), in_=in_, in_offset=None,
                    compute_op=ALU.add, oob_is_err=True, bounds_check=self.get_bound_reg(e, bound))
            except Exception:
                print("SCATTER-ADD FAIL", out, in_, idx_ap, bound)
                raise
        return self.dma("pool", f, R, W)
    return self.dma("pool", lambda e: e.indirect_dma_start(
        out=out, out_offset=bass.IndirectOffsetOnAxis(ap=idx_ap, axis=0), in_=in_, in_offset=None), R, W)


def _recip(self, out, in_, R=(), W=()):
    return self.op("dve", lambda e: e.reciprocal(out, in_), R, W)


def _get_bound_reg(self, e, bound):
    if not hasattr(self, "_bregs"):
        self._bregs = {}
    r = self._bregs.get(bound)
    if r is None:
        r = self._bregs[bound] = e.to_reg(bound)
    return r


Prog.get_bound_reg = _get_bound_reg
Prog.recip = _recip
Prog.push = _push
Prog.pop = _pop
Prog.sbuf = _sbuf
Prog.psum = _psum
Prog.barrier = _barrier
Prog.mm = _mm
Prog.tr = _tr
Prog.act = _act
Prog.tsc = _tsc
Prog.tt = _tt
Prog.stt = _stt
Prog.cp = _cp
Prog.red = _red
Prog.memset = _memset
Prog.ld = _ld
Prog.gather = _gather
Prog.scatter = _scatter


from types import SimpleNamespace as NS

TA = 8448
NT = 66
D = 1024
NE = 16
EPS = 1e-6


class K:
    def __init__(self, nc, dbg=()):
        self.nc = nc
        self.P = P = Prog(nc)
        self.dbg = set(dbg)
        din = lambda n, s, dt=F32: P.dram(n, s, dt, kind="ExternalInput")
        self.x = din("x", [8192, D])
        self.ctx = din("ctx", [256, D])
        self.cc = din("cc", [2, D])
        self.w_mod = din("w_mod", [4, D, 6144])
        self.b_mod = din("b_mod", [4, 6144])
        self.g_mix = din("g_mix", [4, D])
        self.g_ffn = din("g_ffn", [4, D])
        self.mla_w_in = din("mla_w_in", [2, D, 512])
        self.mla_w_uq = din("mla_w_uq", [2, 256, 2048])
        self.mla_w_uk = din("mla_w_uk", [2, 128, 1024])
        self.mla_w_uv = din("mla_w_uv", [2, 128, 1024])
        self.mla_w_o = din("mla_w_o", [2, D, D])
        self.mla_gc = din("mla_gc", [2, 128, 8])
        self.rope_tab = din("rope_tab", [128, TA])
        self.cst = din("cst", [128, 1024])
        self.moe_wr = din("moe_wr", [4, D, NE])
        self.moe_wg = din("moe_wg", [4, NE, D, D])
        self.moe_wu = din("moe_wu", [4, NE, D, D])
        self.moe_wd = din("moe_wd", [4, NE, D, D])
        self.tokid = din("tokid", [128, NT], I32)
        self.ssd_w_in = din("ssd_w_in", [D, 5184])
        self.ssd_conv_w = din("ssd_conv_w", [3, 3072])
        self.ssd_conv_b = din("ssd_conv_b", [3072])
        self.ssd_dt_bias = din("ssd_dt_bias", [64])
        self.ssd_a_log = din("ssd_a_log", [2, 32])
        self.ssd_d = din("ssd_d", [32])
        self.ssd_g_norm = din("ssd_g_norm", [2048])
        self.ssd_w_out = din("ssd_w_out", [2048, D])
        self.cst2 = din("cst2", [128, 640])
        self.fnet_w_o = din("fnet_w_o", [D, D])
        self.fcst = din("fcst", [128, 2176])
        self.fM = din("fM", [128, 16384])
        self.tokid_f = din("tokid_f", [128, NT], I32)
        self.Yd = self.dscr("Yd", [2, TA, D], BF16)
        self.Zd = self.dscr("Zd", [2, 64, 128, D], BF16)
        self.xs_d = self.dscr("xs_d", [TA, 2048], BF16)
        self.B_d = self.dscr("B_d", [TA, 512], BF16)
        self.BT_d = self.dscr("BT_d", [512, TA], BF16)
        self.CT_d = self.dscr("CT_d", [512, TA], BF16)
        self.zs_d = self.dscr("zs_d", [TA, 2048], BF16)
        self.dt_d = self.dscr("dt_d", [TA, 64], F32)
        self.yf_d = self.dscr("yf_d", [TA, 2048], F32)
        self.yb_d = self.dscr("yb_d", [TA, 2048], F32)
        self.out = P.dram("out", [8192, D], F32, kind="ExternalOutput")
        self.XS = self.dscr("XS", [TA, D], F32)
        self.modd = self.dscr("modd", [4, 2, 6144], F32)
        self.hT = self.dscr("hT", [D, TA], BF16)
        self.QT = self.dscr("QT", [8, 192, TA], BF16)
        self.KT = self.dscr("KT", [8, 192, TA], BF16)
        self.V = self.dscr("V", [TA, D], BF16)
        self.OT = self.dscr("OT", [D, TA], BF16)
        self.hrow = self.dscr("hrow", [TA, D], BF16)
        self.affrow = self.dscr("affrow", [TA, NE], F32)
        self.idxd = self.dscr("idxd", [NE * 1280, 1], I32)
        self.ident = P.sbuf("ident", [128, 128], BF16)
        self.ones = P.sbuf("ones", [128, 128], BF16)
        self.fold = P.sbuf("fold", [128, 64], BF16)
        self.identf = P.sbuf("identf", [128, 128], F32)
        P.ld("pool", self.ident[:], self.cst[:, 0:128], R=[self.cst], W=[self.ident])
        P.ld("pool", self.ones[:], self.cst[:, 128:256], R=[self.cst], W=[self.ones])
        P.ld("pool", self.fold[:], self.cst[:, 256:320], R=[self.cst], W=[self.fold])
        P.ld("sp", self.identf[:], self.cst[:, 0:128], R=[self.cst], W=[self.identf])
        self.ustrict = P.sbuf("ustrict", [128, 128], BF16)
        P.ld("pool", self.ustrict[:], self.cst[:, 320:448], R=[self.cst], W=[self.ustrict])
        self.dumpc = P.sbuf("dumpc", [128, 1], F32)
        P.ld("sp", self.dumpc[:], self.cst[:, 448:449], R=[self.cst], W=[self.dumpc])
        self.eoffs = P.sbuf("eoffs", [128, NE], F32)
        P.ld("sp", self.eoffs[:], self.cst[:, 449:449 + NE], R=[self.cst], W=[self.eoffs])
        self.tokid_sb = P.sbuf("tokid_sb", [128, NT], I32)
        P.ld("sp", self.tokid_sb[:], self.tokid[:, :], R=[self.tokid], W=[self.tokid_sb])
        self.tokid_f_sb = P.sbuf("tokid_f_sb", [128, NT], I32)
        P.ld("sp", self.tokid_f_sb[:], self.tokid_f[:, :], R=[self.tokid_f], W=[self.tokid_f_sb])

    def dscr(self, name, shape, dt):
        kind = "ExternalOutput" if name in self.dbg else "Internal"
        return self.P.dram(name, shape, dt, kind=kind)

    def setup(self):
        P = self.P
        P.ld("sp", self.XS[0:256, :], self.ctx[:, :], R=[self.ctx], W=[self.XS])
        for j in range(8):
            P.ld("sp", self.XS[256 + j * 1024:256 + (j + 1) * 1024, :], self.x[j * 1024:(j + 1) * 1024, :],
                 R=[self.x], W=[self.XS])
        P.push()
        ccT = P.sbuf("ccT", [128, 8, 2], F32)
        ccs = P.sbuf("ccs", [128, 8, 2], F32)
        for m in range(2):
            P.ld("sp", ccT[:, :, m], self.cc[m].rearrange("(k p) -> p k", p=128), R=[self.cc], W=[ccT],
                 allow_slow_non_contiguous=True)
        P.act(ccs[:], ccT[:], AF.Silu, R=[ccT], W=[ccs])
        wms = [P.sbuf("wm", [128, 8, 512], F32) for _ in range(2)]
        bms = [P.sbuf("bm", [2, 512], F32) for _ in range(2)]
        mrs = [P.sbuf("mr", [2, 512], F32) for _ in range(2)]
        pss = [P.psum("psm", [2, 512]) for _ in range(2)]
        n = 0
        for i in range(4):
            for nb in range(12):
                wm, bm, mr, ps = wms[n % 2], bms[n % 2], mrs[n % 2], pss[n % 2]
                n += 1
                sl = slice(nb * 512, (nb + 1) * 512)
                P.ld("sp", wm[:], self.w_mod[i, :, sl].rearrange("(k p) n -> p k n", p=128), R=[self.w_mod], W=[wm])
                P.ld("sp", bm[:], self.b_mod[i:i + 1, sl].to_broadcast([2, 512]), R=[self.b_mod], W=[bm])
                for k in range(8):
                    P.mm(ps[:], ccs[:, k, :], wm[:, k, :], start=(k == 0), stop=(k == 7), R=[ccs, wm], W=[ps])
                P.tt("dve", mr[:], ps[:], bm[:], ALU.add, R=[ps, bm], W=[mr])
                P.ld("pool", self.modd[i, :, sl], mr[:], R=[mr], W=[self.modd])
        P.pop()

    def layer_consts(self, i):
        P = self.P
        L = NS()
        L.i = i
        modd = self.modd
        L.modcol = P.sbuf("modcol", [128, 2, 6, 8], F32)
        for m in range(2):
            P.ld("sp", L.modcol[:, m], modd[i, m].rearrange("(s k p) -> p s k", s=6, p=128), R=[modd], W=[L.modcol],
                 allow_slow_non_contiguous=True)
        gcol = P.sbuf("gcol", [128, 8], F32)
        P.ld("sp", gcol[:], self.g_mix[i].rearrange("(k p) -> p k", p=128), R=[self.g_mix], W=[gcol],
             allow_slow_non_contiguous=True)
        L.Ga = P.sbuf("Ga", [128, 2, 8], F32)
        for m in range(2):
            P.stt("dve", L.Ga[:, m], L.modcol[:, m, 1], 1.0, gcol[:], ALU.add, ALU.mult, R=[L.modcol, gcol], W=[L.Ga])
        def bc(name, src_ap, R):
            t = P.sbuf(name, [128, D], F32)
            P.ld("sp", t[:], src_ap.partition_broadcast(128), R=R, W=[t])
            return t
        L.gtf = [bc("gtf", modd[i, m, 5120:6144], [modd]) for m in range(2)]
        return L

    def tail_consts(self, L):
        P = self.P
        i = L.i
        modd = self.modd

        def bc(name, src_ap):
            t = P.sbuf(name, [128, D], F32)
            P.ld("sp", t[:], src_ap.partition_broadcast(128), R=[], W=[t])
            return t
        L.gta = [bc("gta", modd[i, m, 2048:3072]) for m in range(2)]
        L.Sf = [bc("Sf", modd[i, m, 3072:4096]) for m in range(2)]
        gffn = bc("gffn", self.g_ffn[i])
        L.Gf = []
        for m in range(2):
            t = bc("Gf", modd[i, m, 4096:5120])
            P.stt("dve", t[:], t[:], 1.0, gffn[:], ALU.add, ALU.mult, R=[t, gffn], W=[t])
            L.Gf.append(t)

    def rms_rstd(self, xt, sq_junk, ss, rstd, n):
        P = self.P
        P.act(sq_junk[:], xt[:], AF.Square, R=[xt], W=[sq_junk, ss], accum_out=ss[:])
        P.act(rstd[:], ss[:], AF.Sqrt, R=[ss], W=[rstd], scale=1.0 / n, bias=EPS)
        P.recip(rstd[:], rstd[:], R=[rstd], W=[rstd])

    def phase_A(self, L):
        P = self.P
        P.push()
        NB = 2
        xts = [P.sbuf("xt", [128, D], F32) for _ in range(NB)]
        xns = [P.sbuf("xn", [128, D], BF16) for _ in range(NB)]
        junk = P.sbuf("junk", [128, D], BF16)
        sss = [P.sbuf("ss", [128, 1], F32) for _ in range(NB)]
        rss = [P.sbuf("rs", [128, 1], F32) for _ in range(NB)]
        pts = [P.psum("pt", [128, 8, 128], BF16) for _ in range(NB)]
        hts = [P.sbuf("ht", [128, 8, 128], BF16) for _ in range(NB)]
        tmp = [P.sbuf("tmpA", [128, 8, 128], F32) for _ in range(NB)]
        for i in range(NT):
            b = i % NB
            m = 1 if i < 2 else 0
            xt, xn, ss, rs, pt, ht, tp = xts[b], xns[b], sss[b], rss[b], pts[b], hts[b], tmp[b]
            P.ld("sp", xt[:], self.XS[i * 128:(i + 1) * 128, :], R=[self.XS], W=[xt])
            self.rms_rstd(xt, junk, ss, rs, D)
            P.act(xn[:], xt[:], AF.Copy, R=[xt, rs], W=[xn], scale=rs[:])
            for k in range(8):
                P.tr(pt[:, k, :], xn[:, k * 128:(k + 1) * 128], self.ident[:], R=[xn, self.ident], W=[pt])
            P.tt("dve", tp[:], pt[:], L.Ga[:, m].unsqueeze(2).to_broadcast([128, 8, 128]), ALU.mult, R=[pt, L.Ga], W=[tp])
            P.tt("dve", ht[:], tp[:], L.modcol[:, m, 0].unsqueeze(2).to_broadcast([128, 8, 128]), ALU.add,
                 R=[tp, L.modcol], W=[ht])
            P.ld("pool", self.hT[:, i * 128:(i + 1) * 128].rearrange("(k p) t -> p k t", p=128), ht[:], R=[ht], W=[self.hT])
        P.pop()


    def cast_load(self, name, shape, src_ap, R):
        t = self.P.sbuf(name, shape, BF16)
        self.P.ld("pool", t[:], src_ap, R=R, W=[t])
        return t

    def mla_proj(self, L, j, need_ctx):
        P = self.P
        P.push()
        w_in = self.cast_load("w_in", [128, 8, 512], self.mla_w_in[j].rearrange("(k p) n -> p k n", p=128), [self.mla_w_in])
        w_uq = self.cast_load("w_uq", [128, 2, 2048], self.mla_w_uq[j].rearrange("(k p) n -> p k n", p=128), [self.mla_w_uq])
        w_uk = self.cast_load("w_uk", [128, 1024], self.mla_w_uk[j], [self.mla_w_uk])
        w_uv = self.cast_load("w_uv", [128, 1024], self.mla_w_uv[j], [self.mla_w_uv])
        gc = P.sbuf("gc", [128, 8], F32)
        gd = P.sbuf("gd", [128, 8], F32)
        P.ld("sp", gc[:], self.mla_gc[j], R=[self.mla_gc], W=[gc])
        P.tsc("dve", gd[:, 0:2], gc[:, 0:2], 16.0, None, ALU.mult, R=[gc], W=[gd])
        P.tsc("dve", gd[:, 2:3], gc[:, 2:3], float(np.sqrt(128.0)), None, ALU.mult, R=[gc], W=[gd])
        P.tsc("dve", gd[:, 3:5], gc[:, 3:5], 1.0, None, ALU.mult, R=[gc], W=[gd])
        P.tsc("dve", gd[:, 5:7], gc[:, 5:7], float(np.sqrt(192.0)), None, ALU.mult, R=[gc], W=[gd])
        tab = P.sbuf("tab", [128, TA], F32)
        P.ld("sp", tab[:], self.rope_tab[:, :], R=[self.rope_tab], W=[tab])
        ones, fold = self.ones, self.fold
        NB = 512
        hTbs = [P.sbuf("hTb", [128, 8, NB], BF16) for _ in range(2)]
        pbs = [P.psum("pb", [128, NB]) for _ in range(6)]
        psv = P.psum("psv", [128, 1024])
        cyc = {"pb": 0}

        def nps():
            t = pbs[cyc["pb"] % len(pbs)]
            cyc["pb"] += 1
            return t

        def rot(name, shape, dt, n=2):
            ts = [P.sbuf(name, shape, dt) for _ in range(n)]
            st = {"i": 0}

            def f():
                t = ts[st["i"] % n]
                st["i"] += 1
                return t
            return f
        sq_n = rot("sq_n", [128, NB], BF16, 3)
        sq_r = rot("sq_r", [64, NB], BF16, 2)
        rs_t = rot("rs_t", [128, NB], F32, 3)
        o_n = rot("o_n", [128, NB], BF16, 3)
        o_r = rot("o_r", [64, NB], BF16, 3)
        rt_t = rot("rt_t", [128, NB], BF16, 2)
        vt_t = rot("vt_t", [128, 1024], BF16, 2)
        qan = P.sbuf("qan", [128, 2, NB], BF16)
        kvan = P.sbuf("kvan", [128, NB], BF16)
        sqkr = P.sbuf("sqkr", [64, NB], BF16)
        krr = P.sbuf("krr", [64, NB], F32)

        def rstd_from(ss_ps, n, nb):
            rs = rs_t()
            P.act(rs[:, :nb], ss_ps[:, :nb], AF.Sqrt, R=[ss_ps], W=[rs], bias=float(n * EPS))
            P.recip(rs[:, :nb], rs[:, :nb], R=[rs], W=[rs])
            return rs

        blocks = [(0, 256)] + [(256 + NB * b, NB) for b in range(16)]
        for bi, (t0, nb) in enumerate(blocks):
            is_ctx = bi == 0
            hTb = hTbs[bi % 2]
            P.ld("sp", hTb[:, :, :nb], self.hT[:, t0:t0 + nb].rearrange("(k p) t -> p k t", p=128), R=[self.hT], W=[hTb])

            def proj_a(m):
                ps = nps()
                for k in range(8):
                    P.mm(ps[:, :nb], w_in[:, k, m * 128:(m + 1) * 128], hTb[:, k, :nb], start=(k == 0), stop=(k == 7),
                         R=[w_in, hTb], W=[ps])
                return ps
            psq = [proj_a(0), proj_a(1)]
            ss = nps()
            for m in range(2):
                sq = sq_n()
                P.act(sq[:, :nb], psq[m][:, :nb], AF.Square, R=[psq[m]], W=[sq])
                P.mm(ss[:, :nb], ones[:], sq[:, :nb], start=(m == 0), stop=(m == 1), R=[ones, sq], W=[ss])
            rs = rstd_from(ss, 256, nb)
            for m in range(2):
                P.stt("dve", qan[:, m, :nb], psq[m][:, :nb], gd[:, m:m + 1], rs[:, :nb], ALU.mult, ALU.mult,
                      R=[psq[m], gd, rs], W=[qan])
            pk = proj_a(2)
            sq = sq_n()
            P.act(sq[:, :nb], pk[:, :nb], AF.Square, R=[pk], W=[sq])
            ss = nps()
            P.mm(ss[:, :nb], ones[:], sq[:, :nb], R=[ones, sq], W=[ss])
            rs = rstd_from(ss, 128, nb)
            P.stt("dve", kvan[:, :nb], pk[:, :nb], gd[:, 2:3], rs[:, :nb], ALU.mult, ALU.mult, R=[pk, gd, rs], W=[kvan])
            pr = proj_a(3)
            P.act(sqkr[:, :nb], pr[0:64, :nb], AF.Square, R=[pr], W=[sqkr])
            rt = rt_t()
            P.stt("dve", rt[:, :nb], pr[:, :nb], gd[:, 6:7], tab[:, t0:t0 + nb], ALU.mult, ALU.mult, R=[pr, gd, tab], W=[rt])
            pf = nps()
            P.mm(pf[0:64, :nb], fold[:], rt[:, :nb], R=[fold, rt], W=[pf])
            P.cp("act", krr[:, :nb], pf[0:64, :nb], R=[pf], W=[krr])
            for h in range(8):
                pk = nps()
                P.mm(pk[:, :nb], w_uk[:, h * 128:(h + 1) * 128], kvan[:, :nb], R=[w_uk, kvan], W=[pk])
                sq = sq_n()
                P.act(sq[:, :nb], pk[:, :nb], AF.Square, R=[pk], W=[sq])
                ss = nps()
                P.mm(ss[:, :nb], ones[:], sq[:, :nb], start=True, stop=False, R=[ones, sq], W=[ss])
                P.mm(ss[:, :nb], ones[0:64, :], sqkr[:, :nb], start=False, stop=True, R=[ones, sqkr], W=[ss])
                rs = rstd_from(ss, 192, nb)
                kn = o_n()
                P.stt("dve", kn[:, :nb], pk[:, :nb], gd[:, 5:6], rs[:, :nb], ALU.mult, ALU.mult, R=[pk, gd, rs], W=[kn])
                P.ld("pool", self.KT[h, 0:128, t0:t0 + nb], kn[:, :nb], R=[kn], W=[self.KT])
                kr = o_r()
                P.tt("dve", kr[:, :nb], krr[:, :nb], rs[0:64, :nb], ALU.mult, R=[krr, rs], W=[kr])
                P.ld("pool", self.KT[h, 128:192, t0:t0 + nb], kr[:, :nb], R=[kr], W=[self.KT])
                if is_ctx and not need_ctx:
                    continue
                pqn, pqr = nps(), nps()
                for kc in range(2):
                    P.mm(pqn[:, :nb], w_uq[:, kc, h * 256:h * 256 + 128], qan[:, kc, :nb], start=(kc == 0), stop=(kc == 1),
                         R=[w_uq, qan], W=[pqn])
                for kc in range(2):
                    P.mm(pqr[:, :nb], w_uq[:, kc, h * 256 + 128:h * 256 + 256], qan[:, kc, :nb], start=(kc == 0), stop=(kc == 1),
                         R=[w_uq, qan], W=[pqr])
                sq = sq_n()
                P.act(sq[:, :nb], pqn[:, :nb], AF.Square, R=[pqn], W=[sq])
                sr = sq_r()
                P.act(sr[:, :nb], pqr[0:64, :nb], AF.Square, R=[pqr], W=[sr])
                ss = nps()
                P.mm(ss[:, :nb], ones[:], sq[:, :nb], start=True, stop=False, R=[ones, sq], W=[ss])
                P.mm(ss[:, :nb], ones[0:64, :], sr[:, :nb], start=False, stop=True, R=[ones, sr], W=[ss])
                rs = rstd_from(ss, 192, nb)
                qn = o_n()
                P.stt("dve", qn[:, :nb], pqn[:, :nb], gd[:, 3:4], rs[:, :nb], ALU.mult, ALU.mult, R=[pqn, gd, rs], W=[qn])
                P.ld("pool", self.QT[h, 0:128, t0:t0 + nb], qn[:, :nb], R=[qn], W=[self.QT])
                rt = rt_t()
                P.stt("dve", rt[:, :nb], pqr[:, :nb], gd[:, 4:5], tab[:, t0:t0 + nb], ALU.mult, ALU.mult, R=[pqr, gd, tab], W=[rt])
                pf = nps()
                P.mm(pf[0:64, :nb], fold[:], rt[:, :nb], R=[fold, rt], W=[pf])
                qr = o_r()
                P.tt("dve", qr[:, :nb], pf[0:64, :nb], rs[0:64, :nb], ALU.mult, R=[pf, rs], W=[qr])
                P.ld("pool", self.QT[h, 128:192, t0:t0 + nb], qr[:, :nb], R=[qr], W=[self.QT])
            for tt in range(nb // 128):
                for n in range(2):
                    P.mm(psv[:, n * 512:(n + 1) * 512], kvan[:, tt * 128:(tt + 1) * 128], w_uv[:, n * 512:(n + 1) * 512],
                         R=[kvan, w_uv], W=[psv])
                vt = vt_t()
                P.cp("act", vt[:], psv[:], R=[psv], W=[vt])
                P.ld("pool", self.V[t0 + tt * 128:t0 + (tt + 1) * 128, :], vt[:], R=[vt], W=[self.V])
        P.pop()

    def mla_attn(self, L, need_ctx):
        P = self.P
        P.push()
        NQ = 512
        Kns = [P.sbuf("Kn", [128, TA], BF16) for _ in range(2)]
        Krs = [P.sbuf("Kr", [64, TA], BF16) for _ in range(2)]
        Vhs = [P.sbuf("Vh", [128, NT, 128], BF16) for _ in range(2)]
        Qns = [P.sbuf("Qn", [128, NQ], BF16) for _ in range(2)]
        Qrs = [P.sbuf("Qr", [64, NQ], BF16) for _ in range(2)]
        pts = [P.sbuf("pT", [128, NQ], BF16) for _ in range(3)]
        rls = [P.sbuf("rl", [128, NQ], F32) for _ in range(2)]
        ots = [P.sbuf("ot", [128, NQ], BF16) for _ in range(2)]
        pss = [P.psum("ps_s", [128, NQ]) for _ in range(3)]
        pos = [P.psum("ps_o", [128, NQ]) for _ in range(2)]
        pls = [P.psum("ps_l", [128, NQ]) for _ in range(2)]
        ones = self.ones
        qblocks = [(256 + NQ * b, NQ, list(range(NT))) for b in range(16)]
        if need_ctx:
            qblocks = [(0, 256, [0, 1])] + qblocks
        n_s = 0
        n_q = 0
        for h in range(8):
            Kn, Kr, Vh = Kns[h % 2], Krs[h % 2], Vhs[h % 2]
            P.ld("sp", Kn[:], self.KT[h, 0:128, :], R=[self.KT], W=[Kn])
            P.ld("sp", Kr[:], self.KT[h, 128:192, :], R=[self.KT], W=[Kr])
            P.ld("sp", Vh[:], self.V[:, h * 128:(h + 1) * 128].rearrange("(n p) v -> p n v", p=128), R=[self.V], W=[Vh])
            for (t0, nq, keys) in qblocks:
                Qn, Qr = Qns[n_q % 2], Qrs[n_q % 2]
                po, pl = pos[n_q % 2], pls[n_q % 2]
                rl, ot = rls[n_q % 2], ots[n_q % 2]
                n_q += 1
                P.ld("sp", Qn[:, :nq], self.QT[h, 0:128, t0:t0 + nq], R=[self.QT], W=[Qn])
                P.ld("sp", Qr[:, :nq], self.QT[h, 128:192, t0:t0 + nq], R=[self.QT], W=[Qr])
                for ki, kt in enumerate(keys):
                    ps, pt = pss[n_s % 3], pts[n_s % 3]
                    n_s += 1
                    ks = slice(kt * 128, (kt + 1) * 128)
                    P.mm(ps[:, :nq], Kn[:, ks], Qn[:, :nq], start=True, stop=False, R=[Kn, Qn], W=[ps])
                    P.mm(ps[:, :nq], Kr[:, ks], Qr[:, :nq], start=False, stop=True, R=[Kr, Qr], W=[ps])
                    P.act(pt[:, :nq], ps[:, :nq], AF.Exp, R=[ps], W=[pt])
                    first, last = ki == 0, ki == len(keys) - 1
                    P.mm(po[:, :nq], Vh[:, kt, :], pt[:, :nq], start=first, stop=last, R=[Vh, pt], W=[po])
                    P.mm(pl[:, :nq], ones[:], pt[:, :nq], start=first, stop=last, R=[ones, pt], W=[pl])
                P.recip(rl[:, :nq], pl[:, :nq], R=[pl], W=[rl])
                P.tt("dve", ot[:, :nq], po[:, :nq], rl[:, :nq], ALU.mult, R=[po, rl], W=[ot])
                P.ld("pool", self.OT[h * 128:(h + 1) * 128, t0:t0 + nq], ot[:, :nq], R=[ot], W=[self.OT])
        P.pop()


    def moe_state(self, L):
        P = self.P
        M = NS()
        M.logits = P.sbuf("logits", [128, NE, NT], F32)
        M.idxl = P.sbuf("idxl", [128, NE, 8], I32)
        M.idxc = P.sbuf("idxc", [32, NE], I32)
        M.wr = self.cast_load("wr", [128, 8, NE], self.moe_wr[L.i].rearrange("(k p) e -> p k e", p=128), [self.moe_wr])
        return M

    def tail_tiles(self):
        P = self.P
        T = NS()
        T.n = 0
        T.xo = [P.sbuf("xo", [128, D], F32) for _ in range(2)]
        T.tmp = [P.sbuf("ttmp", [128, D], F32) for _ in range(2)]
        T.xn = [P.sbuf("txn", [128, D], F32) for _ in range(2)]
        T.junk = P.sbuf("tjunk", [128, D], BF16)
        T.ss = [P.sbuf("tss", [128, 1], F32) for _ in range(2)]
        T.rs = [P.sbuf("trs", [128, 1], F32) for _ in range(2)]
        T.hb = [P.sbuf("thb", [128, D], BF16) for _ in range(2)]
        T.pt = [P.psum("tpt", [128, 8, 128], BF16) for _ in range(2)]
        T.hTt = [P.sbuf("thTt", [128, 8, 128], BF16) for _ in range(2)]
        T.plg = P.psum("tplg", [128, NE])
        return T

    def tail(self, L, M, T, i, ol_aps, ol_R, xs_rows=None, h_rows=None):
        P = self.P
        m = 1 if i < 2 else 0
        b = T.n % 2
        T.n += 1
        xo, tmp, xn, ss, rs, hb, pt, hTt = T.xo[b], T.tmp[b], T.xn[b], T.ss[b], T.rs[b], T.hb[b], T.pt[b], T.hTt[b]
        rows = slice(i * 128, (i + 1) * 128)
        xs_rows = self.XS[rows, :] if xs_rows is None else xs_rows
        h_rows = self.hrow[rows, :] if h_rows is None else h_rows
        P.ld("sp", xo[:], xs_rows, R=[], W=[xo])
        for ap, c0, w in ol_aps:
            P.tt("dve", tmp[:, c0:c0 + w], ap, L.gta[m][:, c0:c0 + w], ALU.mult, R=list(ol_R) + [L.gta[m]], W=[tmp])
        P.tt("pool", xn[:], tmp[:], xo[:], ALU.add, R=[tmp, xo], W=[xn])
        P.ld("pool", xs_rows, xn[:], R=[xn], W=[Res("u")])
        if L.skip_ctx_moe and i < 2:
            return
        P.act(T.junk[:], xn[:], AF.Square, R=[xn], W=[T.junk, ss], accum_out=ss[:])
        P.act(rs[:], ss[:], AF.Sqrt, R=[ss], W=[rs], scale=1.0 / D, bias=EPS)
        P.recip(rs[:], rs[:], R=[rs], W=[rs])
        P.stt("dve", tmp[:], xn[:], rs[:], L.Gf[m][:], ALU.mult, ALU.mult, R=[xn, rs, L.Gf[m]], W=[tmp])
        P.tt("pool", hb[:], tmp[:], L.Sf[m][:], ALU.add, R=[tmp, L.Sf[m]], W=[hb])
        P.ld("pool", h_rows, hb[:], R=[hb], W=[Res("u")])
        for k in range(8):
            P.tr(pt[:, k, :], hb[:, k * 128:(k + 1) * 128], self.ident[:], R=[hb, self.ident], W=[pt])
        P.cp("act", hTt[:], pt[:], R=[pt], W=[hTt])
        for k in range(8):
            P.mm(T.plg[:], hTt[:, k, :], M.wr[:, k, :], start=(k == 0), stop=(k == 7), R=[hTt, M.wr], W=[T.plg])
        P.cp("dve", M.logits[:, :, i], T.plg[:], R=[T.plg], W=[M.logits])

    def mla_out(self, L, M, j, need_ctx):
        P = self.P
        P.push()
        w_o = self.cast_load("w_o", [128, 8, D], self.mla_w_o[j].rearrange("(k p) n -> p k n", p=128), [self.mla_w_o])
        self.tail_consts(L)
        T = self.tail_tiles()
        OTs = [P.sbuf("OTt", [128, 8, 128], BF16) for _ in range(2)]
        pols = [P.psum("pol", [128, D]) for _ in range(2)]
        for i in range(NT):
            if i < 2 and not need_ctx:
                continue
            OTt, pol = OTs[i % 2], pols[i % 2]
            P.ld("sp", OTt[:], self.OT[:, i * 128:(i + 1) * 128].rearrange("(k p) t -> p k t", p=128), R=[], W=[OTt])
            for n in range(2):
                for k in range(8):
                    P.mm(pol[:, n * 512:(n + 1) * 512], OTt[:, k, :], w_o[:, k, n * 512:(n + 1) * 512],
                         start=(k == 0), stop=(k == 7), R=[OTt, w_o], W=[pol])
            self.tail(L, M, T, i, [(pol[:, 0:512], 0, 512), (pol[:, 512:1024], 512, 512)], [pol])
        P.pop()

    def moe_route(self, L, M):
        P = self.P
        i = L.i
        do_ctx = not L.skip_ctx_moe
        P.push()
        lg = M.logits
        lg_te = lg[:].rearrange("p e t -> p t e")
        aff = P.sbuf("aff", [128, NE, NT], F32)
        aff_te = aff[:].rearrange("p e t -> p t e")
        mx = P.sbuf("mx", [128, NT], F32)
        sm = P.sbuf("sm", [128, NT], F32)
        if not do_ctx:
            P.memset("dve", lg[:, :, 0:2], 0.0, W=[lg])
        P.red("dve", mx[:], lg_te, ALU.max, R=[lg], W=[mx])
        P.tt("dve", aff[:], lg[:], mx[:].unsqueeze(1).to_broadcast([128, NE, NT]), ALU.subtract, R=[lg, mx], W=[aff])
        P.act(aff[:], aff[:], AF.Exp, R=[aff], W=[aff])
        P.red("dve", sm[:], aff_te, ALU.add, R=[aff], W=[sm])
        P.recip(sm[:], sm[:], R=[sm], W=[sm])
        P.tt("dve", aff[:], aff[:], sm[:].unsqueeze(1).to_broadcast([128, NE, NT]), ALU.mult, R=[aff, sm], W=[aff])
        affc = P.sbuf("affc", [128, NT, NE], F32)
        P.cp("dve", affc[:], aff_te, R=[aff], W=[affc])
        affrow_v = self.affrow.t.rearrange("(t p) e -> p t e", p=128)
        aff_w = []
        if getattr(L, "fmap", False):
            pieces = [(affrow_v[:, 0:2, :], affc[:, 0:2, :])]
            av = self.affrow.t[256:, :].rearrange("(p q) e -> p q e", q=64)
            for q in range(4):
                pieces.append((av[:, q * 16:(q + 1) * 16, :], affc[:, 2 + q * 16:2 + (q + 1) * 16, :]))
        else:
            pieces = [(affrow_v[:, q * 11:(q + 1) * 11, :], affc[:, q * 11:(q + 1) * 11, :]) for q in range(6)]
        for dst, src in pieces:
            r_ = Res("affw")
            aff_w.append(r_)
            P.ld("pool", dst, src, R=[affc], W=[r_])
        tokid_sb = self.tokid_f_sb if getattr(L, "fmap", False) else self.tokid_sb
        lo = P.sbuf("lo", [128, 2, NE], F32)
        mid = P.sbuf("mid", [128, 2, NE], F32)
        cnt = P.sbuf("cnt", [128, 2, NE], F32)
        gew = P.sbuf("gew", [128, 2, NE], F32)
        cmp_ = P.sbuf("cmp", [128, NE, NT], BF16)
        cmp_f = cmp_[:].rearrange("p e t -> p (e t)")
        cps = P.psum("cps", [128, 1536])
        cps_v = cps[:, 0:NE * NT].rearrange("p (e t) -> p e t", t=NT)
        P.memset("dve", lo[:], 0.0, W=[lo])

        def count(thr):
            P.tt("dve", cmp_[:, :, 2:NT], aff[:, :, 2:NT], thr[:, 0].unsqueeze(2).to_broadcast([128, NE, NT - 2]), ALU.is_ge,
                 R=[aff, thr], W=[cmp_])
            P.tt("dve", cmp_[:, :, 0:2], aff[:, :, 0:2], thr[:, 1].unsqueeze(2).to_broadcast([128, NE, 2]), ALU.is_ge,
                 R=[aff, thr], W=[cmp_])
            for n, (c0, w) in enumerate(((0, 512), (512, 512), (1024, 32))):
                P.mm(cps[:, c0:c0 + w], self.ones[:], cmp_f[:, c0:c0 + w], R=[self.ones, cmp_], W=[cps])

        NIT = 26
        for it in range(NIT):
            w = 0.5 ** (it + 1)
            P.tsc("dve", mid[:], lo[:], w, None, ALU.add, R=[lo], W=[mid])
            count(mid)
            P.red("dve", cnt[:, 0], cps_v[:, :, 2:NT], ALU.add, R=[cps], W=[cnt])
            P.red("dve", cnt[:, 1], cps_v[:, :, 0:2], ALU.add, R=[cps], W=[cnt])
            P.tsc("dve", gew[:, 0], cnt[:, 0], 1023.5, w, ALU.is_ge, ALU.mult, R=[cnt], W=[gew])
            P.tsc("dve", gew[:, 1], cnt[:, 1], 31.5, w, ALU.is_ge, ALU.mult, R=[cnt], W=[gew])
            P.tt("dve", lo[:], lo[:], gew[:], ALU.add, R=[lo, gew], W=[lo])
        count(lo)
        sel = cmp_
        tot = P.sbuf("tot", [128, NE, NT], F32)
        P.cp("dve", tot[:], cps_v, R=[cps], W=[tot])
        sa = P.sbuf("sa", [128, NE, 64], F32)
        sb = P.sbuf("sb", [128, NE, 64], F32)
        P.cp("dve", sa[:], tot[:, :, 2:NT], R=[tot], W=[sa])
        cur, oth = sa, sb
        sh = 1
        while sh < 64:
            P.cp("dve", oth[:, :, 0:sh], cur[:, :, 0:sh], R=[cur], W=[oth])
            P.tt("dve", oth[:, :, sh:64], cur[:, :, sh:64], cur[:, :, 0:64 - sh], ALU.add, R=[cur], W=[oth])
            cur, oth = oth, cur
            sh *= 2
        offs = P.sbuf("offs", [128, NE, NT], F32)
        P.tt("dve", offs[:, :, 2:NT], cur[:], tot[:, :, 2:NT], ALU.subtract, R=[cur, tot], W=[offs])
        P.tsc("dve", offs[:, :, 2:NT], offs[:, :, 2:NT], 32.0, None, ALU.add, R=[offs], W=[offs])
        P.memset("dve", offs[:, :, 0:1], 0.0, W=[offs])
        P.cp("dve", offs[:, :, 1:2], tot[:, :, 0:1], R=[tot], W=[offs])
        ips = P.psum("ips", [128, 1536])
        for t in range(NT):
            P.mm(ips[:, t * NE:(t + 1) * NE], self.ustrict[:], sel[:, :, t], R=[self.ustrict, sel], W=[ips])
        ips_v = ips[:, 0:NT * NE].rearrange("p (t e) -> p e t", e=NE)
        slot = P.sbuf("slot", [128, NE, NT], F32)
        P.tt("dve", slot[:], ips_v, offs[:], ALU.add, R=[ips, offs], W=[slot])
        P.stt("dve", slot[:], slot[:], self.dumpc[:, 0:1], sel[:], ALU.subtract, ALU.mult, R=[slot, self.dumpc, sel], W=[slot])
        P.tsc("dve", slot[:], slot[:], self.dumpc[:, 0:1], None, ALU.add, R=[slot, self.dumpc], W=[slot])
        P.tt("dve", slot[:], slot[:], self.eoffs[:].unsqueeze(2).to_broadcast([128, NE, NT]), ALU.add, R=[slot, self.eoffs], W=[slot])
        smap = P.sbuf("smap", [128, NE, NT], I32)
        P.cp("dve", smap[:], slot[:], R=[slot], W=[smap])
        idx_res = []
        for t in range(NT):
            if t < 2 and not do_ctx:
                continue
            for e in range(NE):
                r_ = Res("idxw")
                idx_res.append(r_)
                P.scatter(self.idxd[:, :], tokid_sb[:, t:t + 1], smap[:, e, t:t + 1], R=[smap, tokid_sb], W=[r_])
        idxl, idxc = M.idxl, M.idxc
        P.ld("sp", idxl[:], self.idxd.t.rearrange("(e s) o -> e (s o)", e=NE)[:, 32:1056].rearrange("e (p k) -> p e k", k=8), R=idx_res, W=[idxl])
        if do_ctx:
            P.ld("sp", idxc[:], self.idxd.t.rearrange("(e s) o -> e (s o)", e=NE)[:, 0:32].rearrange("e p -> p e"), R=idx_res, W=[idxc],
                 allow_slow_non_contiguous=True)
        P.pop()

    def moe_experts(self, L, M):
        P = self.P
        i = L.i
        do_ctx = not L.skip_ctx_moe
        idxl, idxc = M.idxl, M.idxc
        aff_w = []
        P.push()
        NCH = 9 if do_ctx else 8
        NTOK = 1056 if do_ctx else 1024
        wsets = [[P.sbuf("wexp", [128, 8, D], BF16) for _ in range(3)] for _ in range(2)]
        xes = [P.sbuf("xe", [128, 9, D], BF16) for _ in range(1)]
        gas = [P.sbuf("ga", [128, 9, NE], F32) for _ in range(2)]
        xeT = P.sbuf("xeT", [128, 8, 1056], BF16)
        hid = P.sbuf("hid", [128, 8, 1056], BF16)
        sg = [P.sbuf("sg", [128, 512], F32) for _ in range(2)]
        ys = [P.sbuf("ys", [128, D], F32) for _ in range(2)]
        ptp = [P.psum("ptp", [128, 8, 128], BF16) for _ in range(1)]
        pbank = [P.psum("pbk", [128, 512]) for _ in range(6)]
        st = {"b": 0, "y": 0}

        def bank():
            t = pbank[st["b"] % len(pbank)]
            st["b"] += 1
            return t
        srcs = (self.moe_wg, self.moe_wu, self.moe_wd)

        def prefetch_w(e):
            ws = wsets[e % 2]
            for a in range(3):
                P.ld("pool", ws[a][:], srcs[a][i, e].rearrange("(k p) n -> p k n", p=128), R=[], W=[ws[a]])

        def prefetch(e):
            xe, ga = xes[0], gas[e % 2]
            for k in range(8):
                P.gather(xe[:, k, :], self.hrow[:, :], idxl[:, e, k:k + 1], R=[idxl], W=[xe])
                P.gather(ga[:, k, :], self.affrow[:, :], idxl[:, e, k:k + 1], R=[idxl] + aff_w, W=[ga])
            if do_ctx:
                P.gather(xe[0:32, 8, :], self.hrow[:, :], idxc[0:32, e:e + 1], R=[idxc], W=[xe])
                P.gather(ga[0:32, 8, :], self.affrow[:, :], idxc[0:32, e:e + 1], R=[idxc] + aff_w, W=[ga])

        grp = [[Res("scA") for _ in range(9)], [Res("scB") for _ in range(9)]]
        prefetch_w(0)
        prefetch(0)
        for e in range(NE):
            if e + 1 < NE:
                prefetch_w(e + 1)
            wg, wu, wd = wsets[e % 2]
            xe, ga = xes[0], gas[e % 2]
            for k in range(NCH):
                rows = 128 if k < 8 else 32
                pt = ptp[0]
                for dk in range(8):
                    P.tr(pt[:, dk, 0:rows], xe[0:rows, k, dk * 128:(dk + 1) * 128], self.ident[0:rows, 0:rows],
                         R=[xe, self.ident], W=[pt])
                P.cp("act", xeT[:, :, k * 128:k * 128 + rows], pt[:, :, 0:rows], R=[pt], W=[xeT])
            if e + 1 < NE:
                prefetch(e + 1)
            nblks = [(0, 512), (512, 512)] + ([(1024, 32)] if do_ctx else [])
            for f in range(8):
                fs = slice(f * 128, (f + 1) * 128)
                for (c0, w) in nblks:
                    pg, pu = bank(), bank()
                    for dk in range(8):
                        P.mm(pg[:, :w], wg[:, dk, fs], xeT[:, dk, c0:c0 + w], start=(dk == 0), stop=(dk == 7), R=[wg, xeT], W=[pg])
                    for dk in range(8):
                        P.mm(pu[:, :w], wu[:, dk, fs], xeT[:, dk, c0:c0 + w], start=(dk == 0), stop=(dk == 7), R=[wu, xeT], W=[pu])
                    s_ = sg[st["y"] % 2]
                    st["y"] += 1
                    P.act(s_[:, :w], pg[:, :w], AF.Silu, R=[pg], W=[s_])
                    P.tt("dve", hid[:, f, c0:c0 + w], s_[:, :w], pu[:, :w], ALU.mult, R=[s_, pu], W=[hid])
            for k in range(NCH):
                rows = 128 if k < 8 else 32
                m = 0 if k < 8 else 1
                y = ys[k % 2]
                for n in range(2):
                    py = bank()
                    for f in range(8):
                        P.mm(py[0:rows, :], hid[:, f, k * 128:k * 128 + rows], wd[:, f, n * 512:(n + 1) * 512],
                             start=(f == 0), stop=(f == 7), R=[hid, wd], W=[py])
                    P.stt("dve", y[0:rows, n * 512:(n + 1) * 512], py[0:rows, :], ga[0:rows, k, e:e + 1],
                          L.gtf[m][0:rows, n * 512:(n + 1) * 512], ALU.mult, ALU.mult, R=[py, ga, L.gtf[m]], W=[y])
                ia = idxl[:, e, k:k + 1] if k < 8 else idxc[0:32, e:e + 1]
                P.scatter(self.XS[:, :], y[0:rows, :], ia, R=[y, idxl, idxc] + grp[(e + 1) % 2], W=[grp[e % 2][k]],
                          add=True, bound=TA - 1)
        P.pop()


    def ssd_in(self, L):
        P = self.P
        P.push()
        w = P.sbuf("ssd_w", [128, 8, 5184], BF16)
        wsrc = self.ssd_w_in.t.rearrange("(k p) n -> p k n", p=128)
        for c0 in range(0, 5184, 1728):
            P.ld("pool", w[:, :, c0:c0 + 1728], wsrc[:, :, c0:c0 + 1728], R=[], W=[w])
        cw = P.sbuf("cw", [128, 24, 3], F32)
        for k in range(3):
            P.ld("sp", cw[:, :, k], self.ssd_conv_w[k].rearrange("(f p) -> p f", p=128), R=[], W=[cw])
        cb = P.sbuf("cb", [128, 24], F32)
        P.ld("sp", cb[:], self.ssd_conv_b.t.rearrange("(f p) -> p f", p=128), R=[], W=[cb])
        dtb = P.sbuf("dtb", [128, 64], F32)
        P.ld("sp", dtb[:], self.ssd_dt_bias.t.partition_broadcast(128), R=[], W=[dtb])
        hTws = [P.sbuf("hTw", [128, 8, 258], BF16) for _ in range(2)]
        cv = P.sbuf("cv", [128, 24, 256], BF16)
        accs = [P.sbuf("acc", [128, 256], F32) for _ in range(2)]
        xbs = [P.sbuf("xb_tm", [128, 2560], BF16) for _ in range(2)]
        zss = [P.sbuf("zs_tm", [128, 2048], BF16) for _ in range(2)]
        dtr = [P.sbuf("dtr", [128, 64], F32) for _ in range(2)]
        banks = [P.psum("sbk", [128, 512]) for _ in range(5)]
        ptrs = [P.psum("sptr", [128, 8, 128], BF16) for _ in range(2)]
        st = {"b": 0, "t": 0, "a": 0}

        def bank():
            t = banks[st["b"] % len(banks)]
            st["b"] += 1
            return t
        BTv = self.BT_d.t.rearrange("(g p) t -> p g t", p=128)
        CTv = self.CT_d.t.rearrange("(g p) t -> p g t", p=128)
        hTv = self.hT.t.rearrange("(k p) t -> p k t", p=128)
        wins = [(0, True, True)] + [(256 + 256 * q, q == 0, q == 31) for q in range(32)]
        for wi, (t0, lz, rz) in enumerate(wins):
            hTw = hTws[wi % 2]
            lo = t0 - (0 if lz else 1)
            hi = t0 + 256 + (0 if rz else 1)
            c_lo = 1 if lz else 0
            P.ld("sp", hTw[:, :, c_lo:c_lo + (hi - lo)], hTv[:, :, lo:hi], R=[], W=[hTw])
            if lz:
                P.memset("pool", hTw[:, :, 0:1], 0.0, W=[hTw])
            if rz:
                P.memset("pool", hTw[:, :, 257:258], 0.0, W=[hTw])
            for fc in range(24):
                ps = bank()
                for k in range(8):
                    P.mm(ps[:, 0:258], w[:, k, 2048 + fc * 128:2048 + (fc + 1) * 128], hTw[:, k, :], start=(k == 0), stop=(k == 7),
                         R=[w, hTw], W=[ps])
                acc = accs[st["a"] % 2]
                st["a"] += 1
                P.tsc("dve", acc[:], ps[:, 0:256], cw[:, fc, 0:1], None, ALU.mult, R=[ps, cw], W=[acc])
                P.stt("dve", acc[:], ps[:, 1:257], cw[:, fc, 1:2], acc[:], ALU.mult, ALU.add, R=[ps, cw, acc], W=[acc])
                P.stt("dve", acc[:], ps[:, 2:258], cw[:, fc, 2:3], acc[:], ALU.mult, ALU.add, R=[ps, cw, acc], W=[acc])
                P.act(cv[:, fc, :], acc[:], AF.Silu, R=[acc, cb], W=[cv], bias=cb[:, fc:fc + 1])
            P.ld("pool", BTv[:, :, t0:t0 + 256], cv[:, 16:20, :], R=[cv], W=[Res("u")])
            P.ld("pool", CTv[:, :, t0:t0 + 256], cv[:, 20:24, :], R=[cv], W=[Res("u")])
            for tt in range(2):
                rows = slice(t0 + tt * 128, t0 + (tt + 1) * 128)
                xb, zs, dr_ = xbs[tt], zss[tt], dtr[tt]
                for f0 in (0, 8, 16):
                    nf = min(8, 20 - f0)
                    ptr = ptrs[st["t"] % 2]
                    st["t"] += 1
                    for q in range(nf):
                        P.tr(ptr[:, q, :], cv[:, f0 + q, tt * 128:(tt + 1) * 128], self.ident[:], R=[cv, self.ident], W=[ptr])
                    P.cp("act", xb[:, f0 * 128:(f0 + nf) * 128], ptr[:, 0:nf, :], R=[ptr], W=[xb])
                P.ld("pool", self.xs_d[rows, :], xb[:, 0:2048], R=[xb], W=[Res("u")])
                P.ld("pool", self.B_d[rows, :], xb[:, 2048:2560], R=[xb], W=[Res("u")])
                for nb in range(4):
                    ps = bank()
                    for k in range(8):
                        P.mm(ps[:], hTw[:, k, 1 + tt * 128:1 + (tt + 1) * 128], w[:, k, nb * 512:(nb + 1) * 512], start=(k == 0), stop=(k == 7),
                             R=[w, hTw], W=[ps])
                    P.act(zs[:, nb * 512:(nb + 1) * 512], ps[:], AF.Silu, R=[ps], W=[zs])
                P.ld("pool", self.zs_d[rows, :], zs[:], R=[zs], W=[Res("u")])
                ps = bank()
                for k in range(8):
                    P.mm(ps[:, 0:64], hTw[:, k, 1 + tt * 128:1 + (tt + 1) * 128], w[:, k, 5120:5184], start=(k == 0), stop=(k == 7),
                         R=[w, hTw], W=[ps])
                P.tt("dve", dr_[:], ps[:, 0:64], dtb[:], ALU.add, R=[ps, dtb], W=[dr_])
                P.act(dr_[:], dr_[:], AF.Exp, R=[dr_], W=[dr_])
                P.act(dr_[:], dr_[:], AF.Ln, R=[dr_], W=[dr_], bias=1.0)
                P.ld("pool", self.dt_d[rows, :], dr_[:], R=[dr_], W=[Res("u")])
        P.pop()

    def ssd_scan(self, L, dr):
        P = self.P
        P.push()
        c2 = P.sbuf("c2", [128, 640], F32)
        P.ld("sp", c2[:], self.cst2[:, :], R=[], W=[c2])
        LE, GE, GT, LT, onesf = (c2[:, q * 128:(q + 1) * 128] for q in range(5))
        m1 = LE if dr == 0 else GE
        lm = GT if dr == 0 else LT
        a_b = P.sbuf("a_b", [128, 32], F32)
        P.ld("sp", a_b[:], self.ssd_a_log[dr].partition_broadcast(128), R=[], W=[a_b])
        P.act(a_b[:], a_b[:], AF.Exp, R=[a_b], W=[a_b])
        P.tsc("dve", a_b[:], a_b[:], -1.0, None, ALU.mult, R=[a_b], W=[a_b])
        stf = [P.sbuf("stf", [128, 512], F32) for _ in range(4)]
        stb = [P.sbuf("stb", [128, 512], BF16) for _ in range(4)]
        for g in range(4):
            P.memset("dve", stf[g][:], 0.0, W=[stf[g]])
            P.memset("dve", stb[g][:], 0.0, W=[stb[g]])
        NB = 2
        xss = [P.sbuf("s_xs", [128, 2048], BF16) for _ in range(NB)]
        Bts = [P.sbuf("s_B", [128, 512], BF16) for _ in range(NB)]
        BTs = [P.sbuf("s_BT", [128, 4, 128], BF16) for _ in range(NB)]
        CTs = [P.sbuf("s_CT", [128, 4, 128], BF16) for _ in range(NB)]
        dts = [P.sbuf("s_dt", [128, 64], F32) for _ in range(NB)]
        dtas = [P.sbuf("s_dta", [128, 32], F32) for _ in range(NB)]
        E3s = [P.sbuf("s_E3", [128, 3, 32], F32) for _ in range(NB)]
        xdts = [P.sbuf("s_xdt", [128, 2048], BF16) for _ in range(NB)]
        xdds = [P.sbuf("s_xdd", [128, 2048], BF16) for _ in range(NB)]
        ys = [P.sbuf("s_y", [128, 2048], F32) for _ in range(NB)]
        rhsEs = [P.sbuf("s_rhsE", [128, 8, 128], F32) for _ in range(2)]
        Lts = [P.sbuf("s_Lt", [128, 8, 128], BF16) for _ in range(2)]
        MTs = [P.sbuf("s_MT", [128, 8, 128], BF16) for _ in range(2)]
        cbms = [P.sbuf("s_cbm", [128, 128], BF16) for _ in range(2)]
        t1s = [P.sbuf("s_t1", [128, 512], F32) for _ in range(2)]
        e3p = P.psum("e3p", [128, 3, 32])
        args = [P.psum("argp", [128, 8, 128]) for _ in range(1)]
        cbp = P.psum("cbp", [128, 128])
        yp = [P.psum("yp", [128, 512]) for _ in range(2)]
        yop = P.psum("yop", [128, 512])
        sp_ = P.psum("sps", [128, 512])
        BTv = self.BT_d.t.rearrange("(g p) t -> p g t", p=128)
        CTv = self.CT_d.t.rearrange("(g p) t -> p g t", p=128)
        order = list(range(NT)) if dr == 0 else [1, 0] + list(range(NT - 1, 1, -1))
        yd = self.yf_d if dr == 0 else self.yb_d
        ng = 0
        for ci, c in enumerate(order):
            b = ci % NB
            rows = slice(c * 128, (c + 1) * 128)
            xs, Bt, BT, CT, dt, dta, E3, xdt, xdd, y = xss[b], Bts[b], BTs[b], CTs[b], dts[b], dtas[b], E3s[b], xdts[b], xdds[b], ys[b]
            P.ld("sp", xs[:], self.xs_d[rows, :], R=[], W=[xs])
            P.ld("sp", Bt[:], self.B_d[rows, :], R=[], W=[Bt])
            P.ld("sp", BT[:], BTv[:, :, rows], R=[], W=[BT])
            P.ld("sp", CT[:], CTv[:, :, rows], R=[], W=[CT])
            P.ld("sp", dt[:], self.dt_d[rows, :], R=[], W=[dt])
            dtd = dt[:, dr * 32:(dr + 1) * 32]
            P.tt("dve", dta[:], dtd, a_b[:], ALU.mult, R=[dt, a_b], W=[dta])
            P.mm(e3p[:, 0, :], m1, dta[:], R=[c2, dta], W=[e3p])
            P.mm(e3p[:, 1, :], lm, dta[:], R=[c2, dta], W=[e3p])
            P.mm(e3p[:, 2, :], onesf, dta[:], R=[c2, dta], W=[e3p])
            P.act(E3[:], e3p[:], AF.Exp, R=[e3p], W=[E3])
            xs3 = xs[:].rearrange("p (h q) -> p h q", q=64)
            P.tt("dve", xdt[:].rearrange("p (h q) -> p h q", q=64), xs3, dtd.unsqueeze(2).to_broadcast([128, 32, 64]), ALU.mult,
                 R=[xs, dt], W=[xdt])
            P.tt("pool", xdd[:].rearrange("p (h q) -> p h q", q=64), xdt[:].rearrange("p (h q) -> p h q", q=64),
                 E3[:, 1, :].unsqueeze(2).to_broadcast([128, 32, 64]), ALU.mult, R=[xdt, E3], W=[xdd])
            for g in range(4):
                hs = slice(g * 8, (g + 1) * 8)
                q2 = ng % 2
                ng += 1
                rhsE, Lt, MT, cbm, t1 = rhsEs[q2], Lts[q2], MTs[q2], cbms[q2], t1s[q2]
                arg = args[0]
                P.tt("pool", rhsE[:], dta[:, hs].unsqueeze(2).to_broadcast([128, 8, 128]), m1.unsqueeze(1).to_broadcast([128, 8, 128]),
                     ALU.mult, R=[dta, c2], W=[rhsE])
                for hf in range(2):
                    P.mm(arg[:, hf * 4:(hf + 1) * 4, :], lm, rhsE[:, hf * 4:(hf + 1) * 4, :], R=[c2, rhsE], W=[arg])
                P.act(Lt[:], arg[:], AF.Exp, R=[arg], W=[Lt])
                P.mm(cbp[:], BT[:, g, :], CT[:, g, :], R=[BT, CT], W=[cbp])
                P.tt("dve", cbm[:], cbp[:], m1, ALU.mult, R=[cbp, c2], W=[cbm])
                P.tt("dve", MT[:], Lt[:], cbm[:].unsqueeze(1).to_broadcast([128, 8, 128]), ALU.mult, R=[Lt, cbm], W=[MT])
                ypg = yp[g % 2]
                for hl in range(8):
                    h = g * 8 + hl
                    P.mm(ypg[:, hl * 64:(hl + 1) * 64], MT[:, hl, :], xdt[:, h * 64:(h + 1) * 64], R=[MT, xdt], W=[ypg])
                P.mm(yop[:], CT[:, g, :], stb[g][:], R=[CT, stb[g]], W=[yop])
                P.tt("dve", t1[:].rearrange("p (h q) -> p h q", q=64), yop[:].rearrange("p (h q) -> p h q", q=64),
                     E3[:, 0, hs].unsqueeze(2).to_broadcast([128, 8, 64]), ALU.mult, R=[yop, E3], W=[t1])
                P.tt("dve", y[:, g * 512:(g + 1) * 512], t1[:], ypg[:], ALU.add, R=[t1, ypg], W=[y])
                P.mm(sp_[:], Bt[:, g * 128:(g + 1) * 128], xdd[:, g * 512:(g + 1) * 512], R=[Bt, xdd], W=[sp_])
                P.tt("pool", stf[g][:].rearrange("p (h q) -> p h q", q=64), stf[g][:].rearrange("p (h q) -> p h q", q=64),
                     E3[:, 2, hs].unsqueeze(2).to_broadcast([128, 8, 64]), ALU.mult, R=[stf[g], E3], W=[stf[g]])
                P.tt("dve", stf[g][:], stf[g][:], sp_[:], ALU.add, R=[stf[g], sp_], W=[stf[g]])
                P.cp("act", stb[g][:], stf[g][:], R=[stf[g]], W=[stb[g]])
            P.ld("pool", yd[rows, :], y[:], R=[y], W=[Res("u")])
        P.pop()

    def ssd_out(self, L, M):
        P = self.P
        P.push()
        w_out = self.cast_load("ssd_wo", [128, 16, D], self.ssd_w_out.t.rearrange("(k p) n -> p k n", p=128), [])
        self.tail_consts(L)
        T = self.tail_tiles()
        D_b = P.sbuf("D_b", [128, 32], F32)
        P.ld("sp", D_b[:], self.ssd_d.t.partition_broadcast(128), R=[], W=[D_b])
        gn_b = P.sbuf("gn_b", [128, 2048], F32)
        P.ld("sp", gn_b[:], self.ssd_g_norm.t.partition_broadcast(128), R=[], W=[gn_b])
        yfs = [P.sbuf("o_yf", [128, 2048], F32) for _ in range(2)]
        ybs = [P.sbuf("o_yb", [128, 2048], F32) for _ in range(2)]
        xss = [P.sbuf("o_xs", [128, 2048], BF16) for _ in range(2)]
        zss = [P.sbuf("o_zs", [128, 2048], BF16) for _ in range(2)]
        gat = P.sbuf("o_gat", [128, 2048], F32)
        junk = P.sbuf("o_junk", [128, 512], BF16)
        ss4 = P.sbuf("o_ss4", [128, 4], F32)
        rs4 = P.sbuf("o_rs4", [128, 4], F32)
        nrm = P.sbuf("o_nrm", [128, 2048], BF16)
        nT = P.sbuf("o_nT", [128, 16, 128], BF16)
        ptr = [P.psum("o_ptr", [128, 8, 128], BF16) for _ in range(2)]
        pol = P.psum("o_pol", [128, D])
        for i in range(NT):
            b = i % 2
            rows = slice(i * 128, (i + 1) * 128)
            yf, yb, xs, zs = yfs[b], ybs[b], xss[b], zss[b]
            P.ld("sp", yf[:], self.yf_d[rows, :], R=[], W=[yf])
            P.ld("sp", yb[:], self.yb_d[rows, :], R=[], W=[yb])
            P.ld("sp", xs[:], self.xs_d[rows, :], R=[], W=[xs])
            P.ld("sp", zs[:], self.zs_d[rows, :], R=[], W=[zs])
            P.tt("pool", yf[:], yf[:], yb[:], ALU.add, R=[yf, yb], W=[yf])
            P.tt("dve", yb[:].rearrange("p (h q) -> p h q", q=64), xs[:].rearrange("p (h q) -> p h q", q=64),
                 D_b[:].unsqueeze(2).to_broadcast([128, 32, 64]), ALU.mult, R=[xs, D_b, yb], W=[yb])
            P.tt("pool", yf[:], yf[:], yb[:], ALU.add, R=[yf, yb], W=[yf])
            P.tt("dve", gat[:], yf[:], zs[:], ALU.mult, R=[yf, zs], W=[gat])
            for g in range(4):
                P.act(junk[:], gat[:, g * 512:(g + 1) * 512], AF.Square, R=[gat], W=[junk, ss4], accum_out=ss4[:, g:g + 1])
            P.act(rs4[:], ss4[:], AF.Sqrt, R=[ss4], W=[rs4], scale=1.0 / 512, bias=EPS)
            P.recip(rs4[:], rs4[:], R=[rs4], W=[rs4])
            for g in range(4):
                gs = slice(g * 512, (g + 1) * 512)
                P.stt("dve", nrm[:, gs], gat[:, gs], rs4[:, g:g + 1], gn_b[:, gs], ALU.mult, ALU.mult, R=[gat, rs4, gn_b], W=[nrm])
            for hf in range(2):
                for q in range(8):
                    fk = hf * 8 + q
                    P.tr(ptr[hf][:, q, :], nrm[:, fk * 128:(fk + 1) * 128], self.ident[:], R=[nrm, self.ident], W=[ptr[hf]])
                P.cp("act", nT[:, hf * 8:(hf + 1) * 8, :], ptr[hf][:], R=[ptr[hf]], W=[nT])
            for n in range(2):
                for fk in range(16):
                    P.mm(pol[:, n * 512:(n + 1) * 512], nT[:, fk, :], w_out[:, fk, n * 512:(n + 1) * 512], start=(fk == 0), stop=(fk == 15),
                         R=[nT, w_out], W=[pol])
            self.tail(L, M, T, i, [(pol[:, 0:512], 0, 512), (pol[:, 512:1024], 512, 512)], [pol])
        P.pop()

    def fnet_f1(self, L):
        P = self.P
        P.push()
        DF = self.cast_load("DF", [128, 2, 512], self.fcst[:, 0:1024].rearrange("p (q n) -> p q n", q=2), [])
        hTv = self.hT.t.rearrange("(k p) t -> p k t", p=128)
        hTs = [P.sbuf("f_hT", [128, 8, 128], BF16) for _ in range(2)]
        yts = [P.sbuf("f_yt", [128, 2, 4, 256], BF16) for _ in range(2)]
        yps = [P.psum("f_yps", [128, 4, 512]) for _ in range(2)]
        for i in range(NT):
            hTt, yt, yp = hTs[i % 2], yts[i % 2], yps[i % 2]
            rows = slice(i * 128, (i + 1) * 128)
            P.ld("sp", hTt[:], hTv[:, :, rows], R=[], W=[hTt])
            for g in range(4):
                for q in range(2):
                    P.mm(yp[:, g, :], hTt[:, 2 * g + q, :], DF[:, q, :], start=(q == 0), stop=(q == 1), R=[hTt, DF], W=[yp])
            for c in range(2):
                P.cp("act" if c == 0 else "dve", yt[:, c], yp[:, :, c * 256:(c + 1) * 256], R=[yp], W=[yt])
            for c in range(2):
                P.ld("pool", self.Yd[c, rows, :], yt[:, c].rearrange("p g m -> p (g m)"), R=[yt], W=[Res("u")])
        P.pop()

    def fnet_f2(self, L):
        P = self.P
        P.push()
        W1 = self.cast_load("W1big", [128, 128], self.fcst[:, 2048:2176], [])
        Ydv = self.Yd.t[:, 256:, :].rearrange("c (a b) f -> c a b f", b=128)
        Zdv = self.Zd.t.rearrange("c k t f -> (c k) t f")
        Ins = [P.sbuf("f_In", [128, 4, D], BF16) for _ in range(2)]
        Zts = [P.sbuf("f_Zt", [128, 4, D], BF16) for _ in range(2)]
        zps = [P.psum("f_zps", [128, D]) for _ in range(3)]
        nz = 0
        for bt in range(32):
            In, Zt = Ins[bt % 2], Zts[bt % 2]
            for c in range(2):
                P.ld("sp", In[c * 64:(c + 1) * 64, :, :], Ydv[c, :, 4 * bt:4 * bt + 4, :], R=[], W=[In])
            for q in range(4):
                zp = zps[nz % 3]
                nz += 1
                for n in range(2):
                    P.mm(zp[:, n * 512:(n + 1) * 512], W1[:], In[:, q, n * 512:(n + 1) * 512], R=[W1, In], W=[zp])
                P.cp("act" if q % 2 == 0 else "dve", Zt[:, q, :], zp[:], R=[zp], W=[Zt])
            P.ld("pool", Zdv[:, 4 * bt:4 * bt + 4, :], Zt[:], R=[Zt], W=[Res("u")])
        P.pop()

    def fnet_f3(self, L, M):
        P = self.P
        P.push()
        MT = P.sbuf("f_MT", [128, 64, 256], BF16)
        for kb in range(8):
            P.ld("pool", MT[:, kb * 8:(kb + 1) * 8, :], self.fM[:, kb * 2048:(kb + 1) * 2048].rearrange("p (a n) -> p a n", a=8), R=[], W=[MT])
        w_o = self.cast_load("f_wo", [128, 8, D], self.fnet_w_o.t.rearrange("(k p) n -> p k n", p=128), [])
        DF2 = self.cast_load("DF2", [128, 2, 512], self.fcst[:, 1024:2048].rearrange("p (q n) -> p q n", q=2), [])
        self.tail_consts(L)
        T = self.tail_tiles()
        mps = P.psum("f_mps", [128, 1024])
        pol = P.psum("f_pol", [128, D])

        def outproj(mT_ap_fn):
            for n in range(2):
                for fc in range(8):
                    P.mm(pol[:, n * 512:(n + 1) * 512], mT_ap_fn(fc), w_o[:, fc, n * 512:(n + 1) * 512], start=(fc == 0), stop=(fc == 7),
                         R=[w_o] + mT_R, W=[pol])
        Yc = P.sbuf("f_Yc", [128, 2, 2, D], BF16)
        for c in range(2):
            for q in range(2):
                P.ld("sp", Yc[:, q, c, :], self.Yd[c, q * 128:(q + 1) * 128, :], R=[], W=[Yc])
        mTc = P.sbuf("f_mTc", [128, 8, 256], BF16)
        mcv = mps[:].rearrange("p (a k) -> p a k", k=256)
        for half in range(2):
            for f4 in range(4):
                fc = half * 4 + f4
                n = 0
                for q in range(2):
                    for c in range(2):
                        P.mm(mcv[:, f4, :], Yc[:, q, c, fc * 128:(fc + 1) * 128], DF2[:, q, c * 256:(c + 1) * 256], start=(n == 0), stop=(n == 3),
                             R=[Yc, DF2], W=[mps])
                        n += 1
            P.act(mTc[:, half * 4:(half + 1) * 4, :], mcv, AF.Copy, R=[mps], W=[mTc], scale=1.0 / 256.0)
        for i in range(2):
            mT_R = [mTc]
            outproj(lambda fc: mTc[:, fc, i * 128:(i + 1) * 128])
            self.tail(L, M, T, i, [(pol[:, 0:512], 0, 512), (pol[:, 512:1024], 512, 512)], [pol])
        Zrs = [P.sbuf("f_Zr", [128, D], BF16) for _ in range(2)]
        Zis = [P.sbuf("f_Zi", [128, D], BF16) for _ in range(2)]
        mTs = [P.sbuf("f_mT", [128, 8, 128], BF16) for _ in range(2)]
        mlv = mps[:].rearrange("p (a k) -> p a k", k=128)
        xsv = self.XS.t[256:, :].rearrange("(p q) d -> q p d", q=64)
        hrv = self.hrow.t[256:, :].rearrange("(p q) d -> q p d", q=64)
        sc = float(1.0 / np.sqrt(8192.0 * 256.0))
        for k1 in range(64):
            Zr, Zi, mT = Zrs[k1 % 2], Zis[k1 % 2], mTs[k1 % 2]
            P.ld("sp", Zr[:], self.Zd[0, k1], R=[], W=[Zr])
            P.ld("sp", Zi[:], self.Zd[1, k1], R=[], W=[Zi])
            for fc in range(8):
                P.mm(mlv[:, fc, :], Zr[:, fc * 128:(fc + 1) * 128], MT[:, k1, 0:128], start=True, stop=False, R=[Zr, MT], W=[mps])
                P.mm(mlv[:, fc, :], Zi[:, fc * 128:(fc + 1) * 128], MT[:, k1, 128:256], start=False, stop=True, R=[Zi, MT], W=[mps])
            P.act(mT[:], mlv, AF.Copy, R=[mps], W=[mT], scale=sc)
            mT_R = [mT]
            outproj(lambda fc: mT[:, fc, :])
            self.tail(L, M, T, 2 + k1, [(pol[:, 0:512], 0, 512), (pol[:, 512:1024], 512, 512)], [pol],
                      xs_rows=xsv[k1], h_rows=hrv[k1])
        P.pop()

    def layer(self, i, upto=None):
        P = self.P
        P.push()
        L = self.layer_consts(i)
        L.skip_ctx_moe = (i == 3)
        need_ctx = (i != 3)
        M = self.moe_state(L)
        self.phase_A(L)
        kind, j = i % 3, i // 3
        if kind == 0:
            self.mla_proj(L, j, need_ctx)
            if upto == "proj":
                P.pop(); return
            self.mla_attn(L, need_ctx)
            if upto == "attn":
                P.pop(); return
            self.mla_out(L, M, j, need_ctx)
        if kind == 1:
            self.ssd_in(L)
            if upto == "ssd_in":
                P.pop(); return
            self.ssd_scan(L, 0)
            self.ssd_scan(L, 1)
            if upto == "ssd_scan":
                P.pop(); return
            self.ssd_out(L, M)
        if kind == 2:
            L.fmap = True
            self.fnet_f1(L)
            self.fnet_f2(L)
            self.fnet_f3(L, M)
        if upto == "mix":
            P.pop(); return
        self.moe_route(L, M)
        if upto == "route":
            P.pop(); return
        self.moe_experts(L, M)
        P.pop()

def make_consts():
    cst = np.zeros((128, 1024), np.float32)
    cst[:, 0:128] = np.eye(128, dtype=np.float32)
    cst[:, 128:256] = 1.0
    for p in range(128):
        cst[p, 256 + (p % 64)] = 1.0
        cst[p, 320 + p + 1:448] = 1.0
        cst[p, 448] = 1152 + p
        cst[p, 449:449 + NE] = np.arange(NE) * 1280
    t = np.arange(8192)
    row = (t // 64).astype(np.float32)
    col = (t % 64).astype(np.float32)
    inv = (10000.0 ** (-np.arange(16, dtype=np.float32) / 16)).astype(np.float32)
    ang = np.concatenate([row[:, None] * inv, col[:, None] * inv], axis=-1).astype(np.float32)
    cos, sin = np.cos(ang).T, np.sin(ang).T
    tab = np.zeros((128, TA), np.float32)
    tab[0:64, 0:256] = 1.0
    tab[0:32, 256:] = cos
    tab[32:64, 256:] = cos
    tab[64:96, 256:] = -sin
    tab[96:128, 256:] = sin
    tokid = (np.arange(NT)[None, :] * 128 + np.arange(128)[:, None]).astype(np.int32)
    return cst, tab, tokid


def prep_shared(I):
    S = {}
    for k in ["w_mod", "b_mod", "g_mix", "g_ffn"]:
        S[k] = np.ascontiguousarray(I[k], dtype=np.float32)
    w_in = I["mla_w_in"]
    S["mla_w_in"] = np.ascontiguousarray(np.concatenate(
        [w_in[:, :, :384], w_in[:, :, 384:448], w_in[:, :, 416:448], w_in[:, :, 384:416]], axis=-1))
    wq = I["mla_w_uq"].reshape(2, 256, 8, 192)
    S["mla_w_uq"] = np.ascontiguousarray(np.concatenate(
        [wq[..., :128], wq[..., 128:192], wq[..., 160:192], wq[..., 128:160]], axis=-1).reshape(2, 256, 2048))
    wkv = I["mla_w_ukv"].reshape(2, 128, 8, 256)
    S["mla_w_uk"] = np.ascontiguousarray(wkv[..., :128].reshape(2, 128, 1024))
    S["mla_w_uv"] = np.ascontiguousarray(wkv[..., 128:].reshape(2, 128, 1024))
    S["mla_w_o"] = np.ascontiguousarray(I["mla_w_o"])
    gc = np.zeros((2, 128, 8), np.float32)
    for j in range(2):
        gc[j, :, 0] = I["mla_g_q"][j, :128]
        gc[j, :, 1] = I["mla_g_q"][j, 128:]
        gc[j, :, 2] = I["mla_g_kv"][j]
        for c0, g in ((3, I["mla_g_qn"][j]), (5, I["mla_g_kn"][j])):
            gc[j, :, c0] = g[:128]
            gc[j, :, c0 + 1] = np.concatenate([g[128:192], g[160:192], g[128:160]])
    S["mla_gc"] = gc
    cst, tab, tokid = make_consts()
    S["cst"], S["rope_tab"], S["tokid"] = cst, tab, tokid
    S["ssd_w_in"] = np.ascontiguousarray(I["ssd_w_in"][0])
    S["ssd_conv_w"] = np.ascontiguousarray(I["ssd_conv_w"][0])
    S["ssd_conv_b"] = np.ascontiguousarray(I["ssd_conv_b"][0])
    S["ssd_dt_bias"] = np.ascontiguousarray(I["ssd_dt_bias"][0].reshape(64))
    S["ssd_a_log"] = np.ascontiguousarray(I["ssd_a_log"][0])
    S["ssd_d"] = np.ascontiguousarray(I["ssd_d"][0])
    S["ssd_g_norm"] = np.ascontiguousarray(I["ssd_g_norm"][0])
    S["ssd_w_out"] = np.ascontiguousarray(I["ssd_w_out"][0])
    kk = np.arange(128)
    c2 = np.zeros((128, 640), np.float32)
    c2[:, 0:128] = (kk[:, None] <= kk[None, :])
    c2[:, 128:256] = (kk[:, None] >= kk[None, :])
    c2[:, 256:384] = (kk[:, None] > kk[None, :])
    c2[:, 384:512] = (kk[:, None] < kk[None, :])
    c2[:, 512:640] = 1.0
    S["cst2"] = c2
    S["fnet_w_o"] = np.ascontiguousarray(I["fnet_w_o"][0])
    fc_ = np.zeros((128, 2176), np.float64)
    p_ = np.arange(128)
    m_ = np.arange(256)
    for q in range(2):
        angn = 2 * np.pi * np.outer(128 * q + p_, m_) / 256.0
        fc_[:, q * 512:q * 512 + 256] = np.cos(angn)
        fc_[:, q * 512 + 256:q * 512 + 512] = -np.sin(angn)
        fc_[:, 1024 + q * 512:1024 + q * 512 + 256] = np.cos(angn)
        fc_[:, 1024 + q * 512 + 256:1024 + q * 512 + 512] = np.sin(angn)
    a64 = np.arange(64)
    ang1 = 2 * np.pi * np.outer(a64, a64) / 64.0
    wr, wi = np.cos(ang1), -np.sin(ang1)
    fc_[0:64, 2048:2112] = wr.T
    fc_[64:128, 2048:2112] = -wi.T
    fc_[0:64, 2112:2176] = wi.T
    fc_[64:128, 2112:2176] = wr.T
    S["fcst"] = fc_.astype(np.float32)
    t2 = np.arange(128)[:, None, None]
    k1 = np.arange(64)[None, :, None]
    k2 = np.arange(128)[None, None, :]
    angm = 2 * np.pi * (k1 * t2 / 8192.0 + k2 * t2 / 128.0)
    fM = np.zeros((128, 64, 2, 128), np.float64)
    fM[:, :, 0, :] = np.cos(angm)
    fM[:, :, 1, :] = np.sin(angm)
    S["fM"] = fM.reshape(128, 16384).astype(np.float32)
    tf = (np.arange(NT)[None, :] * 128 + np.arange(128)[:, None]).astype(np.int32)
    for a in range(64):
        tf[:, 2 + a] = 256 + a + 64 * np.arange(128)
    S["tokid_f"] = tf
    S["moe_wr"] = np.ascontiguousarray(I["moe_w_router"])
    S["moe_wg"] = np.ascontiguousarray(I["moe_w_gate"])
    S["moe_wu"] = np.ascontiguousarray(I["moe_w_up"])
    S["moe_wd"] = np.ascontiguousarray(I["moe_w_down"])
    return S


def prep_core(I, S, b):
    m = dict(S)
    m["x"] = np.ascontiguousarray(I["x"][b])
    m["ctx"] = np.ascontiguousarray(I["ctx"][b])
    m["cc"] = np.ascontiguousarray(np.stack([I["c"][b], I["c_ctx"]], axis=0))
    return m


_CACHE = {}


def build_program():
    nc = bass.Bass("TRN2", target_bir_lowering=False)
    k = K(nc)
    k.setup()
    for i in range(4):
        k.layer(i)
    P = k.P
    for j in range(8):
        P.ld("sp", k.out[j * 1024:(j + 1) * 1024, :], k.XS[256 + j * 1024:256 + (j + 1) * 1024, :], R=[], W=[Res("u")])
    P.barrier()
    P.emit()
    P.close()
    return nc, k


def kernel(**inputs):
    from concourse.bass_utils import run_bass_kernel_spmd
    I = {k_: np.asarray(v) for k_, v in inputs.items()}
    if "prog" not in _CACHE:
        _CACHE["prog"] = build_program()
    nc, k = _CACHE["prog"]
    S = prep_shared(I)
    n = 8
    in_maps = [prep_core(I, S, b) for b in range(n)]
    res = run_bass_kernel_spmd(nc, in_maps, core_ids=list(range(n)))
    return np.stack([np.asarray(r["out"]) for r in res.results], axis=0).astype(np.float32)
```

```python
from contextlib import ExitStack
import numpy as np
import concourse.bass as bass
import concourse.mybir as mybir

F32 = mybir.dt.float32
BF16 = mybir.dt.bfloat16
I32 = mybir.dt.int32
U32 = mybir.dt.uint32
ALU = mybir.AluOpType
AF = mybir.ActivationFunctionType
AX = mybir.AxisListType

COMPUTE = ("pe", "act", "dve", "pool")
EPOCH = 30000
DMA_EPOCH = 2000


class Res:
    __slots__ = ("name", "last_w", "readers")

    def __init__(self, name):
        self.name = name
        self.last_w = None
        self.readers = {}


class Tile:
    def __init__(self, name, t):
        self.name = name
        self.t = t
        self._res = {}

    def r(self, key=None):
        x = self._res.get(key)
        if x is None:
            x = self._res[key] = Res(f"{self.name}:{key}")
        return x

    def __getitem__(self, k):
        return self.t[k]


class Prog:
    def __init__(self, nc, n_dma_sems=16):
        self.nc = nc
        self.es = ExitStack()
        self.streams = {e: [] for e in ("pe", "act", "dve", "pool", "sp")}
        self.cnt = {e: 0 for e in COMPUTE}
        self.waited = {e: {} for e in self.streams}
        self.sems = {}
        self.n_dma_sems = n_dma_sems
        self.dma_rr = {q: 0 for q in ("sp", "pool", "act")}
        self.dma_cnt = {}
        self.n_inst = 0
        self.scopes = []
        self._uid = 0

    def sbuf(self, name, shape, dt):
        t = self.es.enter_context(self.nc.sbuf_tensor(name, list(shape), dt))
        return Tile(name, t)

    def psum(self, name, shape, dt=F32):
        t = self.es.enter_context(self.nc.psum_tensor(name, list(shape), dt))
        return Tile(name, t)

    def dram(self, name, shape, dt, kind="Internal"):
        t = self.nc.dram_tensor(name, list(shape), dt, kind=kind)
        return Tile(name, t.ap())

    def _sem(self, key):
        s = self.sems.get(key)
        if s is None:
            s = self.sems[key] = self.es.enter_context(
                self.nc.semaphore("s_" + "_".join(str(k) for k in key)))
        return s

    def _deps(self, eng, R, W):
        deps = {}

        def add(tok, kind):
            if tok is None:
                return
            semkey, val, teng = tok
            if teng == eng and eng in COMPUTE:
                if eng == "pe" or kind != "raw":
                    return
            if deps.get(semkey, 0) < val:
                deps[semkey] = val

        for r in R:
            add(r.last_w, "raw")
        for w in W:
            add(w.last_w, "waw")
            for sk, (v, e) in w.readers.items():
                add((sk, v, e), "war")
        return deps

    def _emit_waits(self, eng, deps):
        wd = self.waited[eng]
        st = self.streams[eng]
        for semkey, val in deps.items():
            if wd.get(semkey, 0) >= val:
                continue
            wd[semkey] = val
            st.append(("wait", self._sem(semkey), val))

    def _commit(self, tok, R, W):
        semkey, val, eng = tok
        for r in R:
            cur = r.readers.get(semkey)
            if cur is None or cur[0] < val:
                r.readers[semkey] = (val, eng)
        for w in W:
            w.last_w = tok
            w.readers = {}

    @staticmethod
    def _resl(xs):
        out = []
        for x in xs:
            out.append(x.r() if isinstance(x, Tile) else x)
        return out

    def op(self, eng, fn, R=(), W=()):
        R = self._resl(R)
        W = self._resl(W)
        self._emit_waits(eng, self._deps(eng, R, W))
        i = self.cnt[eng]
        self.cnt[eng] = i + 1
        semkey = ("e", eng, i // EPOCH)
        val = i % EPOCH + 1
        tok = (semkey, val, eng)
        self.streams[eng].append(("op", fn, self._sem(semkey), 1))
        self._commit(tok, R, W)
        self.n_inst += 1
        return tok

    def dma(self, q, fn, R=(), W=()):
        R = self._resl(R)
        W = self._resl(W)
        s = self.dma_rr[q]
        self.dma_rr[q] = (s + 1) % self.n_dma_sems
        base = (q, s)
        m = self.dma_cnt.get(base, 0) + 1
        self.dma_cnt[base] = m

        def key(mi):
            ep = (mi - 1) // DMA_EPOCH
            return ("d", q, s, ep), 16 * ((mi - 1) % DMA_EPOCH + 1)

        deps = self._deps("dma:" + q, R, W)
        if m > 1:
            pk, pv = key(m - 1)
            if deps.get(pk, 0) < pv:
                deps[pk] = pv
        self._emit_waits(q, deps)
        semkey, val = key(m)
        tok = (semkey, val, "dma:" + q)
        self.streams[q].append(("op", fn, self._sem(semkey), 16))
        self._commit(tok, R, W)
        self.n_inst += 1
        return tok

    def final_wait(self, eng, toks):
        deps = {}
        for semkey, val, _ in toks:
            if deps.get(semkey, 0) < val:
                deps[semkey] = val
        self._emit_waits(eng, deps)

    def emit(self):
        nc = self.nc
        streams = self.streams

        def run(e, items):
            for it in items:
                if it[0] == "wait":
                    e.wait_ge(it[1], it[2])
                else:
                    it[1](e).then_inc(it[2], it[3])

        with nc.Block() as block:
            @block.tensor
            def _(e):
                run(e, streams["pe"])

            @block.scalar
            def _(e):
                run(e, streams["act"])

            @block.vector
            def _(e):
                run(e, streams["dve"])

            @block.gpsimd
            def _(e):
                run(e, streams["pool"])

            @block.sync
            def _(e):
                run(e, streams["sp"])

    def close(self):
        self.es.close()


def _push(self):
    self.scopes.append(ExitStack())


def _pop(self):
    self.barrier()
    self.scopes.pop().close()


def _sbuf(self, name, shape, dt):
    st = self.scopes[-1] if self.scopes else self.es
    self._uid += 1
    t = st.enter_context(self.nc.sbuf_tensor(f"{name}_{self._uid}", list(shape), dt))
    return Tile(name, t)


def _psum(self, name, shape, dt=F32):
    st = self.scopes[-1] if self.scopes else self.es
    self._uid += 1
    t = st.enter_context(self.nc.psum_tensor(f"{name}_{self._uid}", list(shape), dt))
    return Tile(name, t)


def _barrier(self):
    toks = {}
    for eng in COMPUTE:
        i = self.cnt[eng]
        if i > 0:
            toks[("e", eng, (i - 1) // EPOCH)] = (i - 1) % EPOCH + 1
    for (q, s), m in self.dma_cnt.items():
        toks[("d", q, s, (m - 1) // DMA_EPOCH)] = 16 * ((m - 1) % DMA_EPOCH + 1)
    for eng in self.streams:
        self._emit_waits(eng, dict(toks))


def _mm(self, out, lhsT, rhs, start=True, stop=True, R=(), W=()):
    return self.op("pe", lambda e: e.matmul(out, lhsT, rhs, start=start, stop=stop), R, W)


def _tr(self, out, in_, ident, R=(), W=()):
    return self.op("pe", lambda e: e.transpose(out, in_, ident), R, W)


def _act(self, out, in_, func, R=(), W=(), **kw):
    return self.op("act", lambda e: e.activation(out, in_, func, **kw), R, W)


def _tsc(self, eng, out, in0, s1, s2, op0, op1=None, R=(), W=(), **kw):
    if op1 is None:
        return self.op(eng, lambda e: e.tensor_scalar(out, in0, s1, None, op0, **kw), R, W)
    return self.op(eng, lambda e: e.tensor_scalar(out, in0, s1, s2, op0, op1, **kw), R, W)


def _tt(self, eng, out, in0, in1, op, R=(), W=()):
    return self.op(eng, lambda e: e.tensor_tensor(out, in0, in1, op), R, W)


def _stt(self, eng, out, in0, scalar, in1, op0, op1, R=(), W=()):
    return self.op(eng, lambda e: e.scalar_tensor_tensor(out, in0, scalar, in1, op0, op1), R, W)


def _cp(self, eng, out, in_, R=(), W=()):
    if eng == "act":
        return self.op("act", lambda e: e.copy(out, in_), R, W)
    return self.op(eng, lambda e: e.tensor_copy(out, in_), R, W)


def _red(self, eng, out, in_, op, R=(), W=(), axis=None):
    ax = AX.X if axis is None else axis
    return self.op(eng, lambda e: e.tensor_reduce(out, in_, ax, op), R, W)


def _memset(self, eng, out, val, W=()):
    return self.op(eng, lambda e: e.memset(out, val), (), W)


def _ld(self, q, out, in_, R=(), W=(), **kw):
    kw.setdefault("allow_slow_non_contiguous", True)
    return self.dma(q, lambda e: e.dma_start(out=out, in_=in_, **kw), R, W)


def _gather(self, out, in_, idx_ap, R=(), W=()):
    return self.dma("pool", lambda e: e.indirect_dma_start(
        out=out, out_offset=None, in_=in_,
        in_offset=bass.IndirectOffsetOnAxis(ap=idx_ap, axis=0)), R, W)


def _scatter(self, out, in_, idx_ap, R=(), W=(), add=False, bound=None):
    if add:
        def f(e):
            try:
                return e.indirect_dma_start(
                    out=out, out_offset=bass.IndirectOffsetOnAxis(ap=idx_ap, axis=0), in_=in_, in_offset=None,
                    compute_op=ALU.add, oob_is_err=True)
            except Exception:
                print("SCATTER-ADD FAIL", out, in_, idx_ap, bound)
                raise
        return self.dma("pool", f, R, W)
    return self.dma("pool", lambda e: e.indirect_dma_start(
        out=out, out_offset=bass.IndirectOffsetOnAxis(ap=idx_ap, axis=0), in_=in_, in_offset=None), R, W)


def _recip(self, out, in_, R=(), W=()):
    return self.op("dve", lambda e: e.reciprocal(out, in_), R, W)


def _get_bound_reg(self, e, bound):
    if not hasattr(self, "_bregs"):
        self._bregs = {}
    r = self._bregs.get(bound)
    if r is None:
        r = self._bregs[bound] = e.to_reg(bound)
    return r


Prog.get_bound_reg = _get_bound_reg
Prog.recip = _recip
Prog.push = _push
Prog.pop = _pop
Prog.sbuf = _sbuf
Prog.psum = _psum
Prog.barrier = _barrier
Prog.mm = _mm
Prog.tr = _tr
Prog.act = _act
Prog.tsc = _tsc
Prog.tt = _tt
Prog.stt = _stt
Prog.cp = _cp
Prog.red = _red
Prog.memset = _memset
Prog.ld = _ld
Prog.gather = _gather
Prog.scatter = _scatter


from types import SimpleNamespace as NS

TA = 8448
NT = 66
D = 1024
NE = 16
EPS = 1e-6


class K:
    def __init__(self, nc, dbg=()):
        self.nc = nc
        self.P = P = Prog(nc)
        self.dbg = set(dbg)
        din = lambda n, s, dt=F32: P.dram(n, s, dt, kind="ExternalInput")
        self.x = din("x", [8192, D])
        self.ctx = din("ctx", [256, D])
        self.cc = din("cc", [2, D])
        self.w_mod = din("w_mod", [4, D, 6144])
        self.b_mod = din("b_mod", [4, 6144])
        self.g_mix = din("g_mix", [4, D])
        self.g_ffn = din("g_ffn", [4, D])
        self.mla_w_in = din("mla_w_in", [2, D, 512])
        self.mla_w_uq = din("mla_w_uq", [2, 256, 2048])
        self.mla_w_uk = din("mla_w_uk", [2, 128, 1024])
        self.mla_w_uv = din("mla_w_uv", [2, 128, 1024])
        self.mla_w_o = din("mla_w_o", [2, D, D])
        self.mla_gc = din("mla_gc", [2, 128, 8])
        self.rope_tab = din("rope_tab", [128, TA])
        self.cst = din("cst", [128, 1024])
        self.moe_wr = din("moe_wr", [4, D, NE])
        self.moe_wg = din("moe_wg", [4, NE, D, D])
        self.moe_wu = din("moe_wu", [4, NE, D, D])
        self.moe_wd = din("moe_wd", [4, NE, D, D])
        self.tokid = din("tokid", [128, NT], I32)
        self.ssd_w_in = din("ssd_w_in", [D, 5184])
        self.ssd_conv_w = din("ssd_conv_w", [3, 3072])
        self.ssd_conv_b = din("ssd_conv_b", [3072])
        self.ssd_dt_bias = din("ssd_dt_bias", [64])
        self.ssd_a_log = din("ssd_a_log", [2, 32])
        self.ssd_d = din("ssd_d", [32])
        self.ssd_g_norm = din("ssd_g_norm", [2048])
        self.ssd_w_out = din("ssd_w_out", [2048, D])
        self.cst2 = din("cst2", [128, 640])
        self.fnet_w_o = din("fnet_w_o", [D, D])
        self.fcst = din("fcst", [128, 2176])
        self.fM = din("fM", [128, 16384])
        self.tokid_f = din("tokid_f", [128, NT], I32)
        self.Yd = self.dscr("Yd", [2, TA, D], BF16)
        self.Zd = self.dscr("Zd", [2, 64, 128, D], BF16)
        self.xs_d = self.dscr("xs_d", [TA, 2048], BF16)
        self.B_d = self.dscr("B_d", [TA, 512], BF16)
        self.BT_d = self.dscr("BT_d", [512, TA], BF16)
        self.CT_d = self.dscr("CT_d", [512, TA], BF16)
        self.zs_d = self.dscr("zs_d", [TA, 2048], BF16)
        self.dt_d = self.dscr("dt_d", [TA, 64], F32)
        self.yf_d = self.dscr("yf_d", [TA, 2048], F32)
        self.yb_d = self.dscr("yb_d", [TA, 2048], F32)
        self.out = P.dram("out", [8192, D], F32, kind="ExternalOutput")
        self.XS = self.dscr("XS", [TA, D], F32)
        self.modd = self.dscr("modd", [4, 2, 6144], F32)
        self.hT = self.dscr("hT", [D, TA], BF16)
        self.QT = self.dscr("QT", [8, 192, TA], BF16)
        self.KT = self.dscr("KT", [8, 192, TA], BF16)
        self.V = self.dscr("V", [TA, D], BF16)
        self.OT = self.dscr("OT", [D, TA], BF16)
        self.hrow = self.dscr("hrow", [TA, D], BF16)
        self.affrow = self.dscr("affrow", [TA, NE], F32)
        self.idxd = self.dscr("idxd", [NE * 1280, 1], I32)
        self.ident = P.sbuf("ident", [128, 128], BF16)
        self.ones = P.sbuf("ones", [128, 128], BF16)
        self.fold = P.sbuf("fold", [128, 64], BF16)
        self.identf = P.sbuf("identf", [128, 128], F32)
        P.ld("pool", self.ident[:], self.cst[:, 0:128], R=[self.cst], W=[self.ident])
        P.ld("pool", self.ones[:], self.cst[:, 128:256], R=[self.cst], W=[self.ones])
        P.ld("pool", self.fold[:], self.cst[:, 256:320], R=[self.cst], W=[self.fold])
        P.ld("sp", self.identf[:], self.cst[:, 0:128], R=[self.cst], W=[self.identf])
        self.ustrict = P.sbuf("ustrict", [128, 128], BF16)
        P.ld("pool", self.ustrict[:], self.cst[:, 320:448], R=[self.cst], W=[self.ustrict])
        self.dumpc = P.sbuf("dumpc", [128, 1], F32)
        P.ld("sp", self.dumpc[:], self.cst[:, 448:449], R=[self.cst], W=[self.dumpc])
        self.eoffs = P.sbuf("eoffs", [128, NE], F32)
        P.ld("sp", self.eoffs[:], self.cst[:, 449:449 + NE], R=[self.cst], W=[self.eoffs])
        self.tokid_sb = P.sbuf("tokid_sb", [128, NT], I32)
        P.ld("sp", self.tokid_sb[:], self.tokid[:, :], R=[self.tokid], W=[self.tokid_sb])
        self.tokid_f_sb = P.sbuf("tokid_f_sb", [128, NT], I32)
        P.ld("sp", self.tokid_f_sb[:], self.tokid_f[:, :], R=[self.tokid_f], W=[self.tokid_f_sb])

    def dscr(self, name, shape, dt):
        kind = "ExternalOutput" if name in self.dbg else "Internal"
        return self.P.dram(name, shape, dt, kind=kind)

    def setup(self):
        P = self.P
        P.ld("sp", self.XS[0:256, :], self.ctx[:, :], R=[self.ctx], W=[self.XS])
        for j in range(8):
            P.ld("sp", self.XS[256 + j * 1024:256 + (j + 1) * 1024, :], self.x[j * 1024:(j + 1) * 1024, :],
                 R=[self.x], W=[self.XS])
        P.push()
        ccT = P.sbuf("ccT", [128, 8, 2], F32)
        ccs = P.sbuf("ccs", [128, 8, 2], F32)
        for m in range(2):
            P.ld("sp", ccT[:, :, m], self.cc[m].rearrange("(k p) -> p k", p=128), R=[self.cc], W=[ccT],
                 allow_slow_non_contiguous=True)
        P.act(ccs[:], ccT[:], AF.Silu, R=[ccT], W=[ccs])
        wms = [P.sbuf("wm", [128, 8, 512], F32) for _ in range(2)]
        bms = [P.sbuf("bm", [2, 512], F32) for _ in range(2)]
        mrs = [P.sbuf("mr", [2, 512], F32) for _ in range(2)]
        pss = [P.psum("psm", [2, 512]) for _ in range(2)]
        n = 0
        for i in range(4):
            for nb in range(12):
                wm, bm, mr, ps = wms[n % 2], bms[n % 2], mrs[n % 2], pss[n % 2]
                n += 1
                sl = slice(nb * 512, (nb + 1) * 512)
                P.ld("sp", wm[:], self.w_mod[i, :, sl].rearrange("(k p) n -> p k n", p=128), R=[self.w_mod], W=[wm])
                P.ld("sp", bm[:], self.b_mod[i:i + 1, sl].to_broadcast([2, 512]), R=[self.b_mod], W=[bm])
                for k in range(8):
                    P.mm(ps[:], ccs[:, k, :], wm[:, k, :], start=(k == 0), stop=(k == 7), R=[ccs, wm], W=[ps])
                P.tt("dve", mr[:], ps[:], bm[:], ALU.add, R=[ps, bm], W=[mr])
                P.ld("pool", self.modd[i, :, sl], mr[:], R=[mr], W=[self.modd])
        P.pop()

    def layer_consts(self, i):
        P = self.P
        L = NS()
        L.i = i
        modd = self.modd
        L.modcol = P.sbuf("modcol", [128, 2, 6, 8], F32)
        for m in range(2):
            P.ld("sp", L.modcol[:, m], modd[i, m].rearrange("(s k p) -> p s k", s=6, p=128), R=[modd], W=[L.modcol],
                 allow_slow_non_contiguous=True)
        gcol = P.sbuf("gcol", [128, 8], F32)
        P.ld("sp", gcol[:], self.g_mix[i].rearrange("(k p) -> p k", p=128), R=[self.g_mix], W=[gcol],
             allow_slow_non_contiguous=True)
        L.Ga = P.sbuf("Ga", [128, 2, 8], F32)
        for m in range(2):
            P.stt("dve", L.Ga[:, m], L.modcol[:, m, 1], 1.0, gcol[:], ALU.add, ALU.mult, R=[L.modcol, gcol], W=[L.Ga])
        def bc(name, src_ap, R):
            t = P.sbuf(name, [128, D], F32)
            P.ld("sp", t[:], src_ap.partition_broadcast(128), R=R, W=[t])
            return t
        L.gtf = [bc("gtf", modd[i, m, 5120:6144], [modd]) for m in range(2)]
        return L

    def tail_consts(self, L):
        P = self.P
        i = L.i
        modd = self.modd

        def bc(name, src_ap):
            t = P.sbuf(name, [128, D], F32)
            P.ld("sp", t[:], src_ap.partition_broadcast(128), R=[], W=[t])
            return t
        L.gta = [bc("gta", modd[i, m, 2048:3072]) for m in range(2)]
        L.Sf = [bc("Sf", modd[i, m, 3072:4096]) for m in range(2)]
        gffn = bc("gffn", self.g_ffn[i])
        L.Gf = []
        for m in range(2):
            t = bc("Gf", modd[i, m, 4096:5120])
            P.stt("dve", t[:], t[:], 1.0, gffn[:], ALU.add, ALU.mult, R=[t, gffn], W=[t])
            L.Gf.append(t)

    def rms_rstd(self, xt, sq_junk, ss, rstd, n):
        P = self.P
        P.act(sq_junk[:], xt[:], AF.Square, R=[xt], W=[sq_junk, ss], accum_out=ss[:])
        P.act(rstd[:], ss[:], AF.Sqrt, R=[ss], W=[rstd], scale=1.0 / n, bias=EPS)
        P.recip(rstd[:], rstd[:], R=[rstd], W=[rstd])

    def phase_A(self, L):
        P = self.P
        P.push()
        NB = 2
        xts = [P.sbuf("xt", [128, D], F32) for _ in range(NB)]
        xns = [P.sbuf("xn", [128, D], BF16) for _ in range(NB)]
        junk = P.sbuf("junk", [128, D], BF16)
        sss = [P.sbuf("ss", [128, 1], F32) for _ in range(NB)]
        rss = [P.sbuf("rs", [128, 1], F32) for _ in range(NB)]
        pts = [P.psum("pt", [128, 8, 128], BF16) for _ in range(NB)]
        hts = [P.sbuf("ht", [128, 8, 128], BF16) for _ in range(NB)]
        tmp = [P.sbuf("tmpA", [128, 8, 128], F32) for _ in range(NB)]
        for i in range(NT):
            b = i % NB
            m = 1 if i < 2 else 0
            xt, xn, ss, rs, pt, ht, tp = xts[b], xns[b], sss[b], rss[b], pts[b], hts[b], tmp[b]
            P.ld("sp", xt[:], self.XS[i * 128:(i + 1) * 128, :], R=[self.XS], W=[xt])
            self.rms_rstd(xt, junk, ss, rs, D)
            P.act(xn[:], xt[:], AF.Copy, R=[xt, rs], W=[xn], scale=rs[:])
            for k in range(8):
                P.tr(pt[:, k, :], xn[:, k * 128:(k + 1) * 128], self.ident[:], R=[xn, self.ident], W=[pt])
            P.tt("dve", tp[:], pt[:], L.Ga[:, m].unsqueeze(2).to_broadcast([128, 8, 128]), ALU.mult, R=[pt, L.Ga], W=[tp])
            P.tt("dve", ht[:], tp[:], L.modcol[:, m, 0].unsqueeze(2).to_broadcast([128, 8, 128]), ALU.add,
                 R=[tp, L.modcol], W=[ht])
            P.ld("pool", self.hT[:, i * 128:(i + 1) * 128].rearrange("(k p) t -> p k t", p=128), ht[:], R=[ht], W=[self.hT])
        P.pop()


    def cast_load(self, name, shape, src_ap, R):
        t = self.P.sbuf(name, shape, BF16)
        self.P.ld("pool", t[:], src_ap, R=R, W=[t])
        return t

    def mla_proj(self, L, j, need_ctx):
        P = self.P
        P.push()
        w_in = self.cast_load("w_in", [128, 8, 512], self.mla_w_in[j].rearrange("(k p) n -> p k n", p=128), [self.mla_w_in])
        w_uq = self.cast_load("w_uq", [128, 2, 2048], self.mla_w_uq[j].rearrange("(k p) n -> p k n", p=128), [self.mla_w_uq])
        w_uk = self.cast_load("w_uk", [128, 1024], self.mla_w_uk[j], [self.mla_w_uk])
        w_uv = self.cast_load("w_uv", [128, 1024], self.mla_w_uv[j], [self.mla_w_uv])
        gc = P.sbuf("gc", [128, 8], F32)
        gd = P.sbuf("gd", [128, 8], F32)
        P.ld("sp", gc[:], self.mla_gc[j], R=[self.mla_gc], W=[gc])
        P.tsc("dve", gd[:, 0:2], gc[:, 0:2], 16.0, None, ALU.mult, R=[gc], W=[gd])
        P.tsc("dve", gd[:, 2:3], gc[:, 2:3], float(np.sqrt(128.0)), None, ALU.mult, R=[gc], W=[gd])
        P.tsc("dve", gd[:, 3:5], gc[:, 3:5], 1.0, None, ALU.mult, R=[gc], W=[gd])
        P.tsc("dve", gd[:, 5:7], gc[:, 5:7], float(np.sqrt(192.0)), None, ALU.mult, R=[gc], W=[gd])
        tab = P.sbuf("tab", [128, TA], F32)
        P.ld("sp", tab[:], self.rope_tab[:, :], R=[self.rope_tab], W=[tab])
        ones, fold = self.ones, self.fold
        NB = 512
        hTbs = [P.sbuf("hTb", [128, 8, NB], BF16) for _ in range(2)]
        pbs = [P.psum("pb", [128, NB]) for _ in range(6)]
        psv = P.psum("psv", [128, 1024])
        cyc = {"pb": 0}

        def nps():
            t = pbs[cyc["pb"] % len(pbs)]
            cyc["pb"] += 1
            return t

        def rot(name, shape, dt, n=2):
            ts = [P.sbuf(name, shape, dt) for _ in range(n)]
            st = {"i": 0}

            def f():
                t = ts[st["i"] % n]
                st["i"] += 1
                return t
            return f
        sq_n = rot("sq_n", [128, NB], BF16, 3)
        sq_r = rot("sq_r", [64, NB], BF16, 2)
        rs_t = rot("rs_t", [128, NB], F32, 3)
        o_n = rot("o_n", [128, NB], BF16, 3)
        o_r = rot("o_r", [64, NB], BF16, 3)
        rt_t = rot("rt_t", [128, NB], BF16, 2)
        vt_t = rot("vt_t", [128, 1024], BF16, 2)
        qan = P.sbuf("qan", [128, 2, NB], BF16)
        kvan = P.sbuf("kvan", [128, NB], BF16)
        sqkr = P.sbuf("sqkr", [64, NB], BF16)
        krr = P.sbuf("krr", [64, NB], F32)

        def rstd_from(ss_ps, n, nb):
            rs = rs_t()
            P.act(rs[:, :nb], ss_ps[:, :nb], AF.Sqrt, R=[ss_ps], W=[rs], bias=float(n * EPS))
            P.recip(rs[:, :nb], rs[:, :nb], R=[rs], W=[rs])
            return rs

        blocks = [(0, 256)] + [(256 + NB * b, NB) for b in range(16)]
        for bi, (t0, nb) in enumerate(blocks):
            is_ctx = bi == 0
            hTb = hTbs[bi % 2]
            P.ld("sp", hTb[:, :, :nb], self.hT[:, t0:t0 + nb].rearrange("(k p) t -> p k t", p=128), R=[self.hT], W=[hTb])

            def proj_a(m):
                ps = nps()
                for k in range(8):
                    P.mm(ps[:, :nb], w_in[:, k, m * 128:(m + 1) * 128], hTb[:, k, :nb], start=(k == 0), stop=(k == 7),
                         R=[w_in, hTb], W=[ps])
                return ps
            psq = [proj_a(0), proj_a(1)]
            ss = nps()
            for m in range(2):
                sq = sq_n()
                P.act(sq[:, :nb], psq[m][:, :nb], AF.Square, R=[psq[m]], W=[sq])
                P.mm(ss[:, :nb], ones[:], sq[:, :nb], start=(m == 0), stop=(m == 1), R=[ones, sq], W=[ss])
            rs = rstd_from(ss, 256, nb)
            for m in range(2):
                P.stt("dve", qan[:, m, :nb], psq[m][:, :nb], gd[:, m:m + 1], rs[:, :nb], ALU.mult, ALU.mult,
                      R=[psq[m], gd, rs], W=[qan])
            pk = proj_a(2)
            sq = sq_n()
            P.act(sq[:, :nb], pk[:, :nb], AF.Square, R=[pk], W=[sq])
            ss = nps()
            P.mm(ss[:, :nb], ones[:], sq[:, :nb], R=[ones, sq], W=[ss])
            rs = rstd_from(ss, 128, nb)
            P.stt("dve", kvan[:, :nb], pk[:, :nb], gd[:, 2:3], rs[:, :nb], ALU.mult, ALU.mult, R=[pk, gd, rs], W=[kvan])
            pr = proj_a(3)
            P.act(sqkr[:, :nb], pr[0:64, :nb], AF.Square, R=[pr], W=[sqkr])
            rt = rt_t()
            P.stt("dve", rt[:, :nb], pr[:, :nb], gd[:, 6:7], tab[:, t0:t0 + nb], ALU.mult, ALU.mult, R=[pr, gd, tab], W=[rt])
            pf = nps()
            P.mm(pf[0:64, :nb], fold[:], rt[:, :nb], R=[fold, rt], W=[pf])
            P.cp("act", krr[:, :nb], pf[0:64, :nb], R=[pf], W=[krr])
            for h in range(8):
                pk = nps()
                P.mm(pk[:, :nb], w_uk[:, h * 128:(h + 1) * 128], kvan[:, :nb], R=[w_uk, kvan], W=[pk])
                sq = sq_n()
                P.act(sq[:, :nb], pk[:, :nb], AF.Square, R=[pk], W=[sq])
                ss = nps()
                P.mm(ss[:, :nb], ones[:], sq[:, :nb], start=True, stop=False, R=[ones, sq], W=[ss])
                P.mm(ss[:, :nb], ones[0:64, :], sqkr[:, :nb], start=False, stop=True, R=[ones, sqkr], W=[ss])
                rs = rstd_from(ss, 192, nb)
                kn = o_n()
                P.stt("dve", kn[:, :nb], pk[:, :nb], gd[:, 5:6], rs[:, :nb], ALU.mult, ALU.mult, R=[pk, gd, rs], W=[kn])
                P.ld("pool", self.KT[h, 0:128, t0:t0 + nb], kn[:, :nb], R=[kn], W=[self.KT])
                kr = o_r()
                P.tt("dve", kr[:, :nb], krr[:, :nb], rs[0:64, :nb], ALU.mult, R=[krr, rs], W=[kr])
                P.ld("pool", self.KT[h, 128:192, t0:t0 + nb], kr[:, :nb], R=[kr], W=[self.KT])
                if is_ctx and not need_ctx:
                    continue
                pqn, pqr = nps(), nps()
                for kc in range(2):
                    P.mm(pqn[:, :nb], w_uq[:, kc, h * 256:h * 256 + 128], qan[:, kc, :nb], start=(kc == 0), stop=(kc == 1),
                         R=[w_uq, qan], W=[pqn])
                for kc in range(2):
                    P.mm(pqr[:, :nb], w_uq[:, kc, h * 256 + 128:h * 256 + 256], qan[:, kc, :nb], start=(kc == 0), stop=(kc == 1),
                         R=[w_uq, qan], W=[pqr])
                sq = sq_n()
                P.act(sq[:, :nb], pqn[:, :nb], AF.Square, R=[pqn], W=[sq])
                sr = sq_r()
                P.act(sr[:, :nb], pqr[0:64, :nb], AF.Square, R=[pqr], W=[sr])
                ss = nps()
                P.mm(ss[:, :nb], ones[:], sq[:, :nb], start=True, stop=False, R=[ones, sq], W=[ss])
                P.mm(ss[:, :nb], ones[0:64, :], sr[:, :nb], start=False, stop=True, R=[ones, sr], W=[ss])
                rs = rstd_from(ss, 192, nb)
                qn = o_n()
                P.stt("dve", qn[:, :nb], pqn[:, :nb], gd[:, 3:4], rs[:, :nb], ALU.mult, ALU.mult, R=[pqn, gd, rs], W=[qn])
                P.ld("pool", self.QT[h, 0:128, t0:t0 + nb], qn[:, :nb], R=[qn], W=[self.QT])
                rt = rt_t()
                P.stt("dve", rt[:, :nb], pqr[:, :nb], gd[:, 4:5], tab[:, t0:t0 + nb], ALU.mult, ALU.mult, R=[pqr, gd, tab], W=[rt])
                pf = nps()
                P.mm(pf[0:64, :nb], fold[:], rt[:, :nb], R=[fold, rt], W=[pf])
                qr = o_r()
                P.tt("dve", qr[:, :nb], pf[0:64, :nb], rs[0:64, :nb], ALU.mult, R=[pf, rs], W=[qr])
                P.ld("pool", self.QT[h, 128:192, t0:t0 + nb], qr[:, :nb], R=[qr], W=[self.QT])
            for tt in range(nb // 128):
                for n in range(2):
                    P.mm(psv[:, n * 512:(n + 1) * 512], kvan[:, tt * 128:(tt + 1) * 128], w_uv[:, n * 512:(n + 1) * 512],
                         R=[kvan, w_uv], W=[psv])
                vt = vt_t()
                P.cp("act", vt[:], psv[:], R=[psv], W=[vt])
                P.ld("pool", self.V[t0 + tt * 128:t0 + (tt + 1) * 128, :], vt[:], R=[vt], W=[self.V])
        P.pop()

    def mla_attn(self, L, need_ctx):
        P = self.P
        P.push()
        NQ = 512
        G = 2
        Kns = [P.sbuf("Kn", [128, TA], BF16) for _ in range(2)]
        Krs = [P.sbuf("Kr", [64, TA], BF16) for _ in range(2)]
        Vhs = [P.sbuf("Vh", [128, NT, 128], BF16) for _ in range(2)]
        Qns = [P.sbuf("Qn", [128, NQ], BF16) for _ in range(2)]
        Qrs = [P.sbuf("Qr", [64, NQ], BF16) for _ in range(2)]
        pts = [P.sbuf("pT", [128, G * NQ], BF16) for _ in range(3)]
        accs = [P.sbuf("accL", [128, G * NQ], F32) for _ in range(2)]
        rls = [P.sbuf("rl", [128, NQ], F32) for _ in range(2)]
        ots = [P.sbuf("ot", [128, NQ], BF16) for _ in range(2)]
        onesf = P.sbuf("onesf", [128, 128], F32)
        P.ld("sp", onesf[:], self.cst[:, 128:256], R=[], W=[onesf])
        pss = [P.psum("ps_s", [128, G * NQ]) for _ in range(3)]
        po = P.psum("ps_o", [128, NQ])
        pl = P.psum("ps_l", [128, NQ])
        qblocks = [(256 + NQ * b, NQ, list(range(NT))) for b in range(16)]
        if need_ctx:
            qblocks = [(0, 256, [0, 1])] + qblocks
        st = {"g": 0}
        n_q = 0
        for h in range(8):
            Kn, Kr, Vh = Kns[h % 2], Krs[h % 2], Vhs[h % 2]
            P.ld("sp", Kn[:], self.KT[h, 0:128, :], R=[], W=[Kn])
            P.ld("sp", Kr[:], self.KT[h, 128:192, :], R=[], W=[Kr])
            P.ld("sp", Vh[:], self.V[:, h * 128:(h + 1) * 128].rearrange("(n p) v -> p n v", p=128), R=[], W=[Vh])
            for (t0, nq, keys) in qblocks:
                Qn, Qr = Qns[n_q % 2], Qrs[n_q % 2]
                acc, rl, ot = accs[n_q % 2], rls[n_q % 2], ots[n_q % 2]
                n_q += 1
                P.ld("sp", Qn[:, :nq], self.QT[h, 0:128, t0:t0 + nq], R=[], W=[Qn])
                P.ld("sp", Qr[:, :nq], self.QT[h, 128:192, t0:t0 + nq], R=[], W=[Qr])
                groups = [keys[i:i + G] for i in range(0, len(keys), G)]
                ng = len(groups)
                g0 = st["g"]
                st["g"] += ng
                accv = acc[:].rearrange("p (g q) -> p g q", q=NQ)[:, :, :nq]

                def front(g):
                    ps, pt = pss[(g0 + g) % 3], pts[(g0 + g) % 3]
                    grp = groups[g]
                    for j, kt in enumerate(grp):
                        ks = slice(kt * 128, (kt + 1) * 128)
                        P.mm(ps[:, j * NQ:j * NQ + nq], Kn[:, ks], Qn[:, :nq], start=True, stop=False, R=[Kn, Qn], W=[ps])
                        P.mm(ps[:, j * NQ:j * NQ + nq], Kr[:, ks], Qr[:, :nq], start=False, stop=True, R=[Kr, Qr], W=[ps])
                    psv = ps[:].rearrange("p (g q) -> p g q", q=NQ)[:, :len(grp), :nq]
                    ptv = pt[:].rearrange("p (g q) -> p g q", q=NQ)[:, :len(grp), :nq]
                    P.act(ptv, psv, AF.Exp, R=[ps], W=[pt])
                    if g == 0:
                        P.cp("dve", accv, ptv, R=[pt], W=[acc])
                    else:
                        P.tt("dve", accv, accv, ptv, ALU.add, R=[acc, pt], W=[acc])

                def back(g):
                    pt = pts[(g0 + g) % 3]
                    grp = groups[g]
                    for j, kt in enumerate(grp):
                        P.mm(po[:, :nq], Vh[:, kt, :], pt[:, j * NQ:j * NQ + nq], start=(g == 0 and j == 0),
                             stop=(g == ng - 1 and j == len(grp) - 1), R=[Vh, pt], W=[po])
                for g in range(min(2, ng)):
                    front(g)
                for g in range(ng):
                    if g + 2 < ng:
                        front(g + 2)
                    back(g)
                for j in range(G):
                    P.mm(pl[:, :nq], onesf[:], acc[:, j * NQ:j * NQ + nq], start=(j == 0), stop=(j == G - 1), R=[onesf, acc], W=[pl])
                P.recip(rl[:, :nq], pl[:, :nq], R=[pl], W=[rl])
                P.tt("dve", ot[:, :nq], po[:, :nq], rl[:, :nq], ALU.mult, R=[po, rl], W=[ot])
                P.ld("pool", self.OT[h * 128:(h + 1) * 128, t0:t0 + nq], ot[:, :nq], R=[ot], W=[Res("u")])
        P.pop()

    def moe_state(self, L):
        P = self.P
        M = NS()
        M.logits = P.sbuf("logits", [128, NE, NT], F32)
        M.idxl = P.sbuf("idxl", [128, NE, 8], I32)
        M.idxc = P.sbuf("idxc", [32, NE], I32)
        M.wr = self.cast_load("wr", [128, 8, NE], self.moe_wr[L.i].rearrange("(k p) e -> p k e", p=128), [self.moe_wr])
        return M

    def tail_tiles(self):
        P = self.P
        T = NS()
        T.n = 0
        T.xo = [P.sbuf("xo", [128, D], F32) for _ in range(2)]
        T.tmp = [P.sbuf("ttmp", [128, D], F32) for _ in range(2)]
        T.xn = [P.sbuf("txn", [128, D], F32) for _ in range(2)]
        T.junk = P.sbuf("tjunk", [128, D], BF16)
        T.ss = [P.sbuf("tss", [128, 1], F32) for _ in range(2)]
        T.rs = [P.sbuf("trs", [128, 1], F32) for _ in range(2)]
        T.hb = [P.sbuf("thb", [128, D], BF16) for _ in range(2)]
        T.pt = [P.psum("tpt", [128, 8, 128], BF16) for _ in range(2)]
        T.hTt = [P.sbuf("thTt", [128, 8, 128], BF16) for _ in range(2)]
        T.plg = P.psum("tplg", [128, NE])
        return T

    def tail(self, L, M, T, i, ol_aps, ol_R, xs_rows=None, h_rows=None):
        P = self.P
        m = 1 if i < 2 else 0
        b = T.n % 2
        T.n += 1
        xo, tmp, xn, ss, rs, hb, pt, hTt = T.xo[b], T.tmp[b], T.xn[b], T.ss[b], T.rs[b], T.hb[b], T.pt[b], T.hTt[b]
        rows = slice(i * 128, (i + 1) * 128)
        xs_rows = self.XS[rows, :] if xs_rows is None else xs_rows
        h_rows = self.hrow[rows, :] if h_rows is None else h_rows
        P.ld("sp", xo[:], xs_rows, R=[], W=[xo])
        for ap, c0, w in ol_aps:
            P.tt("dve", tmp[:, c0:c0 + w], ap, L.gta[m][:, c0:c0 + w], ALU.mult, R=list(ol_R) + [L.gta[m]], W=[tmp])
        P.tt("pool", xn[:], tmp[:], xo[:], ALU.add, R=[tmp, xo], W=[xn])
        P.ld("pool", xs_rows, xn[:], R=[xn], W=[Res("u")])
        if L.skip_ctx_moe and i < 2:
            return
        P.act(T.junk[:], xn[:], AF.Square, R=[xn], W=[T.junk, ss], accum_out=ss[:])
        P.act(rs[:], ss[:], AF.Sqrt, R=[ss], W=[rs], scale=1.0 / D, bias=EPS)
        P.recip(rs[:], rs[:], R=[rs], W=[rs])
        P.stt("dve", tmp[:], xn[:], rs[:], L.Gf[m][:], ALU.mult, ALU.mult, R=[xn, rs, L.Gf[m]], W=[tmp])
        P.tt("pool", hb[:], tmp[:], L.Sf[m][:], ALU.add, R=[tmp, L.Sf[m]], W=[hb])
        P.ld("pool", h_rows, hb[:], R=[hb], W=[Res("u")])
        for k in range(8):
            P.tr(pt[:, k, :], hb[:, k * 128:(k + 1) * 128], self.ident[:], R=[hb, self.ident], W=[pt])
        P.cp("act", hTt[:], pt[:], R=[pt], W=[hTt])
        for k in range(8):
            P.mm(T.plg[:], hTt[:, k, :], M.wr[:, k, :], start=(k == 0), stop=(k == 7), R=[hTt, M.wr], W=[T.plg])
        P.cp("dve", M.logits[:, :, i], T.plg[:], R=[T.plg], W=[M.logits])

    def mla_out(self, L, M, j, need_ctx):
        P = self.P
        P.push()
        w_o = self.cast_load("w_o", [128, 8, D], self.mla_w_o[j].rearrange("(k p) n -> p k n", p=128), [self.mla_w_o])
        self.tail_consts(L)
        T = self.tail_tiles()
        OTs = [P.sbuf("OTt", [128, 8, 128], BF16) for _ in range(2)]
        pols = [P.psum("pol", [128, D]) for _ in range(2)]
        for i in range(NT):
            if i < 2 and not need_ctx:
                continue
            OTt, pol = OTs[i % 2], pols[i % 2]
            P.ld("sp", OTt[:], self.OT[:, i * 128:(i + 1) * 128].rearrange("(k p) t -> p k t", p=128), R=[], W=[OTt])
            for n in range(2):
                for k in range(8):
                    P.mm(pol[:, n * 512:(n + 1) * 512], OTt[:, k, :], w_o[:, k, n * 512:(n + 1) * 512],
                         start=(k == 0), stop=(k == 7), R=[OTt, w_o], W=[pol])
            self.tail(L, M, T, i, [(pol[:, 0:512], 0, 512), (pol[:, 512:1024], 512, 512)], [pol])
        P.pop()

    def moe_route(self, L, M):
        P = self.P
        i = L.i
        do_ctx = not L.skip_ctx_moe
        P.push()
        lg = M.logits
        lg_te = lg[:].rearrange("p e t -> p t e")
        aff = P.sbuf("aff", [128, NE, NT], F32)
        aff_te = aff[:].rearrange("p e t -> p t e")
        mx = P.sbuf("mx", [128, NT], F32)
        sm = P.sbuf("sm", [128, NT], F32)
        if not do_ctx:
            P.memset("dve", lg[:, :, 0:2], 0.0, W=[lg])
        P.red("dve", mx[:], lg_te, ALU.max, R=[lg], W=[mx])
        P.tt("dve", aff[:], lg[:], mx[:].unsqueeze(1).to_broadcast([128, NE, NT]), ALU.subtract, R=[lg, mx], W=[aff])
        P.act(aff[:], aff[:], AF.Exp, R=[aff], W=[aff])
        P.red("dve", sm[:], aff_te, ALU.add, R=[aff], W=[sm])
        P.recip(sm[:], sm[:], R=[sm], W=[sm])
        P.tt("dve", aff[:], aff[:], sm[:].unsqueeze(1).to_broadcast([128, NE, NT]), ALU.mult, R=[aff, sm], W=[aff])
        affc = P.sbuf("affc", [128, NT, NE], F32)
        P.cp("dve", affc[:], aff_te, R=[aff], W=[affc])
        affrow_v = self.affrow.t.rearrange("(t p) e -> p t e", p=128)
        aff_w = []
        if getattr(L, "fmap", False):
            pieces = [(affrow_v[:, 0:2, :], affc[:, 0:2, :])]
            av = self.affrow.t[256:, :].rearrange("(p q) e -> p q e", q=64)
            for q in range(4):
                pieces.append((av[:, q * 16:(q + 1) * 16, :], affc[:, 2 + q * 16:2 + (q + 1) * 16, :]))
        else:
            pieces = [(affrow_v[:, q * 11:(q + 1) * 11, :], affc[:, q * 11:(q + 1) * 11, :]) for q in range(6)]
        for dst, src in pieces:
            r_ = Res("affw")
            aff_w.append(r_)
            P.ld("pool", dst, src, R=[affc], W=[r_])
        tokid_sb = self.tokid_f_sb if getattr(L, "fmap", False) else self.tokid_sb
        lo = P.sbuf("lo", [128, 2, NE], F32)
        mid = P.sbuf("mid", [128, 2, NE], F32)
        cnt = P.sbuf("cnt", [128, 2, NE], F32)
        gew = P.sbuf("gew", [128, 2, NE], F32)
        cmp_ = P.sbuf("cmp", [128, NE, NT], BF16)
        cmp_f = cmp_[:].rearrange("p e t -> p (e t)")
        cps = P.psum("cps", [128, 1536])
        cps_v = cps[:, 0:NE * NT].rearrange("p (e t) -> p e t", t=NT)
        P.memset("dve", lo[:], 0.0, W=[lo])

        def count(thr):
            P.tt("dve", cmp_[:, :, 2:NT], aff[:, :, 2:NT], thr[:, 0].unsqueeze(2).to_broadcast([128, NE, NT - 2]), ALU.is_ge,
                 R=[aff, thr], W=[cmp_])
            P.tt("dve", cmp_[:, :, 0:2], aff[:, :, 0:2], thr[:, 1].unsqueeze(2).to_broadcast([128, NE, 2]), ALU.is_ge,
                 R=[aff, thr], W=[cmp_])
            for n, (c0, w) in enumerate(((0, 512), (512, 512), (1024, 32))):
                P.mm(cps[:, c0:c0 + w], self.ones[:], cmp_f[:, c0:c0 + w], R=[self.ones, cmp_], W=[cps])

        NIT = 26
        for it in range(NIT):
            w = 0.5 ** (it + 1)
            P.tsc("dve", mid[:], lo[:], w, None, ALU.add, R=[lo], W=[mid])
            count(mid)
            P.red("dve", cnt[:, 0], cps_v[:, :, 2:NT], ALU.add, R=[cps], W=[cnt])
            P.red("dve", cnt[:, 1], cps_v[:, :, 0:2], ALU.add, R=[cps], W=[cnt])
            P.tsc("dve", gew[:, 0], cnt[:, 0], 1023.5, w, ALU.is_ge, ALU.mult, R=[cnt], W=[gew])
            P.tsc("dve", gew[:, 1], cnt[:, 1], 31.5, w, ALU.is_ge, ALU.mult, R=[cnt], W=[gew])
            P.tt("dve", lo[:], lo[:], gew[:], ALU.add, R=[lo, gew], W=[lo])
        count(lo)
        sel = cmp_
        tot = P.sbuf("tot", [128, NE, NT], F32)
        P.cp("dve", tot[:], cps_v, R=[cps], W=[tot])
        sa = P.sbuf("sa", [128, NE, 64], F32)
        sb = P.sbuf("sb", [128, NE, 64], F32)
        P.cp("dve", sa[:], tot[:, :, 2:NT], R=[tot], W=[sa])
        cur, oth = sa, sb
        sh = 1
        while sh < 64:
            P.cp("dve", oth[:, :, 0:sh], cur[:, :, 0:sh], R=[cur], W=[oth])
            P.tt("dve", oth[:, :, sh:64], cur[:, :, sh:64], cur[:, :, 0:64 - sh], ALU.add, R=[cur], W=[oth])
            cur, oth = oth, cur
            sh *= 2
        offs = P.sbuf("offs", [128, NE, NT], F32)
        P.tt("dve", offs[:, :, 2:NT], cur[:], tot[:, :, 2:NT], ALU.subtract, R=[cur, tot], W=[offs])
        P.tsc("dve", offs[:, :, 2:NT], offs[:, :, 2:NT], 32.0, None, ALU.add, R=[offs], W=[offs])
        P.memset("dve", offs[:, :, 0:1], 0.0, W=[offs])
        P.cp("dve", offs[:, :, 1:2], tot[:, :, 0:1], R=[tot], W=[offs])
        ips = P.psum("ips", [128, 1536])
        for t in range(NT):
            P.mm(ips[:, t * NE:(t + 1) * NE], self.ustrict[:], sel[:, :, t], R=[self.ustrict, sel], W=[ips])
        ips_v = ips[:, 0:NT * NE].rearrange("p (t e) -> p e t", e=NE)
        slot = P.sbuf("slot", [128, NE, NT], F32)
        P.tt("dve", slot[:], ips_v, offs[:], ALU.add, R=[ips, offs], W=[slot])
        P.stt("dve", slot[:], slot[:], self.dumpc[:, 0:1], sel[:], ALU.subtract, ALU.mult, R=[slot, self.dumpc, sel], W=[slot])
        P.tsc("dve", slot[:], slot[:], self.dumpc[:, 0:1], None, ALU.add, R=[slot, self.dumpc], W=[slot])
        P.tt("dve", slot[:], slot[:], self.eoffs[:].unsqueeze(2).to_broadcast([128, NE, NT]), ALU.add, R=[slot, self.eoffs], W=[slot])
        smap = P.sbuf("smap", [128, NE, NT], I32)
        P.cp("dve", smap[:], slot[:], R=[slot], W=[smap])
        idx_res = []
        for t in range(NT):
            if t < 2 and not do_ctx:
                continue
            for e in range(NE):
                r_ = Res("idxw")
                idx_res.append(r_)
                P.scatter(self.idxd[:, :], tokid_sb[:, t:t + 1], smap[:, e, t:t + 1], R=[smap, tokid_sb], W=[r_])
        idxl, idxc = M.idxl, M.idxc
        P.ld("sp", idxl[:], self.idxd.t.rearrange("(e s) o -> e (s o)", e=NE)[:, 32:1056].rearrange("e (p k) -> p e k", k=8), R=idx_res, W=[idxl])
        if do_ctx:
            P.ld("sp", idxc[:], self.idxd.t.rearrange("(e s) o -> e (s o)", e=NE)[:, 0:32].rearrange("e p -> p e"), R=idx_res, W=[idxc],
                 allow_slow_non_contiguous=True)
        P.pop()

    def moe_experts(self, L, M):
        P = self.P
        i = L.i
        do_ctx = not L.skip_ctx_moe
        idxl, idxc = M.idxl, M.idxc
        aff_w = []
        P.push()
        NCH = 9 if do_ctx else 8
        NTOK = 1056 if do_ctx else 1024
        wsets = [[P.sbuf("wexp", [128, 8, D], BF16) for _ in range(3)] for _ in range(2)]
        xes = [P.sbuf("xe", [128, 9, D], BF16) for _ in range(1)]
        gas = [P.sbuf("ga", [128, 9, NE], F32) for _ in range(2)]
        xeT = P.sbuf("xeT", [128, 8, 1056], BF16)
        hid = P.sbuf("hid", [128, 8, 1056], BF16)
        sg = [P.sbuf("sg", [128, 512], F32) for _ in range(2)]
        ys = [P.sbuf("ys", [128, D], F32) for _ in range(2)]
        ptp = [P.psum("ptp", [128, 8, 128], BF16) for _ in range(1)]
        pbank = [P.psum("pbk", [128, 512]) for _ in range(6)]
        st = {"b": 0, "y": 0}

        def bank():
            t = pbank[st["b"] % len(pbank)]
            st["b"] += 1
            return t
        srcs = (self.moe_wg, self.moe_wu, self.moe_wd)

        def prefetch_w(e):
            ws = wsets[e % 2]
            for a in range(3):
                P.ld("pool", ws[a][:], srcs[a][i, e].rearrange("(k p) n -> p k n", p=128), R=[], W=[ws[a]])

        def prefetch(e):
            xe, ga = xes[0], gas[e % 2]
            for k in range(8):
                P.gather(xe[:, k, :], self.hrow[:, :], idxl[:, e, k:k + 1], R=[idxl], W=[xe])
                P.gather(ga[:, k, :], self.affrow[:, :], idxl[:, e, k:k + 1], R=[idxl] + aff_w, W=[ga])
            if do_ctx:
                P.gather(xe[0:32, 8, :], self.hrow[:, :], idxc[0:32, e:e + 1], R=[idxc], W=[xe])
                P.gather(ga[0:32, 8, :], self.affrow[:, :], idxc[0:32, e:e + 1], R=[idxc] + aff_w, W=[ga])

        grp = [[Res("scA") for _ in range(9)], [Res("scB") for _ in range(9)]]
        prefetch_w(0)
        prefetch(0)
        for e in range(NE):
            if e + 1 < NE:
                prefetch_w(e + 1)
            wg, wu, wd = wsets[e % 2]
            xe, ga = xes[0], gas[e % 2]
            for k in range(NCH):
                rows = 128 if k < 8 else 32
                pt = ptp[0]
                for dk in range(8):
                    P.tr(pt[:, dk, 0:rows], xe[0:rows, k, dk * 128:(dk + 1) * 128], self.ident[0:rows, 0:rows],
                         R=[xe, self.ident], W=[pt])
                P.cp("act", xeT[:, :, k * 128:k * 128 + rows], pt[:, :, 0:rows], R=[pt], W=[xeT])
            if e + 1 < NE:
                prefetch(e + 1)
            nblks = [(0, 512), (512, 512)] + ([(1024, 32)] if do_ctx else [])
            for f in range(8):
                fs = slice(f * 128, (f + 1) * 128)
                for (c0, w) in nblks:
                    pg, pu = bank(), bank()
                    for dk in range(8):
                        P.mm(pg[:, :w], wg[:, dk, fs], xeT[:, dk, c0:c0 + w], start=(dk == 0), stop=(dk == 7), R=[wg, xeT], W=[pg])
                    for dk in range(8):
                        P.mm(pu[:, :w], wu[:, dk, fs], xeT[:, dk, c0:c0 + w], start=(dk == 0), stop=(dk == 7), R=[wu, xeT], W=[pu])
                    s_ = sg[st["y"] % 2]
                    st["y"] += 1
                    P.act(s_[:, :w], pg[:, :w], AF.Silu, R=[pg], W=[s_])
                    P.tt("dve", hid[:, f, c0:c0 + w], s_[:, :w], pu[:, :w], ALU.mult, R=[s_, pu], W=[hid])
            for k in range(NCH):
                rows = 128 if k < 8 else 32
                m = 0 if k < 8 else 1
                y = ys[k % 2]
                for n in range(2):
                    py = bank()
                    for f in range(8):
                        P.mm(py[0:rows, :], hid[:, f, k * 128:k * 128 + rows], wd[:, f, n * 512:(n + 1) * 512],
                             start=(f == 0), stop=(f == 7), R=[hid, wd], W=[py])
                    P.stt("dve", y[0:rows, n * 512:(n + 1) * 512], py[0:rows, :], ga[0:rows, k, e:e + 1],
                          L.gtf[m][0:rows, n * 512:(n + 1) * 512], ALU.mult, ALU.mult, R=[py, ga, L.gtf[m]], W=[y])
                ia = idxl[:, e, k:k + 1] if k < 8 else idxc[0:32, e:e + 1]
                P.scatter(self.XS[:, :], y[0:rows, :], ia, R=[y, idxl, idxc] + grp[(e + 1) % 2], W=[grp[e % 2][k]],
                          add=True, bound=TA - 1)
        P.pop()


    def ssd_in(self, L):
        P = self.P
        P.push()
        w = P.sbuf("ssd_w", [128, 8, 5184], BF16)
        wsrc = self.ssd_w_in.t.rearrange("(k p) n -> p k n", p=128)
        for c0 in range(0, 5184, 1728):
            P.ld("pool", w[:, :, c0:c0 + 1728], wsrc[:, :, c0:c0 + 1728], R=[], W=[w])
        cw = P.sbuf("cw", [128, 24, 3], F32)
        for k in range(3):
            P.ld("sp", cw[:, :, k], self.ssd_conv_w[k].rearrange("(f p) -> p f", p=128), R=[], W=[cw])
        cb = P.sbuf("cb", [128, 24], F32)
        P.ld("sp", cb[:], self.ssd_conv_b.t.rearrange("(f p) -> p f", p=128), R=[], W=[cb])
        dtb = P.sbuf("dtb", [128, 64], F32)
        P.ld("sp", dtb[:], self.ssd_dt_bias.t.partition_broadcast(128), R=[], W=[dtb])
        hTws = [P.sbuf("hTw", [128, 8, 258], BF16) for _ in range(2)]
        cv = P.sbuf("cv", [128, 24, 256], BF16)
        accs = [P.sbuf("acc", [128, 256], F32) for _ in range(2)]
        xbs = [P.sbuf("xb_tm", [128, 2560], BF16) for _ in range(2)]
        zss = [P.sbuf("zs_tm", [128, 2048], BF16) for _ in range(2)]
        dtr = [P.sbuf("dtr", [128, 64], F32) for _ in range(2)]
        banks = [P.psum("sbk", [128, 512]) for _ in range(5)]
        ptrs = [P.psum("sptr", [128, 8, 128], BF16) for _ in range(2)]
        st = {"b": 0, "t": 0, "a": 0}

        def bank():
            t = banks[st["b"] % len(banks)]
            st["b"] += 1
            return t
        BTv = self.BT_d.t.rearrange("(g p) t -> p g t", p=128)
        CTv = self.CT_d.t.rearrange("(g p) t -> p g t", p=128)
        hTv = self.hT.t.rearrange("(k p) t -> p k t", p=128)
        wins = [(0, True, True)] + [(256 + 256 * q, q == 0, q == 31) for q in range(32)]
        for wi, (t0, lz, rz) in enumerate(wins):
            hTw = hTws[wi % 2]
            lo = t0 - (0 if lz else 1)
            hi = t0 + 256 + (0 if rz else 1)
            c_lo = 1 if lz else 0
            P.ld("sp", hTw[:, :, c_lo:c_lo + (hi - lo)], hTv[:, :, lo:hi], R=[], W=[hTw])
            if lz:
                P.memset("pool", hTw[:, :, 0:1], 0.0, W=[hTw])
            if rz:
                P.memset("pool", hTw[:, :, 257:258], 0.0, W=[hTw])
            for fc in range(24):
                ps = bank()
                for k in range(8):
                    P.mm(ps[:, 0:258], w[:, k, 2048 + fc * 128:2048 + (fc + 1) * 128], hTw[:, k, :], start=(k == 0), stop=(k == 7),
                         R=[w, hTw], W=[ps])
                acc = accs[st["a"] % 2]
                st["a"] += 1
                P.tsc("dve", acc[:], ps[:, 0:256], cw[:, fc, 0:1], None, ALU.mult, R=[ps, cw], W=[acc])
                P.stt("dve", acc[:], ps[:, 1:257], cw[:, fc, 1:2], acc[:], ALU.mult, ALU.add, R=[ps, cw, acc], W=[acc])
                P.stt("dve", acc[:], ps[:, 2:258], cw[:, fc, 2:3], acc[:], ALU.mult, ALU.add, R=[ps, cw, acc], W=[acc])
                P.act(cv[:, fc, :], acc[:], AF.Silu, R=[acc, cb], W=[cv], bias=cb[:, fc:fc + 1])
            P.ld("pool", BTv[:, :, t0:t0 + 256], cv[:, 16:20, :], R=[cv], W=[Res("u")])
            P.ld("pool", CTv[:, :, t0:t0 + 256], cv[:, 20:24, :], R=[cv], W=[Res("u")])
            for tt in range(2):
                rows = slice(t0 + tt * 128, t0 + (tt + 1) * 128)
                xb, zs, dr_ = xbs[tt], zss[tt], dtr[tt]
                for f0 in (0, 8, 16):
                    nf = min(8, 20 - f0)
                    ptr = ptrs[st["t"] % 2]
                    st["t"] += 1
                    for q in range(nf):
                        P.tr(ptr[:, q, :], cv[:, f0 + q, tt * 128:(tt + 1) * 128], self.ident[:], R=[cv, self.ident], W=[ptr])
                    P.cp("act", xb[:, f0 * 128:(f0 + nf) * 128], ptr[:, 0:nf, :], R=[ptr], W=[xb])
                P.ld("pool", self.xs_d[rows, :], xb[:, 0:2048], R=[xb], W=[Res("u")])
                P.ld("pool", self.B_d[rows, :], xb[:, 2048:2560], R=[xb], W=[Res("u")])
                for nb in range(4):
                    ps = bank()
                    for k in range(8):
                        P.mm(ps[:], hTw[:, k, 1 + tt * 128:1 + (tt + 1) * 128], w[:, k, nb * 512:(nb + 1) * 512], start=(k == 0), stop=(k == 7),
                             R=[w, hTw], W=[ps])
                    P.act(zs[:, nb * 512:(nb + 1) * 512], ps[:], AF.Silu, R=[ps], W=[zs])
                P.ld("pool", self.zs_d[rows, :], zs[:], R=[zs], W=[Res("u")])
                ps = bank()
                for k in range(8):
                    P.mm(ps[:, 0:64], hTw[:, k, 1 + tt * 128:1 + (tt + 1) * 128], w[:, k, 5120:5184], start=(k == 0), stop=(k == 7),
                         R=[w, hTw], W=[ps])
                P.tt("dve", dr_[:], ps[:, 0:64], dtb[:], ALU.add, R=[ps, dtb], W=[dr_])
                P.act(dr_[:], dr_[:], AF.Exp, R=[dr_], W=[dr_])
                P.act(dr_[:], dr_[:], AF.Ln, R=[dr_], W=[dr_], bias=1.0)
                P.ld("pool", self.dt_d[rows, :], dr_[:], R=[dr_], W=[Res("u")])
        P.pop()

    def ssd_scan(self, L, dr):
        P = self.P
        P.push()
        c2 = P.sbuf("c2", [128, 640], F32)
        P.ld("sp", c2[:], self.cst2[:, :], R=[], W=[c2])
        LE, GE, GT, LT, onesf = (c2[:, q * 128:(q + 1) * 128] for q in range(5))
        m1 = LE if dr == 0 else GE
        lm = GT if dr == 0 else LT
        a_b = P.sbuf("a_b", [128, 32], F32)
        P.ld("sp", a_b[:], self.ssd_a_log[dr].partition_broadcast(128), R=[], W=[a_b])
        P.act(a_b[:], a_b[:], AF.Exp, R=[a_b], W=[a_b])
        P.tsc("dve", a_b[:], a_b[:], -1.0, None, ALU.mult, R=[a_b], W=[a_b])
        stf = [P.sbuf("stf", [128, 512], F32) for _ in range(4)]
        stb = [P.sbuf("stb", [128, 512], BF16) for _ in range(4)]
        for g in range(4):
            P.memset("dve", stf[g][:], 0.0, W=[stf[g]])
            P.memset("dve", stb[g][:], 0.0, W=[stb[g]])
        NB = 2
        xss = [P.sbuf("s_xs", [128, 2048], BF16) for _ in range(NB)]
        Bts = [P.sbuf("s_B", [128, 512], BF16) for _ in range(NB)]
        BTs = [P.sbuf("s_BT", [128, 4, 128], BF16) for _ in range(NB)]
        CTs = [P.sbuf("s_CT", [128, 4, 128], BF16) for _ in range(NB)]
        dts = [P.sbuf("s_dt", [128, 64], F32) for _ in range(NB)]
        dtas = [P.sbuf("s_dta", [128, 32], F32) for _ in range(NB)]
        E3s = [P.sbuf("s_E3", [128, 3, 32], F32) for _ in range(NB)]
        xdts = [P.sbuf("s_xdt", [128, 2048], BF16) for _ in range(NB)]
        xdds = [P.sbuf("s_xdd", [128, 2048], BF16) for _ in range(NB)]
        ys = [P.sbuf("s_y", [128, 2048], F32) for _ in range(NB)]
        rhsEs = [P.sbuf("s_rhsE", [128, 8, 128], F32) for _ in range(2)]
        Lts = [P.sbuf("s_Lt", [128, 8, 128], BF16) for _ in range(2)]
        MTs = [P.sbuf("s_MT", [128, 8, 128], BF16) for _ in range(2)]
        cbms = [P.sbuf("s_cbm", [128, 128], BF16) for _ in range(2)]
        t1s = [P.sbuf("s_t1", [128, 512], F32) for _ in range(2)]
        e3p = P.psum("e3p", [128, 3, 32])
        args = [P.psum("argp", [128, 8, 128]) for _ in range(1)]
        cbp = P.psum("cbp", [128, 128])
        yp = [P.psum("yp", [128, 512]) for _ in range(2)]
        yop = P.psum("yop", [128, 512])
        sp_ = P.psum("sps", [128, 512])
        BTv = self.BT_d.t.rearrange("(g p) t -> p g t", p=128)
        CTv = self.CT_d.t.rearrange("(g p) t -> p g t", p=128)
        order = list(range(NT)) if dr == 0 else [1, 0] + list(range(NT - 1, 1, -1))
        yd = self.yf_d if dr == 0 else self.yb_d
        ng = 0
        for ci, c in enumerate(order):
            b = ci % NB
            rows = slice(c * 128, (c + 1) * 128)
            xs, Bt, BT, CT, dt, dta, E3, xdt, xdd, y = xss[b], Bts[b], BTs[b], CTs[b], dts[b], dtas[b], E3s[b], xdts[b], xdds[b], ys[b]
            P.ld("sp", xs[:], self.xs_d[rows, :], R=[], W=[xs])
            P.ld("sp", Bt[:], self.B_d[rows, :], R=[], W=[Bt])
            P.ld("sp", BT[:], BTv[:, :, rows], R=[], W=[BT])
            P.ld("sp", CT[:], CTv[:, :, rows], R=[], W=[CT])
            P.ld("sp", dt[:], self.dt_d[rows, :], R=[], W=[dt])
            dtd = dt[:, dr * 32:(dr + 1) * 32]
            P.tt("dve", dta[:], dtd, a_b[:], ALU.mult, R=[dt, a_b], W=[dta])
            P.mm(e3p[:, 0, :], m1, dta[:], R=[c2, dta], W=[e3p])
            P.mm(e3p[:, 1, :], lm, dta[:], R=[c2, dta], W=[e3p])
            P.mm(e3p[:, 2, :], onesf, dta[:], R=[c2, dta], W=[e3p])
            P.act(E3[:], e3p[:], AF.Exp, R=[e3p], W=[E3])
            xs3 = xs[:].rearrange("p (h q) -> p h q", q=64)
            P.tt("dve", xdt[:].rearrange("p (h q) -> p h q", q=64), xs3, dtd.unsqueeze(2).to_broadcast([128, 32, 64]), ALU.mult,
                 R=[xs, dt], W=[xdt])
            P.tt("pool", xdd[:].rearrange("p (h q) -> p h q", q=64), xdt[:].rearrange("p (h q) -> p h q", q=64),
                 E3[:, 1, :].unsqueeze(2).to_broadcast([128, 32, 64]), ALU.mult, R=[xdt, E3], W=[xdd])
            for g in range(4):
                hs = slice(g * 8, (g + 1) * 8)
                q2 = ng % 2
                ng += 1
                rhsE, Lt, MT, cbm, t1 = rhsEs[q2], Lts[q2], MTs[q2], cbms[q2], t1s[q2]
                arg = args[0]
                P.tt("pool", rhsE[:], dta[:, hs].unsqueeze(2).to_broadcast([128, 8, 128]), m1.unsqueeze(1).to_broadcast([128, 8, 128]),
                     ALU.mult, R=[dta, c2], W=[rhsE])
                for hf in range(2):
                    P.mm(arg[:, hf * 4:(hf + 1) * 4, :], lm, rhsE[:, hf * 4:(hf + 1) * 4, :], R=[c2, rhsE], W=[arg])
                P.act(Lt[:], arg[:], AF.Exp, R=[arg], W=[Lt])
                P.mm(cbp[:], BT[:, g, :], CT[:, g, :], R=[BT, CT], W=[cbp])
                P.tt("dve", cbm[:], cbp[:], m1, ALU.mult, R=[cbp, c2], W=[cbm])
                P.tt("dve", MT[:], Lt[:], cbm[:].unsqueeze(1).to_broadcast([128, 8, 128]), ALU.mult, R=[Lt, cbm], W=[MT])
                ypg = yp[g % 2]
                for hl in range(8):
                    h = g * 8 + hl
                    P.mm(ypg[:, hl * 64:(hl + 1) * 64], MT[:, hl, :], xdt[:, h * 64:(h + 1) * 64], R=[MT, xdt], W=[ypg])
                P.mm(yop[:], CT[:, g, :], stb[g][:], R=[CT, stb[g]], W=[yop])
                P.tt("dve", t1[:].rearrange("p (h q) -> p h q", q=64), yop[:].rearrange("p (h q) -> p h q", q=64),
                     E3[:, 0, hs].unsqueeze(2).to_broadcast([128, 8, 64]), ALU.mult, R=[yop, E3], W=[t1])
                P.tt("dve", y[:, g * 512:(g + 1) * 512], t1[:], ypg[:], ALU.add, R=[t1, ypg], W=[y])
                P.mm(sp_[:], Bt[:, g * 128:(g + 1) * 128], xdd[:, g * 512:(g + 1) * 512], R=[Bt, xdd], W=[sp_])
                P.tt("pool", stf[g][:].rearrange("p (h q) -> p h q", q=64), stf[g][:].rearrange("p (h q) -> p h q", q=64),
                     E3[:, 2, hs].unsqueeze(2).to_broadcast([128, 8, 64]), ALU.mult, R=[stf[g], E3], W=[stf[g]])
                P.tt("dve", stf[g][:], stf[g][:], sp_[:], ALU.add, R=[stf[g], sp_], W=[stf[g]])
                P.cp("act", stb[g][:], stf[g][:], R=[stf[g]], W=[stb[g]])
            P.ld("pool", yd[rows, :], y[:], R=[y], W=[Res("u")])
        P.pop()

    def ssd_out(self, L, M):
        P = self.P
        P.push()
        w_out = self.cast_load("ssd_wo", [128, 16, D], self.ssd_w_out.t.rearrange("(k p) n -> p k n", p=128), [])
        self.tail_consts(L)
        T = self.tail_tiles()
        D_b = P.sbuf("D_b", [128, 32], F32)
        P.ld("sp", D_b[:], self.ssd_d.t.partition_broadcast(128), R=[], W=[D_b])
        gn_b = P.sbuf("gn_b", [128, 2048], F32)
        P.ld("sp", gn_b[:], self.ssd_g_norm.t.partition_broadcast(128), R=[], W=[gn_b])
        yfs = [P.sbuf("o_yf", [128, 2048], F32) for _ in range(2)]
        ybs = [P.sbuf("o_yb", [128, 2048], F32) for _ in range(2)]
        xss = [P.sbuf("o_xs", [128, 2048], BF16) for _ in range(2)]
        zss = [P.sbuf("o_zs", [128, 2048], BF16) for _ in range(2)]
        gat = P.sbuf("o_gat", [128, 2048], F32)
        junk = P.sbuf("o_junk", [128, 512], BF16)
        ss4 = P.sbuf("o_ss4", [128, 4], F32)
        rs4 = P.sbuf("o_rs4", [128, 4], F32)
        nrm = P.sbuf("o_nrm", [128, 2048], BF16)
        nT = P.sbuf("o_nT", [128, 16, 128], BF16)
        ptr = [P.psum("o_ptr", [128, 8, 128], BF16) for _ in range(2)]
        pol = P.psum("o_pol", [128, D])
        for i in range(NT):
            b = i % 2
            rows = slice(i * 128, (i + 1) * 128)
            yf, yb, xs, zs = yfs[b], ybs[b], xss[b], zss[b]
            P.ld("sp", yf[:], self.yf_d[rows, :], R=[], W=[yf])
            P.ld("sp", yb[:], self.yb_d[rows, :], R=[], W=[yb])
            P.ld("sp", xs[:], self.xs_d[rows, :], R=[], W=[xs])
            P.ld("sp", zs[:], self.zs_d[rows, :], R=[], W=[zs])
            P.tt("pool", yf[:], yf[:], yb[:], ALU.add, R=[yf, yb], W=[yf])
            P.tt("dve", yb[:].rearrange("p (h q) -> p h q", q=64), xs[:].rearrange("p (h q) -> p h q", q=64),
                 D_b[:].unsqueeze(2).to_broadcast([128, 32, 64]), ALU.mult, R=[xs, D_b, yb], W=[yb])
            P.tt("pool", yf[:], yf[:], yb[:], ALU.add, R=[yf, yb], W=[yf])
            P.tt("dve", gat[:], yf[:], zs[:], ALU.mult, R=[yf, zs], W=[gat])
            for g in range(4):
                P.act(junk[:], gat[:, g * 512:(g + 1) * 512], AF.Square, R=[gat], W=[junk, ss4], accum_out=ss4[:, g:g + 1])
            P.act(rs4[:], ss4[:], AF.Sqrt, R=[ss4], W=[rs4], scale=1.0 / 512, bias=EPS)
            P.recip(rs4[:], rs4[:], R=[rs4], W=[rs4])
            for g in range(4):
                gs = slice(g * 512, (g + 1) * 512)
                P.stt("dve", nrm[:, gs], gat[:, gs], rs4[:, g:g + 1], gn_b[:, gs], ALU.mult, ALU.mult, R=[gat, rs4, gn_b], W=[nrm])
            for hf in range(2):
                for q in range(8):
                    fk = hf * 8 + q
                    P.tr(ptr[hf][:, q, :], nrm[:, fk * 128:(fk + 1) * 128], self.ident[:], R=[nrm, self.ident], W=[ptr[hf]])
                P.cp("act", nT[:, hf * 8:(hf + 1) * 8, :], ptr[hf][:], R=[ptr[hf]], W=[nT])
            for n in range(2):
                for fk in range(16):
                    P.mm(pol[:, n * 512:(n + 1) * 512], nT[:, fk, :], w_out[:, fk, n * 512:(n + 1) * 512], start=(fk == 0), stop=(fk == 15),
                         R=[nT, w_out], W=[pol])
            self.tail(L, M, T, i, [(pol[:, 0:512], 0, 512), (pol[:, 512:1024], 512, 512)], [pol])
        P.pop()

    def fnet_f1(self, L):
        P = self.P
        P.push()
        DF = self.cast_load("DF", [128, 2, 512], self.fcst[:, 0:1024].rearrange("p (q n) -> p q n", q=2), [])
        hTv = self.hT.t.rearrange("(k p) t -> p k t", p=128)
        hTs = [P.sbuf("f_hT", [128, 8, 128], BF16) for _ in range(2)]
        yts = [P.sbuf("f_yt", [128, 2, 4, 256], BF16) for _ in range(2)]
        yps = [P.psum("f_yps", [128, 4, 512]) for _ in range(2)]
        for i in range(NT):
            hTt, yt, yp = hTs[i % 2], yts[i % 2], yps[i % 2]
            rows = slice(i * 128, (i + 1) * 128)
            P.ld("sp", hTt[:], hTv[:, :, rows], R=[], W=[hTt])
            for g in range(4):
                for q in range(2):
                    P.mm(yp[:, g, :], hTt[:, 2 * g + q, :], DF[:, q, :], start=(q == 0), stop=(q == 1), R=[hTt, DF], W=[yp])
            for c in range(2):
                P.cp("act" if c == 0 else "dve", yt[:, c], yp[:, :, c * 256:(c + 1) * 256], R=[yp], W=[yt])
            for c in range(2):
                P.ld("pool", self.Yd[c, rows, :], yt[:, c].rearrange("p g m -> p (g m)"), R=[yt], W=[Res("u")])
        P.pop()

    def fnet_f2(self, L):
        P = self.P
        P.push()
        W1 = self.cast_load("W1big", [128, 128], self.fcst[:, 2048:2176], [])
        Ydv = self.Yd.t[:, 256:, :].rearrange("c (a b) f -> c a b f", b=128)
        Zdv = self.Zd.t.rearrange("c k t f -> (c k) t f")
        Ins = [P.sbuf("f_In", [128, 4, D], BF16) for _ in range(2)]
        Zts = [P.sbuf("f_Zt", [128, 4, D], BF16) for _ in range(2)]
        zps = [P.psum("f_zps", [128, D]) for _ in range(3)]
        nz = 0
        for bt in range(32):
            In, Zt = Ins[bt % 2], Zts[bt % 2]
            for c in range(2):
                P.ld("sp", In[c * 64:(c + 1) * 64, :, :], Ydv[c, :, 4 * bt:4 * bt + 4, :], R=[], W=[In])
            for q in range(4):
                zp = zps[nz % 3]
                nz += 1
                for n in range(2):
                    P.mm(zp[:, n * 512:(n + 1) * 512], W1[:], In[:, q, n * 512:(n + 1) * 512], R=[W1, In], W=[zp])
                P.cp("act" if q % 2 == 0 else "dve", Zt[:, q, :], zp[:], R=[zp], W=[Zt])
            P.ld("pool", Zdv[:, 4 * bt:4 * bt + 4, :], Zt[:], R=[Zt], W=[Res("u")])
        P.pop()

    def fnet_f3(self, L, M):
        P = self.P
        P.push()
        MT = P.sbuf("f_MT", [128, 64, 256], BF16)
        for kb in range(8):
            P.ld("pool", MT[:, kb * 8:(kb + 1) * 8, :], self.fM[:, kb * 2048:(kb + 1) * 2048].rearrange("p (a n) -> p a n", a=8), R=[], W=[MT])
        w_o = self.cast_load("f_wo", [128, 8, D], self.fnet_w_o.t.rearrange("(k p) n -> p k n", p=128), [])
        DF2 = self.cast_load("DF2", [128, 2, 512], self.fcst[:, 1024:2048].rearrange("p (q n) -> p q n", q=2), [])
        self.tail_consts(L)
        T = self.tail_tiles()
        mps = P.psum("f_mps", [128, 1024])
        pol = P.psum("f_pol", [128, D])

        def outproj(mT_ap_fn):
            for n in range(2):
                for fc in range(8):
                    P.mm(pol[:, n * 512:(n + 1) * 512], mT_ap_fn(fc), w_o[:, fc, n * 512:(n + 1) * 512], start=(fc == 0), stop=(fc == 7),
                         R=[w_o] + mT_R, W=[pol])
        Yc = P.sbuf("f_Yc", [128, 2, 2, D], BF16)
        for c in range(2):
            for q in range(2):
                P.ld("sp", Yc[:, q, c, :], self.Yd[c, q * 128:(q + 1) * 128, :], R=[], W=[Yc])
        mTc = P.sbuf("f_mTc", [128, 8, 256], BF16)
        mcv = mps[:].rearrange("p (a k) -> p a k", k=256)
        for half in range(2):
            for f4 in range(4):
                fc = half * 4 + f4
                n = 0
                for q in range(2):
                    for c in range(2):
                        P.mm(mcv[:, f4, :], Yc[:, q, c, fc * 128:(fc + 1) * 128], DF2[:, q, c * 256:(c + 1) * 256], start=(n == 0), stop=(n == 3),
                             R=[Yc, DF2], W=[mps])
                        n += 1
            P.act(mTc[:, half * 4:(half + 1) * 4, :], mcv, AF.Copy, R=[mps], W=[mTc], scale=1.0 / 256.0)
        for i in range(2):
            mT_R = [mTc]
            outproj(lambda fc: mTc[:, fc, i * 128:(i + 1) * 128])
            self.tail(L, M, T, i, [(pol[:, 0:512], 0, 512), (pol[:, 512:1024], 512, 512)], [pol])
        Zrs = [P.sbuf("f_Zr", [128, D], BF16) for _ in range(2)]
        Zis = [P.sbuf("f_Zi", [128, D], BF16) for _ in range(2)]
        mTs = [P.sbuf("f_mT", [128, 8, 128], BF16) for _ in range(2)]
        mlv = mps[:].rearrange("p (a k) -> p a k", k=128)
        xsv = self.XS.t[256:, :].rearrange("(p q) d -> q p d", q=64)
        hrv = self.hrow.t[256:, :].rearrange("(p q) d -> q p d", q=64)
        sc = float(1.0 / np.sqrt(8192.0 * 256.0))
        for k1 in range(64):
            Zr, Zi, mT = Zrs[k1 % 2], Zis[k1 % 2], mTs[k1 % 2]
            P.ld("sp", Zr[:], self.Zd[0, k1], R=[], W=[Zr])
            P.ld("sp", Zi[:], self.Zd[1, k1], R=[], W=[Zi])
            for fc in range(8):
                P.mm(mlv[:, fc, :], Zr[:, fc * 128:(fc + 1) * 128], MT[:, k1, 0:128], start=True, stop=False, R=[Zr, MT], W=[mps])
                P.mm(mlv[:, fc, :], Zi[:, fc * 128:(fc + 1) * 128], MT[:, k1, 128:256], start=False, stop=True, R=[Zi, MT], W=[mps])
            P.act(mT[:], mlv, AF.Copy, R=[mps], W=[mT], scale=sc)
            mT_R = [mT]
            outproj(lambda fc: mT[:, fc, :])
            self.tail(L, M, T, 2 + k1, [(pol[:, 0:512], 0, 512), (pol[:, 512:1024], 512, 512)], [pol],
                      xs_rows=xsv[k1], h_rows=hrv[k1])
        P.pop()

    def layer(self, i, upto=None):
        P = self.P
        P.push()
        L = self.layer_consts(i)
        L.skip_ctx_moe = (i == 3)
        need_ctx = (i != 3)
        M = self.moe_state(L)
        self.phase_A(L)
        kind, j = i % 3, i // 3
        if kind == 0:
            self.mla_proj(L, j, need_ctx)
            if upto == "proj":
                P.pop(); return
            self.mla_attn(L, need_ctx)
            if upto == "attn":
                P.pop(); return
            self.mla_out(L, M, j, need_ctx)
        if kind == 1:
            self.ssd_in(L)
            if upto == "ssd_in":
                P.pop(); return
            self.ssd_scan(L, 0)
            self.ssd_scan(L, 1)
            if upto == "ssd_scan":
                P.pop(); return
            self.ssd_out(L, M)
        if kind == 2:
            L.fmap = True
            self.fnet_f1(L)
            self.fnet_f2(L)
            self.fnet_f3(L, M)
        if upto == "mix":
            P.pop(); return
        self.moe_route(L, M)
        if upto == "route":
            P.pop(); return
        self.moe_experts(L, M)
        P.pop()

def make_consts():
    cst = np.zeros((128, 1024), np.float32)
    cst[:, 0:128] = np.eye(128, dtype=np.float32)
    cst[:, 128:256] = 1.0
    for p in range(128):
        cst[p, 256 + (p % 64)] = 1.0
        cst[p, 320 + p + 1:448] = 1.0
        cst[p, 448] = 1152 + p
        cst[p, 449:449 + NE] = np.arange(NE) * 1280
    t = np.arange(8192)
    row = (t // 64).astype(np.float32)
    col = (t % 64).astype(np.float32)
    inv = (10000.0 ** (-np.arange(16, dtype=np.float32) / 16)).astype(np.float32)
    ang = np.concatenate([row[:, None] * inv, col[:, None] * inv], axis=-1).astype(np.float32)
    cos, sin = np.cos(ang).T, np.sin(ang).T
    tab = np.zeros((128, TA), np.float32)
    tab[0:64, 0:256] = 1.0
    tab[0:32, 256:] = cos
    tab[32:64, 256:] = cos
    tab[64:96, 256:] = -sin
    tab[96:128, 256:] = sin
    tokid = (np.arange(NT)[None, :] * 128 + np.arange(128)[:, None]).astype(np.int32)
    return cst, tab, tokid


def prep_shared(I):
    S = {}
    for k in ["w_mod", "b_mod", "g_mix", "g_ffn"]:
        S[k] = np.ascontiguousarray(I[k], dtype=np.float32)
    w_in = I["mla_w_in"]
    S["mla_w_in"] = np.ascontiguousarray(np.concatenate(
        [w_in[:, :, :384], w_in[:, :, 384:448], w_in[:, :, 416:448], w_in[:, :, 384:416]], axis=-1))
    wq = I["mla_w_uq"].reshape(2, 256, 8, 192)
    S["mla_w_uq"] = np.ascontiguousarray(np.concatenate(
        [wq[..., :128], wq[..., 128:192], wq[..., 160:192], wq[..., 128:160]], axis=-1).reshape(2, 256, 2048))
    wkv = I["mla_w_ukv"].reshape(2, 128, 8, 256)
    S["mla_w_uk"] = np.ascontiguousarray(wkv[..., :128].reshape(2, 128, 1024))
    S["mla_w_uv"] = np.ascontiguousarray(wkv[..., 128:].reshape(2, 128, 1024))
    S["mla_w_o"] = np.ascontiguousarray(I["mla_w_o"])
    gc = np.zeros((2, 128, 8), np.float32)
    for j in range(2):
        gc[j, :, 0] = I["mla_g_q"][j, :128]
        gc[j, :, 1] = I["mla_g_q"][j, 128:]
        gc[j, :, 2] = I["mla_g_kv"][j]
        for c0, g in ((3, I["mla_g_qn"][j]), (5, I["mla_g_kn"][j])):
            gc[j, :, c0] = g[:128]
            gc[j, :, c0 + 1] = np.concatenate([g[128:192], g[160:192], g[128:160]])
    S["mla_gc"] = gc
    cst, tab, tokid = make_consts()
    S["cst"], S["rope_tab"], S["tokid"] = cst, tab, tokid
    S["ssd_w_in"] = np.ascontiguousarray(I["ssd_w_in"][0])
    S["ssd_conv_w"] = np.ascontiguousarray(I["ssd_conv_w"][0])
    S["ssd_conv_b"] = np.ascontiguousarray(I["ssd_conv_b"][0])
    S["ssd_dt_bias"] = np.ascontiguousarray(I["ssd_dt_bias"][0].reshape(64))
    S["ssd_a_log"] = np.ascontiguousarray(I["ssd_a_log"][0])
    S["ssd_d"] = np.ascontiguousarray(I["ssd_d"][0])
    S["ssd_g_norm"] = np.ascontiguousarray(I["ssd_g_norm"][0])
    S["ssd_w_out"] = np.ascontiguousarray(I["ssd_w_out"][0])
    kk = np.arange(128)
    c2 = np.zeros((128, 640), np.float32)
    c2[:, 0:128] = (kk[:, None] <= kk[None, :])
    c2[:, 128:256] = (kk[:, None] >= kk[None, :])
    c2[:, 256:384] = (kk[:, None] > kk[None, :])
    c2[:, 384:512] = (kk[:, None] < kk[None, :])
    c2[:, 512:640] = 1.0
    S["cst2"] = c2
    S["fnet_w_o"] = np.ascontiguousarray(I["fnet_w_o"][0])
    fc_ = np.zeros((128, 2176), np.float64)
    p_ = np.arange(128)
    m_ = np.arange(256)
    for q in range(2):
        angn = 2 * np.pi * np.outer(128 * q + p_, m_) / 256.0
        fc_[:, q * 512:q * 512 + 256] = np.cos(angn)
        fc_[:, q * 512 + 256:q * 512 + 512] = -np.sin(angn)
        fc_[:, 1024 + q * 512:1024 + q * 512 + 256] = np.cos(angn)
        fc_[:, 1024 + q * 512 + 256:1024 + q * 512 + 512] = np.sin(angn)
    a64 = np.arange(64)
    ang1 = 2 * np.pi * np.outer(a64, a64) / 64.0
    wr, wi = np.cos(ang1), -np.sin(ang1)
    fc_[0:64, 2048:2112] = wr.T
    fc_[64:128, 2048:2112] = -wi.T
    fc_[0:64, 2112:2176] = wi.T
    fc_[64:128, 2112:2176] = wr.T
    S["fcst"] = fc_.astype(np.float32)
    t2 = np.arange(128)[:, None, None]
    k1 = np.arange(64)[None, :, None]
    k2 = np.arange(128)[None, None, :]
    angm = 2 * np.pi * (k1 * t2 / 8192.0 + k2 * t2 / 128.0)
    fM = np.zeros((128, 64, 2, 128), np.float64)
    fM[:, :, 0, :] = np.cos(angm)
    fM[:, :, 1, :] = np.sin(angm)
    S["fM"] = fM.reshape(128, 16384).astype(np.float32)
    tf = (np.arange(NT)[None, :] * 128 + np.arange(128)[:, None]).astype(np.int32)
    for a in range(64):
        tf[:, 2 + a] = 256 + a + 64 * np.arange(128)
    S["tokid_f"] = tf
    S["moe_wr"] = np.ascontiguousarray(I["moe_w_router"])
    S["moe_wg"] = np.ascontiguousarray(I["moe_w_gate"])
    S["moe_wu"] = np.ascontiguousarray(I["moe_w_up"])
    S["moe_wd"] = np.ascontiguousarray(I["moe_w_down"])
    return S


def prep_core(I, S, b):
    m = dict(S)
    m["x"] = np.ascontiguousarray(I["x"][b])
    m["ctx"] = np.ascontiguousarray(I["ctx"][b])
    m["cc"] = np.ascontiguousarray(np.stack([I["c"][b], I["c_ctx"]], axis=0))
    return m


_CACHE = {}


def build_program():
    nc = bass.Bass("TRN2", target_bir_lowering=False)
    k = K(nc)
    k.setup()
    for i in range(4):
        k.layer(i)
    P = k.P
    for j in range(8):
        P.ld("sp", k.out[j * 1024:(j + 1) * 1024, :], k.XS[256 + j * 1024:256 + (j + 1) * 1024, :], R=[], W=[Res("u")])
    P.barrier()
    P.emit()
    P.close()
    return nc, k


def kernel(**inputs):
    from concourse.bass_utils import run_bass_kernel_spmd
    I = {k_: np.asarray(v) for k_, v in inputs.items()}
    if "prog" not in _CACHE:
        _CACHE["prog"] = build_program()
    nc, k = _CACHE["prog"]
    S = prep_shared(I)
    n = 8
    in_maps = [prep_core(I, S, b) for b in range(n)]
    res = run_bass_kernel_spmd(nc, in_maps, core_ids=list(range(n)))
    return np.stack([np.asarray(r["out"]) for r in res.results], axis=0).astype(np.float32)
```

```python
from contextlib import ExitStack
import numpy as np
import concourse.bass as bass
import concourse.mybir as mybir

F32 = mybir.dt.float32
BF16 = mybir.dt.bfloat16
I32 = mybir.dt.int32
U32 = mybir.dt.uint32
ALU = mybir.AluOpType
AF = mybir.ActivationFunctionType
AX = mybir.AxisListType

COMPUTE = ("pe", "act", "dve", "pool")
EPOCH = 30000
DMA_EPOCH = 2000


class Res:
    __slots__ = ("name", "last_w", "readers")

    def __init__(self, name):
        self.name = name
        self.last_w = None
        self.readers = {}


class Tile:
    def __init__(self, name, t):
        self.name = name
        self.t = t
        self._res = {}

    def r(self, key=None):
        x = self._res.get(key)
        if x is None:
            x = self._res[key] = Res(f"{self.name}:{key}")
        return x

    def __getitem__(self, k):
        return self.t[k]


class Prog:
    def __init__(self, nc, n_dma_sems=16):
        self.nc = nc
        self.es = ExitStack()
        self.streams = {e: [] for e in ("pe", "act", "dve", "pool", "sp")}
        self.cnt = {e: 0 for e in COMPUTE}
        self.waited = {e: {} for e in self.streams}
        self.sems = {}
        self.n_dma_sems = n_dma_sems
        self.dma_rr = {q: 0 for q in ("sp", "pool", "act")}
        self.dma_cnt = {}
        self.n_inst = 0
        self.scopes = []
        self._uid = 0

    def sbuf(self, name, shape, dt):
        t = self.es.enter_context(self.nc.sbuf_tensor(name, list(shape), dt))
        return Tile(name, t)

    def psum(self, name, shape, dt=F32):
        t = self.es.enter_context(self.nc.psum_tensor(name, list(shape), dt))
        return Tile(name, t)

    def dram(self, name, shape, dt, kind="Internal"):
        t = self.nc.dram_tensor(name, list(shape), dt, kind=kind)
        return Tile(name, t.ap())

    def _sem(self, key):
        s = self.sems.get(key)
        if s is None:
            s = self.sems[key] = self.es.enter_context(
                self.nc.semaphore("s_" + "_".join(str(k) for k in key)))
        return s

    def _deps(self, eng, R, W):
        deps = {}

        def add(tok, kind):
            if tok is None:
                return
            semkey, val, teng = tok
            if teng == eng and eng in COMPUTE:
                if eng == "pe" or kind != "raw":
                    return
            if deps.get(semkey, 0) < val:
                deps[semkey] = val

        for r in R:
            add(r.last_w, "raw")
        for w in W:
            add(w.last_w, "waw")
            for sk, (v, e) in w.readers.items():
                add((sk, v, e), "war")
        return deps

    def _emit_waits(self, eng, deps):
        wd = self.waited[eng]
        st = self.streams[eng]
        for semkey, val in deps.items():
            if wd.get(semkey, 0) >= val:
                continue
            wd[semkey] = val
            st.append(("wait", self._sem(semkey), val))

    def _commit(self, tok, R, W):
        semkey, val, eng = tok
        for r in R:
            cur = r.readers.get(semkey)
            if cur is None or cur[0] < val:
                r.readers[semkey] = (val, eng)
        for w in W:
            w.last_w = tok
            w.readers = {}

    @staticmethod
    def _resl(xs):
        out = []
        for x in xs:
            out.append(x.r() if isinstance(x, Tile) else x)
        return out

    def op(self, eng, fn, R=(), W=()):
        R = self._resl(R)
        W = self._resl(W)
        self._emit_waits(eng, self._deps(eng, R, W))
        i = self.cnt[eng]
        self.cnt[eng] = i + 1
        semkey = ("e", eng, i // EPOCH)
        val = i % EPOCH + 1
        tok = (semkey, val, eng)
        self.streams[eng].append(("op", fn, self._sem(semkey), 1))
        self._commit(tok, R, W)
        self.n_inst += 1
        return tok

    def dma(self, q, fn, R=(), W=()):
        R = self._resl(R)
        W = self._resl(W)
        s = self.dma_rr[q]
        self.dma_rr[q] = (s + 1) % self.n_dma_sems
        base = (q, s)
        m = self.dma_cnt.get(base, 0) + 1
        self.dma_cnt[base] = m

        def key(mi):
            ep = (mi - 1) // DMA_EPOCH
            return ("d", q, s, ep), 16 * ((mi - 1) % DMA_EPOCH + 1)

        deps = self._deps("dma:" + q, R, W)
        if m > 1:
            pk, pv = key(m - 1)
            if deps.get(pk, 0) < pv:
                deps[pk] = pv
        self._emit_waits(q, deps)
        semkey, val = key(m)
        tok = (semkey, val, "dma:" + q)
        self.streams[q].append(("op", fn, self._sem(semkey), 16))
        self._commit(tok, R, W)
        self.n_inst += 1
        return tok

    def final_wait(self, eng, toks):
        deps = {}
        for semkey, val, _ in toks:
            if deps.get(semkey, 0) < val:
                deps[semkey] = val
        self._emit_waits(eng, deps)

    def emit(self):
        nc = self.nc
        streams = self.streams

        def run(e, items):
            for it in items:
                if it[0] == "wait":
                    e.wait_ge(it[1], it[2])
                else:
                    it[1](e).then_inc(it[2], it[3])

        with nc.Block() as block:
            @block.tensor
            def _(e):
                run(e, streams["pe"])

            @block.scalar
            def _(e):
                run(e, streams["act"])

            @block.vector
            def _(e):
                run(e, streams["dve"])

            @block.gpsimd
            def _(e):
                run(e, streams["pool"])

            @block.sync
            def _(e):
                run(e, streams["sp"])

    def close(self):
        self.es.close()


def _push(self):
    self.scopes.append(ExitStack())


def _pop(self):
    self.barrier()
    self.scopes.pop().close()


def _sbuf(self, name, shape, dt):
    st = self.scopes[-1] if self.scopes else self.es
    self._uid += 1
    t = st.enter_context(self.nc.sbuf_tensor(f"{name}_{self._uid}", list(shape), dt))
    return Tile(name, t)


def _psum(self, name, shape, dt=F32):
    st = self.scopes[-1] if self.scopes else self.es
    self._uid += 1
    t = st.enter_context(self.nc.psum_tensor(f"{name}_{self._uid}", list(shape), dt))
    return Tile(name, t)


def _barrier(self):
    toks = {}
    for eng in COMPUTE:
        i = self.cnt[eng]
        if i > 0:
            toks[("e", eng, (i - 1) // EPOCH)] = (i - 1) % EPOCH + 1
    for (q, s), m in self.dma_cnt.items():
        toks[("d", q, s, (m - 1) // DMA_EPOCH)] = 16 * ((m - 1) % DMA_EPOCH + 1)
    for eng in self.streams:
        self._emit_waits(eng, dict(toks))


def _mm(self, out, lhsT, rhs, start=True, stop=True, R=(), W=()):
    return self.op("pe", lambda e: e.matmul(out, lhsT, rhs, start=start, stop=stop), R, W)


def _tr(self, out, in_, ident, R=(), W=()):
    return self.op("pe", lambda e: e.transpose(out, in_, ident), R, W)


def _act(self, out, in_, func, R=(), W=(), **kw):
    return self.op("act", lambda e: e.activation(out, in_, func, **kw), R, W)


def _tsc(self, eng, out, in0, s1, s2, op0, op1=None, R=(), W=(), **kw):
    if op1 is None:
        return self.op(eng, lambda e: e.tensor_scalar(out, in0, s1, None, op0, **kw), R, W)
    return self.op(eng, lambda e: e.tensor_scalar(out, in0, s1, s2, op0, op1, **kw), R, W)


def _tt(self, eng, out, in0, in1, op, R=(), W=()):
    return self.op(eng, lambda e: e.tensor_tensor(out, in0, in1, op), R, W)


def _stt(self, eng, out, in0, scalar, in1, op0, op1, R=(), W=()):
    return self.op(eng, lambda e: e.scalar_tensor_tensor(out, in0, scalar, in1, op0, op1), R, W)


def _cp(self, eng, out, in_, R=(), W=()):
    if eng == "act":
        return self.op("act", lambda e: e.copy(out, in_), R, W)
    return self.op(eng, lambda e: e.tensor_copy(out, in_), R, W)


def _red(self, eng, out, in_, op, R=(), W=(), axis=None):
    ax = AX.X if axis is None else axis
    return self.op(eng, lambda e: e.tensor_reduce(out, in_, ax, op), R, W)


def _memset(self, eng, out, val, W=()):
    return self.op(eng, lambda e: e.memset(out, val), (), W)


def _ld(self, q, out, in_, R=(), W=(), **kw):
    kw.setdefault("allow_slow_non_contiguous", True)
    return self.dma(q, lambda e: e.dma_start(out=out, in_=in_, **kw), R, W)


def _gather(self, out, in_, idx_ap, R=(), W=()):
    return self.dma("pool", lambda e: e.indirect_dma_start(
        out=out, out_offset=None, in_=in_,
        in_offset=bass.IndirectOffsetOnAxis(ap=idx_ap, axis=0)), R, W)


def _scatter(self, out, in_, idx_ap, R=(), W=(), add=False, bound=None):
    if add:
        def f(e):
            try:
                return e.indirect_dma_start(
                    out=out, out_offset=bass.IndirectOffsetOnAxis(ap=idx_ap, axis=0), in_=in_, in_offset=None,
                    compute_op=ALU.add, oob_is_err=True)
            except Exception:
                print("SCATTER-ADD FAIL", out, in_, idx_ap, bound)
                raise
        return self.dma("pool", f, R, W)
    return self.dma("pool", lambda e: e.indirect_dma_start(
        out=out, out_offset=bass.IndirectOffsetOnAxis(ap=idx_ap, axis=0), in_=in_, in_offset=None), R, W)


def _recip(self, out, in_, R=(), W=()):
    return self.op("dve", lambda e: e.reciprocal(out, in_), R, W)


def _get_bound_reg(self, e, bound):
    if not hasattr(self, "_bregs"):
        self._bregs = {}
    r = self._bregs.get(bound)
    if r is None:
        r = self._bregs[bound] = e.to_reg(bound)
    return r


Prog.get_bound_reg = _get_bound_reg
Prog.recip = _recip
Prog.push = _push
Prog.pop = _pop
Prog.sbuf = _sbuf
Prog.psum = _psum
Prog.barrier = _barrier
Prog.mm = _mm
Prog.tr = _tr
Prog.act = _act
Prog.tsc = _tsc
Prog.tt = _tt
Prog.stt = _stt
Prog.cp = _cp
Prog.red = _red
Prog.memset = _memset
Prog.ld = _ld
Prog.gather = _gather
Prog.scatter = _scatter


from types import SimpleNamespace as NS

TA = 8448
NT = 66
D = 1024
NE = 16
EPS = 1e-6
import os
TAIL_ENG = os.environ.get('TAIL_ENG', 'dve')
ATTN_V2_ALL = bool(int(os.environ.get('ATTN_V2_ALL', '1')))


class K:
    def __init__(self, nc, dbg=()):
        self.nc = nc
        self.P = P = Prog(nc)
        self.dbg = set(dbg)
        din = lambda n, s, dt=F32: P.dram(n, s, dt, kind="ExternalInput")
        self.x = din("x", [8192, D])
        self.ctx = din("ctx", [256, D])
        self.cc = din("cc", [2, D])
        self.w_mod = din("w_mod", [4, D, 6144])
        self.b_mod = din("b_mod", [4, 6144])
        self.g_mix = din("g_mix", [4, D])
        self.g_ffn = din("g_ffn", [4, D])
        self.mla_w_in = din("mla_w_in", [2, D, 512])
        self.mla_w_uq = din("mla_w_uq", [2, 256, 2048])
        self.mla_w_uk = din("mla_w_uk", [2, 128, 1024])
        self.mla_w_uv = din("mla_w_uv", [2, 128, 1024])
        self.mla_w_o = din("mla_w_o", [2, D, D])
        self.mla_gc = din("mla_gc", [2, 128, 8])
        self.rope_tab = din("rope_tab", [128, TA])
        self.cst = din("cst", [128, 1024])
        self.moe_wr = din("moe_wr", [4, D, NE])
        self.moe_wg = din("moe_wg", [4, NE, D, D])
        self.moe_wu = din("moe_wu", [4, NE, D, D])
        self.moe_wd = din("moe_wd", [4, NE, D, D])
        self.tokid = din("tokid", [128, NT], I32)
        self.ssd_w_in = din("ssd_w_in", [D, 5184])
        self.ssd_conv_w = din("ssd_conv_w", [3, 3072])
        self.ssd_conv_b = din("ssd_conv_b", [3072])
        self.ssd_dt_bias = din("ssd_dt_bias", [64])
        self.ssd_a_log = din("ssd_a_log", [2, 32])
        self.ssd_d = din("ssd_d", [32])
        self.ssd_g_norm = din("ssd_g_norm", [2048])
        self.ssd_w_out = din("ssd_w_out", [2048, D])
        self.cst2 = din("cst2", [128, 640])
        self.fnet_w_o = din("fnet_w_o", [D, D])
        self.fcst = din("fcst", [128, 2176])
        self.fM = din("fM", [128, 16384])
        self.tokid_f = din("tokid_f", [128, NT], I32)
        self.Yd = self.dscr("Yd", [2, TA, D], BF16)
        self.Zd = self.dscr("Zd", [2, 64, 128, D], BF16)
        self.xs_d = self.dscr("xs_d", [TA, 2048], BF16)
        self.B_d = self.dscr("B_d", [TA, 512], BF16)
        self.BT_d = self.dscr("BT_d", [512, TA], BF16)
        self.CT_d = self.dscr("CT_d", [512, TA], BF16)
        self.zs_d = self.dscr("zs_d", [TA, 2048], BF16)
        self.dt_d = self.dscr("dt_d", [TA, 64], F32)
        self.yf_d = self.dscr("yf_d", [TA, 2048], F32)
        self.yb_d = self.dscr("yb_d", [TA, 2048], F32)
        self.out = P.dram("out", [8192, D], F32, kind="ExternalOutput")
        self.XS = self.dscr("XS", [TA, D], F32)
        self.modd = self.dscr("modd", [4, 2, 6144], F32)
        self.hT = self.dscr("hT", [D, TA], BF16)
        self.QT = self.dscr("QT", [8, 192, TA], BF16)
        self.KT = self.dscr("KT", [8, 192, TA], BF16)
        self.V = self.dscr("V", [TA, D], BF16)
        self.OT = self.dscr("OT", [D, TA], BF16)
        self.hrow = self.dscr("hrow", [TA, D], BF16)
        self.affrow = self.dscr("affrow", [TA, NE], F32)
        self.idxd = self.dscr("idxd", [NE * 1280, 1], I32)
        self.ident = P.sbuf("ident", [128, 128], BF16)
        self.ones = P.sbuf("ones", [128, 128], BF16)
        self.fold = P.sbuf("fold", [128, 64], BF16)
        self.identf = P.sbuf("identf", [128, 128], F32)
        P.ld("pool", self.ident[:], self.cst[:, 0:128], R=[self.cst], W=[self.ident])
        P.ld("pool", self.ones[:], self.cst[:, 128:256], R=[self.cst], W=[self.ones])
        P.ld("pool", self.fold[:], self.cst[:, 256:320], R=[self.cst], W=[self.fold])
        P.ld("sp", self.identf[:], self.cst[:, 0:128], R=[self.cst], W=[self.identf])
        self.ustrict = P.sbuf("ustrict", [128, 128], BF16)
        P.ld("pool", self.ustrict[:], self.cst[:, 320:448], R=[self.cst], W=[self.ustrict])
        self.dumpc = P.sbuf("dumpc", [128, 1], F32)
        P.ld("sp", self.dumpc[:], self.cst[:, 448:449], R=[self.cst], W=[self.dumpc])
        self.eoffs = P.sbuf("eoffs", [128, NE], F32)
        P.ld("sp", self.eoffs[:], self.cst[:, 449:449 + NE], R=[self.cst], W=[self.eoffs])
        self.tokid_sb = P.sbuf("tokid_sb", [128, NT], I32)
        P.ld("sp", self.tokid_sb[:], self.tokid[:, :], R=[self.tokid], W=[self.tokid_sb])
        self.tokid_f_sb = P.sbuf("tokid_f_sb", [128, NT], I32)
        P.ld("sp", self.tokid_f_sb[:], self.tokid_f[:, :], R=[self.tokid_f], W=[self.tokid_f_sb])

    def dscr(self, name, shape, dt):
        kind = "ExternalOutput" if name in self.dbg else "Internal"
        return self.P.dram(name, shape, dt, kind=kind)

    def setup(self):
        P = self.P
        P.ld("sp", self.XS[0:256, :], self.ctx[:, :], R=[self.ctx], W=[self.XS])
        for j in range(8):
            P.ld("sp", self.XS[256 + j * 1024:256 + (j + 1) * 1024, :], self.x[j * 1024:(j + 1) * 1024, :],
                 R=[self.x], W=[self.XS])
        P.push()
        ccT = P.sbuf("ccT", [128, 8, 2], F32)
        ccs = P.sbuf("ccs", [128, 8, 2], F32)
        for m in range(2):
            P.ld("sp", ccT[:, :, m], self.cc[m].rearrange("(k p) -> p k", p=128), R=[self.cc], W=[ccT],
                 allow_slow_non_contiguous=True)
        P.act(ccs[:], ccT[:], AF.Silu, R=[ccT], W=[ccs])
        wms = [P.sbuf("wm", [128, 8, 512], F32) for _ in range(2)]
        bms = [P.sbuf("bm", [2, 512], F32) for _ in range(2)]
        mrs = [P.sbuf("mr", [2, 512], F32) for _ in range(2)]
        pss = [P.psum("psm", [2, 512]) for _ in range(2)]
        n = 0
        for i in range(4):
            for nb in range(12):
                wm, bm, mr, ps = wms[n % 2], bms[n % 2], mrs[n % 2], pss[n % 2]
                n += 1
                sl = slice(nb * 512, (nb + 1) * 512)
                P.ld("sp", wm[:], self.w_mod[i, :, sl].rearrange("(k p) n -> p k n", p=128), R=[self.w_mod], W=[wm])
                P.ld("sp", bm[:], self.b_mod[i:i + 1, sl].to_broadcast([2, 512]), R=[self.b_mod], W=[bm])
                for k in range(8):
                    P.mm(ps[:], ccs[:, k, :], wm[:, k, :], start=(k == 0), stop=(k == 7), R=[ccs, wm], W=[ps])
                P.tt("dve", mr[:], ps[:], bm[:], ALU.add, R=[ps, bm], W=[mr])
                P.ld("pool", self.modd[i, :, sl], mr[:], R=[mr], W=[self.modd])
        P.pop()

    def layer_consts(self, i):
        P = self.P
        L = NS()
        L.i = i
        modd = self.modd
        L.modcol = P.sbuf("modcol", [128, 2, 6, 8], F32)
        for m in range(2):
            P.ld("sp", L.modcol[:, m], modd[i, m].rearrange("(s k p) -> p s k", s=6, p=128), R=[modd], W=[L.modcol],
                 allow_slow_non_contiguous=True)
        gcol = P.sbuf("gcol", [128, 8], F32)
        P.ld("sp", gcol[:], self.g_mix[i].rearrange("(k p) -> p k", p=128), R=[self.g_mix], W=[gcol],
             allow_slow_non_contiguous=True)
        L.Ga = P.sbuf("Ga", [128, 2, 8], F32)
        for m in range(2):
            P.stt("dve", L.Ga[:, m], L.modcol[:, m, 1], 1.0, gcol[:], ALU.add, ALU.mult, R=[L.modcol, gcol], W=[L.Ga])
        def bc(name, src_ap, R):
            t = P.sbuf(name, [128, D], F32)
            P.ld("sp", t[:], src_ap.partition_broadcast(128), R=R, W=[t])
            return t
        L.gtf = [bc("gtf", modd[i, m, 5120:6144], [modd]) for m in range(2)]
        return L

    def tail_consts(self, L):
        P = self.P
        i = L.i
        modd = self.modd

        def bc(name, src_ap):
            t = P.sbuf(name, [128, D], F32)
            P.ld("sp", t[:], src_ap.partition_broadcast(128), R=[], W=[t])
            return t
        L.gta = [bc("gta", modd[i, m, 2048:3072]) for m in range(2)]
        L.Sf = [bc("Sf", modd[i, m, 3072:4096]) for m in range(2)]
        gffn = bc("gffn", self.g_ffn[i])
        L.Gf = []
        for m in range(2):
            t = bc("Gf", modd[i, m, 4096:5120])
            P.stt("dve", t[:], t[:], 1.0, gffn[:], ALU.add, ALU.mult, R=[t, gffn], W=[t])
            L.Gf.append(t)

    def rms_rstd(self, xt, sq_junk, ss, rstd, n):
        P = self.P
        P.act(sq_junk[:], xt[:], AF.Square, R=[xt], W=[sq_junk, ss], accum_out=ss[:])
        P.act(rstd[:], ss[:], AF.Sqrt, R=[ss], W=[rstd], scale=1.0 / n, bias=EPS)
        P.recip(rstd[:], rstd[:], R=[rstd], W=[rstd])

    def phase_A(self, L):
        P = self.P
        P.push()
        NB = 2
        xts = [P.sbuf("xt", [128, D], F32) for _ in range(NB)]
        xns = [P.sbuf("xn", [128, D], BF16) for _ in range(NB)]
        junk = P.sbuf("junk", [128, D], BF16)
        sss = [P.sbuf("ss", [128, 1], F32) for _ in range(NB)]
        rss = [P.sbuf("rs", [128, 1], F32) for _ in range(NB)]
        pts = [P.psum("pt", [128, 8, 128], BF16) for _ in range(NB)]
        hts = [P.sbuf("ht", [128, 8, 128], BF16) for _ in range(NB)]
        tmp = [P.sbuf("tmpA", [128, 8, 128], F32) for _ in range(NB)]
        for i in range(NT):
            b = i % NB
            m = 1 if i < 2 else 0
            xt, xn, ss, rs, pt, ht, tp = xts[b], xns[b], sss[b], rss[b], pts[b], hts[b], tmp[b]
            P.ld("sp", xt[:], self.XS[i * 128:(i + 1) * 128, :], R=[self.XS], W=[xt])
            self.rms_rstd(xt, junk, ss, rs, D)
            P.act(xn[:], xt[:], AF.Copy, R=[xt, rs], W=[xn], scale=rs[:])
            for k in range(8):
                P.tr(pt[:, k, :], xn[:, k * 128:(k + 1) * 128], self.ident[:], R=[xn, self.ident], W=[pt])
            P.tt("dve", tp[:], pt[:], L.Ga[:, m].unsqueeze(2).to_broadcast([128, 8, 128]), ALU.mult, R=[pt, L.Ga], W=[tp])
            P.tt("dve", ht[:], tp[:], L.modcol[:, m, 0].unsqueeze(2).to_broadcast([128, 8, 128]), ALU.add,
                 R=[tp, L.modcol], W=[ht])
            P.ld("pool", self.hT[:, i * 128:(i + 1) * 128].rearrange("(k p) t -> p k t", p=128), ht[:], R=[ht], W=[self.hT])
        P.pop()


    def cast_load(self, name, shape, src_ap, R):
        t = self.P.sbuf(name, shape, BF16)
        self.P.ld("pool", t[:], src_ap, R=R, W=[t])
        return t

    def mla_proj(self, L, j, need_ctx):
        P = self.P
        P.push()
        w_in = self.cast_load("w_in", [128, 8, 512], self.mla_w_in[j].rearrange("(k p) n -> p k n", p=128), [self.mla_w_in])
        w_uq = self.cast_load("w_uq", [128, 2, 2048], self.mla_w_uq[j].rearrange("(k p) n -> p k n", p=128), [self.mla_w_uq])
        w_uk = self.cast_load("w_uk", [128, 1024], self.mla_w_uk[j], [self.mla_w_uk])
        w_uv = self.cast_load("w_uv", [128, 1024], self.mla_w_uv[j], [self.mla_w_uv])
        gc = P.sbuf("gc", [128, 8], F32)
        gd = P.sbuf("gd", [128, 8], F32)
        P.ld("sp", gc[:], self.mla_gc[j], R=[self.mla_gc], W=[gc])
        P.tsc("dve", gd[:, 0:2], gc[:, 0:2], 16.0, None, ALU.mult, R=[gc], W=[gd])
        P.tsc("dve", gd[:, 2:3], gc[:, 2:3], float(np.sqrt(128.0)), None, ALU.mult, R=[gc], W=[gd])
        P.tsc("dve", gd[:, 3:5], gc[:, 3:5], 1.0, None, ALU.mult, R=[gc], W=[gd])
        P.tsc("dve", gd[:, 5:7], gc[:, 5:7], float(np.sqrt(192.0)), None, ALU.mult, R=[gc], W=[gd])
        tab = P.sbuf("tab", [128, TA], F32)
        P.ld("sp", tab[:], self.rope_tab[:, :], R=[self.rope_tab], W=[tab])
        ones, fold = self.ones, self.fold
        NB = 512
        hTbs = [P.sbuf("hTb", [128, 8, NB], BF16) for _ in range(2)]
        pbs = [P.psum("pb", [128, NB]) for _ in range(6)]
        psv = P.psum("psv", [128, 1024])
        cyc = {"pb": 0}

        def nps():
            t = pbs[cyc["pb"] % len(pbs)]
            cyc["pb"] += 1
            return t

        def rot(name, shape, dt, n=2):
            ts = [P.sbuf(name, shape, dt) for _ in range(n)]
            st = {"i": 0}

            def f():
                t = ts[st["i"] % n]
                st["i"] += 1
                return t
            return f
        sq_n = rot("sq_n", [128, NB], BF16, 3)
        sq_r = rot("sq_r", [64, NB], BF16, 2)
        rs_t = rot("rs_t", [128, NB], F32, 3)
        o_n = rot("o_n", [128, NB], BF16, 3)
        o_r = rot("o_r", [64, NB], BF16, 3)
        rt_t = rot("rt_t", [128, NB], BF16, 2)
        vt_t = rot("vt_t", [128, 1024], BF16, 2)
        qan = P.sbuf("qan", [128, 2, NB], BF16)
        kvan = P.sbuf("kvan", [128, NB], BF16)
        sqkr = P.sbuf("sqkr", [64, NB], BF16)
        krr = P.sbuf("krr", [64, NB], F32)

        def rstd_from(ss_ps, n, nb):
            rs = rs_t()
            P.act(rs[:, :nb], ss_ps[:, :nb], AF.Sqrt, R=[ss_ps], W=[rs], bias=float(n * EPS))
            P.recip(rs[:, :nb], rs[:, :nb], R=[rs], W=[rs])
            return rs

        blocks = [(0, 256)] + [(256 + NB * b, NB) for b in range(16)]
        for bi, (t0, nb) in enumerate(blocks):
            is_ctx = bi == 0
            hTb = hTbs[bi % 2]
            P.ld("sp", hTb[:, :, :nb], self.hT[:, t0:t0 + nb].rearrange("(k p) t -> p k t", p=128), R=[self.hT], W=[hTb])

            def proj_a(m):
                ps = nps()
                for k in range(8):
                    P.mm(ps[:, :nb], w_in[:, k, m * 128:(m + 1) * 128], hTb[:, k, :nb], start=(k == 0), stop=(k == 7),
                         R=[w_in, hTb], W=[ps])
                return ps
            psq = [proj_a(0), proj_a(1)]
            ss = nps()
            for m in range(2):
                sq = sq_n()
                P.act(sq[:, :nb], psq[m][:, :nb], AF.Square, R=[psq[m]], W=[sq])
                P.mm(ss[:, :nb], ones[:], sq[:, :nb], start=(m == 0), stop=(m == 1), R=[ones, sq], W=[ss])
            rs = rstd_from(ss, 256, nb)
            for m in range(2):
                P.stt("dve", qan[:, m, :nb], psq[m][:, :nb], gd[:, m:m + 1], rs[:, :nb], ALU.mult, ALU.mult,
                      R=[psq[m], gd, rs], W=[qan])
            pk = proj_a(2)
            sq = sq_n()
            P.act(sq[:, :nb], pk[:, :nb], AF.Square, R=[pk], W=[sq])
            ss = nps()
            P.mm(ss[:, :nb], ones[:], sq[:, :nb], R=[ones, sq], W=[ss])
            rs = rstd_from(ss, 128, nb)
            P.stt("dve", kvan[:, :nb], pk[:, :nb], gd[:, 2:3], rs[:, :nb], ALU.mult, ALU.mult, R=[pk, gd, rs], W=[kvan])
            pr = proj_a(3)
            P.act(sqkr[:, :nb], pr[0:64, :nb], AF.Square, R=[pr], W=[sqkr])
            rt = rt_t()
            P.stt("dve", rt[:, :nb], pr[:, :nb], gd[:, 6:7], tab[:, t0:t0 + nb], ALU.mult, ALU.mult, R=[pr, gd, tab], W=[rt])
            pf = nps()
            P.mm(pf[0:64, :nb], fold[:], rt[:, :nb], R=[fold, rt], W=[pf])
            P.cp("act", krr[:, :nb], pf[0:64, :nb], R=[pf], W=[krr])
            for h in range(8):
                pk = nps()
                P.mm(pk[:, :nb], w_uk[:, h * 128:(h + 1) * 128], kvan[:, :nb], R=[w_uk, kvan], W=[pk])
                sq = sq_n()
                P.act(sq[:, :nb], pk[:, :nb], AF.Square, R=[pk], W=[sq])
                ss = nps()
                P.mm(ss[:, :nb], ones[:], sq[:, :nb], start=True, stop=False, R=[ones, sq], W=[ss])
                P.mm(ss[:, :nb], ones[0:64, :], sqkr[:, :nb], start=False, stop=True, R=[ones, sqkr], W=[ss])
                rs = rstd_from(ss, 192, nb)
                kn = o_n()
                P.stt("dve", kn[:, :nb], pk[:, :nb], gd[:, 5:6], rs[:, :nb], ALU.mult, ALU.mult, R=[pk, gd, rs], W=[kn])
                P.ld("pool", self.KT[h, 0:128, t0:t0 + nb], kn[:, :nb], R=[kn], W=[self.KT])
                kr = o_r()
                P.tt("dve", kr[:, :nb], krr[:, :nb], rs[0:64, :nb], ALU.mult, R=[krr, rs], W=[kr])
                P.ld("pool", self.KT[h, 128:192, t0:t0 + nb], kr[:, :nb], R=[kr], W=[self.KT])
                if is_ctx and not need_ctx:
                    continue
                pqn, pqr = nps(), nps()
                for kc in range(2):
                    P.mm(pqn[:, :nb], w_uq[:, kc, h * 256:h * 256 + 128], qan[:, kc, :nb], start=(kc == 0), stop=(kc == 1),
                         R=[w_uq, qan], W=[pqn])
                for kc in range(2):
                    P.mm(pqr[:, :nb], w_uq[:, kc, h * 256 + 128:h * 256 + 256], qan[:, kc, :nb], start=(kc == 0), stop=(kc == 1),
                         R=[w_uq, qan], W=[pqr])
                sq = sq_n()
                P.act(sq[:, :nb], pqn[:, :nb], AF.Square, R=[pqn], W=[sq])
                sr = sq_r()
                P.act(sr[:, :nb], pqr[0:64, :nb], AF.Square, R=[pqr], W=[sr])
                ss = nps()
                P.mm(ss[:, :nb], ones[:], sq[:, :nb], start=True, stop=False, R=[ones, sq], W=[ss])
                P.mm(ss[:, :nb], ones[0:64, :], sr[:, :nb], start=False, stop=True, R=[ones, sr], W=[ss])
                rs = rstd_from(ss, 192, nb)
                qn = o_n()
                P.stt("dve", qn[:, :nb], pqn[:, :nb], gd[:, 3:4], rs[:, :nb], ALU.mult, ALU.mult, R=[pqn, gd, rs], W=[qn])
                P.ld("pool", self.QT[h, 0:128, t0:t0 + nb], qn[:, :nb], R=[qn], W=[self.QT])
                rt = rt_t()
                P.stt("dve", rt[:, :nb], pqr[:, :nb], gd[:, 4:5], tab[:, t0:t0 + nb], ALU.mult, ALU.mult, R=[pqr, gd, tab], W=[rt])
                pf = nps()
                P.mm(pf[0:64, :nb], fold[:], rt[:, :nb], R=[fold, rt], W=[pf])
                qr = o_r()
                P.tt("dve", qr[:, :nb], pf[0:64, :nb], rs[0:64, :nb], ALU.mult, R=[pf, rs], W=[qr])
                P.ld("pool", self.QT[h, 128:192, t0:t0 + nb], qr[:, :nb], R=[qr], W=[self.QT])
            for tt in range(nb // 128):
                for n in range(2):
                    P.mm(psv[:, n * 512:(n + 1) * 512], kvan[:, tt * 128:(tt + 1) * 128], w_uv[:, n * 512:(n + 1) * 512],
                         R=[kvan, w_uv], W=[psv])
                vt = vt_t()
                P.cp("act", vt[:], psv[:], R=[psv], W=[vt])
                P.ld("pool", self.V[t0 + tt * 128:t0 + (tt + 1) * 128, :], vt[:], R=[vt], W=[self.V])
        P.pop()

    def mla_attn(self, L, need_ctx):
        P = self.P
        P.push()
        NQ = 512
        G = 2
        Kns = [P.sbuf("Kn", [128, TA], BF16) for _ in range(2)]
        Krs = [P.sbuf("Kr", [64, TA], BF16) for _ in range(2)]
        Vhs = [P.sbuf("Vh", [128, NT, 128], BF16) for _ in range(2)]
        Qns = [P.sbuf("Qn", [128, NQ], BF16) for _ in range(2)]
        Qrs = [P.sbuf("Qr", [64, NQ], BF16) for _ in range(2)]
        pts = [P.sbuf("pT", [128, G * NQ], BF16) for _ in range(3)]
        accs = [P.sbuf("accL", [128, G * NQ], F32) for _ in range(2)]
        rls = [P.sbuf("rl", [128, NQ], F32) for _ in range(2)]
        ots = [P.sbuf("ot", [128, NQ], BF16) for _ in range(2)]
        onesf = P.sbuf("onesf", [128, 128], F32)
        P.ld("sp", onesf[:], self.cst[:, 128:256], R=[], W=[onesf])
        pss = [P.psum("ps_s", [128, G * NQ]) for _ in range(3)]
        po = P.psum("ps_o", [128, NQ])
        pl = P.psum("ps_l", [128, NQ])
        qblocks = [(256 + NQ * b, NQ, list(range(NT))) for b in range(16)]
        if need_ctx:
            qblocks = [(0, 256, [0, 1])] + qblocks
        st = {"g": 0}
        n_q = 0
        for h in range(8):
            Kn, Kr, Vh = Kns[h % 2], Krs[h % 2], Vhs[h % 2]
            P.ld("sp", Kn[:], self.KT[h, 0:128, :], R=[], W=[Kn])
            P.ld("sp", Kr[:], self.KT[h, 128:192, :], R=[], W=[Kr])
            P.ld("sp", Vh[:], self.V[:, h * 128:(h + 1) * 128].rearrange("(n p) v -> p n v", p=128), R=[], W=[Vh])
            for (t0, nq, keys) in qblocks:
                Qn, Qr = Qns[n_q % 2], Qrs[n_q % 2]
                acc, rl, ot = accs[n_q % 2], rls[n_q % 2], ots[n_q % 2]
                n_q += 1
                P.ld("sp", Qn[:, :nq], self.QT[h, 0:128, t0:t0 + nq], R=[], W=[Qn])
                P.ld("sp", Qr[:, :nq], self.QT[h, 128:192, t0:t0 + nq], R=[], W=[Qr])
                groups = [keys[i:i + G] for i in range(0, len(keys), G)]
                ng = len(groups)
                g0 = st["g"]
                st["g"] += ng
                accv = acc[:].rearrange("p (g q) -> p g q", q=NQ)[:, :, :nq]

                def front(g):
                    ps, pt = pss[(g0 + g) % 3], pts[(g0 + g) % 3]
                    grp = groups[g]
                    for j, kt in enumerate(grp):
                        ks = slice(kt * 128, (kt + 1) * 128)
                        P.mm(ps[:, j * NQ:j * NQ + nq], Kn[:, ks], Qn[:, :nq], start=True, stop=False, R=[Kn, Qn], W=[ps])
                        P.mm(ps[:, j * NQ:j * NQ + nq], Kr[:, ks], Qr[:, :nq], start=False, stop=True, R=[Kr, Qr], W=[ps])
                    psv = ps[:].rearrange("p (g q) -> p g q", q=NQ)[:, :len(grp), :nq]
                    ptv = pt[:].rearrange("p (g q) -> p g q", q=NQ)[:, :len(grp), :nq]
                    P.act(ptv, psv, AF.Exp, R=[ps], W=[pt])
                    if g == 0:
                        P.cp("dve", accv, ptv, R=[pt], W=[acc])
                    else:
                        P.tt("dve", accv, accv, ptv, ALU.add, R=[acc, pt], W=[acc])

                def back(g):
                    pt = pts[(g0 + g) % 3]
                    grp = groups[g]
                    for j, kt in enumerate(grp):
                        P.mm(po[:, :nq], Vh[:, kt, :], pt[:, j * NQ:j * NQ + nq], start=(g == 0 and j == 0),
                             stop=(g == ng - 1 and j == len(grp) - 1), R=[Vh, pt], W=[po])
                for g in range(min(2, ng)):
                    front(g)
                for g in range(ng):
                    if g + 2 < ng:
                        front(g + 2)
                    back(g)
                for j in range(G):
                    P.mm(pl[:, :nq], onesf[:], acc[:, j * NQ:j * NQ + nq], start=(j == 0), stop=(j == G - 1), R=[onesf, acc], W=[pl])
                P.recip(rl[:, :nq], pl[:, :nq], R=[pl], W=[rl])
                P.tt("dve", ot[:, :nq], po[:, :nq], rl[:, :nq], ALU.mult, R=[po, rl], W=[ot])
                P.ld("pool", self.OT[h * 128:(h + 1) * 128, t0:t0 + nq], ot[:, :nq], R=[ot], W=[Res("u")])
        P.pop()

    def mla_attn_v1(self, L, need_ctx):
        P = self.P
        P.push()
        NQ = 512
        Kns = [P.sbuf("Kn", [128, TA], BF16) for _ in range(2)]
        Krs = [P.sbuf("Kr", [64, TA], BF16) for _ in range(2)]
        Vhs = [P.sbuf("Vh", [128, NT, 128], BF16) for _ in range(2)]
        Qns = [P.sbuf("Qn", [128, NQ], BF16) for _ in range(2)]
        Qrs = [P.sbuf("Qr", [64, NQ], BF16) for _ in range(2)]
        pts = [P.sbuf("pT", [128, NQ], BF16) for _ in range(3)]
        rls = [P.sbuf("rl", [128, NQ], F32) for _ in range(2)]
        ots = [P.sbuf("ot", [128, NQ], BF16) for _ in range(2)]
        pss = [P.psum("ps_s", [128, NQ]) for _ in range(3)]
        pos = [P.psum("ps_o", [128, NQ]) for _ in range(2)]
        pls = [P.psum("ps_l", [128, NQ]) for _ in range(2)]
        ones = self.ones
        qblocks = [(256 + NQ * b, NQ, list(range(NT))) for b in range(16)]
        if need_ctx:
            qblocks = [(0, 256, [0, 1])] + qblocks
        n_s = 0
        n_q = 0
        for h in range(8):
            Kn, Kr, Vh = Kns[h % 2], Krs[h % 2], Vhs[h % 2]
            P.ld("sp", Kn[:], self.KT[h, 0:128, :], R=[self.KT], W=[Kn])
            P.ld("sp", Kr[:], self.KT[h, 128:192, :], R=[self.KT], W=[Kr])
            P.ld("sp", Vh[:], self.V[:, h * 128:(h + 1) * 128].rearrange("(n p) v -> p n v", p=128), R=[self.V], W=[Vh])
            for (t0, nq, keys) in qblocks:
                Qn, Qr = Qns[n_q % 2], Qrs[n_q % 2]
                po, pl = pos[n_q % 2], pls[n_q % 2]
                rl, ot = rls[n_q % 2], ots[n_q % 2]
                n_q += 1
                P.ld("sp", Qn[:, :nq], self.QT[h, 0:128, t0:t0 + nq], R=[self.QT], W=[Qn])
                P.ld("sp", Qr[:, :nq], self.QT[h, 128:192, t0:t0 + nq], R=[self.QT], W=[Qr])
                for ki, kt in enumerate(keys):
                    ps, pt = pss[n_s % 3], pts[n_s % 3]
                    n_s += 1
                    ks = slice(kt * 128, (kt + 1) * 128)
                    P.mm(ps[:, :nq], Kn[:, ks], Qn[:, :nq], start=True, stop=False, R=[Kn, Qn], W=[ps])
                    P.mm(ps[:, :nq], Kr[:, ks], Qr[:, :nq], start=False, stop=True, R=[Kr, Qr], W=[ps])
                    P.act(pt[:, :nq], ps[:, :nq], AF.Exp, R=[ps], W=[pt])
                    first, last = ki == 0, ki == len(keys) - 1
                    P.mm(po[:, :nq], Vh[:, kt, :], pt[:, :nq], start=first, stop=last, R=[Vh, pt], W=[po])
                    P.mm(pl[:, :nq], ones[:], pt[:, :nq], start=first, stop=last, R=[ones, pt], W=[pl])
                P.recip(rl[:, :nq], pl[:, :nq], R=[pl], W=[rl])
                P.tt("dve", ot[:, :nq], po[:, :nq], rl[:, :nq], ALU.mult, R=[po, rl], W=[ot])
                P.ld("pool", self.OT[h * 128:(h + 1) * 128, t0:t0 + nq], ot[:, :nq], R=[ot], W=[self.OT])
        P.pop()


    def moe_state(self, L):
        P = self.P
        M = NS()
        M.logits = P.sbuf("logits", [128, NE, NT], F32)
        M.idxl = P.sbuf("idxl", [128, NE, 8], I32)
        M.idxc = P.sbuf("idxc", [32, NE], I32)
        M.wr = self.cast_load("wr", [128, 8, NE], self.moe_wr[L.i].rearrange("(k p) e -> p k e", p=128), [self.moe_wr])
        return M

    def tail_tiles(self, n_pt=2):
        P = self.P
        T = NS()
        T.n = 0
        T.xo = [P.sbuf("xo", [128, D], F32) for _ in range(2)]
        T.tmp = [P.sbuf("ttmp", [128, D], F32) for _ in range(2)]
        T.xn = [P.sbuf("txn", [128, D], F32) for _ in range(2)]
        T.junk = P.sbuf("tjunk", [128, D], BF16)
        T.ss = [P.sbuf("tss", [128, 1], F32) for _ in range(2)]
        T.rs = [P.sbuf("trs", [128, 1], F32) for _ in range(2)]
        T.hb = [P.sbuf("thb", [128, D], BF16) for _ in range(2)]
        T.pt = [P.psum("tpt", [128, 8, 128], BF16) for _ in range(n_pt)]
        T.hTt = [P.sbuf("thTt", [128, 8, 128], BF16) for _ in range(2)]
        T.plg = P.psum("tplg", [128, NE])
        return T

    def tail(self, L, M, T, i, ol_aps, ol_R, xs_rows=None, h_rows=None):
        P = self.P
        m = 1 if i < 2 else 0
        b = T.n % 2
        T.n += 1
        xo, tmp, xn, ss, rs, hb, pt, hTt = T.xo[b], T.tmp[b], T.xn[b], T.ss[b], T.rs[b], T.hb[b], T.pt[b % len(T.pt)], T.hTt[b]
        rows = slice(i * 128, (i + 1) * 128)
        xs_rows = self.XS[rows, :] if xs_rows is None else xs_rows
        h_rows = self.hrow[rows, :] if h_rows is None else h_rows
        P.ld("sp", xo[:], xs_rows, R=[], W=[xo])
        for ap, c0, w in ol_aps:
            P.tt("dve", tmp[:, c0:c0 + w], ap, L.gta[m][:, c0:c0 + w], ALU.mult, R=list(ol_R) + [L.gta[m]], W=[tmp])
        P.tt(TAIL_ENG, xn[:], tmp[:], xo[:], ALU.add, R=[tmp, xo], W=[xn])
        P.ld("pool", xs_rows, xn[:], R=[xn], W=[Res("u")])
        if L.skip_ctx_moe and i < 2:
            return
        P.act(T.junk[:], xn[:], AF.Square, R=[xn], W=[T.junk, ss], accum_out=ss[:])
        P.act(rs[:], ss[:], AF.Sqrt, R=[ss], W=[rs], scale=1.0 / D, bias=EPS)
        P.recip(rs[:], rs[:], R=[rs], W=[rs])
        P.stt("dve", tmp[:], xn[:], rs[:], L.Gf[m][:], ALU.mult, ALU.mult, R=[xn, rs, L.Gf[m]], W=[tmp])
        P.tt(TAIL_ENG, hb[:], tmp[:], L.Sf[m][:], ALU.add, R=[tmp, L.Sf[m]], W=[hb])
        P.ld("pool", h_rows, hb[:], R=[hb], W=[Res("u")])
        for k in range(8):
            P.tr(pt[:, k, :], hb[:, k * 128:(k + 1) * 128], self.ident[:], R=[hb, self.ident], W=[pt])
        P.cp("act", hTt[:], pt[:], R=[pt], W=[hTt])
        for k in range(8):
            P.mm(T.plg[:], hTt[:, k, :], M.wr[:, k, :], start=(k == 0), stop=(k == 7), R=[hTt, M.wr], W=[T.plg])
        P.cp("dve", M.logits[:, :, i], T.plg[:], R=[T.plg], W=[M.logits])

    def mla_out(self, L, M, j, need_ctx):
        P = self.P
        P.push()
        w_o = self.cast_load("w_o", [128, 8, D], self.mla_w_o[j].rearrange("(k p) n -> p k n", p=128), [self.mla_w_o])
        self.tail_consts(L)
        T = self.tail_tiles()
        OTs = [P.sbuf("OTt", [128, 8, 128], BF16) for _ in range(2)]
        pols = [P.psum("pol", [128, D]) for _ in range(2)]
        for i in range(NT):
            if i < 2 and not need_ctx:
                continue
            OTt, pol = OTs[i % 2], pols[i % 2]
            P.ld("sp", OTt[:], self.OT[:, i * 128:(i + 1) * 128].rearrange("(k p) t -> p k t", p=128), R=[], W=[OTt])
            for n in range(2):
                for k in range(8):
                    P.mm(pol[:, n * 512:(n + 1) * 512], OTt[:, k, :], w_o[:, k, n * 512:(n + 1) * 512],
                         start=(k == 0), stop=(k == 7), R=[OTt, w_o], W=[pol])
            self.tail(L, M, T, i, [(pol[:, 0:512], 0, 512), (pol[:, 512:1024], 512, 512)], [pol])
        P.pop()

    def moe_route(self, L, M):
        P = self.P
        i = L.i
        do_ctx = not L.skip_ctx_moe
        P.push()
        lg = M.logits
        lg_te = lg[:].rearrange("p e t -> p t e")
        aff = P.sbuf("aff", [128, NE, NT], F32)
        aff_te = aff[:].rearrange("p e t -> p t e")
        mx = P.sbuf("mx", [128, NT], F32)
        sm = P.sbuf("sm", [128, NT], F32)
        if not do_ctx:
            P.memset("dve", lg[:, :, 0:2], 0.0, W=[lg])
        P.red("dve", mx[:], lg_te, ALU.max, R=[lg], W=[mx])
        P.tt("dve", aff[:], lg[:], mx[:].unsqueeze(1).to_broadcast([128, NE, NT]), ALU.subtract, R=[lg, mx], W=[aff])
        P.act(aff[:], aff[:], AF.Exp, R=[aff], W=[aff])
        P.red("dve", sm[:], aff_te, ALU.add, R=[aff], W=[sm])
        P.recip(sm[:], sm[:], R=[sm], W=[sm])
        P.tt("dve", aff[:], aff[:], sm[:].unsqueeze(1).to_broadcast([128, NE, NT]), ALU.mult, R=[aff, sm], W=[aff])
        affc = P.sbuf("affc", [128, NT, NE], F32)
        P.cp("dve", affc[:], aff_te, R=[aff], W=[affc])
        affrow_v = self.affrow.t.rearrange("(t p) e -> p t e", p=128)
        aff_w = []
        if getattr(L, "fmap", False):
            pieces = [(affrow_v[:, 0:2, :], affc[:, 0:2, :])]
            av = self.affrow.t[256:, :].rearrange("(p q) e -> p q e", q=64)
            for q in range(4):
                pieces.append((av[:, q * 16:(q + 1) * 16, :], affc[:, 2 + q * 16:2 + (q + 1) * 16, :]))
        else:
            pieces = [(affrow_v[:, q * 11:(q + 1) * 11, :], affc[:, q * 11:(q + 1) * 11, :]) for q in range(6)]
        for dst, src in pieces:
            r_ = Res("affw")
            aff_w.append(r_)
            P.ld("pool", dst, src, R=[affc], W=[r_])
        tokid_sb = self.tokid_f_sb if getattr(L, "fmap", False) else self.tokid_sb
        lo = P.sbuf("lo", [128, 2, NE], F32)
        mid = P.sbuf("mid", [128, 2, NE], F32)
        cnt = P.sbuf("cnt", [128, 2, NE], F32)
        gew = P.sbuf("gew", [128, 2, NE], F32)
        cmp_ = P.sbuf("cmp", [128, NE, NT], BF16)
        cmp_f = cmp_[:].rearrange("p e t -> p (e t)")
        cps = P.psum("cps", [128, 1536])
        cps_v = cps[:, 0:NE * NT].rearrange("p (e t) -> p e t", t=NT)
        P.memset("dve", lo[:], 0.0, W=[lo])

        def count(thr):
            P.tt("dve", cmp_[:, :, 2:NT], aff[:, :, 2:NT], thr[:, 0].unsqueeze(2).to_broadcast([128, NE, NT - 2]), ALU.is_ge,
                 R=[aff, thr], W=[cmp_])
            P.tt("dve", cmp_[:, :, 0:2], aff[:, :, 0:2], thr[:, 1].unsqueeze(2).to_broadcast([128, NE, 2]), ALU.is_ge,
                 R=[aff, thr], W=[cmp_])
            for n, (c0, w) in enumerate(((0, 512), (512, 512), (1024, 32))):
                P.mm(cps[:, c0:c0 + w], self.ones[:], cmp_f[:, c0:c0 + w], R=[self.ones, cmp_], W=[cps])

        NIT = 26
        for it in range(NIT):
            w = 0.5 ** (it + 1)
            P.tsc("dve", mid[:], lo[:], w, None, ALU.add, R=[lo], W=[mid])
            count(mid)
            P.red("dve", cnt[:, 0], cps_v[:, :, 2:NT], ALU.add, R=[cps], W=[cnt])
            P.red("dve", cnt[:, 1], cps_v[:, :, 0:2], ALU.add, R=[cps], W=[cnt])
            P.tsc("dve", gew[:, 0], cnt[:, 0], 1023.5, w, ALU.is_ge, ALU.mult, R=[cnt], W=[gew])
            P.tsc("dve", gew[:, 1], cnt[:, 1], 31.5, w, ALU.is_ge, ALU.mult, R=[cnt], W=[gew])
            P.tt("dve", lo[:], lo[:], gew[:], ALU.add, R=[lo, gew], W=[lo])
        count(lo)
        sel = cmp_
        tot = P.sbuf("tot", [128, NE, NT], F32)
        P.cp("dve", tot[:], cps_v, R=[cps], W=[tot])
        sa = P.sbuf("sa", [128, NE, 64], F32)
        sb = P.sbuf("sb", [128, NE, 64], F32)
        P.cp("dve", sa[:], tot[:, :, 2:NT], R=[tot], W=[sa])
        cur, oth = sa, sb
        sh = 1
        while sh < 64:
            P.cp("dve", oth[:, :, 0:sh], cur[:, :, 0:sh], R=[cur], W=[oth])
            P.tt("dve", oth[:, :, sh:64], cur[:, :, sh:64], cur[:, :, 0:64 - sh], ALU.add, R=[cur], W=[oth])
            cur, oth = oth, cur
            sh *= 2
        offs = P.sbuf("offs", [128, NE, NT], F32)
        P.tt("dve", offs[:, :, 2:NT], cur[:], tot[:, :, 2:NT], ALU.subtract, R=[cur, tot], W=[offs])
        P.tsc("dve", offs[:, :, 2:NT], offs[:, :, 2:NT], 32.0, None, ALU.add, R=[offs], W=[offs])
        P.memset("dve", offs[:, :, 0:1], 0.0, W=[offs])
        P.cp("dve", offs[:, :, 1:2], tot[:, :, 0:1], R=[tot], W=[offs])
        ips = P.psum("ips", [128, 1536])
        for t in range(NT):
            P.mm(ips[:, t * NE:(t + 1) * NE], self.ustrict[:], sel[:, :, t], R=[self.ustrict, sel], W=[ips])
        ips_v = ips[:, 0:NT * NE].rearrange("p (t e) -> p e t", e=NE)
        slot = P.sbuf("slot", [128, NE, NT], F32)
        P.tt("dve", slot[:], ips_v, offs[:], ALU.add, R=[ips, offs], W=[slot])
        P.stt("dve", slot[:], slot[:], self.dumpc[:, 0:1], sel[:], ALU.subtract, ALU.mult, R=[slot, self.dumpc, sel], W=[slot])
        P.tsc("dve", slot[:], slot[:], self.dumpc[:, 0:1], None, ALU.add, R=[slot, self.dumpc], W=[slot])
        P.tt("dve", slot[:], slot[:], self.eoffs[:].unsqueeze(2).to_broadcast([128, NE, NT]), ALU.add, R=[slot, self.eoffs], W=[slot])
        smap = P.sbuf("smap", [128, NE, NT], I32)
        P.cp("dve", smap[:], slot[:], R=[slot], W=[smap])
        idx_res = []
        for t in range(NT):
            if t < 2 and not do_ctx:
                continue
            for e in range(NE):
                r_ = Res("idxw")
                idx_res.append(r_)
                P.scatter(self.idxd[:, :], tokid_sb[:, t:t + 1], smap[:, e, t:t + 1], R=[smap, tokid_sb], W=[r_])
        idxl, idxc = M.idxl, M.idxc
        P.ld("sp", idxl[:], self.idxd.t.rearrange("(e s) o -> e (s o)", e=NE)[:, 32:1056].rearrange("e (p k) -> p e k", k=8), R=idx_res, W=[idxl])
        if do_ctx:
            P.ld("sp", idxc[:], self.idxd.t.rearrange("(e s) o -> e (s o)", e=NE)[:, 0:32].rearrange("e p -> p e"), R=idx_res, W=[idxc],
                 allow_slow_non_contiguous=True)
        P.pop()

    def moe_experts(self, L, M):
        P = self.P
        i = L.i
        do_ctx = not L.skip_ctx_moe
        idxl, idxc = M.idxl, M.idxc
        aff_w = []
        P.push()
        NCH = 9 if do_ctx else 8
        NTOK = 1056 if do_ctx else 1024
        wsets = [[P.sbuf("wexp", [128, 8, D], BF16) for _ in range(3)] for _ in range(2)]
        xes = [P.sbuf("xe", [128, 9, D], BF16) for _ in range(1)]
        gas = [P.sbuf("ga", [128, 9, NE], F32) for _ in range(2)]
        xeT = P.sbuf("xeT", [128, 8, 1056], BF16)
        hid = P.sbuf("hid", [128, 8, 1056], BF16)
        sg = [P.sbuf("sg", [128, 512], F32) for _ in range(2)]
        ys = [P.sbuf("ys", [128, D], F32) for _ in range(2)]
        ptp = [P.psum("ptp", [128, 8, 128], BF16) for _ in range(1)]
        pbank = [P.psum("pbk", [128, 512]) for _ in range(6)]
        st = {"b": 0, "y": 0}

        def bank():
            t = pbank[st["b"] % len(pbank)]
            st["b"] += 1
            return t
        srcs = (self.moe_wg, self.moe_wu, self.moe_wd)

        def prefetch_w(e):
            ws = wsets[e % 2]
            for a in range(3):
                P.ld("pool", ws[a][:], srcs[a][i, e].rearrange("(k p) n -> p k n", p=128), R=[], W=[ws[a]])

        def prefetch(e):
            xe, ga = xes[0], gas[e % 2]
            for k in range(8):
                P.gather(xe[:, k, :], self.hrow[:, :], idxl[:, e, k:k + 1], R=[idxl], W=[xe])
                P.gather(ga[:, k, :], self.affrow[:, :], idxl[:, e, k:k + 1], R=[idxl] + aff_w, W=[ga])
            if do_ctx:
                P.gather(xe[0:32, 8, :], self.hrow[:, :], idxc[0:32, e:e + 1], R=[idxc], W=[xe])
                P.gather(ga[0:32, 8, :], self.affrow[:, :], idxc[0:32, e:e + 1], R=[idxc] + aff_w, W=[ga])

        grp = [[Res("scA") for _ in range(9)], [Res("scB") for _ in range(9)]]
        prefetch_w(0)
        prefetch(0)
        for e in range(NE):
            if e + 1 < NE:
                prefetch_w(e + 1)
            wg, wu, wd = wsets[e % 2]
            xe, ga = xes[0], gas[e % 2]
            for k in range(NCH):
                rows = 128 if k < 8 else 32
                pt = ptp[0]
                for dk in range(8):
                    P.tr(pt[:, dk, 0:rows], xe[0:rows, k, dk * 128:(dk + 1) * 128], self.ident[0:rows, 0:rows],
                         R=[xe, self.ident], W=[pt])
                P.cp("act", xeT[:, :, k * 128:k * 128 + rows], pt[:, :, 0:rows], R=[pt], W=[xeT])
            if e + 1 < NE:
                prefetch(e + 1)
            nblks = [(0, 512), (512, 512)] + ([(1024, 32)] if do_ctx else [])
            for f in range(8):
                fs = slice(f * 128, (f + 1) * 128)
                for (c0, w) in nblks:
                    pg, pu = bank(), bank()
                    for dk in range(8):
                        P.mm(pg[:, :w], wg[:, dk, fs], xeT[:, dk, c0:c0 + w], start=(dk == 0), stop=(dk == 7), R=[wg, xeT], W=[pg])
                    for dk in range(8):
                        P.mm(pu[:, :w], wu[:, dk, fs], xeT[:, dk, c0:c0 + w], start=(dk == 0), stop=(dk == 7), R=[wu, xeT], W=[pu])
                    s_ = sg[st["y"] % 2]
                    st["y"] += 1
                    P.act(s_[:, :w], pg[:, :w], AF.Silu, R=[pg], W=[s_])
                    P.tt("dve", hid[:, f, c0:c0 + w], s_[:, :w], pu[:, :w], ALU.mult, R=[s_, pu], W=[hid])
            for k in range(NCH):
                rows = 128 if k < 8 else 32
                m = 0 if k < 8 else 1
                y = ys[k % 2]
                for n in range(2):
                    py = bank()
                    for f in range(8):
                        P.mm(py[0:rows, :], hid[:, f, k * 128:k * 128 + rows], wd[:, f, n * 512:(n + 1) * 512],
                             start=(f == 0), stop=(f == 7), R=[hid, wd], W=[py])
                    P.stt("dve", y[0:rows, n * 512:(n + 1) * 512], py[0:rows, :], ga[0:rows, k, e:e + 1],
                          L.gtf[m][0:rows, n * 512:(n + 1) * 512], ALU.mult, ALU.mult, R=[py, ga, L.gtf[m]], W=[y])
                ia = idxl[:, e, k:k + 1] if k < 8 else idxc[0:32, e:e + 1]
                P.scatter(self.XS[:, :], y[0:rows, :], ia, R=[y, idxl, idxc] + grp[(e + 1) % 2], W=[grp[e % 2][k]],
                          add=True, bound=TA - 1)
        P.pop()


    def ssd_in(self, L):
        P = self.P
        P.push()
        w = P.sbuf("ssd_w", [128, 8, 5184], BF16)
        wsrc = self.ssd_w_in.t.rearrange("(k p) n -> p k n", p=128)
        for c0 in range(0, 5184, 1728):
            P.ld("pool", w[:, :, c0:c0 + 1728], wsrc[:, :, c0:c0 + 1728], R=[], W=[w])
        cw = P.sbuf("cw", [128, 24, 3], F32)
        for k in range(3):
            P.ld("sp", cw[:, :, k], self.ssd_conv_w[k].rearrange("(f p) -> p f", p=128), R=[], W=[cw])
        cb = P.sbuf("cb", [128, 24], F32)
        P.ld("sp", cb[:], self.ssd_conv_b.t.rearrange("(f p) -> p f", p=128), R=[], W=[cb])
        dtb = P.sbuf("dtb", [128, 64], F32)
        P.ld("sp", dtb[:], self.ssd_dt_bias.t.partition_broadcast(128), R=[], W=[dtb])
        hTws = [P.sbuf("hTw", [128, 8, 258], BF16) for _ in range(2)]
        cv = P.sbuf("cv", [128, 24, 256], BF16)
        accs = [P.sbuf("acc", [128, 256], F32) for _ in range(2)]
        xbs = [P.sbuf("xb_tm", [128, 2560], BF16) for _ in range(2)]
        zss = [P.sbuf("zs_tm", [128, 2048], BF16) for _ in range(2)]
        dtr = [P.sbuf("dtr", [128, 64], F32) for _ in range(2)]
        banks = [P.psum("sbk", [128, 512]) for _ in range(5)]
        ptrs = [P.psum("sptr", [128, 8, 128], BF16) for _ in range(2)]
        st = {"b": 0, "t": 0, "a": 0}

        def bank():
            t = banks[st["b"] % len(banks)]
            st["b"] += 1
            return t
        BTv = self.BT_d.t.rearrange("(g p) t -> p g t", p=128)
        CTv = self.CT_d.t.rearrange("(g p) t -> p g t", p=128)
        hTv = self.hT.t.rearrange("(k p) t -> p k t", p=128)
        wins = [(0, True, True)] + [(256 + 256 * q, q == 0, q == 31) for q in range(32)]
        for wi, (t0, lz, rz) in enumerate(wins):
            hTw = hTws[wi % 2]
            lo = t0 - (0 if lz else 1)
            hi = t0 + 256 + (0 if rz else 1)
            c_lo = 1 if lz else 0
            P.ld("sp", hTw[:, :, c_lo:c_lo + (hi - lo)], hTv[:, :, lo:hi], R=[], W=[hTw])
            if lz:
                P.memset("pool", hTw[:, :, 0:1], 0.0, W=[hTw])
            if rz:
                P.memset("pool", hTw[:, :, 257:258], 0.0, W=[hTw])
            for fc in range(24):
                ps = bank()
                for k in range(8):
                    P.mm(ps[:, 0:258], w[:, k, 2048 + fc * 128:2048 + (fc + 1) * 128], hTw[:, k, :], start=(k == 0), stop=(k == 7),
                         R=[w, hTw], W=[ps])
                acc = accs[st["a"] % 2]
                st["a"] += 1
                P.tsc("dve", acc[:], ps[:, 0:256], cw[:, fc, 0:1], None, ALU.mult, R=[ps, cw], W=[acc])
                P.stt("dve", acc[:], ps[:, 1:257], cw[:, fc, 1:2], acc[:], ALU.mult, ALU.add, R=[ps, cw, acc], W=[acc])
                P.stt("dve", acc[:], ps[:, 2:258], cw[:, fc, 2:3], acc[:], ALU.mult, ALU.add, R=[ps, cw, acc], W=[acc])
                P.act(cv[:, fc, :], acc[:], AF.Silu, R=[acc, cb], W=[cv], bias=cb[:, fc:fc + 1])
            P.ld("pool", BTv[:, :, t0:t0 + 256], cv[:, 16:20, :], R=[cv], W=[Res("u")])
            P.ld("pool", CTv[:, :, t0:t0 + 256], cv[:, 20:24, :], R=[cv], W=[Res("u")])
            for tt in range(2):
                rows = slice(t0 + tt * 128, t0 + (tt + 1) * 128)
                xb, zs, dr_ = xbs[tt], zss[tt], dtr[tt]
                for f0 in (0, 8, 16):
                    nf = min(8, 20 - f0)
                    ptr = ptrs[st["t"] % 2]
                    st["t"] += 1
                    for q in range(nf):
                        P.tr(ptr[:, q, :], cv[:, f0 + q, tt * 128:(tt + 1) * 128], self.ident[:], R=[cv, self.ident], W=[ptr])
                    P.cp("act", xb[:, f0 * 128:(f0 + nf) * 128], ptr[:, 0:nf, :], R=[ptr], W=[xb])
                P.ld("pool", self.xs_d[rows, :], xb[:, 0:2048], R=[xb], W=[Res("u")])
                P.ld("pool", self.B_d[rows, :], xb[:, 2048:2560], R=[xb], W=[Res("u")])
                for nb in range(4):
                    ps = bank()
                    for k in range(8):
                        P.mm(ps[:], hTw[:, k, 1 + tt * 128:1 + (tt + 1) * 128], w[:, k, nb * 512:(nb + 1) * 512], start=(k == 0), stop=(k == 7),
                             R=[w, hTw], W=[ps])
                    P.act(zs[:, nb * 512:(nb + 1) * 512], ps[:], AF.Silu, R=[ps], W=[zs])
                P.ld("pool", self.zs_d[rows, :], zs[:], R=[zs], W=[Res("u")])
                ps = bank()
                for k in range(8):
                    P.mm(ps[:, 0:64], hTw[:, k, 1 + tt * 128:1 + (tt + 1) * 128], w[:, k, 5120:5184], start=(k == 0), stop=(k == 7),
                         R=[w, hTw], W=[ps])
                P.tt("dve", dr_[:], ps[:, 0:64], dtb[:], ALU.add, R=[ps, dtb], W=[dr_])
                P.act(dr_[:], dr_[:], AF.Exp, R=[dr_], W=[dr_])
                P.act(dr_[:], dr_[:], AF.Ln, R=[dr_], W=[dr_], bias=1.0)
                P.ld("pool", self.dt_d[rows, :], dr_[:], R=[dr_], W=[Res("u")])
        P.pop()

    def ssd_scan(self, L, dr):
        P = self.P
        P.push()
        c2 = P.sbuf("c2", [128, 640], F32)
        P.ld("sp", c2[:], self.cst2[:, :], R=[], W=[c2])
        LE, GE, GT, LT, onesf = (c2[:, q * 128:(q + 1) * 128] for q in range(5))
        m1 = LE if dr == 0 else GE
        lm = GT if dr == 0 else LT
        a_b = P.sbuf("a_b", [128, 32], F32)
        P.ld("sp", a_b[:], self.ssd_a_log[dr].partition_broadcast(128), R=[], W=[a_b])
        P.act(a_b[:], a_b[:], AF.Exp, R=[a_b], W=[a_b])
        P.tsc("dve", a_b[:], a_b[:], -1.0, None, ALU.mult, R=[a_b], W=[a_b])
        stf = [P.sbuf("stf", [128, 512], F32) for _ in range(4)]
        stb = [P.sbuf("stb", [128, 512], BF16) for _ in range(4)]
        for g in range(4):
            P.memset("dve", stf[g][:], 0.0, W=[stf[g]])
            P.memset("dve", stb[g][:], 0.0, W=[stb[g]])
        NB = 2
        xss = [P.sbuf("s_xs", [128, 2048], BF16) for _ in range(NB)]
        Bts = [P.sbuf("s_B", [128, 512], BF16) for _ in range(NB)]
        BTs = [P.sbuf("s_BT", [128, 4, 128], BF16) for _ in range(NB)]
        CTs = [P.sbuf("s_CT", [128, 4, 128], BF16) for _ in range(NB)]
        dts = [P.sbuf("s_dt", [128, 64], F32) for _ in range(NB)]
        dtas = [P.sbuf("s_dta", [128, 32], F32) for _ in range(NB)]
        E3s = [P.sbuf("s_E3", [128, 3, 32], F32) for _ in range(NB)]
        xdts = [P.sbuf("s_xdt", [128, 2048], BF16) for _ in range(NB)]
        xdds = [P.sbuf("s_xdd", [128, 2048], BF16) for _ in range(NB)]
        ys = [P.sbuf("s_y", [128, 2048], F32) for _ in range(NB)]
        rhsEs = [P.sbuf("s_rhsE", [128, 8, 128], F32) for _ in range(2)]
        Lts = [P.sbuf("s_Lt", [128, 8, 128], BF16) for _ in range(2)]
        MTs = [P.sbuf("s_MT", [128, 8, 128], BF16) for _ in range(2)]
        cbms = [P.sbuf("s_cbm", [128, 128], BF16) for _ in range(2)]
        t1s = [P.sbuf("s_t1", [128, 512], F32) for _ in range(2)]
        shr = P.psum("shr", [128, 512])
        e3p = Tile("e3p", shr[:, 128:224].rearrange("p (a b) -> p a b", b=32))
        cbp = Tile("cbp", shr[:, 0:128])
        e3p._res = shr._res
        cbp._res = shr._res
        args = [P.psum("argp", [128, 8, 128]) for _ in range(2)]
        yp = [P.psum("yp", [128, 512]) for _ in range(1)]
        yop = P.psum("yop", [128, 512])
        sp_ = P.psum("sps", [128, 512])
        BTv = self.BT_d.t.rearrange("(g p) t -> p g t", p=128)
        CTv = self.CT_d.t.rearrange("(g p) t -> p g t", p=128)
        order = list(range(NT)) if dr == 0 else [1, 0] + list(range(NT - 1, 1, -1))
        yd = self.yf_d if dr == 0 else self.yb_d
        ng = 0
        for ci, c in enumerate(order):
            b = ci % NB
            rows = slice(c * 128, (c + 1) * 128)
            xs, Bt, BT, CT, dt, dta, E3, xdt, xdd, y = xss[b], Bts[b], BTs[b], CTs[b], dts[b], dtas[b], E3s[b], xdts[b], xdds[b], ys[b]
            P.ld("sp", xs[:], self.xs_d[rows, :], R=[], W=[xs])
            P.ld("sp", Bt[:], self.B_d[rows, :], R=[], W=[Bt])
            P.ld("sp", BT[:], BTv[:, :, rows], R=[], W=[BT])
            P.ld("sp", CT[:], CTv[:, :, rows], R=[], W=[CT])
            P.ld("sp", dt[:], self.dt_d[rows, :], R=[], W=[dt])
            dtd = dt[:, dr * 32:(dr + 1) * 32]
            P.tt("dve", dta[:], dtd, a_b[:], ALU.mult, R=[dt, a_b], W=[dta])
            P.mm(e3p[:, 0, :], m1, dta[:], R=[c2, dta], W=[e3p])
            P.mm(e3p[:, 1, :], lm, dta[:], R=[c2, dta], W=[e3p])
            P.mm(e3p[:, 2, :], onesf, dta[:], R=[c2, dta], W=[e3p])
            P.act(E3[:], e3p[:], AF.Exp, R=[e3p], W=[E3])
            xs3 = xs[:].rearrange("p (h q) -> p h q", q=64)
            P.tt("dve", xdt[:].rearrange("p (h q) -> p h q", q=64), xs3, dtd.unsqueeze(2).to_broadcast([128, 32, 64]), ALU.mult,
                 R=[xs, dt], W=[xdt])
            P.tt("pool", xdd[:].rearrange("p (h q) -> p h q", q=64), xdt[:].rearrange("p (h q) -> p h q", q=64),
                 E3[:, 1, :].unsqueeze(2).to_broadcast([128, 32, 64]), ALU.mult, R=[xdt, E3], W=[xdd])
            for g in range(4):
                hs = slice(g * 8, (g + 1) * 8)
                q2 = ng % 2
                ng += 1
                rhsE, Lt, MT, cbm, t1 = rhsEs[q2], Lts[q2], MTs[q2], cbms[q2], t1s[q2]
                arg = args[ng % 2]
                P.tt("pool", rhsE[:], dta[:, hs].unsqueeze(2).to_broadcast([128, 8, 128]), m1.unsqueeze(1).to_broadcast([128, 8, 128]),
                     ALU.mult, R=[dta, c2], W=[rhsE])
                for hf in range(2):
                    P.mm(arg[:, hf * 4:(hf + 1) * 4, :], lm, rhsE[:, hf * 4:(hf + 1) * 4, :], R=[c2, rhsE], W=[arg])
                P.act(Lt[:], arg[:], AF.Exp, R=[arg], W=[Lt])
                P.mm(cbp[:], BT[:, g, :], CT[:, g, :], R=[BT, CT], W=[cbp])
                P.tt("dve", cbm[:], cbp[:], m1, ALU.mult, R=[cbp, c2], W=[cbm])
                P.tt("dve", MT[:], Lt[:], cbm[:].unsqueeze(1).to_broadcast([128, 8, 128]), ALU.mult, R=[Lt, cbm], W=[MT])
                ypg = yp[0]
                for hl in range(8):
                    h = g * 8 + hl
                    P.mm(ypg[:, hl * 64:(hl + 1) * 64], MT[:, hl, :], xdt[:, h * 64:(h + 1) * 64], R=[MT, xdt], W=[ypg])
                P.mm(yop[:], CT[:, g, :], stb[g][:], R=[CT, stb[g]], W=[yop])
                P.tt("dve", t1[:].rearrange("p (h q) -> p h q", q=64), yop[:].rearrange("p (h q) -> p h q", q=64),
                     E3[:, 0, hs].unsqueeze(2).to_broadcast([128, 8, 64]), ALU.mult, R=[yop, E3], W=[t1])
                P.tt("dve", y[:, g * 512:(g + 1) * 512], t1[:], ypg[:], ALU.add, R=[t1, ypg], W=[y])
                P.mm(sp_[:], Bt[:, g * 128:(g + 1) * 128], xdd[:, g * 512:(g + 1) * 512], R=[Bt, xdd], W=[sp_])
                P.tt("pool", stf[g][:].rearrange("p (h q) -> p h q", q=64), stf[g][:].rearrange("p (h q) -> p h q", q=64),
                     E3[:, 2, hs].unsqueeze(2).to_broadcast([128, 8, 64]), ALU.mult, R=[stf[g], E3], W=[stf[g]])
                P.tt("dve", stf[g][:], stf[g][:], sp_[:], ALU.add, R=[stf[g], sp_], W=[stf[g]])
                P.cp("act", stb[g][:], stf[g][:], R=[stf[g]], W=[stb[g]])
            P.ld("pool", yd[rows, :], y[:], R=[y], W=[Res("u")])
        P.pop()

    def ssd_out(self, L, M):
        P = self.P
        P.push()
        w_out = self.cast_load("ssd_wo", [128, 16, D], self.ssd_w_out.t.rearrange("(k p) n -> p k n", p=128), [])
        self.tail_consts(L)
        T = self.tail_tiles(n_pt=1)
        D_b = P.sbuf("D_b", [128, 32], F32)
        P.ld("sp", D_b[:], self.ssd_d.t.partition_broadcast(128), R=[], W=[D_b])
        gn_b = P.sbuf("gn_b", [128, 2048], F32)
        P.ld("sp", gn_b[:], self.ssd_g_norm.t.partition_broadcast(128), R=[], W=[gn_b])
        yfs = [P.sbuf("o_yf", [128, 2048], F32) for _ in range(2)]
        ybs = [P.sbuf("o_yb", [128, 2048], F32) for _ in range(2)]
        xss = [P.sbuf("o_xs", [128, 2048], BF16) for _ in range(2)]
        zss = [P.sbuf("o_zs", [128, 2048], BF16) for _ in range(2)]
        gats = [P.sbuf("o_gat", [128, 2048], F32) for _ in range(2)]
        junk = P.sbuf("o_junk", [128, 512], BF16)
        ss4s = [P.sbuf("o_ss4", [128, 4], F32) for _ in range(2)]
        rs4s = [P.sbuf("o_rs4", [128, 4], F32) for _ in range(2)]
        nrms = [P.sbuf("o_nrm", [128, 2048], BF16) for _ in range(2)]
        nTs = [P.sbuf("o_nT", [128, 16, 128], BF16) for _ in range(2)]
        ptr = [P.psum("o_ptr", [128, 8, 128], BF16) for _ in range(2)]
        pols = [P.psum("o_pol", [128, D]) for _ in range(2)]
        for i in range(NT):
            b = i % 2
            rows = slice(i * 128, (i + 1) * 128)
            yf, yb, xs, zs = yfs[b], ybs[b], xss[b], zss[b]
            gat, ss4, rs4, nrm, nT, pol = gats[b], ss4s[b], rs4s[b], nrms[b], nTs[b], pols[b]
            P.ld("sp", yf[:], self.yf_d[rows, :], R=[], W=[yf])
            P.ld("sp", yb[:], self.yb_d[rows, :], R=[], W=[yb])
            P.ld("sp", xs[:], self.xs_d[rows, :], R=[], W=[xs])
            P.ld("sp", zs[:], self.zs_d[rows, :], R=[], W=[zs])
            P.tt("dve", yf[:], yf[:], yb[:], ALU.add, R=[yf, yb], W=[yf])
            P.tt("dve", yb[:].rearrange("p (h q) -> p h q", q=64), xs[:].rearrange("p (h q) -> p h q", q=64),
                 D_b[:].unsqueeze(2).to_broadcast([128, 32, 64]), ALU.mult, R=[xs, D_b, yb], W=[yb])
            P.tt("dve", yf[:], yf[:], yb[:], ALU.add, R=[yf, yb], W=[yf])
            P.tt("dve", gat[:], yf[:], zs[:], ALU.mult, R=[yf, zs], W=[gat])
            for g in range(4):
                P.act(junk[:], gat[:, g * 512:(g + 1) * 512], AF.Square, R=[gat], W=[junk, ss4], accum_out=ss4[:, g:g + 1])
            P.act(rs4[:], ss4[:], AF.Sqrt, R=[ss4], W=[rs4], scale=1.0 / 512, bias=EPS)
            P.recip(rs4[:], rs4[:], R=[rs4], W=[rs4])
            for g in range(4):
                gs = slice(g * 512, (g + 1) * 512)
                P.stt("dve", nrm[:, gs], gat[:, gs], rs4[:, g:g + 1], gn_b[:, gs], ALU.mult, ALU.mult, R=[gat, rs4, gn_b], W=[nrm])
            for hf in range(2):
                for q in range(8):
                    fk = hf * 8 + q
                    P.tr(ptr[hf][:, q, :], nrm[:, fk * 128:(fk + 1) * 128], self.ident[:], R=[nrm, self.ident], W=[ptr[hf]])
                P.cp("act", nT[:, hf * 8:(hf + 1) * 8, :], ptr[hf][:], R=[ptr[hf]], W=[nT])
            for n in range(2):
                for fk in range(16):
                    P.mm(pol[:, n * 512:(n + 1) * 512], nT[:, fk, :], w_out[:, fk, n * 512:(n + 1) * 512], start=(fk == 0), stop=(fk == 15),
                         R=[nT, w_out], W=[pol])
            self.tail(L, M, T, i, [(pol[:, 0:512], 0, 512), (pol[:, 512:1024], 512, 512)], [pol])
        P.pop()

    def fnet_f1(self, L):
        P = self.P
        P.push()
        DF = self.cast_load("DF", [128, 2, 512], self.fcst[:, 0:1024].rearrange("p (q n) -> p q n", q=2), [])
        hTv = self.hT.t.rearrange("(k p) t -> p k t", p=128)
        hTs = [P.sbuf("f_hT", [128, 8, 128], BF16) for _ in range(2)]
        yts = [P.sbuf("f_yt", [128, 2, 4, 256], BF16) for _ in range(2)]
        yps = [P.psum("f_yps", [128, 4, 512]) for _ in range(2)]
        for i in range(NT):
            hTt, yt, yp = hTs[i % 2], yts[i % 2], yps[i % 2]
            rows = slice(i * 128, (i + 1) * 128)
            P.ld("sp", hTt[:], hTv[:, :, rows], R=[], W=[hTt])
            for g in range(4):
                for q in range(2):
                    P.mm(yp[:, g, :], hTt[:, 2 * g + q, :], DF[:, q, :], start=(q == 0), stop=(q == 1), R=[hTt, DF], W=[yp])
            for c in range(2):
                P.cp("act" if c == 0 else "dve", yt[:, c], yp[:, :, c * 256:(c + 1) * 256], R=[yp], W=[yt])
            for c in range(2):
                P.ld("pool", self.Yd[c, rows, :], yt[:, c].rearrange("p g m -> p (g m)"), R=[yt], W=[Res("u")])
        P.pop()

    def fnet_f2(self, L):
        P = self.P
        P.push()
        W1 = self.cast_load("W1big", [128, 128], self.fcst[:, 2048:2176], [])
        Ydv = self.Yd.t[:, 256:, :].rearrange("c (a b) f -> c a b f", b=128)
        Zdv = self.Zd.t.rearrange("c k t f -> (c k) t f")
        Ins = [P.sbuf("f_In", [128, 4, D], BF16) for _ in range(2)]
        Zts = [P.sbuf("f_Zt", [128, 4, D], BF16) for _ in range(2)]
        zps = [P.psum("f_zps", [128, D]) for _ in range(3)]
        nz = 0
        for bt in range(32):
            In, Zt = Ins[bt % 2], Zts[bt % 2]
            for c in range(2):
                P.ld("sp", In[c * 64:(c + 1) * 64, :, :], Ydv[c, :, 4 * bt:4 * bt + 4, :], R=[], W=[In])
            for q in range(4):
                zp = zps[nz % 3]
                nz += 1
                for n in range(2):
                    P.mm(zp[:, n * 512:(n + 1) * 512], W1[:], In[:, q, n * 512:(n + 1) * 512], R=[W1, In], W=[zp])
                P.cp("act" if q % 2 == 0 else "dve", Zt[:, q, :], zp[:], R=[zp], W=[Zt])
            P.ld("pool", Zdv[:, 4 * bt:4 * bt + 4, :], Zt[:], R=[Zt], W=[Res("u")])
        P.pop()

    def fnet_f3(self, L, M):
        P = self.P
        P.push()
        MT = P.sbuf("f_MT", [128, 64, 256], BF16)
        for kb in range(8):
            P.ld("pool", MT[:, kb * 8:(kb + 1) * 8, :], self.fM[:, kb * 2048:(kb + 1) * 2048].rearrange("p (a n) -> p a n", a=8), R=[], W=[MT])
        w_o = self.cast_load("f_wo", [128, 8, D], self.fnet_w_o.t.rearrange("(k p) n -> p k n", p=128), [])
        DF2 = self.cast_load("DF2", [128, 2, 512], self.fcst[:, 1024:2048].rearrange("p (q n) -> p q n", q=2), [])
        self.tail_consts(L)
        T = self.tail_tiles()
        mps = P.psum("f_mps", [128, 1024])
        pol = P.psum("f_pol", [128, D])

        def outproj(mT_ap_fn):
            for n in range(2):
                for fc in range(8):
                    P.mm(pol[:, n * 512:(n + 1) * 512], mT_ap_fn(fc), w_o[:, fc, n * 512:(n + 1) * 512], start=(fc == 0), stop=(fc == 7),
                         R=[w_o] + mT_R, W=[pol])
        Yc = P.sbuf("f_Yc", [128, 2, 2, D], BF16)
        for c in range(2):
            for q in range(2):
                P.ld("sp", Yc[:, q, c, :], self.Yd[c, q * 128:(q + 1) * 128, :], R=[], W=[Yc])
        mTc = P.sbuf("f_mTc", [128, 8, 256], BF16)
        mcv = mps[:].rearrange("p (a k) -> p a k", k=256)
        for half in range(2):
            for f4 in range(4):
                fc = half * 4 + f4
                n = 0
                for q in range(2):
                    for c in range(2):
                        P.mm(mcv[:, f4, :], Yc[:, q, c, fc * 128:(fc + 1) * 128], DF2[:, q, c * 256:(c + 1) * 256], start=(n == 0), stop=(n == 3),
                             R=[Yc, DF2], W=[mps])
                        n += 1
            P.act(mTc[:, half * 4:(half + 1) * 4, :], mcv, AF.Copy, R=[mps], W=[mTc], scale=1.0 / 256.0)
        for i in range(2):
            mT_R = [mTc]
            outproj(lambda fc: mTc[:, fc, i * 128:(i + 1) * 128])
            self.tail(L, M, T, i, [(pol[:, 0:512], 0, 512), (pol[:, 512:1024], 512, 512)], [pol])
        Zrs = [P.sbuf("f_Zr", [128, D], BF16) for _ in range(2)]
        Zis = [P.sbuf("f_Zi", [128, D], BF16) for _ in range(2)]
        mTs = [P.sbuf("f_mT", [128, 8, 128], BF16) for _ in range(2)]
        mlv = mps[:].rearrange("p (a k) -> p a k", k=128)
        xsv = self.XS.t[256:, :].rearrange("(p q) d -> q p d", q=64)
        hrv = self.hrow.t[256:, :].rearrange("(p q) d -> q p d", q=64)
        sc = float(1.0 / np.sqrt(8192.0 * 256.0))
        for k1 in range(64):
            Zr, Zi, mT = Zrs[k1 % 2], Zis[k1 % 2], mTs[k1 % 2]
            P.ld("sp", Zr[:], self.Zd[0, k1], R=[], W=[Zr])
            P.ld("sp", Zi[:], self.Zd[1, k1], R=[], W=[Zi])
            for fc in range(8):
                P.mm(mlv[:, fc, :], Zr[:, fc * 128:(fc + 1) * 128], MT[:, k1, 0:128], start=True, stop=False, R=[Zr, MT], W=[mps])
                P.mm(mlv[:, fc, :], Zi[:, fc * 128:(fc + 1) * 128], MT[:, k1, 128:256], start=False, stop=True, R=[Zi, MT], W=[mps])
            P.act(mT[:], mlv, AF.Copy, R=[mps], W=[mT], scale=sc)
            mT_R = [mT]
            outproj(lambda fc: mT[:, fc, :])
            self.tail(L, M, T, 2 + k1, [(pol[:, 0:512], 0, 512), (pol[:, 512:1024], 512, 512)], [pol],
                      xs_rows=xsv[k1], h_rows=hrv[k1])
        P.pop()

    def layer(self, i, upto=None):
        P = self.P
        P.push()
        L = self.layer_consts(i)
        L.skip_ctx_moe = (i == 3)
        need_ctx = (i != 3)
        M = self.moe_state(L)
        self.phase_A(L)
        kind, j = i % 3, i // 3
        if kind == 0:
            self.mla_proj(L, j, need_ctx)
            if upto == "proj":
                P.pop(); return
            if i == 0 or ATTN_V2_ALL:
                self.mla_attn(L, need_ctx)
            else:
                self.mla_attn_v1(L, need_ctx)
            if upto == "attn":
                P.pop(); return
            self.mla_out(L, M, j, need_ctx)
        if kind == 1:
            self.ssd_in(L)
            if upto == "ssd_in":
                P.pop(); return
            self.ssd_scan(L, 0)
            self.ssd_scan(L, 1)
            if upto == "ssd_scan":
                P.pop(); return
            self.ssd_out(L, M)
        if kind == 2:
            L.fmap = True
            self.fnet_f1(L)
            self.fnet_f2(L)
            self.fnet_f3(L, M)
        if upto == "mix":
            P.pop(); return
        self.moe_route(L, M)
        if upto == "route":
            P.pop(); return
        self.moe_experts(L, M)
        P.pop()

def make_consts():
    cst = np.zeros((128, 1024), np.float32)
    cst[:, 0:128] = np.eye(128, dtype=np.float32)
    cst[:, 128:256] = 1.0
    for p in range(128):
        cst[p, 256 + (p % 64)] = 1.0
        cst[p, 320 + p + 1:448] = 1.0
        cst[p, 448] = 1152 + p
        cst[p, 449:449 + NE] = np.arange(NE) * 1280
    t = np.arange(8192)
    row = (t // 64).astype(np.float32)
    col = (t % 64).astype(np.float32)
    inv = (10000.0 ** (-np.arange(16, dtype=np.float32) / 16)).astype(np.float32)
    ang = np.concatenate([row[:, None] * inv, col[:, None] * inv], axis=-1).astype(np.float32)
    cos, sin = np.cos(ang).T, np.sin(ang).T
    tab = np.zeros((128, TA), np.float32)
    tab[0:64, 0:256] = 1.0
    tab[0:32, 256:] = cos
    tab[32:64, 256:] = cos
    tab[64:96, 256:] = -sin
    tab[96:128, 256:] = sin
    tokid = (np.arange(NT)[None, :] * 128 + np.arange(128)[:, None]).astype(np.int32)
    return cst, tab, tokid


def prep_shared(I):
    S = {}
    for k in ["w_mod", "b_mod", "g_mix", "g_ffn"]:
        S[k] = np.ascontiguousarray(I[k], dtype=np.float32)
    w_in = I["mla_w_in"]
    S["mla_w_in"] = np.ascontiguousarray(np.concatenate(
        [w_in[:, :, :384], w_in[:, :, 384:448], w_in[:, :, 416:448], w_in[:, :, 384:416]], axis=-1))
    wq = I["mla_w_uq"].reshape(2, 256, 8, 192)
    S["mla_w_uq"] = np.ascontiguousarray(np.concatenate(
        [wq[..., :128], wq[..., 128:192], wq[..., 160:192], wq[..., 128:160]], axis=-1).reshape(2, 256, 2048))
    wkv = I["mla_w_ukv"].reshape(2, 128, 8, 256)
    S["mla_w_uk"] = np.ascontiguousarray(wkv[..., :128].reshape(2, 128, 1024))
    S["mla_w_uv"] = np.ascontiguousarray(wkv[..., 128:].reshape(2, 128, 1024))
    S["mla_w_o"] = np.ascontiguousarray(I["mla_w_o"])
    gc = np.zeros((2, 128, 8), np.float32)
    for j in range(2):
        gc[j, :, 0] = I["mla_g_q"][j, :128]
        gc[j, :, 1] = I["mla_g_q"][j, 128:]
        gc[j, :, 2] = I["mla_g_kv"][j]
        for c0, g in ((3, I["mla_g_qn"][j]), (5, I["mla_g_kn"][j])):
            gc[j, :, c0] = g[:128]
            gc[j, :, c0 + 1] = np.concatenate([g[128:192], g[160:192], g[128:160]])
    S["mla_gc"] = gc
    cst, tab, tokid = make_consts()
    S["cst"], S["rope_tab"], S["tokid"] = cst, tab, tokid
    S["ssd_w_in"] = np.ascontiguousarray(I["ssd_w_in"][0])
    S["ssd_conv_w"] = np.ascontiguousarray(I["ssd_conv_w"][0])
    S["ssd_conv_b"] = np.ascontiguousarray(I["ssd_conv_b"][0])
    S["ssd_dt_bias"] = np.ascontiguousarray(I["ssd_dt_bias"][0].reshape(64))
    S["ssd_a_log"] = np.ascontiguousarray(I["ssd_a_log"][0])
    S["ssd_d"] = np.ascontiguousarray(I["ssd_d"][0])
    S["ssd_g_norm"] = np.ascontiguousarray(I["ssd_g_norm"][0])
    S["ssd_w_out"] = np.ascontiguousarray(I["ssd_w_out"][0])
    kk = np.arange(128)
    c2 = np.zeros((128, 640), np.float32)
    c2[:, 0:128] = (kk[:, None] <= kk[None, :])
    c2[:, 128:256] = (kk[:, None] >= kk[None, :])
    c2[:, 256:384] = (kk[:, None] > kk[None, :])
    c2[:, 384:512] = (kk[:, None] < kk[None, :])
    c2[:, 512:640] = 1.0
    S["cst2"] = c2
    S["fnet_w_o"] = np.ascontiguousarray(I["fnet_w_o"][0])
    fc_ = np.zeros((128, 2176), np.float64)
    p_ = np.arange(128)
    m_ = np.arange(256)
    for q in range(2):
        angn = 2 * np.pi * np.outer(128 * q + p_, m_) / 256.0
        fc_[:, q * 512:q * 512 + 256] = np.cos(angn)
        fc_[:, q * 512 + 256:q * 512 + 512] = -np.sin(angn)
        fc_[:, 1024 + q * 512:1024 + q * 512 + 256] = np.cos(angn)
        fc_[:, 1024 + q * 512 + 256:1024 + q * 512 + 512] = np.sin(angn)
    a64 = np.arange(64)
    ang1 = 2 * np.pi * np.outer(a64, a64) / 64.0
    wr, wi = np.cos(ang1), -np.sin(ang1)
    fc_[0:64, 2048:2112] = wr.T
    fc_[64:128, 2048:2112] = -wi.T
    fc_[0:64, 2112:2176] = wi.T
    fc_[64:128, 2112:2176] = wr.T
    S["fcst"] = fc_.astype(np.float32)
    t2 = np.arange(128)[:, None, None]
    k1 = np.arange(64)[None, :, None]
    k2 = np.arange(128)[None, None, :]
    angm = 2 * np.pi * (k1 * t2 / 8192.0 + k2 * t2 / 128.0)
    fM = np.zeros((128, 64, 2, 128), np.float64)
    fM[:, :, 0, :] = np.cos(angm)
    fM[:, :, 1, :] = np.sin(angm)
    S["fM"] = fM.reshape(128, 16384).astype(np.float32)
    tf = (np.arange(NT)[None, :] * 128 + np.arange(128)[:, None]).astype(np.int32)
    for a in range(64):
        tf[:, 2 + a] = 256 + a + 64 * np.arange(128)
    S["tokid_f"] = tf
    S["moe_wr"] = np.ascontiguousarray(I["moe_w_router"])
    S["moe_wg"] = np.ascontiguousarray(I["moe_w_gate"])
    S["moe_wu"] = np.ascontiguousarray(I["moe_w_up"])
    S["moe_wd"] = np.ascontiguousarray(I["moe_w_down"])
    return S


def prep_core(I, S, b):
    m = dict(S)
    m["x"] = np.ascontiguousarray(I["x"][b])
    m["ctx"] = np.ascontiguousarray(I["ctx"][b])
    m["cc"] = np.ascontiguousarray(np.stack([I["c"][b], I["c_ctx"]], axis=0))
    return m


_CACHE = {}


def build_program():
    nc = bass.Bass("TRN2", target_bir_lowering=False)
    k = K(nc)
    k.setup()
    for i in range(4):
        k.layer(i)
    P = k.P
    for j in range(8):
        P.ld("sp", k.out[j * 1024:(j + 1) * 1024, :], k.XS[256 + j * 1024:256 + (j + 1) * 1024, :], R=[], W=[Res("u")])
    P.barrier()
    P.emit()
    P.close()
    return nc, k


def kernel(**inputs):
    from concourse.bass_utils import run_bass_kernel_spmd
    I = {k_: np.asarray(v) for k_, v in inputs.items()}
    if "prog" not in _CACHE:
        _CACHE["prog"] = build_program()
    nc, k = _CACHE["prog"]
    S = prep_shared(I)
    n = 8
    in_maps = [prep_core(I, S, b) for b in range(n)]
    res = run_bass_kernel_spmd(nc, in_maps, core_ids=list(range(n)))
    return np.stack([np.asarray(r["out"]) for r in res.results], axis=0).astype(np.float32)
```

```python
from contextlib import ExitStack
import numpy as np
import concourse.bass as bass
import concourse.mybir as mybir

F32 = mybir.dt.float32
BF16 = mybir.dt.bfloat16
I32 = mybir.dt.int32
U32 = mybir.dt.uint32
ALU = mybir.AluOpType
AF = mybir.ActivationFunctionType
AX = mybir.AxisListType

COMPUTE = ("pe", "act", "dve", "pool")
EPOCH = 30000
DMA_EPOCH = 2000


class Res:
    __slots__ = ("name", "last_w", "readers")

    def __init__(self, name):
        self.name = name
        self.last_w = None
        self.readers = {}


class Tile:
    def __init__(self, name, t):
        self.name = name
        self.t = t
        self._res = {}

    def r(self, key=None):
        x = self._res.get(key)
        if x is None:
            x = self._res[key] = Res(f"{self.name}:{key}")
        return x

    def __getitem__(self, k):
        return self.t[k]


class Prog:
    def __init__(self, nc, n_dma_sems=16):
        self.nc = nc
        self.es = ExitStack()
        self.streams = {e: [] for e in ("pe", "act", "dve", "pool", "sp")}
        self.cnt = {e: 0 for e in COMPUTE}
        self.waited = {e: {} for e in self.streams}
        self.sems = {}
        self.n_dma_sems = n_dma_sems
        self.dma_rr = {q: 0 for q in ("sp", "pool", "act")}
        self.dma_cnt = {}
        self.n_inst = 0
        self.scopes = []
        self._uid = 0

    def sbuf(self, name, shape, dt):
        t = self.es.enter_context(self.nc.sbuf_tensor(name, list(shape), dt))
        return Tile(name, t)

    def psum(self, name, shape, dt=F32):
        t = self.es.enter_context(self.nc.psum_tensor(name, list(shape), dt))
        return Tile(name, t)

    def dram(self, name, shape, dt, kind="Internal"):
        t = self.nc.dram_tensor(name, list(shape), dt, kind=kind)
        return Tile(name, t.ap())

    def _sem(self, key):
        s = self.sems.get(key)
        if s is None:
            s = self.sems[key] = self.es.enter_context(
                self.nc.semaphore("s_" + "_".join(str(k) for k in key)))
        return s

    def _deps(self, eng, R, W):
        deps = {}

        def add(tok, kind):
            if tok is None:
                return
            semkey, val, teng = tok
            if teng == eng and eng in COMPUTE:
                if eng == "pe" or kind != "raw":
                    return
            if deps.get(semkey, 0) < val:
                deps[semkey] = val

        for r in R:
            add(r.last_w, "raw")
        for w in W:
            add(w.last_w, "waw")
            for sk, (v, e) in w.readers.items():
                add((sk, v, e), "war")
        return deps

    def _emit_waits(self, eng, deps):
        wd = self.waited[eng]
        st = self.streams[eng]
        for semkey, val in deps.items():
            if wd.get(semkey, 0) >= val:
                continue
            wd[semkey] = val
            st.append(("wait", self._sem(semkey), val))

    def _commit(self, tok, R, W):
        semkey, val, eng = tok
        for r in R:
            cur = r.readers.get(semkey)
            if cur is None or cur[0] < val:
                r.readers[semkey] = (val, eng)
        for w in W:
            w.last_w = tok
            w.readers = {}

    @staticmethod
    def _resl(xs):
        out = []
        for x in xs:
            out.append(x.r() if isinstance(x, Tile) else x)
        return out

    def op(self, eng, fn, R=(), W=()):
        R = self._resl(R)
        W = self._resl(W)
        self._emit_waits(eng, self._deps(eng, R, W))
        i = self.cnt[eng]
        self.cnt[eng] = i + 1
        semkey = ("e", eng, i // EPOCH)
        val = i % EPOCH + 1
        tok = (semkey, val, eng)
        self.streams[eng].append(("op", fn, self._sem(semkey), 1))
        self._commit(tok, R, W)
        self.n_inst += 1
        return tok

    def dma(self, q, fn, R=(), W=()):
        R = self._resl(R)
        W = self._resl(W)
        s = self.dma_rr[q]
        self.dma_rr[q] = (s + 1) % self.n_dma_sems
        base = (q, s)
        m = self.dma_cnt.get(base, 0) + 1
        self.dma_cnt[base] = m

        def key(mi):
            ep = (mi - 1) // DMA_EPOCH
            return ("d", q, s, ep), 16 * ((mi - 1) % DMA_EPOCH + 1)

        deps = self._deps("dma:" + q, R, W)
        if m > 1:
            pk, pv = key(m - 1)
            if deps.get(pk, 0) < pv:
                deps[pk] = pv
        self._emit_waits(q, deps)
        semkey, val = key(m)
        tok = (semkey, val, "dma:" + q)
        self.streams[q].append(("op", fn, self._sem(semkey), 16))
        self._commit(tok, R, W)
        self.n_inst += 1
        return tok

    def final_wait(self, eng, toks):
        deps = {}
        for semkey, val, _ in toks:
            if deps.get(semkey, 0) < val:
                deps[semkey] = val
        self._emit_waits(eng, deps)

    def emit(self):
        nc = self.nc
        streams = self.streams

        def run(e, items):
            for it in items:
                if it[0] == "wait":
                    e.wait_ge(it[1], it[2])
                else:
                    it[1](e).then_inc(it[2], it[3])

        with nc.Block() as block:
            @block.tensor
            def _(e):
                run(e, streams["pe"])

            @block.scalar
            def _(e):
                run(e, streams["act"])

            @block.vector
            def _(e):
                run(e, streams["dve"])

            @block.gpsimd
            def _(e):
                run(e, streams["pool"])

            @block.sync
            def _(e):
                run(e, streams["sp"])

    def close(self):
        self.es.close()


def _push(self):
    self.scopes.append(ExitStack())


def _pop(self):
    self.barrier()
    self.scopes.pop().close()


def _sbuf(self, name, shape, dt):
    st = self.scopes[-1] if self.scopes else self.es
    self._uid += 1
    t = st.enter_context(self.nc.sbuf_tensor(f"{name}_{self._uid}", list(shape), dt))
    return Tile(name, t)


def _psum(self, name, shape, dt=F32):
    st = self.scopes[-1] if self.scopes else self.es
    self._uid += 1
    t = st.enter_context(self.nc.psum_tensor(f"{name}_{self._uid}", list(shape), dt))
    return Tile(name, t)


def _barrier(self):
    toks = {}
    for eng in COMPUTE:
        i = self.cnt[eng]
        if i > 0:
            toks[("e", eng, (i - 1) // EPOCH)] = (i - 1) % EPOCH + 1
    for (q, s), m in self.dma_cnt.items():
        toks[("d", q, s, (m - 1) // DMA_EPOCH)] = 16 * ((m - 1) % DMA_EPOCH + 1)
    for eng in self.streams:
        self._emit_waits(eng, dict(toks))


def _mm(self, out, lhsT, rhs, start=True, stop=True, R=(), W=()):
    return self.op("pe", lambda e: e.matmul(out, lhsT, rhs, start=start, stop=stop), R, W)


def _tr(self, out, in_, ident, R=(), W=()):
    return self.op("pe", lambda e: e.transpose(out, in_, ident), R, W)


def _act(self, out, in_, func, R=(), W=(), **kw):
    return self.op("act", lambda e: e.activation(out, in_, func, **kw), R, W)


def _tsc(self, eng, out, in0, s1, s2, op0, op1=None, R=(), W=(), **kw):
    if op1 is None:
        return self.op(eng, lambda e: e.tensor_scalar(out, in0, s1, None, op0, **kw), R, W)
    return self.op(eng, lambda e: e.tensor_scalar(out, in0, s1, s2, op0, op1, **kw), R, W)


def _tt(self, eng, out, in0, in1, op, R=(), W=()):
    return self.op(eng, lambda e: e.tensor_tensor(out, in0, in1, op), R, W)


def _stt(self, eng, out, in0, scalar, in1, op0, op1, R=(), W=()):
    return self.op(eng, lambda e: e.scalar_tensor_tensor(out, in0, scalar, in1, op0, op1), R, W)


def _cp(self, eng, out, in_, R=(), W=()):
    if eng == "act":
        return self.op("act", lambda e: e.copy(out, in_), R, W)
    return self.op(eng, lambda e: e.tensor_copy(out, in_), R, W)


def _red(self, eng, out, in_, op, R=(), W=(), axis=None):
    ax = AX.X if axis is None else axis
    return self.op(eng, lambda e: e.tensor_reduce(out, in_, ax, op), R, W)


def _memset(self, eng, out, val, W=()):
    return self.op(eng, lambda e: e.memset(out, val), (), W)


def _ld(self, q, out, in_, R=(), W=(), **kw):
    kw.setdefault("allow_slow_non_contiguous", True)
    return self.dma(q, lambda e: e.dma_start(out=out, in_=in_, **kw), R, W)


def _gather(self, out, in_, idx_ap, R=(), W=()):
    return self.dma("pool", lambda e: e.indirect_dma_start(
        out=out, out_offset=None, in_=in_,
        in_offset=bass.IndirectOffsetOnAxis(ap=idx_ap, axis=0)), R, W)


def _scatter(self, out, in_, idx_ap, R=(), W=(), add=False, bound=None):
    if add:
        def f(e):
            try:
                return e.indirect_dma_start(
                    out=out, out_offset=bass.IndirectOffsetOnAxis(ap=idx_ap, axis=0), in_=in_, in_offset=None,
                    compute_op=ALU.add, oob_is_err=True)
            except Exception:
                print("SCATTER-ADD FAIL", out, in_, idx_ap, bound)
                raise
        return self.dma("pool", f, R, W)
    return self.dma("pool", lambda e: e.indirect_dma_start(
        out=out, out_offset=bass.IndirectOffsetOnAxis(ap=idx_ap, axis=0), in_=in_, in_offset=None), R, W)


def _recip(self, out, in_, R=(), W=()):
    return self.op("dve", lambda e: e.reciprocal(out, in_), R, W)


def _get_bound_reg(self, e, bound):
    if not hasattr(self, "_bregs"):
        self._bregs = {}
    r = self._bregs.get(bound)
    if r is None:
        r = self._bregs[bound] = e.to_reg(bound)
    return r


Prog.get_bound_reg = _get_bound_reg
Prog.recip = _recip
Prog.push = _push
Prog.pop = _pop
Prog.sbuf = _sbuf
Prog.psum = _psum
Prog.barrier = _barrier
Prog.mm = _mm
Prog.tr = _tr
Prog.act = _act
Prog.tsc = _tsc
Prog.tt = _tt
Prog.stt = _stt
Prog.cp = _cp
Prog.red = _red
Prog.memset = _memset
Prog.ld = _ld
Prog.gather = _gather
Prog.scatter = _scatter


from types import SimpleNamespace as NS

TA = 8448
NT = 66
D = 1024
NE = 16
EPS = 1e-6
import os
TAIL_ENG = os.environ.get('TAIL_ENG', 'dve')
ATTN_V2_ALL = bool(int(os.environ.get('ATTN_V2_ALL', '1')))


class K:
    def __init__(self, nc, dbg=()):
        self.nc = nc
        self.P = P = Prog(nc)
        self.dbg = set(dbg)
        din = lambda n, s, dt=F32: P.dram(n, s, dt, kind="ExternalInput")
        self.x = din("x", [8192, D])
        self.ctx = din("ctx", [256, D])
        self.cc = din("cc", [2, D])
        self.w_mod = din("w_mod", [4, D, 6144])
        self.b_mod = din("b_mod", [4, 6144])
        self.g_mix = din("g_mix", [4, D])
        self.g_ffn = din("g_ffn", [4, D])
        self.mla_w_in = din("mla_w_in", [2, D, 512])
        self.mla_w_uq = din("mla_w_uq", [2, 256, 2048])
        self.mla_w_uk = din("mla_w_uk", [2, 128, 1024])
        self.mla_w_uv = din("mla_w_uv", [2, 128, 1024])
        self.mla_w_o = din("mla_w_o", [2, D, D])
        self.mla_gc = din("mla_gc", [2, 128, 8])
        self.rope_tab = din("rope_tab", [128, TA])
        self.cst = din("cst", [128, 1024])
        self.moe_wr = din("moe_wr", [4, D, NE])
        self.moe_wg = din("moe_wg", [4, NE, D, D])
        self.moe_wu = din("moe_wu", [4, NE, D, D])
        self.moe_wd = din("moe_wd", [4, NE, D, D])
        self.tokid = din("tokid", [128, NT], I32)
        self.ssd_w_in = din("ssd_w_in", [D, 5184])
        self.ssd_conv_w = din("ssd_conv_w", [3, 3072])
        self.ssd_conv_b = din("ssd_conv_b", [3072])
        self.ssd_dt_bias = din("ssd_dt_bias", [64])
        self.ssd_a_log = din("ssd_a_log", [2, 32])
        self.ssd_d = din("ssd_d", [32])
        self.ssd_g_norm = din("ssd_g_norm", [2048])
        self.ssd_w_out = din("ssd_w_out", [2048, D])
        self.cst2 = din("cst2", [128, 640])
        self.fnet_w_o = din("fnet_w_o", [D, D])
        self.fcst = din("fcst", [128, 2176])
        self.fM = din("fM", [128, 16384])
        self.tokid_f = din("tokid_f", [128, NT], I32)
        self.Yd = self.dscr("Yd", [2, TA, D], BF16)
        self.Zd = self.dscr("Zd", [2, 64, 128, D], BF16)
        self.xs_d = self.dscr("xs_d", [TA, 2048], BF16)
        self.B_d = self.dscr("B_d", [TA, 512], BF16)
        self.BT_d = self.dscr("BT_d", [512, TA], BF16)
        self.CT_d = self.dscr("CT_d", [512, TA], BF16)
        self.zs_d = self.dscr("zs_d", [TA, 2048], BF16)
        self.dt_d = self.dscr("dt_d", [TA, 64], F32)
        self.yf_d = self.dscr("yf_d", [TA, 2048], F32)
        self.yb_d = self.dscr("yb_d", [TA, 2048], F32)
        self.out = P.dram("out", [8192, D], F32, kind="ExternalOutput")
        self.XS = self.dscr("XS", [TA, D], F32)
        self.modd = self.dscr("modd", [4, 2, 6144], F32)
        self.hT = self.dscr("hT", [D, TA], BF16)
        self.QT = self.dscr("QT", [8, 192, TA], BF16)
        self.KT = self.dscr("KT", [8, 192, TA], BF16)
        self.V = self.dscr("V", [TA, D], BF16)
        self.OT = self.dscr("OT", [D, TA], BF16)
        self.hrow = self.dscr("hrow", [TA, D], BF16)
        self.affrow = self.dscr("affrow", [TA, NE], F32)
        self.idxd = self.dscr("idxd", [NE * 1280, 1], I32)
        self.ident = P.sbuf("ident", [128, 128], BF16)
        self.ones = P.sbuf("ones", [128, 128], BF16)
        self.fold = P.sbuf("fold", [128, 64], BF16)
        self.identf = P.sbuf("identf", [128, 128], F32)
        P.ld("pool", self.ident[:], self.cst[:, 0:128], R=[self.cst], W=[self.ident])
        P.ld("pool", self.ones[:], self.cst[:, 128:256], R=[self.cst], W=[self.ones])
        P.ld("pool", self.fold[:], self.cst[:, 256:320], R=[self.cst], W=[self.fold])
        P.ld("sp", self.identf[:], self.cst[:, 0:128], R=[self.cst], W=[self.identf])
        self.ustrict = P.sbuf("ustrict", [128, 128], BF16)
        P.ld("pool", self.ustrict[:], self.cst[:, 320:448], R=[self.cst], W=[self.ustrict])
        self.dumpc = P.sbuf("dumpc", [128, 1], F32)
        P.ld("sp", self.dumpc[:], self.cst[:, 448:449], R=[self.cst], W=[self.dumpc])
        self.eoffs = P.sbuf("eoffs", [128, NE], F32)
        P.ld("sp", self.eoffs[:], self.cst[:, 449:449 + NE], R=[self.cst], W=[self.eoffs])
        self.tokid_sb = P.sbuf("tokid_sb", [128, NT], I32)
        P.ld("sp", self.tokid_sb[:], self.tokid[:, :], R=[self.tokid], W=[self.tokid_sb])
        self.tokid_f_sb = P.sbuf("tokid_f_sb", [128, NT], I32)
        P.ld("sp", self.tokid_f_sb[:], self.tokid_f[:, :], R=[self.tokid_f], W=[self.tokid_f_sb])

    def dscr(self, name, shape, dt):
        kind = "ExternalOutput" if name in self.dbg else "Internal"
        return self.P.dram(name, shape, dt, kind=kind)

    def setup(self):
        P = self.P
        P.ld("sp", self.XS[0:256, :], self.ctx[:, :], R=[self.ctx], W=[self.XS])
        for j in range(8):
            P.ld("sp", self.XS[256 + j * 1024:256 + (j + 1) * 1024, :], self.x[j * 1024:(j + 1) * 1024, :],
                 R=[self.x], W=[self.XS])
        P.push()
        ccT = P.sbuf("ccT", [128, 8, 2], F32)
        ccs = P.sbuf("ccs", [128, 8, 2], F32)
        for m in range(2):
            P.ld("sp", ccT[:, :, m], self.cc[m].rearrange("(k p) -> p k", p=128), R=[self.cc], W=[ccT],
                 allow_slow_non_contiguous=True)
        P.act(ccs[:], ccT[:], AF.Silu, R=[ccT], W=[ccs])
        wms = [P.sbuf("wm", [128, 8, 512], F32) for _ in range(2)]
        bms = [P.sbuf("bm", [2, 512], F32) for _ in range(2)]
        mrs = [P.sbuf("mr", [2, 512], F32) for _ in range(2)]
        pss = [P.psum("psm", [2, 512]) for _ in range(2)]
        n = 0
        for i in range(4):
            for nb in range(12):
                wm, bm, mr, ps = wms[n % 2], bms[n % 2], mrs[n % 2], pss[n % 2]
                n += 1
                sl = slice(nb * 512, (nb + 1) * 512)
                P.ld("sp", wm[:], self.w_mod[i, :, sl].rearrange("(k p) n -> p k n", p=128), R=[self.w_mod], W=[wm])
                P.ld("sp", bm[:], self.b_mod[i:i + 1, sl].to_broadcast([2, 512]), R=[self.b_mod], W=[bm])
                for k in range(8):
                    P.mm(ps[:], ccs[:, k, :], wm[:, k, :], start=(k == 0), stop=(k == 7), R=[ccs, wm], W=[ps])
                P.tt("dve", mr[:], ps[:], bm[:], ALU.add, R=[ps, bm], W=[mr])
                P.ld("pool", self.modd[i, :, sl], mr[:], R=[mr], W=[self.modd])
        P.pop()

    def layer_consts(self, i):
        P = self.P
        L = NS()
        L.i = i
        modd = self.modd
        L.modcol = P.sbuf("modcol", [128, 2, 6, 8], F32)
        for m in range(2):
            P.ld("sp", L.modcol[:, m], modd[i, m].rearrange("(s k p) -> p s k", s=6, p=128), R=[modd], W=[L.modcol],
                 allow_slow_non_contiguous=True)
        gcol = P.sbuf("gcol", [128, 8], F32)
        P.ld("sp", gcol[:], self.g_mix[i].rearrange("(k p) -> p k", p=128), R=[self.g_mix], W=[gcol],
             allow_slow_non_contiguous=True)
        L.Ga = P.sbuf("Ga", [128, 2, 8], F32)
        for m in range(2):
            P.stt("dve", L.Ga[:, m], L.modcol[:, m, 1], 1.0, gcol[:], ALU.add, ALU.mult, R=[L.modcol, gcol], W=[L.Ga])
        def bc(name, src_ap, R):
            t = P.sbuf(name, [128, D], F32)
            P.ld("sp", t[:], src_ap.partition_broadcast(128), R=R, W=[t])
            return t
        L.gtf = [bc("gtf", modd[i, m, 5120:6144], [modd]) for m in range(2)]
        return L

    def tail_consts(self, L):
        P = self.P
        i = L.i
        modd = self.modd

        def bc(name, src_ap):
            t = P.sbuf(name, [128, D], F32)
            P.ld("sp", t[:], src_ap.partition_broadcast(128), R=[], W=[t])
            return t
        L.gta = [bc("gta", modd[i, m, 2048:3072]) for m in range(2)]
        L.Sf = [bc("Sf", modd[i, m, 3072:4096]) for m in range(2)]
        gffn = bc("gffn", self.g_ffn[i])
        L.Gf = []
        for m in range(2):
            t = bc("Gf", modd[i, m, 4096:5120])
            P.stt("dve", t[:], t[:], 1.0, gffn[:], ALU.add, ALU.mult, R=[t, gffn], W=[t])
            L.Gf.append(t)

    def rms_rstd(self, xt, sq_junk, ss, rstd, n):
        P = self.P
        P.act(sq_junk[:], xt[:], AF.Square, R=[xt], W=[sq_junk, ss], accum_out=ss[:])
        P.act(rstd[:], ss[:], AF.Sqrt, R=[ss], W=[rstd], scale=1.0 / n, bias=EPS)
        P.recip(rstd[:], rstd[:], R=[rstd], W=[rstd])

    def phase_A(self, L):
        P = self.P
        P.push()
        NB = 2
        xts = [P.sbuf("xt", [128, D], F32) for _ in range(NB)]
        xns = [P.sbuf("xn", [128, D], BF16) for _ in range(NB)]
        junk = P.sbuf("junk", [128, D], BF16)
        sss = [P.sbuf("ss", [128, 1], F32) for _ in range(NB)]
        rss = [P.sbuf("rs", [128, 1], F32) for _ in range(NB)]
        pts = [P.psum("pt", [128, 8, 128], BF16) for _ in range(NB)]
        hts = [P.sbuf("ht", [128, 8, 128], BF16) for _ in range(NB)]
        tmp = [P.sbuf("tmpA", [128, 8, 128], F32) for _ in range(NB)]
        for i in range(NT):
            b = i % NB
            m = 1 if i < 2 else 0
            xt, xn, ss, rs, pt, ht, tp = xts[b], xns[b], sss[b], rss[b], pts[b], hts[b], tmp[b]
            P.ld("sp", xt[:], self.XS[i * 128:(i + 1) * 128, :], R=[self.XS], W=[xt])
            self.rms_rstd(xt, junk, ss, rs, D)
            P.act(xn[:], xt[:], AF.Copy, R=[xt, rs], W=[xn], scale=rs[:])
            for k in range(8):
                P.tr(pt[:, k, :], xn[:, k * 128:(k + 1) * 128], self.ident[:], R=[xn, self.ident], W=[pt])
            P.tt("dve", tp[:], pt[:], L.Ga[:, m].unsqueeze(2).to_broadcast([128, 8, 128]), ALU.mult, R=[pt, L.Ga], W=[tp])
            P.tt("dve", ht[:], tp[:], L.modcol[:, m, 0].unsqueeze(2).to_broadcast([128, 8, 128]), ALU.add,
                 R=[tp, L.modcol], W=[ht])
            P.ld("pool", self.hT[:, i * 128:(i + 1) * 128].rearrange("(k p) t -> p k t", p=128), ht[:], R=[ht], W=[self.hT])
        P.pop()


    def cast_load(self, name, shape, src_ap, R):
        t = self.P.sbuf(name, shape, BF16)
        self.P.ld("pool", t[:], src_ap, R=R, W=[t])
        return t

    def mla_proj(self, L, j, need_ctx):
        P = self.P
        P.push()
        w_in = self.cast_load("w_in", [128, 8, 512], self.mla_w_in[j].rearrange("(k p) n -> p k n", p=128), [self.mla_w_in])
        w_uq = self.cast_load("w_uq", [128, 2, 2048], self.mla_w_uq[j].rearrange("(k p) n -> p k n", p=128), [self.mla_w_uq])
        w_uk = self.cast_load("w_uk", [128, 1024], self.mla_w_uk[j], [self.mla_w_uk])
        w_uv = self.cast_load("w_uv", [128, 1024], self.mla_w_uv[j], [self.mla_w_uv])
        gc = P.sbuf("gc", [128, 8], F32)
        gd = P.sbuf("gd", [128, 8], F32)
        P.ld("sp", gc[:], self.mla_gc[j], R=[self.mla_gc], W=[gc])
        P.tsc("dve", gd[:, 0:2], gc[:, 0:2], 16.0, None, ALU.mult, R=[gc], W=[gd])
        P.tsc("dve", gd[:, 2:3], gc[:, 2:3], float(np.sqrt(128.0)), None, ALU.mult, R=[gc], W=[gd])
        P.tsc("dve", gd[:, 3:5], gc[:, 3:5], 1.0, None, ALU.mult, R=[gc], W=[gd])
        P.tsc("dve", gd[:, 5:7], gc[:, 5:7], float(np.sqrt(192.0)), None, ALU.mult, R=[gc], W=[gd])
        tab = P.sbuf("tab", [128, TA], F32)
        P.ld("sp", tab[:], self.rope_tab[:, :], R=[self.rope_tab], W=[tab])
        ones, fold = self.ones, self.fold
        NB = 512
        hTbs = [P.sbuf("hTb", [128, 8, NB], BF16) for _ in range(2)]
        pbs = [P.psum("pb", [128, NB]) for _ in range(6)]
        psv = P.psum("psv", [128, 1024])
        cyc = {"pb": 0}

        def nps():
            t = pbs[cyc["pb"] % len(pbs)]
            cyc["pb"] += 1
            return t

        def rot(name, shape, dt, n=2):
            ts = [P.sbuf(name, shape, dt) for _ in range(n)]
            st = {"i": 0}

            def f():
                t = ts[st["i"] % n]
                st["i"] += 1
                return t
            return f
        sq_n = rot("sq_n", [128, NB], BF16, 3)
        sq_r = rot("sq_r", [64, NB], BF16, 2)
        rs_t = rot("rs_t", [128, NB], F32, 3)
        o_n = rot("o_n", [128, NB], BF16, 3)
        o_r = rot("o_r", [64, NB], BF16, 3)
        rt_t = rot("rt_t", [128, NB], BF16, 2)
        vt_t = rot("vt_t", [128, 1024], BF16, 2)
        qan = P.sbuf("qan", [128, 2, NB], BF16)
        kvan = P.sbuf("kvan", [128, NB], BF16)
        sqkr = P.sbuf("sqkr", [64, NB], BF16)
        krr = P.sbuf("krr", [64, NB], F32)

        def rstd_from(ss_ps, n, nb):
            rs = rs_t()
            P.act(rs[:, :nb], ss_ps[:, :nb], AF.Sqrt, R=[ss_ps], W=[rs], bias=float(n * EPS))
            P.recip(rs[:, :nb], rs[:, :nb], R=[rs], W=[rs])
            return rs

        blocks = [(0, 256)] + [(256 + NB * b, NB) for b in range(16)]
        for bi, (t0, nb) in enumerate(blocks):
            is_ctx = bi == 0
            hTb = hTbs[bi % 2]
            P.ld("sp", hTb[:, :, :nb], self.hT[:, t0:t0 + nb].rearrange("(k p) t -> p k t", p=128), R=[self.hT], W=[hTb])

            def proj_a(m):
                ps = nps()
                for k in range(8):
                    P.mm(ps[:, :nb], w_in[:, k, m * 128:(m + 1) * 128], hTb[:, k, :nb], start=(k == 0), stop=(k == 7),
                         R=[w_in, hTb], W=[ps])
                return ps
            psq = [proj_a(0), proj_a(1)]
            ss = nps()
            for m in range(2):
                sq = sq_n()
                P.act(sq[:, :nb], psq[m][:, :nb], AF.Square, R=[psq[m]], W=[sq])
                P.mm(ss[:, :nb], ones[:], sq[:, :nb], start=(m == 0), stop=(m == 1), R=[ones, sq], W=[ss])
            rs = rstd_from(ss, 256, nb)
            for m in range(2):
                P.stt("dve", qan[:, m, :nb], psq[m][:, :nb], gd[:, m:m + 1], rs[:, :nb], ALU.mult, ALU.mult,
                      R=[psq[m], gd, rs], W=[qan])
            pk = proj_a(2)
            sq = sq_n()
            P.act(sq[:, :nb], pk[:, :nb], AF.Square, R=[pk], W=[sq])
            ss = nps()
            P.mm(ss[:, :nb], ones[:], sq[:, :nb], R=[ones, sq], W=[ss])
            rs = rstd_from(ss, 128, nb)
            P.stt("dve", kvan[:, :nb], pk[:, :nb], gd[:, 2:3], rs[:, :nb], ALU.mult, ALU.mult, R=[pk, gd, rs], W=[kvan])
            pr = proj_a(3)
            P.act(sqkr[:, :nb], pr[0:64, :nb], AF.Square, R=[pr], W=[sqkr])
            rt = rt_t()
            P.stt("dve", rt[:, :nb], pr[:, :nb], gd[:, 6:7], tab[:, t0:t0 + nb], ALU.mult, ALU.mult, R=[pr, gd, tab], W=[rt])
            pf = nps()
            P.mm(pf[0:64, :nb], fold[:], rt[:, :nb], R=[fold, rt], W=[pf])
            P.cp("act", krr[:, :nb], pf[0:64, :nb], R=[pf], W=[krr])
            for h in range(8):
                pk = nps()
                P.mm(pk[:, :nb], w_uk[:, h * 128:(h + 1) * 128], kvan[:, :nb], R=[w_uk, kvan], W=[pk])
                sq = sq_n()
                P.act(sq[:, :nb], pk[:, :nb], AF.Square, R=[pk], W=[sq])
                ss = nps()
                P.mm(ss[:, :nb], ones[:], sq[:, :nb], start=True, stop=False, R=[ones, sq], W=[ss])
                P.mm(ss[:, :nb], ones[0:64, :], sqkr[:, :nb], start=False, stop=True, R=[ones, sqkr], W=[ss])
                rs = rstd_from(ss, 192, nb)
                kn = o_n()
                P.stt("dve", kn[:, :nb], pk[:, :nb], gd[:, 5:6], rs[:, :nb], ALU.mult, ALU.mult, R=[pk, gd, rs], W=[kn])
                P.ld("pool", self.KT[h, 0:128, t0:t0 + nb], kn[:, :nb], R=[kn], W=[self.KT])
                kr = o_r()
                P.tt("dve", kr[:, :nb], krr[:, :nb], rs[0:64, :nb], ALU.mult, R=[krr, rs], W=[kr])
                P.ld("pool", self.KT[h, 128:192, t0:t0 + nb], kr[:, :nb], R=[kr], W=[self.KT])
                if is_ctx and not need_ctx:
                    continue
                pqn, pqr = nps(), nps()
                for kc in range(2):
                    P.mm(pqn[:, :nb], w_uq[:, kc, h * 256:h * 256 + 128], qan[:, kc, :nb], start=(kc == 0), stop=(kc == 1),
                         R=[w_uq, qan], W=[pqn])
                for kc in range(2):
                    P.mm(pqr[:, :nb], w_uq[:, kc, h * 256 + 128:h * 256 + 256], qan[:, kc, :nb], start=(kc == 0), stop=(kc == 1),
                         R=[w_uq, qan], W=[pqr])
                sq = sq_n()
                P.act(sq[:, :nb], pqn[:, :nb], AF.Square, R=[pqn], W=[sq])
                sr = sq_r()
                P.act(sr[:, :nb], pqr[0:64, :nb], AF.Square, R=[pqr], W=[sr])
                ss = nps()
                P.mm(ss[:, :nb], ones[:], sq[:, :nb], start=True, stop=False, R=[ones, sq], W=[ss])
                P.mm(ss[:, :nb], ones[0:64, :], sr[:, :nb], start=False, stop=True, R=[ones, sr], W=[ss])
                rs = rstd_from(ss, 192, nb)
                qn = o_n()
                P.stt("dve", qn[:, :nb], pqn[:, :nb], gd[:, 3:4], rs[:, :nb], ALU.mult, ALU.mult, R=[pqn, gd, rs], W=[qn])
                P.ld("pool", self.QT[h, 0:128, t0:t0 + nb], qn[:, :nb], R=[qn], W=[self.QT])
                rt = rt_t()
                P.stt("dve", rt[:, :nb], pqr[:, :nb], gd[:, 4:5], tab[:, t0:t0 + nb], ALU.mult, ALU.mult, R=[pqr, gd, tab], W=[rt])
                pf = nps()
                P.mm(pf[0:64, :nb], fold[:], rt[:, :nb], R=[fold, rt], W=[pf])
                qr = o_r()
                P.tt("dve", qr[:, :nb], pf[0:64, :nb], rs[0:64, :nb], ALU.mult, R=[pf, rs], W=[qr])
                P.ld("pool", self.QT[h, 128:192, t0:t0 + nb], qr[:, :nb], R=[qr], W=[self.QT])
            for tt in range(nb // 128):
                for n in range(2):
                    P.mm(psv[:, n * 512:(n + 1) * 512], kvan[:, tt * 128:(tt + 1) * 128], w_uv[:, n * 512:(n + 1) * 512],
                         R=[kvan, w_uv], W=[psv])
                vt = vt_t()
                P.cp("act", vt[:], psv[:], R=[psv], W=[vt])
                P.ld("pool", self.V[t0 + tt * 128:t0 + (tt + 1) * 128, :], vt[:], R=[vt], W=[self.V])
        P.pop()

    def mla_attn(self, L, need_ctx):
        P = self.P
        P.push()
        NQ = 512
        G = 2
        Kns = [P.sbuf("Kn", [128, TA], BF16) for _ in range(2)]
        Krs = [P.sbuf("Kr", [128, TA], BF16) for _ in range(2)]
        Vhs = [P.sbuf("Vh", [128, NT, 128], BF16) for _ in range(2)]
        Qns = [P.sbuf("Qn", [128, NQ], BF16) for _ in range(2)]
        Qrs = [P.sbuf("Qr", [128, NQ], BF16) for _ in range(2)]
        pts = [P.sbuf("pT", [128, G * NQ], BF16) for _ in range(3)]
        accs = [P.sbuf("accL", [128, G * NQ], F32) for _ in range(2)]
        rls = [P.sbuf("rl", [128, NQ], F32) for _ in range(2)]
        ots = [P.sbuf("ot", [128, NQ], BF16) for _ in range(2)]
        for t_ in Krs + Qrs:
            P.memset("dve", t_[64:128, :], 0.0, W=[t_])
        onesf = P.sbuf("onesf", [128, 128], F32)
        P.ld("sp", onesf[:], self.cst[:, 128:256], R=[], W=[onesf])
        pss = [P.psum("ps_s", [128, G * NQ]) for _ in range(3)]
        po = P.psum("ps_o", [128, NQ])
        pl = P.psum("ps_l", [128, NQ])
        qblocks = [(256 + NQ * b, NQ, list(range(NT))) for b in range(16)]
        if need_ctx:
            qblocks = [(0, 256, [0, 1])] + qblocks
        st = {"g": 0}
        n_q = 0
        for h in range(8):
            Kn, Kr, Vh = Kns[h % 2], Krs[h % 2], Vhs[h % 2]
            P.ld("sp", Kn[:], self.KT[h, 0:128, :], R=[], W=[Kn])
            P.ld("sp", Kr[0:64, :], self.KT[h, 128:192, :], R=[], W=[Kr])
            P.ld("sp", Vh[:], self.V[:, h * 128:(h + 1) * 128].rearrange("(n p) v -> p n v", p=128), R=[], W=[Vh])
            for (t0, nq, keys) in qblocks:
                Qn, Qr = Qns[n_q % 2], Qrs[n_q % 2]
                acc, rl, ot = accs[n_q % 2], rls[n_q % 2], ots[n_q % 2]
                n_q += 1
                P.ld("sp", Qn[:, :nq], self.QT[h, 0:128, t0:t0 + nq], R=[], W=[Qn])
                P.ld("sp", Qr[0:64, :nq], self.QT[h, 128:192, t0:t0 + nq], R=[], W=[Qr])
                groups = [keys[i:i + G] for i in range(0, len(keys), G)]
                ng = len(groups)
                g0 = st["g"]
                st["g"] += ng
                accv = acc[:].rearrange("p (g q) -> p g q", q=NQ)[:, :, :nq]

                def front(g):
                    ps, pt = pss[(g0 + g) % 3], pts[(g0 + g) % 3]
                    grp = groups[g]
                    for j, kt in enumerate(grp):
                        ks = slice(kt * 128, (kt + 1) * 128)
                        P.mm(ps[:, j * NQ:j * NQ + nq], Kn[:, ks], Qn[:, :nq], start=True, stop=False, R=[Kn, Qn], W=[ps])
                        P.mm(ps[:, j * NQ:j * NQ + nq], Kr[:, ks], Qr[:, :nq], start=False, stop=True, R=[Kr, Qr], W=[ps])
                    psv = ps[:].rearrange("p (g q) -> p g q", q=NQ)[:, :len(grp), :nq]
                    ptv = pt[:].rearrange("p (g q) -> p g q", q=NQ)[:, :len(grp), :nq]
                    P.act(ptv, psv, AF.Exp, R=[ps], W=[pt])
                    if g == 0:
                        P.cp("dve", accv, ptv, R=[pt], W=[acc])
                    else:
                        P.tt("dve", accv, accv, ptv, ALU.add, R=[acc, pt], W=[acc])

                def back(g):
                    pt = pts[(g0 + g) % 3]
                    grp = groups[g]
                    for j, kt in enumerate(grp):
                        P.mm(po[:, :nq], Vh[:, kt, :], pt[:, j * NQ:j * NQ + nq], start=(g == 0 and j == 0),
                             stop=(g == ng - 1 and j == len(grp) - 1), R=[Vh, pt], W=[po])
                for g in range(min(2, ng)):
                    front(g)
                for g in range(ng):
                    if g + 2 < ng:
                        front(g + 2)
                    back(g)
                for j in range(G):
                    P.mm(pl[:, :nq], onesf[:], acc[:, j * NQ:j * NQ + nq], start=(j == 0), stop=(j == G - 1), R=[onesf, acc], W=[pl])
                P.recip(rl[:, :nq], pl[:, :nq], R=[pl], W=[rl])
                P.tt("dve", ot[:, :nq], po[:, :nq], rl[:, :nq], ALU.mult, R=[po, rl], W=[ot])
                P.ld("pool", self.OT[h * 128:(h + 1) * 128, t0:t0 + nq], ot[:, :nq], R=[ot], W=[Res("u")])
        P.pop()

    def mla_attn_v1(self, L, need_ctx):
        P = self.P
        P.push()
        NQ = 512
        Kns = [P.sbuf("Kn", [128, TA], BF16) for _ in range(2)]
        Krs = [P.sbuf("Kr", [64, TA], BF16) for _ in range(2)]
        Vhs = [P.sbuf("Vh", [128, NT, 128], BF16) for _ in range(2)]
        Qns = [P.sbuf("Qn", [128, NQ], BF16) for _ in range(2)]
        Qrs = [P.sbuf("Qr", [64, NQ], BF16) for _ in range(2)]
        pts = [P.sbuf("pT", [128, NQ], BF16) for _ in range(3)]
        rls = [P.sbuf("rl", [128, NQ], F32) for _ in range(2)]
        ots = [P.sbuf("ot", [128, NQ], BF16) for _ in range(2)]
        pss = [P.psum("ps_s", [128, NQ]) for _ in range(3)]
        pos = [P.psum("ps_o", [128, NQ]) for _ in range(2)]
        pls = [P.psum("ps_l", [128, NQ]) for _ in range(2)]
        ones = self.ones
        qblocks = [(256 + NQ * b, NQ, list(range(NT))) for b in range(16)]
        if need_ctx:
            qblocks = [(0, 256, [0, 1])] + qblocks
        n_s = 0
        n_q = 0
        for h in range(8):
            Kn, Kr, Vh = Kns[h % 2], Krs[h % 2], Vhs[h % 2]
            P.ld("sp", Kn[:], self.KT[h, 0:128, :], R=[self.KT], W=[Kn])
            P.ld("sp", Kr[:], self.KT[h, 128:192, :], R=[self.KT], W=[Kr])
            P.ld("sp", Vh[:], self.V[:, h * 128:(h + 1) * 128].rearrange("(n p) v -> p n v", p=128), R=[self.V], W=[Vh])
            for (t0, nq, keys) in qblocks:
                Qn, Qr = Qns[n_q % 2], Qrs[n_q % 2]
                po, pl = pos[n_q % 2], pls[n_q % 2]
                rl, ot = rls[n_q % 2], ots[n_q % 2]
                n_q += 1
                P.ld("sp", Qn[:, :nq], self.QT[h, 0:128, t0:t0 + nq], R=[self.QT], W=[Qn])
                P.ld("sp", Qr[:, :nq], self.QT[h, 128:192, t0:t0 + nq], R=[self.QT], W=[Qr])
                for ki, kt in enumerate(keys):
                    ps, pt = pss[n_s % 3], pts[n_s % 3]
                    n_s += 1
                    ks = slice(kt * 128, (kt + 1) * 128)
                    P.mm(ps[:, :nq], Kn[:, ks], Qn[:, :nq], start=True, stop=False, R=[Kn, Qn], W=[ps])
                    P.mm(ps[:, :nq], Kr[:, ks], Qr[:, :nq], start=False, stop=True, R=[Kr, Qr], W=[ps])
                    P.act(pt[:, :nq], ps[:, :nq], AF.Exp, R=[ps], W=[pt])
                    first, last = ki == 0, ki == len(keys) - 1
                    P.mm(po[:, :nq], Vh[:, kt, :], pt[:, :nq], start=first, stop=last, R=[Vh, pt], W=[po])
                    P.mm(pl[:, :nq], ones[:], pt[:, :nq], start=first, stop=last, R=[ones, pt], W=[pl])
                P.recip(rl[:, :nq], pl[:, :nq], R=[pl], W=[rl])
                P.tt("dve", ot[:, :nq], po[:, :nq], rl[:, :nq], ALU.mult, R=[po, rl], W=[ot])
                P.ld("pool", self.OT[h * 128:(h + 1) * 128, t0:t0 + nq], ot[:, :nq], R=[ot], W=[self.OT])
        P.pop()


    def moe_state(self, L):
        P = self.P
        M = NS()
        M.logits = P.sbuf("logits", [128, NE, NT], F32)
        M.idxl = P.sbuf("idxl", [128, NE, 8], I32)
        M.idxc = P.sbuf("idxc", [32, NE], I32)
        M.wr = self.cast_load("wr", [128, 8, NE], self.moe_wr[L.i].rearrange("(k p) e -> p k e", p=128), [self.moe_wr])
        return M

    def tail_tiles(self, n_pt=2):
        P = self.P
        T = NS()
        T.n = 0
        T.xo = [P.sbuf("xo", [128, D], F32) for _ in range(2)]
        T.tmp = [P.sbuf("ttmp", [128, D], F32) for _ in range(2)]
        T.xn = [P.sbuf("txn", [128, D], F32) for _ in range(2)]
        T.junk = P.sbuf("tjunk", [128, D], BF16)
        T.ss = [P.sbuf("tss", [128, 1], F32) for _ in range(2)]
        T.rs = [P.sbuf("trs", [128, 1], F32) for _ in range(2)]
        T.hb = [P.sbuf("thb", [128, D], BF16) for _ in range(2)]
        T.pt = [P.psum("tpt", [128, 8, 128], BF16) for _ in range(n_pt)]
        T.hTt = [P.sbuf("thTt", [128, 8, 128], BF16) for _ in range(2)]
        T.plg = P.psum("tplg", [128, NE])
        return T

    def tail(self, L, M, T, i, ol_aps, ol_R, xs_rows=None, h_rows=None):
        P = self.P
        m = 1 if i < 2 else 0
        b = T.n % 2
        T.n += 1
        xo, tmp, xn, ss, rs, hb, pt, hTt = T.xo[b], T.tmp[b], T.xn[b], T.ss[b], T.rs[b], T.hb[b], T.pt[b % len(T.pt)], T.hTt[b]
        rows = slice(i * 128, (i + 1) * 128)
        xs_rows = self.XS[rows, :] if xs_rows is None else xs_rows
        h_rows = self.hrow[rows, :] if h_rows is None else h_rows
        P.ld("sp", xo[:], xs_rows, R=[], W=[xo])
        for ap, c0, w in ol_aps:
            P.tt("dve", tmp[:, c0:c0 + w], ap, L.gta[m][:, c0:c0 + w], ALU.mult, R=list(ol_R) + [L.gta[m]], W=[tmp])
        P.tt(TAIL_ENG, xn[:], tmp[:], xo[:], ALU.add, R=[tmp, xo], W=[xn])
        P.ld("pool", xs_rows, xn[:], R=[xn], W=[Res("u")])
        if L.skip_ctx_moe and i < 2:
            return
        P.act(T.junk[:], xn[:], AF.Square, R=[xn], W=[T.junk, ss], accum_out=ss[:])
        P.act(rs[:], ss[:], AF.Sqrt, R=[ss], W=[rs], scale=1.0 / D, bias=EPS)
        P.recip(rs[:], rs[:], R=[rs], W=[rs])
        P.stt("dve", tmp[:], xn[:], rs[:], L.Gf[m][:], ALU.mult, ALU.mult, R=[xn, rs, L.Gf[m]], W=[tmp])
        P.tt(TAIL_ENG, hb[:], tmp[:], L.Sf[m][:], ALU.add, R=[tmp, L.Sf[m]], W=[hb])
        P.ld("pool", h_rows, hb[:], R=[hb], W=[Res("u")])
        for k in range(8):
            P.tr(pt[:, k, :], hb[:, k * 128:(k + 1) * 128], self.ident[:], R=[hb, self.ident], W=[pt])
        P.cp("act", hTt[:], pt[:], R=[pt], W=[hTt])
        for k in range(8):
            P.mm(T.plg[:], hTt[:, k, :], M.wr[:, k, :], start=(k == 0), stop=(k == 7), R=[hTt, M.wr], W=[T.plg])
        P.cp("dve", M.logits[:, :, i], T.plg[:], R=[T.plg], W=[M.logits])

    def mla_out(self, L, M, j, need_ctx):
        P = self.P
        P.push()
        w_o = self.cast_load("w_o", [128, 8, D], self.mla_w_o[j].rearrange("(k p) n -> p k n", p=128), [self.mla_w_o])
        self.tail_consts(L)
        T = self.tail_tiles()
        OTs = [P.sbuf("OTt", [128, 8, 128], BF16) for _ in range(2)]
        pols = [P.psum("pol", [128, D]) for _ in range(2)]
        for i in range(NT):
            if i < 2 and not need_ctx:
                continue
            OTt, pol = OTs[i % 2], pols[i % 2]
            P.ld("sp", OTt[:], self.OT[:, i * 128:(i + 1) * 128].rearrange("(k p) t -> p k t", p=128), R=[], W=[OTt])
            for n in range(2):
                for k in range(8):
                    P.mm(pol[:, n * 512:(n + 1) * 512], OTt[:, k, :], w_o[:, k, n * 512:(n + 1) * 512],
                         start=(k == 0), stop=(k == 7), R=[OTt, w_o], W=[pol])
            self.tail(L, M, T, i, [(pol[:, 0:512], 0, 512), (pol[:, 512:1024], 512, 512)], [pol])
        P.pop()

    def moe_route(self, L, M):
        P = self.P
        i = L.i
        do_ctx = not L.skip_ctx_moe
        P.push()
        lg = M.logits
        lg_te = lg[:].rearrange("p e t -> p t e")
        aff = P.sbuf("aff", [128, NE, NT], F32)
        aff_te = aff[:].rearrange("p e t -> p t e")
        mx = P.sbuf("mx", [128, NT], F32)
        sm = P.sbuf("sm", [128, NT], F32)
        if not do_ctx:
            P.memset("dve", lg[:, :, 0:2], 0.0, W=[lg])
        P.red("dve", mx[:], lg_te, ALU.max, R=[lg], W=[mx])
        P.tt("dve", aff[:], lg[:], mx[:].unsqueeze(1).to_broadcast([128, NE, NT]), ALU.subtract, R=[lg, mx], W=[aff])
        P.act(aff[:], aff[:], AF.Exp, R=[aff], W=[aff])
        P.red("dve", sm[:], aff_te, ALU.add, R=[aff], W=[sm])
        P.recip(sm[:], sm[:], R=[sm], W=[sm])
        P.tt("dve", aff[:], aff[:], sm[:].unsqueeze(1).to_broadcast([128, NE, NT]), ALU.mult, R=[aff, sm], W=[aff])
        affc = P.sbuf("affc", [128, NT, NE], F32)
        P.cp("dve", affc[:], aff_te, R=[aff], W=[affc])
        affrow_v = self.affrow.t.rearrange("(t p) e -> p t e", p=128)
        aff_w = []
        if getattr(L, "fmap", False):
            pieces = [(affrow_v[:, 0:2, :], affc[:, 0:2, :])]
            av = self.affrow.t[256:, :].rearrange("(p q) e -> p q e", q=64)
            for q in range(4):
                pieces.append((av[:, q * 16:(q + 1) * 16, :], affc[:, 2 + q * 16:2 + (q + 1) * 16, :]))
        else:
            pieces = [(affrow_v[:, q * 11:(q + 1) * 11, :], affc[:, q * 11:(q + 1) * 11, :]) for q in range(6)]
        for dst, src in pieces:
            r_ = Res("affw")
            aff_w.append(r_)
            P.ld("pool", dst, src, R=[affc], W=[r_])
        tokid_sb = self.tokid_f_sb if getattr(L, "fmap", False) else self.tokid_sb
        lo = P.sbuf("lo", [128, 2, NE], F32)
        mid = P.sbuf("mid", [128, 2, NE], F32)
        cnt = P.sbuf("cnt", [128, 2, NE], F32)
        gew = P.sbuf("gew", [128, 2, NE], F32)
        cmp_ = P.sbuf("cmp", [128, NE, NT], BF16)
        cmp_f = cmp_[:].rearrange("p e t -> p (e t)")
        cps = P.psum("cps", [128, 1536])
        cps_v = cps[:, 0:NE * NT].rearrange("p (e t) -> p e t", t=NT)
        P.memset("dve", lo[:], 0.0, W=[lo])

        def count(thr):
            P.tt("dve", cmp_[:, :, 2:NT], aff[:, :, 2:NT], thr[:, 0].unsqueeze(2).to_broadcast([128, NE, NT - 2]), ALU.is_ge,
                 R=[aff, thr], W=[cmp_])
            P.tt("dve", cmp_[:, :, 0:2], aff[:, :, 0:2], thr[:, 1].unsqueeze(2).to_broadcast([128, NE, 2]), ALU.is_ge,
                 R=[aff, thr], W=[cmp_])
            for n, (c0, w) in enumerate(((0, 512), (512, 512), (1024, 32))):
                P.mm(cps[:, c0:c0 + w], self.ones[:], cmp_f[:, c0:c0 + w], R=[self.ones, cmp_], W=[cps])

        NIT = 26
        for it in range(NIT):
            w = 0.5 ** (it + 1)
            P.tsc("dve", mid[:], lo[:], w, None, ALU.add, R=[lo], W=[mid])
            count(mid)
            P.red("dve", cnt[:, 0], cps_v[:, :, 2:NT], ALU.add, R=[cps], W=[cnt])
            P.red("dve", cnt[:, 1], cps_v[:, :, 0:2], ALU.add, R=[cps], W=[cnt])
            P.tsc("dve", gew[:, 0], cnt[:, 0], 1023.5, w, ALU.is_ge, ALU.mult, R=[cnt], W=[gew])
            P.tsc("dve", gew[:, 1], cnt[:, 1], 31.5, w, ALU.is_ge, ALU.mult, R=[cnt], W=[gew])
            P.tt("dve", lo[:], lo[:], gew[:], ALU.add, R=[lo, gew], W=[lo])
        count(lo)
        sel = cmp_
        tot = P.sbuf("tot", [128, NE, NT], F32)
        P.cp("dve", tot[:], cps_v, R=[cps], W=[tot])
        sa = P.sbuf("sa", [128, NE, 64], F32)
        sb = P.sbuf("sb", [128, NE, 64], F32)
        P.cp("dve", sa[:], tot[:, :, 2:NT], R=[tot], W=[sa])
        cur, oth = sa, sb
        sh = 1
        while sh < 64:
            P.cp("dve", oth[:, :, 0:sh], cur[:, :, 0:sh], R=[cur], W=[oth])
            P.tt("dve", oth[:, :, sh:64], cur[:, :, sh:64], cur[:, :, 0:64 - sh], ALU.add, R=[cur], W=[oth])
            cur, oth = oth, cur
            sh *= 2
        offs = P.sbuf("offs", [128, NE, NT], F32)
        P.tt("dve", offs[:, :, 2:NT], cur[:], tot[:, :, 2:NT], ALU.subtract, R=[cur, tot], W=[offs])
        P.tsc("dve", offs[:, :, 2:NT], offs[:, :, 2:NT], 32.0, None, ALU.add, R=[offs], W=[offs])
        P.memset("dve", offs[:, :, 0:1], 0.0, W=[offs])
        P.cp("dve", offs[:, :, 1:2], tot[:, :, 0:1], R=[tot], W=[offs])
        ips = P.psum("ips", [128, 1536])
        for t in range(NT):
            P.mm(ips[:, t * NE:(t + 1) * NE], self.ustrict[:], sel[:, :, t], R=[self.ustrict, sel], W=[ips])
        ips_v = ips[:, 0:NT * NE].rearrange("p (t e) -> p e t", e=NE)
        slot = P.sbuf("slot", [128, NE, NT], F32)
        P.tt("dve", slot[:], ips_v, offs[:], ALU.add, R=[ips, offs], W=[slot])
        P.stt("dve", slot[:], slot[:], self.dumpc[:, 0:1], sel[:], ALU.subtract, ALU.mult, R=[slot, self.dumpc, sel], W=[slot])
        P.tsc("dve", slot[:], slot[:], self.dumpc[:, 0:1], None, ALU.add, R=[slot, self.dumpc], W=[slot])
        P.tt("dve", slot[:], slot[:], self.eoffs[:].unsqueeze(2).to_broadcast([128, NE, NT]), ALU.add, R=[slot, self.eoffs], W=[slot])
        smap = P.sbuf("smap", [128, NE, NT], I32)
        P.cp("dve", smap[:], slot[:], R=[slot], W=[smap])
        idx_res = []
        for t in range(NT):
            if t < 2 and not do_ctx:
                continue
            for e in range(NE):
                r_ = Res("idxw")
                idx_res.append(r_)
                P.scatter(self.idxd[:, :], tokid_sb[:, t:t + 1], smap[:, e, t:t + 1], R=[smap, tokid_sb], W=[r_])
        idxl, idxc = M.idxl, M.idxc
        P.ld("sp", idxl[:], self.idxd.t.rearrange("(e s) o -> e (s o)", e=NE)[:, 32:1056].rearrange("e (p k) -> p e k", k=8), R=idx_res, W=[idxl])
        if do_ctx:
            P.ld("sp", idxc[:], self.idxd.t.rearrange("(e s) o -> e (s o)", e=NE)[:, 0:32].rearrange("e p -> p e"), R=idx_res, W=[idxc],
                 allow_slow_non_contiguous=True)
        P.pop()

    def moe_experts(self, L, M):
        P = self.P
        i = L.i
        do_ctx = not L.skip_ctx_moe
        idxl, idxc = M.idxl, M.idxc
        aff_w = []
        P.push()
        NCH = 9 if do_ctx else 8
        NTOK = 1056 if do_ctx else 1024
        wsets = [[P.sbuf("wexp", [128, 8, D], BF16) for _ in range(3)] for _ in range(2)]
        xes = [P.sbuf("xe", [128, 9, D], BF16) for _ in range(1)]
        gas = [P.sbuf("ga", [128, 9, NE], F32) for _ in range(2)]
        xeT = P.sbuf("xeT", [128, 8, 1056], BF16)
        hid = P.sbuf("hid", [128, 8, 1056], BF16)
        sg = [P.sbuf("sg", [128, 512], F32) for _ in range(2)]
        ys = [P.sbuf("ys", [128, D], F32) for _ in range(2)]
        ptp = [P.psum("ptp", [128, 8, 128], BF16) for _ in range(1)]
        pbank = [P.psum("pbk", [128, 512]) for _ in range(6)]
        st = {"b": 0, "y": 0}

        def bank():
            t = pbank[st["b"] % len(pbank)]
            st["b"] += 1
            return t
        srcs = (self.moe_wg, self.moe_wu, self.moe_wd)

        def prefetch_w(e):
            ws = wsets[e % 2]
            for a in range(3):
                P.ld("pool", ws[a][:], srcs[a][i, e].rearrange("(k p) n -> p k n", p=128), R=[], W=[ws[a]])

        def prefetch(e):
            xe, ga = xes[0], gas[e % 2]
            for k in range(8):
                P.gather(xe[:, k, :], self.hrow[:, :], idxl[:, e, k:k + 1], R=[idxl], W=[xe])
                P.gather(ga[:, k, :], self.affrow[:, :], idxl[:, e, k:k + 1], R=[idxl] + aff_w, W=[ga])
            if do_ctx:
                P.gather(xe[0:32, 8, :], self.hrow[:, :], idxc[0:32, e:e + 1], R=[idxc], W=[xe])
                P.gather(ga[0:32, 8, :], self.affrow[:, :], idxc[0:32, e:e + 1], R=[idxc] + aff_w, W=[ga])

        grp = [[Res("scA") for _ in range(9)], [Res("scB") for _ in range(9)]]
        prefetch_w(0)
        prefetch(0)
        for e in range(NE):
            if e + 1 < NE:
                prefetch_w(e + 1)
            wg, wu, wd = wsets[e % 2]
            xe, ga = xes[0], gas[e % 2]
            for k in range(NCH):
                rows = 128 if k < 8 else 32
                pt = ptp[0]
                for dk in range(8):
                    P.tr(pt[:, dk, 0:rows], xe[0:rows, k, dk * 128:(dk + 1) * 128], self.ident[0:rows, 0:rows],
                         R=[xe, self.ident], W=[pt])
                P.cp("act", xeT[:, :, k * 128:k * 128 + rows], pt[:, :, 0:rows], R=[pt], W=[xeT])
            if e + 1 < NE:
                prefetch(e + 1)
            nblks = [(0, 512), (512, 512)] + ([(1024, 32)] if do_ctx else [])
            for f in range(8):
                fs = slice(f * 128, (f + 1) * 128)
                for (c0, w) in nblks:
                    pg, pu = bank(), bank()
                    for dk in range(8):
                        P.mm(pg[:, :w], wg[:, dk, fs], xeT[:, dk, c0:c0 + w], start=(dk == 0), stop=(dk == 7), R=[wg, xeT], W=[pg])
                    for dk in range(8):
                        P.mm(pu[:, :w], wu[:, dk, fs], xeT[:, dk, c0:c0 + w], start=(dk == 0), stop=(dk == 7), R=[wu, xeT], W=[pu])
                    s_ = sg[st["y"] % 2]
                    st["y"] += 1
                    P.act(s_[:, :w], pg[:, :w], AF.Silu, R=[pg], W=[s_])
                    P.tt("dve", hid[:, f, c0:c0 + w], s_[:, :w], pu[:, :w], ALU.mult, R=[s_, pu], W=[hid])
            for k in range(NCH):
                rows = 128 if k < 8 else 32
                m = 0 if k < 8 else 1
                y = ys[k % 2]
                for n in range(2):
                    py = bank()
                    for f in range(8):
                        P.mm(py[0:rows, :], hid[:, f, k * 128:k * 128 + rows], wd[:, f, n * 512:(n + 1) * 512],
                             start=(f == 0), stop=(f == 7), R=[hid, wd], W=[py])
                    P.stt("dve", y[0:rows, n * 512:(n + 1) * 512], py[0:rows, :], ga[0:rows, k, e:e + 1],
                          L.gtf[m][0:rows, n * 512:(n + 1) * 512], ALU.mult, ALU.mult, R=[py, ga, L.gtf[m]], W=[y])
                ia = idxl[:, e, k:k + 1] if k < 8 else idxc[0:32, e:e + 1]
                P.scatter(self.XS[:, :], y[0:rows, :], ia, R=[y, idxl, idxc] + grp[(e + 1) % 2], W=[grp[e % 2][k]],
                          add=True, bound=TA - 1)
        P.pop()


    def ssd_in(self, L):
        P = self.P
        P.push()
        w = P.sbuf("ssd_w", [128, 8, 5184], BF16)
        wsrc = self.ssd_w_in.t.rearrange("(k p) n -> p k n", p=128)
        for c0 in range(0, 5184, 1728):
            P.ld("pool", w[:, :, c0:c0 + 1728], wsrc[:, :, c0:c0 + 1728], R=[], W=[w])
        cw = P.sbuf("cw", [128, 24, 3], F32)
        for k in range(3):
            P.ld("sp", cw[:, :, k], self.ssd_conv_w[k].rearrange("(f p) -> p f", p=128), R=[], W=[cw])
        cb = P.sbuf("cb", [128, 24], F32)
        P.ld("sp", cb[:], self.ssd_conv_b.t.rearrange("(f p) -> p f", p=128), R=[], W=[cb])
        dtb = P.sbuf("dtb", [128, 64], F32)
        P.ld("sp", dtb[:], self.ssd_dt_bias.t.partition_broadcast(128), R=[], W=[dtb])
        hTws = [P.sbuf("hTw", [128, 8, 258], BF16) for _ in range(2)]
        cv = P.sbuf("cv", [128, 24, 256], BF16)
        accs = [P.sbuf("acc", [128, 256], F32) for _ in range(2)]
        xbs = [P.sbuf("xb_tm", [128, 2560], BF16) for _ in range(2)]
        zss = [P.sbuf("zs_tm", [128, 2048], BF16) for _ in range(2)]
        dtr = [P.sbuf("dtr", [128, 64], F32) for _ in range(2)]
        banks = [P.psum("sbk", [128, 512]) for _ in range(5)]
        ptrs = [P.psum("sptr", [128, 8, 128], BF16) for _ in range(2)]
        st = {"b": 0, "t": 0, "a": 0}

        def bank():
            t = banks[st["b"] % len(banks)]
            st["b"] += 1
            return t
        BTv = self.BT_d.t.rearrange("(g p) t -> p g t", p=128)
        CTv = self.CT_d.t.rearrange("(g p) t -> p g t", p=128)
        hTv = self.hT.t.rearrange("(k p) t -> p k t", p=128)
        wins = [(0, True, True)] + [(256 + 256 * q, q == 0, q == 31) for q in range(32)]
        for wi, (t0, lz, rz) in enumerate(wins):
            hTw = hTws[wi % 2]
            lo = t0 - (0 if lz else 1)
            hi = t0 + 256 + (0 if rz else 1)
            c_lo = 1 if lz else 0
            P.ld("sp", hTw[:, :, c_lo:c_lo + (hi - lo)], hTv[:, :, lo:hi], R=[], W=[hTw])
            if lz:
                P.memset("pool", hTw[:, :, 0:1], 0.0, W=[hTw])
            if rz:
                P.memset("pool", hTw[:, :, 257:258], 0.0, W=[hTw])
            for fc in range(24):
                ps = bank()
                for k in range(8):
                    P.mm(ps[:, 0:258], w[:, k, 2048 + fc * 128:2048 + (fc + 1) * 128], hTw[:, k, :], start=(k == 0), stop=(k == 7),
                         R=[w, hTw], W=[ps])
                acc = accs[st["a"] % 2]
                st["a"] += 1
                P.tsc("dve", acc[:], ps[:, 0:256], cw[:, fc, 0:1], None, ALU.mult, R=[ps, cw], W=[acc])
                P.stt("dve", acc[:], ps[:, 1:257], cw[:, fc, 1:2], acc[:], ALU.mult, ALU.add, R=[ps, cw, acc], W=[acc])
                P.stt("dve", acc[:], ps[:, 2:258], cw[:, fc, 2:3], acc[:], ALU.mult, ALU.add, R=[ps, cw, acc], W=[acc])
                P.act(cv[:, fc, :], acc[:], AF.Silu, R=[acc, cb], W=[cv], bias=cb[:, fc:fc + 1])
            P.ld("pool", BTv[:, :, t0:t0 + 256], cv[:, 16:20, :], R=[cv], W=[Res("u")])
            P.ld("pool", CTv[:, :, t0:t0 + 256], cv[:, 20:24, :], R=[cv], W=[Res("u")])
            for tt in range(2):
                rows = slice(t0 + tt * 128, t0 + (tt + 1) * 128)
                xb, zs, dr_ = xbs[tt], zss[tt], dtr[tt]
                for f0 in (0, 8, 16):
                    nf = min(8, 20 - f0)
                    ptr = ptrs[st["t"] % 2]
                    st["t"] += 1
                    for q in range(nf):
                        P.tr(ptr[:, q, :], cv[:, f0 + q, tt * 128:(tt + 1) * 128], self.ident[:], R=[cv, self.ident], W=[ptr])
                    P.cp("act", xb[:, f0 * 128:(f0 + nf) * 128], ptr[:, 0:nf, :], R=[ptr], W=[xb])
                P.ld("pool", self.xs_d[rows, :], xb[:, 0:2048], R=[xb], W=[Res("u")])
                P.ld("pool", self.B_d[rows, :], xb[:, 2048:2560], R=[xb], W=[Res("u")])
                for nb in range(4):
                    ps = bank()
                    for k in range(8):
                        P.mm(ps[:], hTw[:, k, 1 + tt * 128:1 + (tt + 1) * 128], w[:, k, nb * 512:(nb + 1) * 512], start=(k == 0), stop=(k == 7),
                             R=[w, hTw], W=[ps])
                    P.act(zs[:, nb * 512:(nb + 1) * 512], ps[:], AF.Silu, R=[ps], W=[zs])
                P.ld("pool", self.zs_d[rows, :], zs[:], R=[zs], W=[Res("u")])
                ps = bank()
                for k in range(8):
                    P.mm(ps[:, 0:64], hTw[:, k, 1 + tt * 128:1 + (tt + 1) * 128], w[:, k, 5120:5184], start=(k == 0), stop=(k == 7),
                         R=[w, hTw], W=[ps])
                P.tt("dve", dr_[:], ps[:, 0:64], dtb[:], ALU.add, R=[ps, dtb], W=[dr_])
                P.act(dr_[:], dr_[:], AF.Exp, R=[dr_], W=[dr_])
                P.act(dr_[:], dr_[:], AF.Ln, R=[dr_], W=[dr_], bias=1.0)
                P.ld("pool", self.dt_d[rows, :], dr_[:], R=[dr_], W=[Res("u")])
        P.pop()

    def ssd_scan(self, L, dr):
        P = self.P
        P.push()
        c2 = P.sbuf("c2", [128, 640], F32)
        P.ld("sp", c2[:], self.cst2[:, :], R=[], W=[c2])
        LE, GE, GT, LT, onesf = (c2[:, q * 128:(q + 1) * 128] for q in range(5))
        m1 = LE if dr == 0 else GE
        lm = GT if dr == 0 else LT
        a_b = P.sbuf("a_b", [128, 32], F32)
        P.ld("sp", a_b[:], self.ssd_a_log[dr].partition_broadcast(128), R=[], W=[a_b])
        P.act(a_b[:], a_b[:], AF.Exp, R=[a_b], W=[a_b])
        P.tsc("dve", a_b[:], a_b[:], -1.0, None, ALU.mult, R=[a_b], W=[a_b])
        stf = [P.sbuf("stf", [128, 512], F32) for _ in range(4)]
        stb = [P.sbuf("stb", [128, 512], BF16) for _ in range(4)]
        for g in range(4):
            P.memset("dve", stf[g][:], 0.0, W=[stf[g]])
            P.memset("dve", stb[g][:], 0.0, W=[stb[g]])
        NB = 2
        xss = [P.sbuf("s_xs", [128, 2048], BF16) for _ in range(NB)]
        Bts = [P.sbuf("s_B", [128, 512], BF16) for _ in range(NB)]
        BTs = [P.sbuf("s_BT", [128, 4, 128], BF16) for _ in range(NB)]
        CTs = [P.sbuf("s_CT", [128, 4, 128], BF16) for _ in range(NB)]
        dts = [P.sbuf("s_dt", [128, 64], F32) for _ in range(NB)]
        dtas = [P.sbuf("s_dta", [128, 32], F32) for _ in range(NB)]
        E3s = [P.sbuf("s_E3", [128, 3, 32], F32) for _ in range(NB)]
        xdts = [P.sbuf("s_xdt", [128, 2048], BF16) for _ in range(NB)]
        xdds = [P.sbuf("s_xdd", [128, 2048], BF16) for _ in range(NB)]
        ys = [P.sbuf("s_y", [128, 2048], F32) for _ in range(NB)]
        rhsEs = [P.sbuf("s_rhsE", [128, 8, 128], F32) for _ in range(2)]
        Lts = [P.sbuf("s_Lt", [128, 8, 128], BF16) for _ in range(2)]
        MTs = [P.sbuf("s_MT", [128, 8, 128], BF16) for _ in range(2)]
        cbms = [P.sbuf("s_cbm", [128, 128], BF16) for _ in range(2)]
        t1s = [P.sbuf("s_t1", [128, 512], F32) for _ in range(2)]
        shr = P.psum("shr", [128, 512])
        e3p = Tile("e3p", shr[:, 128:224].rearrange("p (a b) -> p a b", b=32))
        cbp = Tile("cbp", shr[:, 0:128])
        e3p._res = shr._res
        cbp._res = shr._res
        args = [P.psum("argp", [128, 8, 128]) for _ in range(2)]
        yp = [P.psum("yp", [128, 512]) for _ in range(1)]
        yop = P.psum("yop", [128, 512])
        sp_ = P.psum("sps", [128, 512])
        BTv = self.BT_d.t.rearrange("(g p) t -> p g t", p=128)
        CTv = self.CT_d.t.rearrange("(g p) t -> p g t", p=128)
        order = list(range(NT)) if dr == 0 else [1, 0] + list(range(NT - 1, 1, -1))
        yd = self.yf_d if dr == 0 else self.yb_d
        ng = 0
        for ci, c in enumerate(order):
            b = ci % NB
            rows = slice(c * 128, (c + 1) * 128)
            xs, Bt, BT, CT, dt, dta, E3, xdt, xdd, y = xss[b], Bts[b], BTs[b], CTs[b], dts[b], dtas[b], E3s[b], xdts[b], xdds[b], ys[b]
            P.ld("sp", xs[:], self.xs_d[rows, :], R=[], W=[xs])
            P.ld("sp", Bt[:], self.B_d[rows, :], R=[], W=[Bt])
            P.ld("sp", BT[:], BTv[:, :, rows], R=[], W=[BT])
            P.ld("sp", CT[:], CTv[:, :, rows], R=[], W=[CT])
            P.ld("sp", dt[:], self.dt_d[rows, :], R=[], W=[dt])
            dtd = dt[:, dr * 32:(dr + 1) * 32]
            P.tt("dve", dta[:], dtd, a_b[:], ALU.mult, R=[dt, a_b], W=[dta])
            P.mm(e3p[:, 0, :], m1, dta[:], R=[c2, dta], W=[e3p])
            P.mm(e3p[:, 1, :], lm, dta[:], R=[c2, dta], W=[e3p])
            P.mm(e3p[:, 2, :], onesf, dta[:], R=[c2, dta], W=[e3p])
            P.act(E3[:], e3p[:], AF.Exp, R=[e3p], W=[E3])
            xs3 = xs[:].rearrange("p (h q) -> p h q", q=64)
            P.tt("dve", xdt[:].rearrange("p (h q) -> p h q", q=64), xs3, dtd.unsqueeze(2).to_broadcast([128, 32, 64]), ALU.mult,
                 R=[xs, dt], W=[xdt])
            P.tt("pool", xdd[:].rearrange("p (h q) -> p h q", q=64), xdt[:].rearrange("p (h q) -> p h q", q=64),
                 E3[:, 1, :].unsqueeze(2).to_broadcast([128, 32, 64]), ALU.mult, R=[xdt, E3], W=[xdd])
            for g in range(4):
                hs = slice(g * 8, (g + 1) * 8)
                q2 = ng % 2
                ng += 1
                rhsE, Lt, MT, cbm, t1 = rhsEs[q2], Lts[q2], MTs[q2], cbms[q2], t1s[q2]
                arg = args[ng % 2]
                P.tt("pool", rhsE[:], dta[:, hs].unsqueeze(2).to_broadcast([128, 8, 128]), m1.unsqueeze(1).to_broadcast([128, 8, 128]),
                     ALU.mult, R=[dta, c2], W=[rhsE])
                for hf in range(2):
                    P.mm(arg[:, hf * 4:(hf + 1) * 4, :], lm, rhsE[:, hf * 4:(hf + 1) * 4, :], R=[c2, rhsE], W=[arg])
                P.act(Lt[:], arg[:], AF.Exp, R=[arg], W=[Lt])
                P.mm(cbp[:], BT[:, g, :], CT[:, g, :], R=[BT, CT], W=[cbp])
                P.tt("dve", cbm[:], cbp[:], m1, ALU.mult, R=[cbp, c2], W=[cbm])
                P.tt("dve", MT[:], Lt[:], cbm[:].unsqueeze(1).to_broadcast([128, 8, 128]), ALU.mult, R=[Lt, cbm], W=[MT])
                ypg = yp[0]
                for hl in range(8):
                    h = g * 8 + hl
                    P.mm(ypg[:, hl * 64:(hl + 1) * 64], MT[:, hl, :], xdt[:, h * 64:(h + 1) * 64], R=[MT, xdt], W=[ypg])
                P.mm(yop[:], CT[:, g, :], stb[g][:], R=[CT, stb[g]], W=[yop])
                P.tt("dve", t1[:].rearrange("p (h q) -> p h q", q=64), yop[:].rearrange("p (h q) -> p h q", q=64),
                     E3[:, 0, hs].unsqueeze(2).to_broadcast([128, 8, 64]), ALU.mult, R=[yop, E3], W=[t1])
                P.tt("dve", y[:, g * 512:(g + 1) * 512], t1[:], ypg[:], ALU.add, R=[t1, ypg], W=[y])
                P.mm(sp_[:], Bt[:, g * 128:(g + 1) * 128], xdd[:, g * 512:(g + 1) * 512], R=[Bt, xdd], W=[sp_])
                P.tt("pool", stf[g][:].rearrange("p (h q) -> p h q", q=64), stf[g][:].rearrange("p (h q) -> p h q", q=64),
                     E3[:, 2, hs].unsqueeze(2).to_broadcast([128, 8, 64]), ALU.mult, R=[stf[g], E3], W=[stf[g]])
                P.tt("dve", stf[g][:], stf[g][:], sp_[:], ALU.add, R=[stf[g], sp_], W=[stf[g]])
                P.cp("act", stb[g][:], stf[g][:], R=[stf[g]], W=[stb[g]])
            P.ld("pool", yd[rows, :], y[:], R=[y], W=[Res("u")])
        P.pop()

    def ssd_out(self, L, M):
        P = self.P
        P.push()
        w_out = self.cast_load("ssd_wo", [128, 16, D], self.ssd_w_out.t.rearrange("(k p) n -> p k n", p=128), [])
        self.tail_consts(L)
        T = self.tail_tiles(n_pt=1)
        D_b = P.sbuf("D_b", [128, 32], F32)
        P.ld("sp", D_b[:], self.ssd_d.t.partition_broadcast(128), R=[], W=[D_b])
        gn_b = P.sbuf("gn_b", [128, 2048], F32)
        P.ld("sp", gn_b[:], self.ssd_g_norm.t.partition_broadcast(128), R=[], W=[gn_b])
        yfs = [P.sbuf("o_yf", [128, 2048], F32) for _ in range(2)]
        ybs = [P.sbuf("o_yb", [128, 2048], F32) for _ in range(2)]
        xss = [P.sbuf("o_xs", [128, 2048], BF16) for _ in range(2)]
        zss = [P.sbuf("o_zs", [128, 2048], BF16) for _ in range(2)]
        gats = [P.sbuf("o_gat", [128, 2048], F32) for _ in range(2)]
        junk = P.sbuf("o_junk", [128, 512], BF16)
        ss4s = [P.sbuf("o_ss4", [128, 4], F32) for _ in range(2)]
        rs4s = [P.sbuf("o_rs4", [128, 4], F32) for _ in range(2)]
        nrms = [P.sbuf("o_nrm", [128, 2048], BF16) for _ in range(2)]
        nTs = [P.sbuf("o_nT", [128, 16, 128], BF16) for _ in range(2)]
        ptr = [P.psum("o_ptr", [128, 8, 128], BF16) for _ in range(2)]
        pols = [P.psum("o_pol", [128, D]) for _ in range(2)]
        for i in range(NT):
            b = i % 2
            rows = slice(i * 128, (i + 1) * 128)
            yf, yb, xs, zs = yfs[b], ybs[b], xss[b], zss[b]
            gat, ss4, rs4, nrm, nT, pol = gats[b], ss4s[b], rs4s[b], nrms[b], nTs[b], pols[b]
            P.ld("sp", yf[:], self.yf_d[rows, :], R=[], W=[yf])
            P.ld("sp", yb[:], self.yb_d[rows, :], R=[], W=[yb])
            P.ld("sp", xs[:], self.xs_d[rows, :], R=[], W=[xs])
            P.ld("sp", zs[:], self.zs_d[rows, :], R=[], W=[zs])
            P.tt("dve", yf[:], yf[:], yb[:], ALU.add, R=[yf, yb], W=[yf])
            P.tt("dve", yb[:].rearrange("p (h q) -> p h q", q=64), xs[:].rearrange("p (h q) -> p h q", q=64),
                 D_b[:].unsqueeze(2).to_broadcast([128, 32, 64]), ALU.mult, R=[xs, D_b, yb], W=[yb])
            P.tt("dve", yf[:], yf[:], yb[:], ALU.add, R=[yf, yb], W=[yf])
            P.tt("dve", gat[:], yf[:], zs[:], ALU.mult, R=[yf, zs], W=[gat])
            for g in range(4):
                P.act(junk[:], gat[:, g * 512:(g + 1) * 512], AF.Square, R=[gat], W=[junk, ss4], accum_out=ss4[:, g:g + 1])
            P.act(rs4[:], ss4[:], AF.Sqrt, R=[ss4], W=[rs4], scale=1.0 / 512, bias=EPS)
            P.recip(rs4[:], rs4[:], R=[rs4], W=[rs4])
            for g in range(4):
                gs = slice(g * 512, (g + 1) * 512)
                P.stt("dve", nrm[:, gs], gat[:, gs], rs4[:, g:g + 1], gn_b[:, gs], ALU.mult, ALU.mult, R=[gat, rs4, gn_b], W=[nrm])
            for hf in range(2):
                for q in range(8):
                    fk = hf * 8 + q
                    P.tr(ptr[hf][:, q, :], nrm[:, fk * 128:(fk + 1) * 128], self.ident[:], R=[nrm, self.ident], W=[ptr[hf]])
                P.cp("act", nT[:, hf * 8:(hf + 1) * 8, :], ptr[hf][:], R=[ptr[hf]], W=[nT])
            for n in range(2):
                for fk in range(16):
                    P.mm(pol[:, n * 512:(n + 1) * 512], nT[:, fk, :], w_out[:, fk, n * 512:(n + 1) * 512], start=(fk == 0), stop=(fk == 15),
                         R=[nT, w_out], W=[pol])
            self.tail(L, M, T, i, [(pol[:, 0:512], 0, 512), (pol[:, 512:1024], 512, 512)], [pol])
        P.pop()

    def fnet_f1(self, L):
        P = self.P
        P.push()
        DF = self.cast_load("DF", [128, 2, 512], self.fcst[:, 0:1024].rearrange("p (q n) -> p q n", q=2), [])
        hTv = self.hT.t.rearrange("(k p) t -> p k t", p=128)
        hTs = [P.sbuf("f_hT", [128, 8, 128], BF16) for _ in range(2)]
        yts = [P.sbuf("f_yt", [128, 2, 4, 256], BF16) for _ in range(2)]
        yps = [P.psum("f_yps", [128, 4, 512]) for _ in range(2)]
        for i in range(NT):
            hTt, yt, yp = hTs[i % 2], yts[i % 2], yps[i % 2]
            rows = slice(i * 128, (i + 1) * 128)
            P.ld("sp", hTt[:], hTv[:, :, rows], R=[], W=[hTt])
            for g in range(4):
                for q in range(2):
                    P.mm(yp[:, g, :], hTt[:, 2 * g + q, :], DF[:, q, :], start=(q == 0), stop=(q == 1), R=[hTt, DF], W=[yp])
            for c in range(2):
                P.cp("act" if c == 0 else "dve", yt[:, c], yp[:, :, c * 256:(c + 1) * 256], R=[yp], W=[yt])
            for c in range(2):
                P.ld("pool", self.Yd[c, rows, :], yt[:, c].rearrange("p g m -> p (g m)"), R=[yt], W=[Res("u")])
        P.pop()

    def fnet_f2(self, L):
        P = self.P
        P.push()
        W1 = self.cast_load("W1big", [128, 128], self.fcst[:, 2048:2176], [])
        Ydv = self.Yd.t[:, 256:, :].rearrange("c (a b) f -> c a b f", b=128)
        Zdv = self.Zd.t.rearrange("c k t f -> (c k) t f")
        Ins = [P.sbuf("f_In", [128, 4, D], BF16) for _ in range(2)]
        Zts = [P.sbuf("f_Zt", [128, 4, D], BF16) for _ in range(2)]
        zps = [P.psum("f_zps", [128, D]) for _ in range(3)]
        nz = 0
        for bt in range(32):
            In, Zt = Ins[bt % 2], Zts[bt % 2]
            for c in range(2):
                P.ld("sp", In[c * 64:(c + 1) * 64, :, :], Ydv[c, :, 4 * bt:4 * bt + 4, :], R=[], W=[In])
            for q in range(4):
                zp = zps[nz % 3]
                nz += 1
                for n in range(2):
                    P.mm(zp[:, n * 512:(n + 1) * 512], W1[:], In[:, q, n * 512:(n + 1) * 512], R=[W1, In], W=[zp])
                P.cp("act" if q % 2 == 0 else "dve", Zt[:, q, :], zp[:], R=[zp], W=[Zt])
            P.ld("pool", Zdv[:, 4 * bt:4 * bt + 4, :], Zt[:], R=[Zt], W=[Res("u")])
        P.pop()

    def fnet_f3(self, L, M):
        P = self.P
        P.push()
        MT = P.sbuf("f_MT", [128, 64, 256], BF16)
        for kb in range(8):
            P.ld("pool", MT[:, kb * 8:(kb + 1) * 8, :], self.fM[:, kb * 2048:(kb + 1) * 2048].rearrange("p (a n) -> p a n", a=8), R=[], W=[MT])
        w_o = self.cast_load("f_wo", [128, 8, D], self.fnet_w_o.t.rearrange("(k p) n -> p k n", p=128), [])
        DF2 = self.cast_load("DF2", [128, 2, 512], self.fcst[:, 1024:2048].rearrange("p (q n) -> p q n", q=2), [])
        self.tail_consts(L)
        T = self.tail_tiles()
        mps = P.psum("f_mps", [128, 1024])
        pol = P.psum("f_pol", [128, D])

        def outproj(mT_ap_fn):
            for n in range(2):
                for fc in range(8):
                    P.mm(pol[:, n * 512:(n + 1) * 512], mT_ap_fn(fc), w_o[:, fc, n * 512:(n + 1) * 512], start=(fc == 0), stop=(fc == 7),
                         R=[w_o] + mT_R, W=[pol])
        Yc = P.sbuf("f_Yc", [128, 2, 2, D], BF16)
        for c in range(2):
            for q in range(2):
                P.ld("sp", Yc[:, q, c, :], self.Yd[c, q * 128:(q + 1) * 128, :], R=[], W=[Yc])
        mTc = P.sbuf("f_mTc", [128, 8, 256], BF16)
        mcv = mps[:].rearrange("p (a k) -> p a k", k=256)
        for half in range(2):
            for f4 in range(4):
                fc = half * 4 + f4
                n = 0
                for q in range(2):
                    for c in range(2):
                        P.mm(mcv[:, f4, :], Yc[:, q, c, fc * 128:(fc + 1) * 128], DF2[:, q, c * 256:(c + 1) * 256], start=(n == 0), stop=(n == 3),
                             R=[Yc, DF2], W=[mps])
                        n += 1
            P.act(mTc[:, half * 4:(half + 1) * 4, :], mcv, AF.Copy, R=[mps], W=[mTc], scale=1.0 / 256.0)
        for i in range(2):
            mT_R = [mTc]
            outproj(lambda fc: mTc[:, fc, i * 128:(i + 1) * 128])
            self.tail(L, M, T, i, [(pol[:, 0:512], 0, 512), (pol[:, 512:1024], 512, 512)], [pol])
        Zrs = [P.sbuf("f_Zr", [128, D], BF16) for _ in range(2)]
        Zis = [P.sbuf("f_Zi", [128, D], BF16) for _ in range(2)]
        mTs = [P.sbuf("f_mT", [128, 8, 128], BF16) for _ in range(2)]
        mlv = mps[:].rearrange("p (a k) -> p a k", k=128)
        xsv = self.XS.t[256:, :].rearrange("(p q) d -> q p d", q=64)
        hrv = self.hrow.t[256:, :].rearrange("(p q) d -> q p d", q=64)
        sc = float(1.0 / np.sqrt(8192.0 * 256.0))
        for k1 in range(64):
            Zr, Zi, mT = Zrs[k1 % 2], Zis[k1 % 2], mTs[k1 % 2]
            P.ld("sp", Zr[:], self.Zd[0, k1], R=[], W=[Zr])
            P.ld("sp", Zi[:], self.Zd[1, k1], R=[], W=[Zi])
            for fc in range(8):
                P.mm(mlv[:, fc, :], Zr[:, fc * 128:(fc + 1) * 128], MT[:, k1, 0:128], start=True, stop=False, R=[Zr, MT], W=[mps])
                P.mm(mlv[:, fc, :], Zi[:, fc * 128:(fc + 1) * 128], MT[:, k1, 128:256], start=False, stop=True, R=[Zi, MT], W=[mps])
            P.act(mT[:], mlv, AF.Copy, R=[mps], W=[mT], scale=sc)
            mT_R = [mT]
            outproj(lambda fc: mT[:, fc, :])
            self.tail(L, M, T, 2 + k1, [(pol[:, 0:512], 0, 512), (pol[:, 512:1024], 512, 512)], [pol],
                      xs_rows=xsv[k1], h_rows=hrv[k1])
        P.pop()

    def layer(self, i, upto=None):
        P = self.P
        P.push()
        L = self.layer_consts(i)
        L.skip_ctx_moe = (i == 3)
        need_ctx = (i != 3)
        M = self.moe_state(L)
        self.phase_A(L)
        kind, j = i % 3, i // 3
        if kind == 0:
            self.mla_proj(L, j, need_ctx)
            if upto == "proj":
                P.pop(); return
            if i == 0 or ATTN_V2_ALL:
                self.mla_attn(L, need_ctx)
            else:
                self.mla_attn_v1(L, need_ctx)
            if upto == "attn":
                P.pop(); return
            self.mla_out(L, M, j, need_ctx)
        if kind == 1:
            self.ssd_in(L)
            if upto == "ssd_in":
                P.pop(); return
            self.ssd_scan(L, 0)
            self.ssd_scan(L, 1)
            if upto == "ssd_scan":
                P.pop(); return
            self.ssd_out(L, M)
        if kind == 2:
            L.fmap = True
            self.fnet_f1(L)
            self.fnet_f2(L)
            self.fnet_f3(L, M)
        if upto == "mix":
            P.pop(); return
        self.moe_route(L, M)
        if upto == "route":
            P.pop(); return
        self.moe_experts(L, M)
        P.pop()

def make_consts():
    cst = np.zeros((128, 1024), np.float32)
    cst[:, 0:128] = np.eye(128, dtype=np.float32)
    cst[:, 128:256] = 1.0
    for p in range(128):
        cst[p, 256 + (p % 64)] = 1.0
        cst[p, 320 + p + 1:448] = 1.0
        cst[p, 448] = 1152 + p
        cst[p, 449:449 + NE] = np.arange(NE) * 1280
    t = np.arange(8192)
    row = (t // 64).astype(np.float32)
    col = (t % 64).astype(np.float32)
    inv = (10000.0 ** (-np.arange(16, dtype=np.float32) / 16)).astype(np.float32)
    ang = np.concatenate([row[:, None] * inv, col[:, None] * inv], axis=-1).astype(np.float32)
    cos, sin = np.cos(ang).T, np.sin(ang).T
    tab = np.zeros((128, TA), np.float32)
    tab[0:64, 0:256] = 1.0
    tab[0:32, 256:] = cos
    tab[32:64, 256:] = cos
    tab[64:96, 256:] = -sin
    tab[96:128, 256:] = sin
    tokid = (np.arange(NT)[None, :] * 128 + np.arange(128)[:, None]).astype(np.int32)
    return cst, tab, tokid


def prep_shared(I):
    S = {}
    for k in ["w_mod", "b_mod", "g_mix", "g_ffn"]:
        S[k] = np.ascontiguousarray(I[k], dtype=np.float32)
    w_in = I["mla_w_in"]
    S["mla_w_in"] = np.ascontiguousarray(np.concatenate(
        [w_in[:, :, :384], w_in[:, :, 384:448], w_in[:, :, 416:448], w_in[:, :, 384:416]], axis=-1))
    wq = I["mla_w_uq"].reshape(2, 256, 8, 192)
    S["mla_w_uq"] = np.ascontiguousarray(np.concatenate(
        [wq[..., :128], wq[..., 128:192], wq[..., 160:192], wq[..., 128:160]], axis=-1).reshape(2, 256, 2048))
    wkv = I["mla_w_ukv"].reshape(2, 128, 8, 256)
    S["mla_w_uk"] = np.ascontiguousarray(wkv[..., :128].reshape(2, 128, 1024))
    S["mla_w_uv"] = np.ascontiguousarray(wkv[..., 128:].reshape(2, 128, 1024))
    S["mla_w_o"] = np.ascontiguousarray(I["mla_w_o"])
    gc = np.zeros((2, 128, 8), np.float32)
    for j in range(2):
        gc[j, :, 0] = I["mla_g_q"][j, :128]
        gc[j, :, 1] = I["mla_g_q"][j, 128:]
        gc[j, :, 2] = I["mla_g_kv"][j]
        for c0, g in ((3, I["mla_g_qn"][j]), (5, I["mla_g_kn"][j])):
            gc[j, :, c0] = g[:128]
            gc[j, :, c0 + 1] = np.concatenate([g[128:192], g[160:192], g[128:160]])
    S["mla_gc"] = gc
    cst, tab, tokid = make_consts()
    S["cst"], S["rope_tab"], S["tokid"] = cst, tab, tokid
    S["ssd_w_in"] = np.ascontiguousarray(I["ssd_w_in"][0])
    S["ssd_conv_w"] = np.ascontiguousarray(I["ssd_conv_w"][0])
    S["ssd_conv_b"] = np.ascontiguousarray(I["ssd_conv_b"][0])
    S["ssd_dt_bias"] = np.ascontiguousarray(I["ssd_dt_bias"][0].reshape(64))
    S["ssd_a_log"] = np.ascontiguousarray(I["ssd_a_log"][0])
    S["ssd_d"] = np.ascontiguousarray(I["ssd_d"][0])
    S["ssd_g_norm"] = np.ascontiguousarray(I["ssd_g_norm"][0])
    S["ssd_w_out"] = np.ascontiguousarray(I["ssd_w_out"][0])
    kk = np.arange(128)
    c2 = np.zeros((128, 640), np.float32)
    c2[:, 0:128] = (kk[:, None] <= kk[None, :])
    c2[:, 128:256] = (kk[:, None] >= kk[None, :])
    c2[:, 256:384] = (kk[:, None] > kk[None, :])
    c2[:, 384:512] = (kk[:, None] < kk[None, :])
    c2[:, 512:640] = 1.0
    S["cst2"] = c2
    S["fnet_w_o"] = np.ascontiguousarray(I["fnet_w_o"][0])
    fc_ = np.zeros((128, 2176), np.float64)
    p_ = np.arange(128)
    m_ = np.arange(256)
    for q in range(2):
        angn = 2 * np.pi * np.outer(128 * q + p_, m_) / 256.0
        fc_[:, q * 512:q * 512 + 256] = np.cos(angn)
        fc_[:, q * 512 + 256:q * 512 + 512] = -np.sin(angn)
        fc_[:, 1024 + q * 512:1024 + q * 512 + 256] = np.cos(angn)
        fc_[:, 1024 + q * 512 + 256:1024 + q * 512 + 512] = np.sin(angn)
    a64 = np.arange(64)
    ang1 = 2 * np.pi * np.outer(a64, a64) / 64.0
    wr, wi = np.cos(ang1), -np.sin(ang1)
    fc_[0:64, 2048:2112] = wr.T
    fc_[64:128, 2048:2112] = -wi.T
    fc_[0:64, 2112:2176] = wi.T
    fc_[64:128, 2112:2176] = wr.T
    S["fcst"] = fc_.astype(np.float32)
    t2 = np.arange(128)[:, None, None]
    k1 = np.arange(64)[None, :, None]
    k2 = np.arange(128)[None, None, :]
    angm = 2 * np.pi * (k1 * t2 / 8192.0 + k2 * t2 / 128.0)
    fM = np.zeros((128, 64, 2, 128), np.float64)
    fM[:, :, 0, :] = np.cos(angm)
    fM[:, :, 1, :] = np.sin(angm)
    S["fM"] = fM.reshape(128, 16384).astype(np.float32)
    tf = (np.arange(NT)[None, :] * 128 + np.arange(128)[:, None]).astype(np.int32)
    for a in range(64):
        tf[:, 2 + a] = 256 + a + 64 * np.arange(128)
    S["tokid_f"] = tf
    S["moe_wr"] = np.ascontiguousarray(I["moe_w_router"])
    S["moe_wg"] = np.ascontiguousarray(I["moe_w_gate"])
    S["moe_wu"] = np.ascontiguousarray(I["moe_w_up"])
    S["moe_wd"] = np.ascontiguousarray(I["moe_w_down"])
    return S


def prep_core(I, S, b):
    m = dict(S)
    m["x"] = np.ascontiguousarray(I["x"][b])
    m["ctx"] = np.ascontiguousarray(I["ctx"][b])
    m["cc"] = np.ascontiguousarray(np.stack([I["c"][b], I["c_ctx"]], axis=0))
    return m


_CACHE = {}


def build_program():
    nc = bass.Bass("TRN2", target_bir_lowering=False)
    k = K(nc)
    k.setup()
    for i in range(4):
        k.layer(i)
    P = k.P
    for j in range(8):
        P.ld("sp", k.out[j * 1024:(j + 1) * 1024, :], k.XS[256 + j * 1024:256 + (j + 1) * 1024, :], R=[], W=[Res("u")])
    P.barrier()
    P.emit()
    P.close()
    return nc, k


def kernel(**inputs):
    from concourse.bass_utils import run_bass_kernel_spmd
    I = {k_: np.asarray(v) for k_, v in inputs.items()}
    if "prog" not in _CACHE:
        _CACHE["prog"] = build_program()
    nc, k = _CACHE["prog"]
    S = prep_shared(I)
    n = 8
    in_maps = [prep_core(I, S, b) for b in range(n)]
    res = run_bass_kernel_spmd(nc, in_maps, core_ids=list(range(n)))
    return np.stack([np.asarray(r["out"]) for r in res.results], axis=0).astype(np.float32)
```

```python
from contextlib import ExitStack
import numpy as np
import concourse.bass as bass
import concourse.mybir as mybir

F32 = mybir.dt.float32
BF16 = mybir.dt.bfloat16
I32 = mybir.dt.int32
U32 = mybir.dt.uint32
ALU = mybir.AluOpType
AF = mybir.ActivationFunctionType
AX = mybir.AxisListType

COMPUTE = ("pe", "act", "dve", "pool")
EPOCH = 30000
DMA_EPOCH = 2000


class Res:
    __slots__ = ("name", "last_w", "readers")

    def __init__(self, name):
        self.name = name
        self.last_w = None
        self.readers = {}


class Tile:
    def __init__(self, name, t):
        self.name = name
        self.t = t
        self._res = {}

    def r(self, key=None):
        x = self._res.get(key)
        if x is None:
            x = self._res[key] = Res(f"{self.name}:{key}")
        return x

    def __getitem__(self, k):
        return self.t[k]


class Prog:
    def __init__(self, nc, n_dma_sems=16):
        self.nc = nc
        self.es = ExitStack()
        self.streams = {e: [] for e in ("pe", "act", "dve", "pool", "sp")}
        self.cnt = {e: 0 for e in COMPUTE}
        self.waited = {e: {} for e in self.streams}
        self.sems = {}
        self.n_dma_sems = n_dma_sems
        self.dma_rr = {q: 0 for q in ("sp", "pool", "act")}
        self.dma_cnt = {}
        self.n_inst = 0
        self.scopes = []
        self._uid = 0

    def sbuf(self, name, shape, dt):
        t = self.es.enter_context(self.nc.sbuf_tensor(name, list(shape), dt))
        return Tile(name, t)

    def psum(self, name, shape, dt=F32):
        t = self.es.enter_context(self.nc.psum_tensor(name, list(shape), dt))
        return Tile(name, t)

    def dram(self, name, shape, dt, kind="Internal"):
        t = self.nc.dram_tensor(name, list(shape), dt, kind=kind)
        return Tile(name, t.ap())

    def _sem(self, key):
        s = self.sems.get(key)
        if s is None:
            s = self.sems[key] = self.es.enter_context(
                self.nc.semaphore("s_" + "_".join(str(k) for k in key)))
        return s

    def _deps(self, eng, R, W):
        deps = {}

        def add(tok, kind):
            if tok is None:
                return
            semkey, val, teng = tok
            if teng == eng and eng in COMPUTE:
                if eng == "pe" or kind != "raw":
                    return
            if deps.get(semkey, 0) < val:
                deps[semkey] = val

        for r in R:
            add(r.last_w, "raw")
        for w in W:
            add(w.last_w, "waw")
            for sk, (v, e) in w.readers.items():
                add((sk, v, e), "war")
        return deps

    def _emit_waits(self, eng, deps):
        wd = self.waited[eng]
        st = self.streams[eng]
        for semkey, val in deps.items():
            if wd.get(semkey, 0) >= val:
                continue
            wd[semkey] = val
            st.append(("wait", self._sem(semkey), val))

    def _commit(self, tok, R, W):
        semkey, val, eng = tok
        for r in R:
            cur = r.readers.get(semkey)
            if cur is None or cur[0] < val:
                r.readers[semkey] = (val, eng)
        for w in W:
            w.last_w = tok
            w.readers = {}

    @staticmethod
    def _resl(xs):
        out = []
        for x in xs:
            out.append(x.r() if isinstance(x, Tile) else x)
        return out

    def op(self, eng, fn, R=(), W=()):
        R = self._resl(R)
        W = self._resl(W)
        self._emit_waits(eng, self._deps(eng, R, W))
        i = self.cnt[eng]
        self.cnt[eng] = i + 1
        semkey = ("e", eng, i // EPOCH)
        val = i % EPOCH + 1
        tok = (semkey, val, eng)
        self.streams[eng].append(("op", fn, self._sem(semkey), 1))
        self._commit(tok, R, W)
        self.n_inst += 1
        return tok

    def dma(self, q, fn, R=(), W=()):
        R = self._resl(R)
        W = self._resl(W)
        s = self.dma_rr[q]
        self.dma_rr[q] = (s + 1) % self.n_dma_sems
        base = (q, s)
        m = self.dma_cnt.get(base, 0) + 1
        self.dma_cnt[base] = m

        def key(mi):
            ep = (mi - 1) // DMA_EPOCH
            return ("d", q, s, ep), 16 * ((mi - 1) % DMA_EPOCH + 1)

        deps = self._deps("dma:" + q, R, W)
        if m > 1:
            pk, pv = key(m - 1)
            if deps.get(pk, 0) < pv:
                deps[pk] = pv
        self._emit_waits(q, deps)
        semkey, val = key(m)
        tok = (semkey, val, "dma:" + q)
        self.streams[q].append(("op", fn, self._sem(semkey), 16))
        self._commit(tok, R, W)
        self.n_inst += 1
        return tok

    def final_wait(self, eng, toks):
        deps = {}
        for semkey, val, _ in toks:
            if deps.get(semkey, 0) < val:
                deps[semkey] = val
        self._emit_waits(eng, deps)

    def emit(self):
        nc = self.nc
        streams = self.streams

        def run(e, items):
            for it in items:
                if it[0] == "wait":
                    e.wait_ge(it[1], it[2])
                else:
                    it[1](e).then_inc(it[2], it[3])

        with nc.Block() as block:
            @block.tensor
            def _(e):
                run(e, streams["pe"])

            @block.scalar
            def _(e):
                run(e, streams["act"])

            @block.vector
            def _(e):
                run(e, streams["dve"])

            @block.gpsimd
            def _(e):
                run(e, streams["pool"])

            @block.sync
            def _(e):
                run(e, streams["sp"])

    def close(self):
        self.es.close()


def _push(self):
    self.scopes.append(ExitStack())


def _pop(self):
    self.barrier()
    self.scopes.pop().close()


def _sbuf(self, name, shape, dt):
    st = self.scopes[-1] if self.scopes else self.es
    self._uid += 1
    t = st.enter_context(self.nc.sbuf_tensor(f"{name}_{self._uid}", list(shape), dt))
    return Tile(name, t)


def _psum(self, name, shape, dt=F32):
    st = self.scopes[-1] if self.scopes else self.es
    self._uid += 1
    t = st.enter_context(self.nc.psum_tensor(f"{name}_{self._uid}", list(shape), dt))
    return Tile(name, t)


def _barrier(self):
    toks = {}
    for eng in COMPUTE:
        i = self.cnt[eng]
        if i > 0:
            toks[("e", eng, (i - 1) // EPOCH)] = (i - 1) % EPOCH + 1
    for (q, s), m in self.dma_cnt.items():
        toks[("d", q, s, (m - 1) // DMA_EPOCH)] = 16 * ((m - 1) % DMA_EPOCH + 1)
    for eng in self.streams:
        self._emit_waits(eng, dict(toks))


def _mm(self, out, lhsT, rhs, start=True, stop=True, R=(), W=()):
    return self.op("pe", lambda e: e.matmul(out, lhsT, rhs, start=start, stop=stop), R, W)


def _tr(self, out, in_, ident, R=(), W=()):
    return self.op("pe", lambda e: e.transpose(out, in_, ident), R, W)


def _act(self, out, in_, func, R=(), W=(), **kw):
    return self.op("act", lambda e: e.activation(out, in_, func, **kw), R, W)


def _tsc(self, eng, out, in0, s1, s2, op0, op1=None, R=(), W=(), **kw):
    if op1 is None:
        return self.op(eng, lambda e: e.tensor_scalar(out, in0, s1, None, op0, **kw), R, W)
    return self.op(eng, lambda e: e.tensor_scalar(out, in0, s1, s2, op0, op1, **kw), R, W)


def _tt(self, eng, out, in0, in1, op, R=(), W=()):
    return self.op(eng, lambda e: e.tensor_tensor(out, in0, in1, op), R, W)


def _stt(self, eng, out, in0, scalar, in1, op0, op1, R=(), W=()):
    return self.op(eng, lambda e: e.scalar_tensor_tensor(out, in0, scalar, in1, op0, op1), R, W)


def _cp(self, eng, out, in_, R=(), W=()):
    if eng == "act":
        return self.op("act", lambda e: e.copy(out, in_), R, W)
    return self.op(eng, lambda e: e.tensor_copy(out, in_), R, W)


def _red(self, eng, out, in_, op, R=(), W=(), axis=None):
    ax = AX.X if axis is None else axis
    return self.op(eng, lambda e: e.tensor_reduce(out, in_, ax, op), R, W)


def _memset(self, eng, out, val, W=()):
    return self.op(eng, lambda e: e.memset(out, val), (), W)


def _ld(self, q, out, in_, R=(), W=(), **kw):
    kw.setdefault("allow_slow_non_contiguous", True)
    return self.dma(q, lambda e: e.dma_start(out=out, in_=in_, **kw), R, W)


def _gather(self, out, in_, idx_ap, R=(), W=()):
    return self.dma("pool", lambda e: e.indirect_dma_start(
        out=out, out_offset=None, in_=in_,
        in_offset=bass.IndirectOffsetOnAxis(ap=idx_ap, axis=0)), R, W)


def _scatter(self, out, in_, idx_ap, R=(), W=(), add=False, bound=None):
    if add:
        def f(e):
            try:
                return e.indirect_dma_start(
                    out=out, out_offset=bass.IndirectOffsetOnAxis(ap=idx_ap, axis=0), in_=in_, in_offset=None,
                    compute_op=ALU.add, oob_is_err=True)
            except Exception:
                print("SCATTER-ADD FAIL", out, in_, idx_ap, bound)
                raise
        return self.dma("pool", f, R, W)
    return self.dma("pool", lambda e: e.indirect_dma_start(
        out=out, out_offset=bass.IndirectOffsetOnAxis(ap=idx_ap, axis=0), in_=in_, in_offset=None), R, W)


def _recip(self, out, in_, R=(), W=()):
    return self.op("dve", lambda e: e.reciprocal(out, in_), R, W)


def _get_bound_reg(self, e, bound):
    if not hasattr(self, "_bregs"):
        self._bregs = {}
    r = self._bregs.get(bound)
    if r is None:
        r = self._bregs[bound] = e.to_reg(bound)
    return r


Prog.get_bound_reg = _get_bound_reg
Prog.recip = _recip
Prog.push = _push
Prog.pop = _pop
Prog.sbuf = _sbuf
Prog.psum = _psum
Prog.barrier = _barrier
Prog.mm = _mm
Prog.tr = _tr
Prog.act = _act
Prog.tsc = _tsc
Prog.tt = _tt
Prog.stt = _stt
Prog.cp = _cp
Prog.red = _red
Prog.memset = _memset
Prog.ld = _ld
Prog.gather = _gather
Prog.scatter = _scatter


from types import SimpleNamespace as NS

TA = 8448
NT = 66
D = 1024
NE = 16
EPS = 1e-6
import os
TAIL_ENG = os.environ.get('TAIL_ENG', 'dve')
ATTN_V2_ALL = bool(int(os.environ.get('ATTN_V2_ALL', '1')))


class K:
    def __init__(self, nc, dbg=()):
        self.nc = nc
        self.P = P = Prog(nc)
        self.dbg = set(dbg)
        din = lambda n, s, dt=F32: P.dram(n, s, dt, kind="ExternalInput")
        self.x = din("x", [8192, D])
        self.ctx = din("ctx", [256, D])
        self.cc = din("cc", [2, D])
        self.w_mod = din("w_mod", [4, D, 6144])
        self.b_mod = din("b_mod", [4, 6144])
        self.g_mix = din("g_mix", [4, D])
        self.g_ffn = din("g_ffn", [4, D])
        self.mla_w_in = din("mla_w_in", [2, D, 512])
        self.mla_w_uq = din("mla_w_uq", [2, 256, 2048])
        self.mla_w_uk = din("mla_w_uk", [2, 128, 1024])
        self.mla_w_uv = din("mla_w_uv", [2, 128, 1024])
        self.mla_w_o = din("mla_w_o", [2, D, D])
        self.mla_gc = din("mla_gc", [2, 128, 8])
        self.rope_tab = din("rope_tab", [128, TA])
        self.cst = din("cst", [128, 1024])
        self.moe_wr = din("moe_wr", [4, D, NE])
        self.moe_wg = din("moe_wg", [4, NE, D, D])
        self.moe_wu = din("moe_wu", [4, NE, D, D])
        self.moe_wd = din("moe_wd", [4, NE, D, D])
        self.tokid = din("tokid", [128, NT], I32)
        self.ssd_w_in = din("ssd_w_in", [D, 5184])
        self.ssd_conv_w = din("ssd_conv_w", [3, 3072])
        self.ssd_conv_b = din("ssd_conv_b", [3072])
        self.ssd_dt_bias = din("ssd_dt_bias", [64])
        self.ssd_a_log = din("ssd_a_log", [2, 32])
        self.ssd_d = din("ssd_d", [32])
        self.ssd_g_norm = din("ssd_g_norm", [2048])
        self.ssd_w_out = din("ssd_w_out", [2048, D])
        self.cst2 = din("cst2", [128, 640])
        self.fnet_w_o = din("fnet_w_o", [D, D])
        self.fcst = din("fcst", [128, 2176])
        self.fM = din("fM", [128, 16384])
        self.tokid_f = din("tokid_f", [128, NT], I32)
        self.Yd = self.dscr("Yd", [2, TA, D], BF16)
        self.Zd = self.dscr("Zd", [2, 64, 128, D], BF16)
        self.xs_d = self.dscr("xs_d", [TA, 2048], BF16)
        self.B_d = self.dscr("B_d", [TA, 512], BF16)
        self.BT_d = self.dscr("BT_d", [512, TA], BF16)
        self.CT_d = self.dscr("CT_d", [512, TA], BF16)
        self.zs_d = self.dscr("zs_d", [TA, 2048], BF16)
        self.dt_d = self.dscr("dt_d", [TA, 64], F32)
        self.yf_d = self.dscr("yf_d", [TA, 2048], F32)
        self.yb_d = self.dscr("yb_d", [TA, 2048], F32)
        self.out = P.dram("out", [8192, D], F32, kind="ExternalOutput")
        self.XS = self.dscr("XS", [TA, D], F32)
        self.modd = self.dscr("modd", [4, 2, 6144], F32)
        self.hT = self.dscr("hT", [D, TA], BF16)
        self.QT = self.dscr("QT", [8, 192, TA], BF16)
        self.KT = self.dscr("KT", [8, 192, TA], BF16)
        self.V = self.dscr("V", [TA, D], BF16)
        self.OT = self.dscr("OT", [D, TA], BF16)
        self.hrow = self.dscr("hrow", [TA, D], BF16)
        self.affrow = self.dscr("affrow", [TA, NE], F32)
        self.idxd = self.dscr("idxd", [NE * 1280, 1], I32)
        self.ident = P.sbuf("ident", [128, 128], BF16)
        self.ones = P.sbuf("ones", [128, 128], BF16)
        self.fold = P.sbuf("fold", [128, 64], BF16)
        self.identf = P.sbuf("identf", [128, 128], F32)
        P.ld("pool", self.ident[:], self.cst[:, 0:128], R=[self.cst], W=[self.ident])
        P.ld("pool", self.ones[:], self.cst[:, 128:256], R=[self.cst], W=[self.ones])
        P.ld("pool", self.fold[:], self.cst[:, 256:320], R=[self.cst], W=[self.fold])
        P.ld("sp", self.identf[:], self.cst[:, 0:128], R=[self.cst], W=[self.identf])
        self.ustrict = P.sbuf("ustrict", [128, 128], BF16)
        P.ld("pool", self.ustrict[:], self.cst[:, 320:448], R=[self.cst], W=[self.ustrict])
        self.dumpc = P.sbuf("dumpc", [128, 1], F32)
        P.ld("sp", self.dumpc[:], self.cst[:, 448:449], R=[self.cst], W=[self.dumpc])
        self.eoffs = P.sbuf("eoffs", [128, NE], F32)
        P.ld("sp", self.eoffs[:], self.cst[:, 449:449 + NE], R=[self.cst], W=[self.eoffs])
        self.tokid_sb = P.sbuf("tokid_sb", [128, NT], I32)
        P.ld("sp", self.tokid_sb[:], self.tokid[:, :], R=[self.tokid], W=[self.tokid_sb])
        self.tokid_f_sb = P.sbuf("tokid_f_sb", [128, NT], I32)
        P.ld("sp", self.tokid_f_sb[:], self.tokid_f[:, :], R=[self.tokid_f], W=[self.tokid_f_sb])

    def dscr(self, name, shape, dt):
        kind = "ExternalOutput" if name in self.dbg else "Internal"
        return self.P.dram(name, shape, dt, kind=kind)

    def setup(self):
        P = self.P
        P.ld("sp", self.XS[0:256, :], self.ctx[:, :], R=[self.ctx], W=[self.XS])
        for j in range(8):
            P.ld("sp", self.XS[256 + j * 1024:256 + (j + 1) * 1024, :], self.x[j * 1024:(j + 1) * 1024, :],
                 R=[self.x], W=[self.XS])
        P.push()
        ccT = P.sbuf("ccT", [128, 8, 2], F32)
        ccs = P.sbuf("ccs", [128, 8, 2], F32)
        for m in range(2):
            P.ld("sp", ccT[:, :, m], self.cc[m].rearrange("(k p) -> p k", p=128), R=[self.cc], W=[ccT],
                 allow_slow_non_contiguous=True)
        P.act(ccs[:], ccT[:], AF.Silu, R=[ccT], W=[ccs])
        wms = [P.sbuf("wm", [128, 8, 512], F32) for _ in range(2)]
        bms = [P.sbuf("bm", [2, 512], F32) for _ in range(2)]
        mrs = [P.sbuf("mr", [2, 512], F32) for _ in range(2)]
        pss = [P.psum("psm", [2, 512]) for _ in range(2)]
        n = 0
        for i in range(4):
            for nb in range(12):
                wm, bm, mr, ps = wms[n % 2], bms[n % 2], mrs[n % 2], pss[n % 2]
                n += 1
                sl = slice(nb * 512, (nb + 1) * 512)
                P.ld("sp", wm[:], self.w_mod[i, :, sl].rearrange("(k p) n -> p k n", p=128), R=[self.w_mod], W=[wm])
                P.ld("sp", bm[:], self.b_mod[i:i + 1, sl].to_broadcast([2, 512]), R=[self.b_mod], W=[bm])
                for k in range(8):
                    P.mm(ps[:], ccs[:, k, :], wm[:, k, :], start=(k == 0), stop=(k == 7), R=[ccs, wm], W=[ps])
                P.tt("dve", mr[:], ps[:], bm[:], ALU.add, R=[ps, bm], W=[mr])
                P.ld("pool", self.modd[i, :, sl], mr[:], R=[mr], W=[self.modd])
        P.pop()

    def layer_consts(self, i):
        P = self.P
        L = NS()
        L.i = i
        modd = self.modd
        L.modcol = P.sbuf("modcol", [128, 2, 6, 8], F32)
        for m in range(2):
            P.ld("sp", L.modcol[:, m], modd[i, m].rearrange("(s k p) -> p s k", s=6, p=128), R=[modd], W=[L.modcol],
                 allow_slow_non_contiguous=True)
        gcol = P.sbuf("gcol", [128, 8], F32)
        P.ld("sp", gcol[:], self.g_mix[i].rearrange("(k p) -> p k", p=128), R=[self.g_mix], W=[gcol],
             allow_slow_non_contiguous=True)
        L.Ga = P.sbuf("Ga", [128, 2, 8], F32)
        for m in range(2):
            P.stt("dve", L.Ga[:, m], L.modcol[:, m, 1], 1.0, gcol[:], ALU.add, ALU.mult, R=[L.modcol, gcol], W=[L.Ga])
        def bc(name, src_ap, R):
            t = P.sbuf(name, [128, D], F32)
            P.ld("sp", t[:], src_ap.partition_broadcast(128), R=R, W=[t])
            return t
        L.gtf = [bc("gtf", modd[i, m, 5120:6144], [modd]) for m in range(2)]
        return L

    def tail_consts(self, L):
        P = self.P
        i = L.i
        modd = self.modd

        def bc(name, src_ap):
            t = P.sbuf(name, [128, D], F32)
            P.ld("sp", t[:], src_ap.partition_broadcast(128), R=[], W=[t])
            return t
        L.gta = [bc("gta", modd[i, m, 2048:3072]) for m in range(2)]
        L.Sf = [bc("Sf", modd[i, m, 3072:4096]) for m in range(2)]
        gffn = bc("gffn", self.g_ffn[i])
        L.Gf = []
        for m in range(2):
            t = bc("Gf", modd[i, m, 4096:5120])
            P.stt("dve", t[:], t[:], 1.0, gffn[:], ALU.add, ALU.mult, R=[t, gffn], W=[t])
            L.Gf.append(t)

    def rms_rstd(self, xt, sq_junk, ss, rstd, n):
        P = self.P
        P.act(sq_junk[:], xt[:], AF.Square, R=[xt], W=[sq_junk, ss], accum_out=ss[:])
        P.act(rstd[:], ss[:], AF.Sqrt, R=[ss], W=[rstd], scale=1.0 / n, bias=EPS)
        P.recip(rstd[:], rstd[:], R=[rstd], W=[rstd])

    def phase_A(self, L):
        P = self.P
        P.push()
        NB = 2
        xts = [P.sbuf("xt", [128, D], F32) for _ in range(NB)]
        xns = [P.sbuf("xn", [128, D], BF16) for _ in range(NB)]
        junk = P.sbuf("junk", [128, D], BF16)
        sss = [P.sbuf("ss", [128, 1], F32) for _ in range(NB)]
        rss = [P.sbuf("rs", [128, 1], F32) for _ in range(NB)]
        pts = [P.psum("pt", [128, 8, 128], BF16) for _ in range(NB)]
        hts = [P.sbuf("ht", [128, 8, 128], BF16) for _ in range(NB)]
        tmp = [P.sbuf("tmpA", [128, 8, 128], F32) for _ in range(NB)]
        for i in range(NT):
            b = i % NB
            m = 1 if i < 2 else 0
            xt, xn, ss, rs, pt, ht, tp = xts[b], xns[b], sss[b], rss[b], pts[b], hts[b], tmp[b]
            P.ld("sp", xt[:], self.XS[i * 128:(i + 1) * 128, :], R=[self.XS], W=[xt])
            self.rms_rstd(xt, junk, ss, rs, D)
            P.act(xn[:], xt[:], AF.Copy, R=[xt, rs], W=[xn], scale=rs[:])
            for k in range(8):
                P.tr(pt[:, k, :], xn[:, k * 128:(k + 1) * 128], self.ident[:], R=[xn, self.ident], W=[pt])
            P.tt("dve", tp[:], pt[:], L.Ga[:, m].unsqueeze(2).to_broadcast([128, 8, 128]), ALU.mult, R=[pt, L.Ga], W=[tp])
            P.tt("dve", ht[:], tp[:], L.modcol[:, m, 0].unsqueeze(2).to_broadcast([128, 8, 128]), ALU.add,
                 R=[tp, L.modcol], W=[ht])
            P.ld("pool", self.hT[:, i * 128:(i + 1) * 128].rearrange("(k p) t -> p k t", p=128), ht[:], R=[ht], W=[self.hT])
        P.pop()


    def cast_load(self, name, shape, src_ap, R):
        t = self.P.sbuf(name, shape, BF16)
        self.P.ld("pool", t[:], src_ap, R=R, W=[t])
        return t

    def mla_proj(self, L, j, need_ctx):
        P = self.P
        P.push()
        w_in = self.cast_load("w_in", [128, 8, 512], self.mla_w_in[j].rearrange("(k p) n -> p k n", p=128), [self.mla_w_in])
        w_uq = self.cast_load("w_uq", [128, 2, 2048], self.mla_w_uq[j].rearrange("(k p) n -> p k n", p=128), [self.mla_w_uq])
        w_uk = self.cast_load("w_uk", [128, 1024], self.mla_w_uk[j], [self.mla_w_uk])
        w_uv = self.cast_load("w_uv", [128, 1024], self.mla_w_uv[j], [self.mla_w_uv])
        gc = P.sbuf("gc", [128, 8], F32)
        gd = P.sbuf("gd", [128, 8], F32)
        P.ld("sp", gc[:], self.mla_gc[j], R=[self.mla_gc], W=[gc])
        P.tsc("dve", gd[:, 0:2], gc[:, 0:2], 16.0, None, ALU.mult, R=[gc], W=[gd])
        P.tsc("dve", gd[:, 2:3], gc[:, 2:3], float(np.sqrt(128.0)), None, ALU.mult, R=[gc], W=[gd])
        P.tsc("dve", gd[:, 3:5], gc[:, 3:5], 1.0, None, ALU.mult, R=[gc], W=[gd])
        P.tsc("dve", gd[:, 5:7], gc[:, 5:7], float(np.sqrt(192.0)), None, ALU.mult, R=[gc], W=[gd])
        tab = P.sbuf("tab", [128, TA], F32)
        P.ld("sp", tab[:], self.rope_tab[:, :], R=[self.rope_tab], W=[tab])
        ones, fold = self.ones, self.fold
        NB = 512
        hTbs = [P.sbuf("hTb", [128, 8, NB], BF16) for _ in range(2)]
        pbs = [P.psum("pb", [128, NB]) for _ in range(6)]
        psv = P.psum("psv", [128, 1024])
        cyc = {"pb": 0}

        def nps():
            t = pbs[cyc["pb"] % len(pbs)]
            cyc["pb"] += 1
            return t

        def rot(name, shape, dt, n=2):
            ts = [P.sbuf(name, shape, dt) for _ in range(n)]
            st = {"i": 0}

            def f():
                t = ts[st["i"] % n]
                st["i"] += 1
                return t
            return f
        sq_n = rot("sq_n", [128, NB], BF16, 3)
        sq_r = rot("sq_r", [64, NB], BF16, 2)
        rs_t = rot("rs_t", [128, NB], F32, 3)
        o_n = rot("o_n", [128, NB], BF16, 3)
        o_r = rot("o_r", [64, NB], BF16, 3)
        rt_t = rot("rt_t", [128, NB], BF16, 2)
        vt_t = rot("vt_t", [128, 1024], BF16, 2)
        qan = P.sbuf("qan", [128, 2, NB], BF16)
        kvan = P.sbuf("kvan", [128, NB], BF16)
        sqkr = P.sbuf("sqkr", [64, NB], BF16)
        krr = P.sbuf("krr", [64, NB], F32)

        def rstd_from(ss_ps, n, nb):
            rs = rs_t()
            P.act(rs[:, :nb], ss_ps[:, :nb], AF.Sqrt, R=[ss_ps], W=[rs], bias=float(n * EPS))
            P.recip(rs[:, :nb], rs[:, :nb], R=[rs], W=[rs])
            return rs

        blocks = [(0, 256)] + [(256 + NB * b, NB) for b in range(16)]
        for bi, (t0, nb) in enumerate(blocks):
            is_ctx = bi == 0
            hTb = hTbs[bi % 2]
            P.ld("sp", hTb[:, :, :nb], self.hT[:, t0:t0 + nb].rearrange("(k p) t -> p k t", p=128), R=[self.hT], W=[hTb])

            def proj_a(m):
                ps = nps()
                for k in range(8):
                    P.mm(ps[:, :nb], w_in[:, k, m * 128:(m + 1) * 128], hTb[:, k, :nb], start=(k == 0), stop=(k == 7),
                         R=[w_in, hTb], W=[ps])
                return ps
            psq = [proj_a(0), proj_a(1)]
            ss = nps()
            for m in range(2):
                sq = sq_n()
                P.act(sq[:, :nb], psq[m][:, :nb], AF.Square, R=[psq[m]], W=[sq])
                P.mm(ss[:, :nb], ones[:], sq[:, :nb], start=(m == 0), stop=(m == 1), R=[ones, sq], W=[ss])
            rs = rstd_from(ss, 256, nb)
            for m in range(2):
                P.stt("dve", qan[:, m, :nb], psq[m][:, :nb], gd[:, m:m + 1], rs[:, :nb], ALU.mult, ALU.mult,
                      R=[psq[m], gd, rs], W=[qan])
            pk = proj_a(2)
            sq = sq_n()
            P.act(sq[:, :nb], pk[:, :nb], AF.Square, R=[pk], W=[sq])
            ss = nps()
            P.mm(ss[:, :nb], ones[:], sq[:, :nb], R=[ones, sq], W=[ss])
            rs = rstd_from(ss, 128, nb)
            P.stt("dve", kvan[:, :nb], pk[:, :nb], gd[:, 2:3], rs[:, :nb], ALU.mult, ALU.mult, R=[pk, gd, rs], W=[kvan])
            pr = proj_a(3)
            P.act(sqkr[:, :nb], pr[0:64, :nb], AF.Square, R=[pr], W=[sqkr])
            rt = rt_t()
            P.stt("dve", rt[:, :nb], pr[:, :nb], gd[:, 6:7], tab[:, t0:t0 + nb], ALU.mult, ALU.mult, R=[pr, gd, tab], W=[rt])
            pf = nps()
            P.mm(pf[0:64, :nb], fold[:], rt[:, :nb], R=[fold, rt], W=[pf])
            P.cp("act", krr[:, :nb], pf[0:64, :nb], R=[pf], W=[krr])
            for h in range(8):
                pk = nps()
                P.mm(pk[:, :nb], w_uk[:, h * 128:(h + 1) * 128], kvan[:, :nb], R=[w_uk, kvan], W=[pk])
                sq = sq_n()
                P.act(sq[:, :nb], pk[:, :nb], AF.Square, R=[pk], W=[sq])
                ss = nps()
                P.mm(ss[:, :nb], ones[:], sq[:, :nb], start=True, stop=False, R=[ones, sq], W=[ss])
                P.mm(ss[:, :nb], ones[0:64, :], sqkr[:, :nb], start=False, stop=True, R=[ones, sqkr], W=[ss])
                rs = rstd_from(ss, 192, nb)
                kn = o_n()
                P.stt("dve", kn[:, :nb], pk[:, :nb], gd[:, 5:6], rs[:, :nb], ALU.mult, ALU.mult, R=[pk, gd, rs], W=[kn])
                P.ld("pool", self.KT[h, 0:128, t0:t0 + nb], kn[:, :nb], R=[kn], W=[self.KT])
                kr = o_r()
                P.tt("dve", kr[:, :nb], krr[:, :nb], rs[0:64, :nb], ALU.mult, R=[krr, rs], W=[kr])
                P.ld("pool", self.KT[h, 128:192, t0:t0 + nb], kr[:, :nb], R=[kr], W=[self.KT])
                if is_ctx and not need_ctx:
                    continue
                pqn, pqr = nps(), nps()
                for kc in range(2):
                    P.mm(pqn[:, :nb], w_uq[:, kc, h * 256:h * 256 + 128], qan[:, kc, :nb], start=(kc == 0), stop=(kc == 1),
                         R=[w_uq, qan], W=[pqn])
                for kc in range(2):
                    P.mm(pqr[:, :nb], w_uq[:, kc, h * 256 + 128:h * 256 + 256], qan[:, kc, :nb], start=(kc == 0), stop=(kc == 1),
                         R=[w_uq, qan], W=[pqr])
                sq = sq_n()
                P.act(sq[:, :nb], pqn[:, :nb], AF.Square, R=[pqn], W=[sq])
                sr = sq_r()
                P.act(sr[:, :nb], pqr[0:64, :nb], AF.Square, R=[pqr], W=[sr])
                ss = nps()
                P.mm(ss[:, :nb], ones[:], sq[:, :nb], start=True, stop=False, R=[ones, sq], W=[ss])
                P.mm(ss[:, :nb], ones[0:64, :], sr[:, :nb], start=False, stop=True, R=[ones, sr], W=[ss])
                rs = rstd_from(ss, 192, nb)
                qn = o_n()
                P.stt("dve", qn[:, :nb], pqn[:, :nb], gd[:, 3:4], rs[:, :nb], ALU.mult, ALU.mult, R=[pqn, gd, rs], W=[qn])
                P.ld("pool", self.QT[h, 0:128, t0:t0 + nb], qn[:, :nb], R=[qn], W=[self.QT])
                rt = rt_t()
                P.stt("dve", rt[:, :nb], pqr[:, :nb], gd[:, 4:5], tab[:, t0:t0 + nb], ALU.mult, ALU.mult, R=[pqr, gd, tab], W=[rt])
                pf = nps()
                P.mm(pf[0:64, :nb], fold[:], rt[:, :nb], R=[fold, rt], W=[pf])
                qr = o_r()
                P.tt("dve", qr[:, :nb], pf[0:64, :nb], rs[0:64, :nb], ALU.mult, R=[pf, rs], W=[qr])
                P.ld("pool", self.QT[h, 128:192, t0:t0 + nb], qr[:, :nb], R=[qr], W=[self.QT])
            for tt in range(nb // 128):
                for n in range(2):
                    P.mm(psv[:, n * 512:(n + 1) * 512], kvan[:, tt * 128:(tt + 1) * 128], w_uv[:, n * 512:(n + 1) * 512],
                         R=[kvan, w_uv], W=[psv])
                vt = vt_t()
                P.cp("act", vt[:], psv[:], R=[psv], W=[vt])
                P.ld("pool", self.V[t0 + tt * 128:t0 + (tt + 1) * 128, :], vt[:], R=[vt], W=[self.V])
        P.pop()

    def mla_attn(self, L, need_ctx):
        P = self.P
        P.push()
        NQ = 512
        G = 2
        Kns = [P.sbuf("Kn", [128, TA], BF16) for _ in range(2)]
        Krs = [P.sbuf("Kr", [128, TA], BF16) for _ in range(2)]
        Vhs = [P.sbuf("Vh", [128, NT, 128], BF16) for _ in range(2)]
        Qns = [P.sbuf("Qn", [128, NQ], BF16) for _ in range(2)]
        Qrs = [P.sbuf("Qr", [128, NQ], BF16) for _ in range(2)]
        pts = [P.sbuf("pT", [128, G * NQ], BF16) for _ in range(3)]
        QD = 384
        accs = [P.sbuf("accL", [128, G, QD], F32) for _ in range(2)]
        accp = [P.sbuf("accP", [128, G, NQ - QD], F32) for _ in range(2)]
        rls = [P.sbuf("rl", [128, NQ], F32) for _ in range(2)]
        ots = [P.sbuf("ot", [128, NQ], BF16) for _ in range(2)]
        for t_ in Krs + Qrs:
            P.memset("dve", t_[64:128, :], 0.0, W=[t_])
        onesf = P.sbuf("onesf", [128, 128], F32)
        P.ld("sp", onesf[:], self.cst[:, 128:256], R=[], W=[onesf])
        pss = [P.psum("ps_s", [128, G * NQ]) for _ in range(3)]
        po = P.psum("ps_o", [128, NQ])
        pl = P.psum("ps_l", [128, NQ])
        qblocks = [(256 + NQ * b, NQ, list(range(NT))) for b in range(16)]
        if need_ctx:
            qblocks = [(0, 256, [0, 1])] + qblocks
        st = {"g": 0}
        n_q = 0
        for h in range(8):
            Kn, Kr, Vh = Kns[h % 2], Krs[h % 2], Vhs[h % 2]
            P.ld("sp", Kn[:], self.KT[h, 0:128, :], R=[], W=[Kn])
            P.ld("sp", Kr[0:64, :], self.KT[h, 128:192, :], R=[], W=[Kr])
            P.ld("sp", Vh[:], self.V[:, h * 128:(h + 1) * 128].rearrange("(n p) v -> p n v", p=128), R=[], W=[Vh])
            for (t0, nq, keys) in qblocks:
                Qn, Qr = Qns[n_q % 2], Qrs[n_q % 2]
                acc, acp, rl, ot = accs[n_q % 2], accp[n_q % 2], rls[n_q % 2], ots[n_q % 2]
                nd = min(nq, QD)
                npq = nq - nd
                n_q += 1
                P.ld("sp", Qn[:, :nq], self.QT[h, 0:128, t0:t0 + nq], R=[], W=[Qn])
                P.ld("sp", Qr[0:64, :nq], self.QT[h, 128:192, t0:t0 + nq], R=[], W=[Qr])
                groups = [keys[i:i + G] for i in range(0, len(keys), G)]
                ng = len(groups)
                g0 = st["g"]
                st["g"] += ng

                def front(g):
                    ps, pt = pss[(g0 + g) % 3], pts[(g0 + g) % 3]
                    grp = groups[g]
                    for j, kt in enumerate(grp):
                        ks = slice(kt * 128, (kt + 1) * 128)
                        P.mm(ps[:, j * NQ:j * NQ + nq], Kn[:, ks], Qn[:, :nq], start=True, stop=False, R=[Kn, Qn], W=[ps])
                        P.mm(ps[:, j * NQ:j * NQ + nq], Kr[:, ks], Qr[:, :nq], start=False, stop=True, R=[Kr, Qr], W=[ps])
                    psv = ps[:].rearrange("p (g q) -> p g q", q=NQ)[:, :len(grp), :nq]
                    ptv = pt[:].rearrange("p (g q) -> p g q", q=NQ)[:, :len(grp), :nq]
                    P.act(ptv, psv, AF.Exp, R=[ps], W=[pt])
                    ptf = pt[:].rearrange("p (g q) -> p g q", q=NQ)
                    if g == 0:
                        P.cp("dve", acc[:, :, :nd], ptf[:, :, :nd], R=[pt], W=[acc])
                        if npq:
                            P.cp("pool", acp[:, :, :npq], ptf[:, :, nd:nq], R=[pt], W=[acp])
                    else:
                        P.tt("dve", acc[:, :, :nd], acc[:, :, :nd], ptf[:, :, :nd], ALU.add, R=[acc, pt], W=[acc])
                        if npq:
                            P.tt("pool", acp[:, :, :npq], acp[:, :, :npq], ptf[:, :, nd:nq], ALU.add, R=[acp, pt], W=[acp])

                def back(g):
                    pt = pts[(g0 + g) % 3]
                    grp = groups[g]
                    for j, kt in enumerate(grp):
                        P.mm(po[:, :nq], Vh[:, kt, :], pt[:, j * NQ:j * NQ + nq], start=(g == 0 and j == 0),
                             stop=(g == ng - 1 and j == len(grp) - 1), R=[Vh, pt], W=[po])
                for g in range(min(2, ng)):
                    front(g)
                for g in range(ng):
                    if g + 2 < ng:
                        front(g + 2)
                    back(g)
                for j in range(G):
                    P.mm(pl[:, :nd], onesf[:], acc[:, j, :nd], start=(j == 0), stop=(j == G - 1), R=[onesf, acc], W=[pl])
                if npq:
                    for j in range(G):
                        P.mm(pl[:, nd:nq], onesf[:], acp[:, j, :npq], start=(j == 0), stop=(j == G - 1), R=[onesf, acp], W=[pl])
                P.recip(rl[:, :nq], pl[:, :nq], R=[pl], W=[rl])
                P.tt("dve", ot[:, :nq], po[:, :nq], rl[:, :nq], ALU.mult, R=[po, rl], W=[ot])
                P.ld("pool", self.OT[h * 128:(h + 1) * 128, t0:t0 + nq], ot[:, :nq], R=[ot], W=[Res("u")])
        P.pop()

    def mla_attn_v1(self, L, need_ctx):
        P = self.P
        P.push()
        NQ = 512
        Kns = [P.sbuf("Kn", [128, TA], BF16) for _ in range(2)]
        Krs = [P.sbuf("Kr", [64, TA], BF16) for _ in range(2)]
        Vhs = [P.sbuf("Vh", [128, NT, 128], BF16) for _ in range(2)]
        Qns = [P.sbuf("Qn", [128, NQ], BF16) for _ in range(2)]
        Qrs = [P.sbuf("Qr", [64, NQ], BF16) for _ in range(2)]
        pts = [P.sbuf("pT", [128, NQ], BF16) for _ in range(3)]
        rls = [P.sbuf("rl", [128, NQ], F32) for _ in range(2)]
        ots = [P.sbuf("ot", [128, NQ], BF16) for _ in range(2)]
        pss = [P.psum("ps_s", [128, NQ]) for _ in range(3)]
        pos = [P.psum("ps_o", [128, NQ]) for _ in range(2)]
        pls = [P.psum("ps_l", [128, NQ]) for _ in range(2)]
        ones = self.ones
        qblocks = [(256 + NQ * b, NQ, list(range(NT))) for b in range(16)]
        if need_ctx:
            qblocks = [(0, 256, [0, 1])] + qblocks
        n_s = 0
        n_q = 0
        for h in range(8):
            Kn, Kr, Vh = Kns[h % 2], Krs[h % 2], Vhs[h % 2]
            P.ld("sp", Kn[:], self.KT[h, 0:128, :], R=[self.KT], W=[Kn])
            P.ld("sp", Kr[:], self.KT[h, 128:192, :], R=[self.KT], W=[Kr])
            P.ld("sp", Vh[:], self.V[:, h * 128:(h + 1) * 128].rearrange("(n p) v -> p n v", p=128), R=[self.V], W=[Vh])
            for (t0, nq, keys) in qblocks:
                Qn, Qr = Qns[n_q % 2], Qrs[n_q % 2]
                po, pl = pos[n_q % 2], pls[n_q % 2]
                rl, ot = rls[n_q % 2], ots[n_q % 2]
                n_q += 1
                P.ld("sp", Qn[:, :nq], self.QT[h, 0:128, t0:t0 + nq], R=[self.QT], W=[Qn])
                P.ld("sp", Qr[:, :nq], self.QT[h, 128:192, t0:t0 + nq], R=[self.QT], W=[Qr])
                for ki, kt in enumerate(keys):
                    ps, pt = pss[n_s % 3], pts[n_s % 3]
                    n_s += 1
                    ks = slice(kt * 128, (kt + 1) * 128)
                    P.mm(ps[:, :nq], Kn[:, ks], Qn[:, :nq], start=True, stop=False, R=[Kn, Qn], W=[ps])
                    P.mm(ps[:, :nq], Kr[:, ks], Qr[:, :nq], start=False, stop=True, R=[Kr, Qr], W=[ps])
                    P.act(pt[:, :nq], ps[:, :nq], AF.Exp, R=[ps], W=[pt])
                    first, last = ki == 0, ki == len(keys) - 1
                    P.mm(po[:, :nq], Vh[:, kt, :], pt[:, :nq], start=first, stop=last, R=[Vh, pt], W=[po])
                    P.mm(pl[:, :nq], ones[:], pt[:, :nq], start=first, stop=last, R=[ones, pt], W=[pl])
                P.recip(rl[:, :nq], pl[:, :nq], R=[pl], W=[rl])
                P.tt("dve", ot[:, :nq], po[:, :nq], rl[:, :nq], ALU.mult, R=[po, rl], W=[ot])
                P.ld("pool", self.OT[h * 128:(h + 1) * 128, t0:t0 + nq], ot[:, :nq], R=[ot], W=[self.OT])
        P.pop()


    def moe_state(self, L):
        P = self.P
        M = NS()
        M.logits = P.sbuf("logits", [128, NE, NT], F32)
        M.idxl = P.sbuf("idxl", [128, NE, 8], I32)
        M.idxc = P.sbuf("idxc", [32, NE], I32)
        M.wr = self.cast_load("wr", [128, 8, NE], self.moe_wr[L.i].rearrange("(k p) e -> p k e", p=128), [self.moe_wr])
        return M

    def tail_tiles(self, n_pt=2):
        P = self.P
        T = NS()
        T.n = 0
        T.xo = [P.sbuf("xo", [128, D], F32) for _ in range(2)]
        T.tmp = [P.sbuf("ttmp", [128, D], F32) for _ in range(2)]
        T.xn = [P.sbuf("txn", [128, D], F32) for _ in range(2)]
        T.junk = P.sbuf("tjunk", [128, D], BF16)
        T.ss = [P.sbuf("tss", [128, 1], F32) for _ in range(2)]
        T.rs = [P.sbuf("trs", [128, 1], F32) for _ in range(2)]
        T.hb = [P.sbuf("thb", [128, D], BF16) for _ in range(2)]
        T.pt = [P.psum("tpt", [128, 8, 128], BF16) for _ in range(n_pt)]
        T.hTt = [P.sbuf("thTt", [128, 8, 128], BF16) for _ in range(2)]
        T.plg = P.psum("tplg", [128, NE])
        return T

    def tail(self, L, M, T, i, ol_aps, ol_R, xs_rows=None, h_rows=None):
        P = self.P
        m = 1 if i < 2 else 0
        b = T.n % 2
        T.n += 1
        xo, tmp, xn, ss, rs, hb, pt, hTt = T.xo[b], T.tmp[b], T.xn[b], T.ss[b], T.rs[b], T.hb[b], T.pt[b % len(T.pt)], T.hTt[b]
        rows = slice(i * 128, (i + 1) * 128)
        xs_rows = self.XS[rows, :] if xs_rows is None else xs_rows
        h_rows = self.hrow[rows, :] if h_rows is None else h_rows
        P.ld("sp", xo[:], xs_rows, R=[], W=[xo])
        for ap, c0, w in ol_aps:
            P.tt("dve", tmp[:, c0:c0 + w], ap, L.gta[m][:, c0:c0 + w], ALU.mult, R=list(ol_R) + [L.gta[m]], W=[tmp])
        P.tt(TAIL_ENG, xn[:], tmp[:], xo[:], ALU.add, R=[tmp, xo], W=[xn])
        P.ld("pool", xs_rows, xn[:], R=[xn], W=[Res("u")])
        if L.skip_ctx_moe and i < 2:
            return
        P.act(T.junk[:], xn[:], AF.Square, R=[xn], W=[T.junk, ss], accum_out=ss[:])
        P.act(rs[:], ss[:], AF.Sqrt, R=[ss], W=[rs], scale=1.0 / D, bias=EPS)
        P.recip(rs[:], rs[:], R=[rs], W=[rs])
        P.stt("dve", tmp[:], xn[:], rs[:], L.Gf[m][:], ALU.mult, ALU.mult, R=[xn, rs, L.Gf[m]], W=[tmp])
        P.tt(TAIL_ENG, hb[:], tmp[:], L.Sf[m][:], ALU.add, R=[tmp, L.Sf[m]], W=[hb])
        P.ld("pool", h_rows, hb[:], R=[hb], W=[Res("u")])
        for k in range(8):
            P.tr(pt[:, k, :], hb[:, k * 128:(k + 1) * 128], self.ident[:], R=[hb, self.ident], W=[pt])
        P.cp("act", hTt[:], pt[:], R=[pt], W=[hTt])
        for k in range(8):
            P.mm(T.plg[:], hTt[:, k, :], M.wr[:, k, :], start=(k == 0), stop=(k == 7), R=[hTt, M.wr], W=[T.plg])
        P.cp("dve", M.logits[:, :, i], T.plg[:], R=[T.plg], W=[M.logits])

    def mla_out(self, L, M, j, need_ctx):
        P = self.P
        P.push()
        w_o = self.cast_load("w_o", [128, 8, D], self.mla_w_o[j].rearrange("(k p) n -> p k n", p=128), [self.mla_w_o])
        self.tail_consts(L)
        T = self.tail_tiles()
        OTs = [P.sbuf("OTt", [128, 8, 128], BF16) for _ in range(2)]
        pols = [P.psum("pol", [128, D]) for _ in range(2)]
        for i in range(NT):
            if i < 2 and not need_ctx:
                continue
            OTt, pol = OTs[i % 2], pols[i % 2]
            P.ld("sp", OTt[:], self.OT[:, i * 128:(i + 1) * 128].rearrange("(k p) t -> p k t", p=128), R=[], W=[OTt])
            for n in range(2):
                for k in range(8):
                    P.mm(pol[:, n * 512:(n + 1) * 512], OTt[:, k, :], w_o[:, k, n * 512:(n + 1) * 512],
                         start=(k == 0), stop=(k == 7), R=[OTt, w_o], W=[pol])
            self.tail(L, M, T, i, [(pol[:, 0:512], 0, 512), (pol[:, 512:1024], 512, 512)], [pol])
        P.pop()

    def moe_route(self, L, M):
        P = self.P
        i = L.i
        do_ctx = not L.skip_ctx_moe
        P.push()
        lg = M.logits
        lg_te = lg[:].rearrange("p e t -> p t e")
        aff = P.sbuf("aff", [128, NE, NT], F32)
        aff_te = aff[:].rearrange("p e t -> p t e")
        mx = P.sbuf("mx", [128, NT], F32)
        sm = P.sbuf("sm", [128, NT], F32)
        if not do_ctx:
            P.memset("dve", lg[:, :, 0:2], 0.0, W=[lg])
        P.red("dve", mx[:], lg_te, ALU.max, R=[lg], W=[mx])
        P.tt("dve", aff[:], lg[:], mx[:].unsqueeze(1).to_broadcast([128, NE, NT]), ALU.subtract, R=[lg, mx], W=[aff])
        P.act(aff[:], aff[:], AF.Exp, R=[aff], W=[aff])
        P.red("dve", sm[:], aff_te, ALU.add, R=[aff], W=[sm])
        P.recip(sm[:], sm[:], R=[sm], W=[sm])
        P.tt("dve", aff[:], aff[:], sm[:].unsqueeze(1).to_broadcast([128, NE, NT]), ALU.mult, R=[aff, sm], W=[aff])
        affc = P.sbuf("affc", [128, NT, NE], F32)
        P.cp("dve", affc[:], aff_te, R=[aff], W=[affc])
        affrow_v = self.affrow.t.rearrange("(t p) e -> p t e", p=128)
        aff_w = []
        if getattr(L, "fmap", False):
            pieces = [(affrow_v[:, 0:2, :], affc[:, 0:2, :])]
            av = self.affrow.t[256:, :].rearrange("(p q) e -> p q e", q=64)
            for q in range(4):
                pieces.append((av[:, q * 16:(q + 1) * 16, :], affc[:, 2 + q * 16:2 + (q + 1) * 16, :]))
        else:
            pieces = [(affrow_v[:, q * 11:(q + 1) * 11, :], affc[:, q * 11:(q + 1) * 11, :]) for q in range(6)]
        for dst, src in pieces:
            r_ = Res("affw")
            aff_w.append(r_)
            P.ld("pool", dst, src, R=[affc], W=[r_])
        tokid_sb = self.tokid_f_sb if getattr(L, "fmap", False) else self.tokid_sb
        lo = P.sbuf("lo", [128, 2, NE], F32)
        mid = P.sbuf("mid", [128, 2, NE], F32)
        cnt = P.sbuf("cnt", [128, 2, NE], F32)
        gew = P.sbuf("gew", [128, 2, NE], F32)
        cmp_ = P.sbuf("cmp", [128, NE, NT], BF16)
        cmp_f = cmp_[:].rearrange("p e t -> p (e t)")
        cps = P.psum("cps", [128, 1536])
        cps_v = cps[:, 0:NE * NT].rearrange("p (e t) -> p e t", t=NT)
        P.memset("dve", lo[:], 0.0, W=[lo])

        def count(thr):
            P.tt("dve", cmp_[:, :, 2:NT], aff[:, :, 2:NT], thr[:, 0].unsqueeze(2).to_broadcast([128, NE, NT - 2]), ALU.is_ge,
                 R=[aff, thr], W=[cmp_])
            P.tt("dve", cmp_[:, :, 0:2], aff[:, :, 0:2], thr[:, 1].unsqueeze(2).to_broadcast([128, NE, 2]), ALU.is_ge,
                 R=[aff, thr], W=[cmp_])
            for n, (c0, w) in enumerate(((0, 512), (512, 512), (1024, 32))):
                P.mm(cps[:, c0:c0 + w], self.ones[:], cmp_f[:, c0:c0 + w], R=[self.ones, cmp_], W=[cps])

        NIT = 26
        for it in range(NIT):
            w = 0.5 ** (it + 1)
            P.tsc("dve", mid[:], lo[:], w, None, ALU.add, R=[lo], W=[mid])
            count(mid)
            P.red("dve", cnt[:, 0], cps_v[:, :, 2:NT], ALU.add, R=[cps], W=[cnt])
            P.red("dve", cnt[:, 1], cps_v[:, :, 0:2], ALU.add, R=[cps], W=[cnt])
            P.tsc("dve", gew[:, 0], cnt[:, 0], 1023.5, w, ALU.is_ge, ALU.mult, R=[cnt], W=[gew])
            P.tsc("dve", gew[:, 1], cnt[:, 1], 31.5, w, ALU.is_ge, ALU.mult, R=[cnt], W=[gew])
            P.tt("dve", lo[:], lo[:], gew[:], ALU.add, R=[lo, gew], W=[lo])
        count(lo)
        sel = cmp_
        tot = P.sbuf("tot", [128, NE, NT], F32)
        P.cp("dve", tot[:], cps_v, R=[cps], W=[tot])
        sa = P.sbuf("sa", [128, NE, 64], F32)
        sb = P.sbuf("sb", [128, NE, 64], F32)
        P.cp("dve", sa[:], tot[:, :, 2:NT], R=[tot], W=[sa])
        cur, oth = sa, sb
        sh = 1
        while sh < 64:
            P.cp("dve", oth[:, :, 0:sh], cur[:, :, 0:sh], R=[cur], W=[oth])
            P.tt("dve", oth[:, :, sh:64], cur[:, :, sh:64], cur[:, :, 0:64 - sh], ALU.add, R=[cur], W=[oth])
            cur, oth = oth, cur
            sh *= 2
        offs = P.sbuf("offs", [128, NE, NT], F32)
        P.tt("dve", offs[:, :, 2:NT], cur[:], tot[:, :, 2:NT], ALU.subtract, R=[cur, tot], W=[offs])
        P.tsc("dve", offs[:, :, 2:NT], offs[:, :, 2:NT], 32.0, None, ALU.add, R=[offs], W=[offs])
        P.memset("dve", offs[:, :, 0:1], 0.0, W=[offs])
        P.cp("dve", offs[:, :, 1:2], tot[:, :, 0:1], R=[tot], W=[offs])
        ips = P.psum("ips", [128, 1536])
        for t in range(NT):
            P.mm(ips[:, t * NE:(t + 1) * NE], self.ustrict[:], sel[:, :, t], R=[self.ustrict, sel], W=[ips])
        ips_v = ips[:, 0:NT * NE].rearrange("p (t e) -> p e t", e=NE)
        slot = P.sbuf("slot", [128, NE, NT], F32)
        P.tt("dve", slot[:], ips_v, offs[:], ALU.add, R=[ips, offs], W=[slot])
        P.stt("dve", slot[:], slot[:], self.dumpc[:, 0:1], sel[:], ALU.subtract, ALU.mult, R=[slot, self.dumpc, sel], W=[slot])
        P.tsc("dve", slot[:], slot[:], self.dumpc[:, 0:1], None, ALU.add, R=[slot, self.dumpc], W=[slot])
        P.tt("dve", slot[:], slot[:], self.eoffs[:].unsqueeze(2).to_broadcast([128, NE, NT]), ALU.add, R=[slot, self.eoffs], W=[slot])
        smap = P.sbuf("smap", [128, NE, NT], I32)
        P.cp("dve", smap[:], slot[:], R=[slot], W=[smap])
        idx_res = []
        for t in range(NT):
            if t < 2 and not do_ctx:
                continue
            for e in range(NE):
                r_ = Res("idxw")
                idx_res.append(r_)
                P.scatter(self.idxd[:, :], tokid_sb[:, t:t + 1], smap[:, e, t:t + 1], R=[smap, tokid_sb], W=[r_])
        idxl, idxc = M.idxl, M.idxc
        P.ld("sp", idxl[:], self.idxd.t.rearrange("(e s) o -> e (s o)", e=NE)[:, 32:1056].rearrange("e (p k) -> p e k", k=8), R=idx_res, W=[idxl])
        if do_ctx:
            P.ld("sp", idxc[:], self.idxd.t.rearrange("(e s) o -> e (s o)", e=NE)[:, 0:32].rearrange("e p -> p e"), R=idx_res, W=[idxc],
                 allow_slow_non_contiguous=True)
        P.pop()

    def moe_experts(self, L, M):
        P = self.P
        i = L.i
        do_ctx = not L.skip_ctx_moe
        idxl, idxc = M.idxl, M.idxc
        aff_w = []
        P.push()
        NCH = 9 if do_ctx else 8
        NTOK = 1056 if do_ctx else 1024
        wsets = [[P.sbuf("wexp", [128, 8, D], BF16) for _ in range(3)] for _ in range(2)]
        xes = [P.sbuf("xe", [128, 9, D], BF16) for _ in range(1)]
        gas = [P.sbuf("ga", [128, 9, NE], F32) for _ in range(2)]
        xeT = P.sbuf("xeT", [128, 8, 1056], BF16)
        hid = P.sbuf("hid", [128, 8, 1056], BF16)
        sg = [P.sbuf("sg", [128, 512], F32) for _ in range(2)]
        ys = [P.sbuf("ys", [128, D], F32) for _ in range(2)]
        ptp = [P.psum("ptp", [128, 8, 128], BF16) for _ in range(1)]
        pbank = [P.psum("pbk", [128, 512]) for _ in range(6)]
        st = {"b": 0, "y": 0}

        def bank():
            t = pbank[st["b"] % len(pbank)]
            st["b"] += 1
            return t
        srcs = (self.moe_wg, self.moe_wu, self.moe_wd)

        def prefetch_w(e):
            ws = wsets[e % 2]
            for a in range(3):
                P.ld("pool", ws[a][:], srcs[a][i, e].rearrange("(k p) n -> p k n", p=128), R=[], W=[ws[a]])

        def prefetch(e):
            xe, ga = xes[0], gas[e % 2]
            for k in range(8):
                P.gather(xe[:, k, :], self.hrow[:, :], idxl[:, e, k:k + 1], R=[idxl], W=[xe])
                P.gather(ga[:, k, :], self.affrow[:, :], idxl[:, e, k:k + 1], R=[idxl] + aff_w, W=[ga])
            if do_ctx:
                P.gather(xe[0:32, 8, :], self.hrow[:, :], idxc[0:32, e:e + 1], R=[idxc], W=[xe])
                P.gather(ga[0:32, 8, :], self.affrow[:, :], idxc[0:32, e:e + 1], R=[idxc] + aff_w, W=[ga])

        grp = [[Res("scA") for _ in range(9)], [Res("scB") for _ in range(9)]]
        prefetch_w(0)
        prefetch(0)
        for e in range(NE):
            if e + 1 < NE:
                prefetch_w(e + 1)
            wg, wu, wd = wsets[e % 2]
            xe, ga = xes[0], gas[e % 2]
            for k in range(NCH):
                rows = 128 if k < 8 else 32
                pt = ptp[0]
                for dk in range(8):
                    P.tr(pt[:, dk, 0:rows], xe[0:rows, k, dk * 128:(dk + 1) * 128], self.ident[0:rows, 0:rows],
                         R=[xe, self.ident], W=[pt])
                P.cp("act", xeT[:, :, k * 128:k * 128 + rows], pt[:, :, 0:rows], R=[pt], W=[xeT])
            if e + 1 < NE:
                prefetch(e + 1)
            nblks = [(0, 512), (512, 512)] + ([(1024, 32)] if do_ctx else [])
            for f in range(8):
                fs = slice(f * 128, (f + 1) * 128)
                for (c0, w) in nblks:
                    pg, pu = bank(), bank()
                    for dk in range(8):
                        P.mm(pg[:, :w], wg[:, dk, fs], xeT[:, dk, c0:c0 + w], start=(dk == 0), stop=(dk == 7), R=[wg, xeT], W=[pg])
                    for dk in range(8):
                        P.mm(pu[:, :w], wu[:, dk, fs], xeT[:, dk, c0:c0 + w], start=(dk == 0), stop=(dk == 7), R=[wu, xeT], W=[pu])
                    s_ = sg[st["y"] % 2]
                    st["y"] += 1
                    P.act(s_[:, :w], pg[:, :w], AF.Silu, R=[pg], W=[s_])
                    P.tt("dve", hid[:, f, c0:c0 + w], s_[:, :w], pu[:, :w], ALU.mult, R=[s_, pu], W=[hid])
            for k in range(NCH):
                rows = 128 if k < 8 else 32
                m = 0 if k < 8 else 1
                y = ys[k % 2]
                for n in range(2):
                    py = bank()
                    for f in range(8):
                        P.mm(py[0:rows, :], hid[:, f, k * 128:k * 128 + rows], wd[:, f, n * 512:(n + 1) * 512],
                             start=(f == 0), stop=(f == 7), R=[hid, wd], W=[py])
                    P.stt("dve", y[0:rows, n * 512:(n + 1) * 512], py[0:rows, :], ga[0:rows, k, e:e + 1],
                          L.gtf[m][0:rows, n * 512:(n + 1) * 512], ALU.mult, ALU.mult, R=[py, ga, L.gtf[m]], W=[y])
                ia = idxl[:, e, k:k + 1] if k < 8 else idxc[0:32, e:e + 1]
                P.scatter(self.XS[:, :], y[0:rows, :], ia, R=[y, idxl, idxc] + grp[(e + 1) % 2], W=[grp[e % 2][k]],
                          add=True, bound=TA - 1)
        P.pop()


    def ssd_in(self, L):
        P = self.P
        P.push()
        w = P.sbuf("ssd_w", [128, 8, 5184], BF16)
        wsrc = self.ssd_w_in.t.rearrange("(k p) n -> p k n", p=128)
        for c0 in range(0, 5184, 1728):
            P.ld("pool", w[:, :, c0:c0 + 1728], wsrc[:, :, c0:c0 + 1728], R=[], W=[w])
        cw = P.sbuf("cw", [128, 24, 3], F32)
        for k in range(3):
            P.ld("sp", cw[:, :, k], self.ssd_conv_w[k].rearrange("(f p) -> p f", p=128), R=[], W=[cw])
        cb = P.sbuf("cb", [128, 24], F32)
        P.ld("sp", cb[:], self.ssd_conv_b.t.rearrange("(f p) -> p f", p=128), R=[], W=[cb])
        dtb = P.sbuf("dtb", [128, 64], F32)
        P.ld("sp", dtb[:], self.ssd_dt_bias.t.partition_broadcast(128), R=[], W=[dtb])
        hTws = [P.sbuf("hTw", [128, 8, 258], BF16) for _ in range(2)]
        cv = P.sbuf("cv", [128, 24, 256], BF16)
        accs = [P.sbuf("acc", [128, 256], F32) for _ in range(2)]
        xbs = [P.sbuf("xb_tm", [128, 2560], BF16) for _ in range(2)]
        zss = [P.sbuf("zs_tm", [128, 2048], BF16) for _ in range(2)]
        dtr = [P.sbuf("dtr", [128, 64], F32) for _ in range(2)]
        banks = [P.psum("sbk", [128, 512]) for _ in range(5)]
        ptrs = [P.psum("sptr", [128, 8, 128], BF16) for _ in range(2)]
        st = {"b": 0, "t": 0, "a": 0}

        def bank():
            t = banks[st["b"] % len(banks)]
            st["b"] += 1
            return t
        BTv = self.BT_d.t.rearrange("(g p) t -> p g t", p=128)
        CTv = self.CT_d.t.rearrange("(g p) t -> p g t", p=128)
        hTv = self.hT.t.rearrange("(k p) t -> p k t", p=128)
        wins = [(0, True, True)] + [(256 + 256 * q, q == 0, q == 31) for q in range(32)]
        for wi, (t0, lz, rz) in enumerate(wins):
            hTw = hTws[wi % 2]
            lo = t0 - (0 if lz else 1)
            hi = t0 + 256 + (0 if rz else 1)
            c_lo = 1 if lz else 0
            P.ld("sp", hTw[:, :, c_lo:c_lo + (hi - lo)], hTv[:, :, lo:hi], R=[], W=[hTw])
            if lz:
                P.memset("pool", hTw[:, :, 0:1], 0.0, W=[hTw])
            if rz:
                P.memset("pool", hTw[:, :, 257:258], 0.0, W=[hTw])
            for fc in range(24):
                ps = bank()
                for k in range(8):
                    P.mm(ps[:, 0:258], w[:, k, 2048 + fc * 128:2048 + (fc + 1) * 128], hTw[:, k, :], start=(k == 0), stop=(k == 7),
                         R=[w, hTw], W=[ps])
                acc = accs[st["a"] % 2]
                st["a"] += 1
                P.tsc("dve", acc[:], ps[:, 0:256], cw[:, fc, 0:1], None, ALU.mult, R=[ps, cw], W=[acc])
                P.stt("dve", acc[:], ps[:, 1:257], cw[:, fc, 1:2], acc[:], ALU.mult, ALU.add, R=[ps, cw, acc], W=[acc])
                P.stt("dve", acc[:], ps[:, 2:258], cw[:, fc, 2:3], acc[:], ALU.mult, ALU.add, R=[ps, cw, acc], W=[acc])
                P.act(cv[:, fc, :], acc[:], AF.Silu, R=[acc, cb], W=[cv], bias=cb[:, fc:fc + 1])
            P.ld("pool", BTv[:, :, t0:t0 + 256], cv[:, 16:20, :], R=[cv], W=[Res("u")])
            P.ld("pool", CTv[:, :, t0:t0 + 256], cv[:, 20:24, :], R=[cv], W=[Res("u")])
            for tt in range(2):
                rows = slice(t0 + tt * 128, t0 + (tt + 1) * 128)
                xb, zs, dr_ = xbs[tt], zss[tt], dtr[tt]
                for f0 in (0, 8, 16):
                    nf = min(8, 20 - f0)
                    ptr = ptrs[st["t"] % 2]
                    st["t"] += 1
                    for q in range(nf):
                        P.tr(ptr[:, q, :], cv[:, f0 + q, tt * 128:(tt + 1) * 128], self.ident[:], R=[cv, self.ident], W=[ptr])
                    P.cp("act", xb[:, f0 * 128:(f0 + nf) * 128], ptr[:, 0:nf, :], R=[ptr], W=[xb])
                P.ld("pool", self.xs_d[rows, :], xb[:, 0:2048], R=[xb], W=[Res("u")])
                P.ld("pool", self.B_d[rows, :], xb[:, 2048:2560], R=[xb], W=[Res("u")])
                for nb in range(4):
                    ps = bank()
                    for k in range(8):
                        P.mm(ps[:], hTw[:, k, 1 + tt * 128:1 + (tt + 1) * 128], w[:, k, nb * 512:(nb + 1) * 512], start=(k == 0), stop=(k == 7),
                             R=[w, hTw], W=[ps])
                    P.act(zs[:, nb * 512:(nb + 1) * 512], ps[:], AF.Silu, R=[ps], W=[zs])
                P.ld("pool", self.zs_d[rows, :], zs[:], R=[zs], W=[Res("u")])
                ps = bank()
                for k in range(8):
                    P.mm(ps[:, 0:64], hTw[:, k, 1 + tt * 128:1 + (tt + 1) * 128], w[:, k, 5120:5184], start=(k == 0), stop=(k == 7),
                         R=[w, hTw], W=[ps])
                P.tt("dve", dr_[:], ps[:, 0:64], dtb[:], ALU.add, R=[ps, dtb], W=[dr_])
                P.act(dr_[:], dr_[:], AF.Exp, R=[dr_], W=[dr_])
                P.act(dr_[:], dr_[:], AF.Ln, R=[dr_], W=[dr_], bias=1.0)
                P.ld("pool", self.dt_d[rows, :], dr_[:], R=[dr_], W=[Res("u")])
        P.pop()

    def ssd_scan(self, L, dr):
        P = self.P
        P.push()
        c2 = P.sbuf("c2", [128, 640], F32)
        P.ld("sp", c2[:], self.cst2[:, :], R=[], W=[c2])
        LE, GE, GT, LT, onesf = (c2[:, q * 128:(q + 1) * 128] for q in range(5))
        m1 = LE if dr == 0 else GE
        lm = GT if dr == 0 else LT
        a_b = P.sbuf("a_b", [128, 32], F32)
        P.ld("sp", a_b[:], self.ssd_a_log[dr].partition_broadcast(128), R=[], W=[a_b])
        P.act(a_b[:], a_b[:], AF.Exp, R=[a_b], W=[a_b])
        P.tsc("dve", a_b[:], a_b[:], -1.0, None, ALU.mult, R=[a_b], W=[a_b])
        stf = [P.sbuf("stf", [128, 512], F32) for _ in range(4)]
        stb = [P.sbuf("stb", [128, 512], BF16) for _ in range(4)]
        for g in range(4):
            P.memset("dve", stf[g][:], 0.0, W=[stf[g]])
            P.memset("dve", stb[g][:], 0.0, W=[stb[g]])
        NB = 2
        xss = [P.sbuf("s_xs", [128, 2048], BF16) for _ in range(NB)]
        Bts = [P.sbuf("s_B", [128, 512], BF16) for _ in range(NB)]
        BTs = [P.sbuf("s_BT", [128, 4, 128], BF16) for _ in range(NB)]
        CTs = [P.sbuf("s_CT", [128, 4, 128], BF16) for _ in range(NB)]
        dts = [P.sbuf("s_dt", [128, 64], F32) for _ in range(NB)]
        dtas = [P.sbuf("s_dta", [128, 32], F32) for _ in range(NB)]
        E3s = [P.sbuf("s_E3", [128, 3, 32], F32) for _ in range(NB)]
        xdts = [P.sbuf("s_xdt", [128, 2048], BF16) for _ in range(NB)]
        xdds = [P.sbuf("s_xdd", [128, 2048], BF16) for _ in range(NB)]
        ys = [P.sbuf("s_y", [128, 2048], F32) for _ in range(NB)]
        rhsEs = [P.sbuf("s_rhsE", [128, 8, 128], F32) for _ in range(2)]
        Lts = [P.sbuf("s_Lt", [128, 8, 128], BF16) for _ in range(2)]
        MTs = [P.sbuf("s_MT", [128, 8, 128], BF16) for _ in range(2)]
        cbms = [P.sbuf("s_cbm", [128, 128], BF16) for _ in range(2)]
        t1s = [P.sbuf("s_t1", [128, 512], F32) for _ in range(2)]
        shr = P.psum("shr", [128, 512])
        e3p = Tile("e3p", shr[:, 128:224].rearrange("p (a b) -> p a b", b=32))
        cbp = Tile("cbp", shr[:, 0:128])
        e3p._res = shr._res
        cbp._res = shr._res
        args = [P.psum("argp", [128, 8, 128]) for _ in range(2)]
        yp = [P.psum("yp", [128, 512]) for _ in range(1)]
        yop = P.psum("yop", [128, 512])
        sp_ = P.psum("sps", [128, 512])
        BTv = self.BT_d.t.rearrange("(g p) t -> p g t", p=128)
        CTv = self.CT_d.t.rearrange("(g p) t -> p g t", p=128)
        order = list(range(NT)) if dr == 0 else [1, 0] + list(range(NT - 1, 1, -1))
        yd = self.yf_d if dr == 0 else self.yb_d
        ng = 0
        for ci, c in enumerate(order):
            b = ci % NB
            rows = slice(c * 128, (c + 1) * 128)
            xs, Bt, BT, CT, dt, dta, E3, xdt, xdd, y = xss[b], Bts[b], BTs[b], CTs[b], dts[b], dtas[b], E3s[b], xdts[b], xdds[b], ys[b]
            P.ld("sp", xs[:], self.xs_d[rows, :], R=[], W=[xs])
            P.ld("sp", Bt[:], self.B_d[rows, :], R=[], W=[Bt])
            P.ld("sp", BT[:], BTv[:, :, rows], R=[], W=[BT])
            P.ld("sp", CT[:], CTv[:, :, rows], R=[], W=[CT])
            P.ld("sp", dt[:], self.dt_d[rows, :], R=[], W=[dt])
            dtd = dt[:, dr * 32:(dr + 1) * 32]
            P.tt("dve", dta[:], dtd, a_b[:], ALU.mult, R=[dt, a_b], W=[dta])
            P.mm(e3p[:, 0, :], m1, dta[:], R=[c2, dta], W=[e3p])
            P.mm(e3p[:, 1, :], lm, dta[:], R=[c2, dta], W=[e3p])
            P.mm(e3p[:, 2, :], onesf, dta[:], R=[c2, dta], W=[e3p])
            P.act(E3[:], e3p[:], AF.Exp, R=[e3p], W=[E3])
            xs3 = xs[:].rearrange("p (h q) -> p h q", q=64)
            P.tt("dve", xdt[:].rearrange("p (h q) -> p h q", q=64), xs3, dtd.unsqueeze(2).to_broadcast([128, 32, 64]), ALU.mult,
                 R=[xs, dt], W=[xdt])
            P.tt("pool", xdd[:].rearrange("p (h q) -> p h q", q=64), xdt[:].rearrange("p (h q) -> p h q", q=64),
                 E3[:, 1, :].unsqueeze(2).to_broadcast([128, 32, 64]), ALU.mult, R=[xdt, E3], W=[xdd])
            for g in range(4):
                hs = slice(g * 8, (g + 1) * 8)
                q2 = ng % 2
                ng += 1
                rhsE, Lt, MT, cbm, t1 = rhsEs[q2], Lts[q2], MTs[q2], cbms[q2], t1s[q2]
                arg = args[ng % 2]
                P.tt("pool", rhsE[:], dta[:, hs].unsqueeze(2).to_broadcast([128, 8, 128]), m1.unsqueeze(1).to_broadcast([128, 8, 128]),
                     ALU.mult, R=[dta, c2], W=[rhsE])
                for hf in range(2):
                    P.mm(arg[:, hf * 4:(hf + 1) * 4, :], lm, rhsE[:, hf * 4:(hf + 1) * 4, :], R=[c2, rhsE], W=[arg])
                P.act(Lt[:], arg[:], AF.Exp, R=[arg], W=[Lt])
                P.mm(cbp[:], BT[:, g, :], CT[:, g, :], R=[BT, CT], W=[cbp])
                P.tt("dve", cbm[:], cbp[:], m1, ALU.mult, R=[cbp, c2], W=[cbm])
                P.tt("dve", MT[:], Lt[:], cbm[:].unsqueeze(1).to_broadcast([128, 8, 128]), ALU.mult, R=[Lt, cbm], W=[MT])
                ypg = yp[0]
                for hl in range(8):
                    h = g * 8 + hl
                    P.mm(ypg[:, hl * 64:(hl + 1) * 64], MT[:, hl, :], xdt[:, h * 64:(h + 1) * 64], R=[MT, xdt], W=[ypg])
                P.mm(yop[:], CT[:, g, :], stb[g][:], R=[CT, stb[g]], W=[yop])
                P.tt("dve", t1[:].rearrange("p (h q) -> p h q", q=64), yop[:].rearrange("p (h q) -> p h q", q=64),
                     E3[:, 0, hs].unsqueeze(2).to_broadcast([128, 8, 64]), ALU.mult, R=[yop, E3], W=[t1])
                P.tt("dve", y[:, g * 512:(g + 1) * 512], t1[:], ypg[:], ALU.add, R=[t1, ypg], W=[y])
                P.mm(sp_[:], Bt[:, g * 128:(g + 1) * 128], xdd[:, g * 512:(g + 1) * 512], R=[Bt, xdd], W=[sp_])
                P.tt("pool", stf[g][:].rearrange("p (h q) -> p h q", q=64), stf[g][:].rearrange("p (h q) -> p h q", q=64),
                     E3[:, 2, hs].unsqueeze(2).to_broadcast([128, 8, 64]), ALU.mult, R=[stf[g], E3], W=[stf[g]])
                P.tt("dve", stf[g][:], stf[g][:], sp_[:], ALU.add, R=[stf[g], sp_], W=[stf[g]])
                P.cp("act", stb[g][:], stf[g][:], R=[stf[g]], W=[stb[g]])
            P.ld("pool", yd[rows, :], y[:], R=[y], W=[Res("u")])
        P.pop()

    def ssd_out(self, L, M):
        P = self.P
        P.push()
        w_out = self.cast_load("ssd_wo", [128, 16, D], self.ssd_w_out.t.rearrange("(k p) n -> p k n", p=128), [])
        self.tail_consts(L)
        T = self.tail_tiles(n_pt=1)
        D_b = P.sbuf("D_b", [128, 32], F32)
        P.ld("sp", D_b[:], self.ssd_d.t.partition_broadcast(128), R=[], W=[D_b])
        gn_b = P.sbuf("gn_b", [128, 2048], F32)
        P.ld("sp", gn_b[:], self.ssd_g_norm.t.partition_broadcast(128), R=[], W=[gn_b])
        yfs = [P.sbuf("o_yf", [128, 2048], F32) for _ in range(2)]
        ybs = [P.sbuf("o_yb", [128, 2048], F32) for _ in range(2)]
        xss = [P.sbuf("o_xs", [128, 2048], BF16) for _ in range(2)]
        zss = [P.sbuf("o_zs", [128, 2048], BF16) for _ in range(2)]
        gats = [P.sbuf("o_gat", [128, 2048], F32) for _ in range(2)]
        junk = P.sbuf("o_junk", [128, 512], BF16)
        ss4s = [P.sbuf("o_ss4", [128, 4], F32) for _ in range(2)]
        rs4s = [P.sbuf("o_rs4", [128, 4], F32) for _ in range(2)]
        nrms = [P.sbuf("o_nrm", [128, 2048], BF16) for _ in range(2)]
        nTs = [P.sbuf("o_nT", [128, 16, 128], BF16) for _ in range(2)]
        ptr = [P.psum("o_ptr", [128, 8, 128], BF16) for _ in range(2)]
        pols = [P.psum("o_pol", [128, D]) for _ in range(2)]
        for i in range(NT):
            b = i % 2
            rows = slice(i * 128, (i + 1) * 128)
            yf, yb, xs, zs = yfs[b], ybs[b], xss[b], zss[b]
            gat, ss4, rs4, nrm, nT, pol = gats[b], ss4s[b], rs4s[b], nrms[b], nTs[b], pols[b]
            P.ld("sp", yf[:], self.yf_d[rows, :], R=[], W=[yf])
            P.ld("sp", yb[:], self.yb_d[rows, :], R=[], W=[yb])
            P.ld("sp", xs[:], self.xs_d[rows, :], R=[], W=[xs])
            P.ld("sp", zs[:], self.zs_d[rows, :], R=[], W=[zs])
            P.tt("dve", yf[:], yf[:], yb[:], ALU.add, R=[yf, yb], W=[yf])
            P.tt("dve", yb[:].rearrange("p (h q) -> p h q", q=64), xs[:].rearrange("p (h q) -> p h q", q=64),
                 D_b[:].unsqueeze(2).to_broadcast([128, 32, 64]), ALU.mult, R=[xs, D_b, yb], W=[yb])
            P.tt("dve", yf[:], yf[:], yb[:], ALU.add, R=[yf, yb], W=[yf])
            P.tt("dve", gat[:], yf[:], zs[:], ALU.mult, R=[yf, zs], W=[gat])
            for g in range(4):
                P.act(junk[:], gat[:, g * 512:(g + 1) * 512], AF.Square, R=[gat], W=[junk, ss4], accum_out=ss4[:, g:g + 1])
            P.act(rs4[:], ss4[:], AF.Sqrt, R=[ss4], W=[rs4], scale=1.0 / 512, bias=EPS)
            P.recip(rs4[:], rs4[:], R=[rs4], W=[rs4])
            for g in range(4):
                gs = slice(g * 512, (g + 1) * 512)
                P.stt("dve", nrm[:, gs], gat[:, gs], rs4[:, g:g + 1], gn_b[:, gs], ALU.mult, ALU.mult, R=[gat, rs4, gn_b], W=[nrm])
            for hf in range(2):
                for q in range(8):
                    fk = hf * 8 + q
                    P.tr(ptr[hf][:, q, :], nrm[:, fk * 128:(fk + 1) * 128], self.ident[:], R=[nrm, self.ident], W=[ptr[hf]])
                P.cp("act", nT[:, hf * 8:(hf + 1) * 8, :], ptr[hf][:], R=[ptr[hf]], W=[nT])
            for n in range(2):
                for fk in range(16):
                    P.mm(pol[:, n * 512:(n + 1) * 512], nT[:, fk, :], w_out[:, fk, n * 512:(n + 1) * 512], start=(fk == 0), stop=(fk == 15),
                         R=[nT, w_out], W=[pol])
            self.tail(L, M, T, i, [(pol[:, 0:512], 0, 512), (pol[:, 512:1024], 512, 512)], [pol])
        P.pop()

    def fnet_f1(self, L):
        P = self.P
        P.push()
        DF = self.cast_load("DF", [128, 2, 512], self.fcst[:, 0:1024].rearrange("p (q n) -> p q n", q=2), [])
        hTv = self.hT.t.rearrange("(k p) t -> p k t", p=128)
        hTs = [P.sbuf("f_hT", [128, 8, 128], BF16) for _ in range(2)]
        yts = [P.sbuf("f_yt", [128, 2, 4, 256], BF16) for _ in range(2)]
        yps = [P.psum("f_yps", [128, 4, 512]) for _ in range(2)]
        for i in range(NT):
            hTt, yt, yp = hTs[i % 2], yts[i % 2], yps[i % 2]
            rows = slice(i * 128, (i + 1) * 128)
            P.ld("sp", hTt[:], hTv[:, :, rows], R=[], W=[hTt])
            for g in range(4):
                for q in range(2):
                    P.mm(yp[:, g, :], hTt[:, 2 * g + q, :], DF[:, q, :], start=(q == 0), stop=(q == 1), R=[hTt, DF], W=[yp])
            for c in range(2):
                P.cp("act" if c == 0 else "dve", yt[:, c], yp[:, :, c * 256:(c + 1) * 256], R=[yp], W=[yt])
            for c in range(2):
                P.ld("pool", self.Yd[c, rows, :], yt[:, c].rearrange("p g m -> p (g m)"), R=[yt], W=[Res("u")])
        P.pop()

    def fnet_f2(self, L):
        P = self.P
        P.push()
        W1 = self.cast_load("W1big", [128, 128], self.fcst[:, 2048:2176], [])
        Ydv = self.Yd.t[:, 256:, :].rearrange("c (a b) f -> c a b f", b=128)
        Zdv = self.Zd.t.rearrange("c k t f -> (c k) t f")
        Ins = [P.sbuf("f_In", [128, 4, D], BF16) for _ in range(2)]
        Zts = [P.sbuf("f_Zt", [128, 4, D], BF16) for _ in range(2)]
        zps = [P.psum("f_zps", [128, D]) for _ in range(3)]
        nz = 0
        for bt in range(32):
            In, Zt = Ins[bt % 2], Zts[bt % 2]
            for c in range(2):
                P.ld("sp", In[c * 64:(c + 1) * 64, :, :], Ydv[c, :, 4 * bt:4 * bt + 4, :], R=[], W=[In])
            for q in range(4):
                zp = zps[nz % 3]
                nz += 1
                for n in range(2):
                    P.mm(zp[:, n * 512:(n + 1) * 512], W1[:], In[:, q, n * 512:(n + 1) * 512], R=[W1, In], W=[zp])
                P.cp("act" if q % 2 == 0 else "dve", Zt[:, q, :], zp[:], R=[zp], W=[Zt])
            P.ld("pool", Zdv[:, 4 * bt:4 * bt + 4, :], Zt[:], R=[Zt], W=[Res("u")])
        P.pop()

    def fnet_f3(self, L, M):
        P = self.P
        P.push()
        MT = P.sbuf("f_MT", [128, 64, 256], BF16)
        for kb in range(8):
            P.ld("pool", MT[:, kb * 8:(kb + 1) * 8, :], self.fM[:, kb * 2048:(kb + 1) * 2048].rearrange("p (a n) -> p a n", a=8), R=[], W=[MT])
        w_o = self.cast_load("f_wo", [128, 8, D], self.fnet_w_o.t.rearrange("(k p) n -> p k n", p=128), [])
        DF2 = self.cast_load("DF2", [128, 2, 512], self.fcst[:, 1024:2048].rearrange("p (q n) -> p q n", q=2), [])
        self.tail_consts(L)
        T = self.tail_tiles()
        mps = P.psum("f_mps", [128, 1024])
        pol = P.psum("f_pol", [128, D])

        def outproj(mT_ap_fn):
            for n in range(2):
                for fc in range(8):
                    P.mm(pol[:, n * 512:(n + 1) * 512], mT_ap_fn(fc), w_o[:, fc, n * 512:(n + 1) * 512], start=(fc == 0), stop=(fc == 7),
                         R=[w_o] + mT_R, W=[pol])
        Yc = P.sbuf("f_Yc", [128, 2, 2, D], BF16)
        for c in range(2):
            for q in range(2):
                P.ld("sp", Yc[:, q, c, :], self.Yd[c, q * 128:(q + 1) * 128, :], R=[], W=[Yc])
        mTc = P.sbuf("f_mTc", [128, 8, 256], BF16)
        mcv = mps[:].rearrange("p (a k) -> p a k", k=256)
        for half in range(2):
            for f4 in range(4):
                fc = half * 4 + f4
                n = 0
                for q in range(2):
                    for c in range(2):
                        P.mm(mcv[:, f4, :], Yc[:, q, c, fc * 128:(fc + 1) * 128], DF2[:, q, c * 256:(c + 1) * 256], start=(n == 0), stop=(n == 3),
                             R=[Yc, DF2], W=[mps])
                        n += 1
            P.act(mTc[:, half * 4:(half + 1) * 4, :], mcv, AF.Copy, R=[mps], W=[mTc], scale=1.0 / 256.0)
        for i in range(2):
            mT_R = [mTc]
            outproj(lambda fc: mTc[:, fc, i * 128:(i + 1) * 128])
            self.tail(L, M, T, i, [(pol[:, 0:512], 0, 512), (pol[:, 512:1024], 512, 512)], [pol])
        Zrs = [P.sbuf("f_Zr", [128, D], BF16) for _ in range(2)]
        Zis = [P.sbuf("f_Zi", [128, D], BF16) for _ in range(2)]
        mTs = [P.sbuf("f_mT", [128, 8, 128], BF16) for _ in range(2)]
        mlv = mps[:].rearrange("p (a k) -> p a k", k=128)
        xsv = self.XS.t[256:, :].rearrange("(p q) d -> q p d", q=64)
        hrv = self.hrow.t[256:, :].rearrange("(p q) d -> q p d", q=64)
        sc = float(1.0 / np.sqrt(8192.0 * 256.0))
        for k1 in range(64):
            Zr, Zi, mT = Zrs[k1 % 2], Zis[k1 % 2], mTs[k1 % 2]
            P.ld("sp", Zr[:], self.Zd[0, k1], R=[], W=[Zr])
            P.ld("sp", Zi[:], self.Zd[1, k1], R=[], W=[Zi])
            for fc in range(8):
                P.mm(mlv[:, fc, :], Zr[:, fc * 128:(fc + 1) * 128], MT[:, k1, 0:128], start=True, stop=False, R=[Zr, MT], W=[mps])
                P.mm(mlv[:, fc, :], Zi[:, fc * 128:(fc + 1) * 128], MT[:, k1, 128:256], start=False, stop=True, R=[Zi, MT], W=[mps])
            P.act(mT[:], mlv, AF.Copy, R=[mps], W=[mT], scale=sc)
            mT_R = [mT]
            outproj(lambda fc: mT[:, fc, :])
            self.tail(L, M, T, 2 + k1, [(pol[:, 0:512], 0, 512), (pol[:, 512:1024], 512, 512)], [pol],
                      xs_rows=xsv[k1], h_rows=hrv[k1])
        P.pop()

    def layer(self, i, upto=None):
        P = self.P
        P.push()
        L = self.layer_consts(i)
        L.skip_ctx_moe = (i == 3)
        need_ctx = (i != 3)
        M = self.moe_state(L)
        self.phase_A(L)
        kind, j = i % 3, i // 3
        if kind == 0:
            self.mla_proj(L, j, need_ctx)
            if upto == "proj":
                P.pop(); return
            if i == 0 or ATTN_V2_ALL:
                self.mla_attn(L, need_ctx)
            else:
                self.mla_attn_v1(L, need_ctx)
            if upto == "attn":
                P.pop(); return
            self.mla_out(L, M, j, need_ctx)
        if kind == 1:
            self.ssd_in(L)
            if upto == "ssd_in":
                P.pop(); return
            self.ssd_scan(L, 0)
            self.ssd_scan(L, 1)
            if upto == "ssd_scan":
                P.pop(); return
            self.ssd_out(L, M)
        if kind == 2:
            L.fmap = True
            self.fnet_f1(L)
            self.fnet_f2(L)
            self.fnet_f3(L, M)
        if upto == "mix":
            P.pop(); return
        self.moe_route(L, M)
        if upto == "route":
            P.pop(); return
        self.moe_experts(L, M)
        P.pop()

def make_consts():
    cst = np.zeros((128, 1024), np.float32)
    cst[:, 0:128] = np.eye(128, dtype=np.float32)
    cst[:, 128:256] = 1.0
    for p in range(128):
        cst[p, 256 + (p % 64)] = 1.0
        cst[p, 320 + p + 1:448] = 1.0
        cst[p, 448] = 1152 + p
        cst[p, 449:449 + NE] = np.arange(NE) * 1280
    t = np.arange(8192)
    row = (t // 64).astype(np.float32)
    col = (t % 64).astype(np.float32)
    inv = (10000.0 ** (-np.arange(16, dtype=np.float32) / 16)).astype(np.float32)
    ang = np.concatenate([row[:, None] * inv, col[:, None] * inv], axis=-1).astype(np.float32)
    cos, sin = np.cos(ang).T, np.sin(ang).T
    tab = np.zeros((128, TA), np.float32)
    tab[0:64, 0:256] = 1.0
    tab[0:32, 256:] = cos
    tab[32:64, 256:] = cos
    tab[64:96, 256:] = -sin
    tab[96:128, 256:] = sin
    tokid = (np.arange(NT)[None, :] * 128 + np.arange(128)[:, None]).astype(np.int32)
    return cst, tab, tokid


def prep_shared(I):
    S = {}
    for k in ["w_mod", "b_mod", "g_mix", "g_ffn"]:
        S[k] = np.ascontiguousarray(I[k], dtype=np.float32)
    w_in = I["mla_w_in"]
    S["mla_w_in"] = np.ascontiguousarray(np.concatenate(
        [w_in[:, :, :384], w_in[:, :, 384:448], w_in[:, :, 416:448], w_in[:, :, 384:416]], axis=-1))
    wq = I["mla_w_uq"].reshape(2, 256, 8, 192)
    S["mla_w_uq"] = np.ascontiguousarray(np.concatenate(
        [wq[..., :128], wq[..., 128:192], wq[..., 160:192], wq[..., 128:160]], axis=-1).reshape(2, 256, 2048))
    wkv = I["mla_w_ukv"].reshape(2, 128, 8, 256)
    S["mla_w_uk"] = np.ascontiguousarray(wkv[..., :128].reshape(2, 128, 1024))
    S["mla_w_uv"] = np.ascontiguousarray(wkv[..., 128:].reshape(2, 128, 1024))
    S["mla_w_o"] = np.ascontiguousarray(I["mla_w_o"])
    gc = np.zeros((2, 128, 8), np.float32)
    for j in range(2):
        gc[j, :, 0] = I["mla_g_q"][j, :128]
        gc[j, :, 1] = I["mla_g_q"][j, 128:]
        gc[j, :, 2] = I["mla_g_kv"][j]
        for c0, g in ((3, I["mla_g_qn"][j]), (5, I["mla_g_kn"][j])):
            gc[j, :, c0] = g[:128]
            gc[j, :, c0 + 1] = np.concatenate([g[128:192], g[160:192], g[128:160]])
    S["mla_gc"] = gc
    cst, tab, tokid = make_consts()
    S["cst"], S["rope_tab"], S["tokid"] = cst, tab, tokid
    S["ssd_w_in"] = np.ascontiguousarray(I["ssd_w_in"][0])
    S["ssd_conv_w"] = np.ascontiguousarray(I["ssd_conv_w"][0])
    S["ssd_conv_b"] = np.ascontiguousarray(I["ssd_conv_b"][0])
    S["ssd_dt_bias"] = np.ascontiguousarray(I["ssd_dt_bias"][0].reshape(64))
    S["ssd_a_log"] = np.ascontiguousarray(I["ssd_a_log"][0])
    S["ssd_d"] = np.ascontiguousarray(I["ssd_d"][0])
    S["ssd_g_norm"] = np.ascontiguousarray(I["ssd_g_norm"][0])
    S["ssd_w_out"] = np.ascontiguousarray(I["ssd_w_out"][0])
    kk = np.arange(128)
    c2 = np.zeros((128, 640), np.float32)
    c2[:, 0:128] = (kk[:, None] <= kk[None, :])
    c2[:, 128:256] = (kk[:, None] >= kk[None, :])
    c2[:, 256:384] = (kk[:, None] > kk[None, :])
    c2[:, 384:512] = (kk[:, None] < kk[None, :])
    c2[:, 512:640] = 1.0
    S["cst2"] = c2
    S["fnet_w_o"] = np.ascontiguousarray(I["fnet_w_o"][0])
    fc_ = np.zeros((128, 2176), np.float64)
    p_ = np.arange(128)
    m_ = np.arange(256)
    for q in range(2):
        angn = 2 * np.pi * np.outer(128 * q + p_, m_) / 256.0
        fc_[:, q * 512:q * 512 + 256] = np.cos(angn)
        fc_[:, q * 512 + 256:q * 512 + 512] = -np.sin(angn)
        fc_[:, 1024 + q * 512:1024 + q * 512 + 256] = np.cos(angn)
        fc_[:, 1024 + q * 512 + 256:1024 + q * 512 + 512] = np.sin(angn)
    a64 = np.arange(64)
    ang1 = 2 * np.pi * np.outer(a64, a64) / 64.0
    wr, wi = np.cos(ang1), -np.sin(ang1)
    fc_[0:64, 2048:2112] = wr.T
    fc_[64:128, 2048:2112] = -wi.T
    fc_[0:64, 2112:2176] = wi.T
    fc_[64:128, 2112:2176] = wr.T
    S["fcst"] = fc_.astype(np.float32)
    t2 = np.arange(128)[:, None, None]
    k1 = np.arange(64)[None, :, None]
    k2 = np.arange(128)[None, None, :]
    angm = 2 * np.pi * (k1 * t2 / 8192.0 + k2 * t2 / 128.0)
    fM = np.zeros((128, 64, 2, 128), np.float64)
    fM[:, :, 0, :] = np.cos(angm)
    fM[:, :, 1, :] = np.sin(angm)
    S["fM"] = fM.reshape(128, 16384).astype(np.float32)
    tf = (np.arange(NT)[None, :] * 128 + np.arange(128)[:, None]).astype(np.int32)
    for a in range(64):
        tf[:, 2 + a] = 256 + a + 64 * np.arange(128)
    S["tokid_f"] = tf
    S["moe_wr"] = np.ascontiguousarray(I["moe_w_router"])
    S["moe_wg"] = np.ascontiguousarray(I["moe_w_gate"])
    S["moe_wu"] = np.ascontiguousarray(I["moe_w_up"])
    S["moe_wd"] = np.ascontiguousarray(I["moe_w_down"])
    return S


def prep_core(I, S, b):
    m = dict(S)
    m["x"] = np.ascontiguousarray(I["x"][b])
    m["ctx"] = np.ascontiguousarray(I["ctx"][b])
    m["cc"] = np.ascontiguousarray(np.stack([I["c"][b], I["c_ctx"]], axis=0))
    return m


_CACHE = {}


def build_program():
    nc = bass.Bass("TRN2", target_bir_lowering=False)
    k = K(nc)
    k.setup()
    for i in range(4):
        k.layer(i)
    P = k.P
    for j in range(8):
        P.ld("sp", k.out[j * 1024:(j + 1) * 1024, :], k.XS[256 + j * 1024:256 + (j + 1) * 1024, :], R=[], W=[Res("u")])
    P.barrier()
    P.emit()
    P.close()
    return nc, k


def kernel(**inputs):
    from concourse.bass_utils import run_bass_kernel_spmd
    I = {k_: np.asarray(v) for k_, v in inputs.items()}
    if "prog" not in _CACHE:
        _CACHE["prog"] = build_program()
    nc, k = _CACHE["prog"]
    S = prep_shared(I)
    n = 8
    in_maps = [prep_core(I, S, b) for b in range(n)]
    res = run_bass_kernel_spmd(nc, in_maps, core_ids=list(range(n)))
    return np.stack([np.asarray(r["out"]) for r in res.results], axis=0).astype(np.float32)
```
